# Optimizing a Trainium2 kernel written in Bass

```python
import math, functools
import jax, jax.numpy as jnp
from jax import lax
import numpy as np

D_MODEL = 1024
BATCH = 8
SEQ = 2048
DEPTH = 1
DEC_BATCH = 128
DEC_SEQ = 1
PAST_LEN = 8192
PAGE_SIZE = 128

N_META = 16
MLA_HEADS = 8
QK_NOPE = 64
QK_ROPE = 32
V_DIM = 64
Q_LORA = 384
KV_LORA = 256
KV_WIDTH = KV_LORA + QK_ROPE
ROPE_BASE = 10000.0
Q_BLOCK = 128
SM_SCALE = (QK_NOPE + QK_ROPE) ** -0.5
RWKV_HEADS = 8
HEAD_SIZE = 64
RWKV_WIDTH = RWKV_HEADS * HEAD_SIZE
W_LORA = 64
A_LORA = 64
G_LORA = 128
RWKV_COLS = 3 * RWKV_WIDTH + W_LORA + A_LORA + G_LORA
RWKV_SPLITS = [RWKV_WIDTH, 2 * RWKV_WIDTH, 3 * RWKV_WIDTH, 3 * RWKV_WIDTH + W_LORA, 3 * RWKV_WIDTH + W_LORA + A_LORA]
GN_EPS = 64e-5
MLA_COLS = Q_LORA + KV_LORA + QK_ROPE
IN_COLS = MLA_COLS + RWKV_COLS + 2 * D_MODEL
IN_SPLITS = [Q_LORA, Q_LORA + KV_LORA, MLA_COLS, MLA_COLS + RWKV_COLS]
D_FF = 4 * D_MODEL
NORM_EPS = 1e-6
NEG_INF = -1e30

kernel_name = 'mla_rwkv7_gated_hybrid_step'


def rms_norm(x, g):
    xf = x.astype(jnp.float32)
    y = xf * lax.rsqrt(jnp.mean(xf * xf, axis=-1, keepdims=True) + NORM_EPS)
    return (y * g.astype(jnp.float32)).astype(x.dtype)


def rope_angles(pos):
    inv = ROPE_BASE ** (-jnp.arange(0, QK_ROPE, 2, dtype=jnp.float32) / QK_ROPE)
    ang = pos.astype(jnp.float32)[:, None] * inv[None, :]
    return jnp.cos(ang), jnp.sin(ang)


def apply_rope(x, cos, sin):
    xf = x.astype(jnp.float32)
    x1, x2 = xf[..., :QK_ROPE // 2], xf[..., QK_ROPE // 2:]
    return jnp.concatenate([x1 * cos - x2 * sin, x1 * sin + x2 * cos], axis=-1).astype(x.dtype)


def project_inputs(h, lw, pos):
    B, T = h.shape[:2]
    z = h @ lw['w_in']
    q_in, kv_in, kr_in, rw_cols, gate_in = jnp.split(z, IN_SPLITS, axis=-1)
    cos, sin = rope_angles(pos)
    q = (rms_norm(q_in, lw['g_q']) @ lw['w_uq']).reshape(B, T, MLA_HEADS, QK_NOPE + QK_ROPE)
    q_nope = q[..., :QK_NOPE]
    q_rope = apply_rope(q[..., QK_NOPE:], cos[:, None, :], sin[:, None, :])
    kv_rows = jnp.concatenate([rms_norm(kv_in, lw['g_kv']), apply_rope(kr_in, cos, sin)], axis=-1)
    return q_nope, q_rope, kv_rows, rw_cols, gate_in


def mla_attend_prompt(q_nope, q_rope, kv_rows, lw):
    f32 = jnp.float32
    B, T = kv_rows.shape[:2]
    c_kv = kv_rows[..., :KV_LORA].astype(f32)
    k_rope = kv_rows[..., KV_LORA:].astype(f32)
    k_nope = jnp.einsum('btc,chn->bthn', c_kv, lw['w_uk'].astype(f32).reshape(KV_LORA, MLA_HEADS, QK_NOPE))
    v = jnp.einsum('btc,chv->bthv', c_kv, lw['w_uv'].astype(f32).reshape(KV_LORA, MLA_HEADS, V_DIM))
    nb = -(-T // Q_BLOCK)
    tp = nb * Q_BLOCK

    def to_blocks(t):
        t = jnp.pad(t.astype(f32), [(0, 0), (0, tp - T)] + [(0, 0)] * (t.ndim - 2))
        return jnp.moveaxis(t.reshape((B, nb, Q_BLOCK) + t.shape[2:]), 1, 0)

    k_pos = jnp.arange(T)

    def one_block(args):
        qn, qr, i = args
        q_pos = i * Q_BLOCK + jnp.arange(Q_BLOCK)
        s = (jnp.einsum('bqhn,bkhn->bhqk', qn, k_nope) + jnp.einsum('bqhr,bkr->bhqk', qr, k_rope)) * SM_SCALE
        s = jnp.where(k_pos[None, :] <= q_pos[:, None], s, NEG_INF)
        return jnp.einsum('bhqk,bkhv->bqhv', jax.nn.softmax(s, axis=-1), v)

    o = lax.map(one_block, (to_blocks(q_nope), to_blocks(q_rope), jnp.arange(nb)))
    return jnp.moveaxis(o, 0, 1).reshape(B, tp, MLA_HEADS, V_DIM)[:, :T]


def softmax_merge(carry, s, vals):
    m, l, acc = carry
    m_new = jnp.maximum(m, s.max(axis=-1))
    alpha = jnp.exp(m - m_new)
    p = jnp.exp(s - m_new[..., None])
    l = l * alpha + p.sum(axis=-1)
    acc = acc * alpha[..., None] + jnp.einsum('bhqk,bkc->bhqc', p, vals)
    return (m_new, l, acc)


def mla_attend_sample(q_nope, q_rope, kv_rows, lw, cache_kv, page_table):
    f32 = jnp.float32
    Bd, S = kv_rows.shape[:2]
    w_uk = lw['w_uk'].astype(f32).reshape(KV_LORA, MLA_HEADS, QK_NOPE)
    w_uv = lw['w_uv'].astype(f32).reshape(KV_LORA, MLA_HEADS, V_DIM)
    q_lat = jnp.einsum('bqhn,chn->bqhc', q_nope.astype(f32), w_uk)
    qf = jnp.concatenate([q_lat, q_rope.astype(f32)], axis=-1) * SM_SCALE

    def page_step(carry, pages):
        rows = cache_kv[pages].astype(f32)
        s = jnp.einsum('bqhc,bkc->bhqk', qf, rows)
        return softmax_merge(carry, s, rows[..., :KV_LORA]), None

    init = (jnp.full((Bd, MLA_HEADS, S), NEG_INF, f32),
            jnp.zeros((Bd, MLA_HEADS, S), f32),
            jnp.zeros((Bd, MLA_HEADS, S, KV_LORA), f32))
    carry, _ = lax.scan(page_step, init, page_table.T)
    kvn = kv_rows.astype(f32)
    s = jnp.einsum('bqhc,bkc->bhqk', qf, kvn)
    s = jnp.where(jnp.tril(jnp.ones((S, S), bool)), s, NEG_INF)
    m, l, acc = softmax_merge(carry, s, kvn[..., :KV_LORA])
    o_lat = acc / l[..., None]
    return jnp.einsum('bhqc,chv->bqhv', o_lat, w_uv)


def wkv_scan(S0, r, w, k, v, a, b):
    def step(S, inp):
        r_t, w_t, k_t, v_t, a_t, b_t = inp
        sa = jnp.einsum('bhvk,bhk->bhv', S, a_t)
        S = S * w_t[:, :, None, :] + sa[..., None] * b_t[:, :, None, :] + v_t[..., None] * k_t[:, :, None, :]
        return S, jnp.einsum('bhvk,bhk->bhv', S, r_t)
    xs = (jnp.swapaxes(r, 0, 1), jnp.swapaxes(w, 0, 1), jnp.swapaxes(k, 0, 1),
          jnp.swapaxes(v, 0, 1), jnp.swapaxes(a, 0, 1), jnp.swapaxes(b, 0, 1))
    S, y = lax.scan(step, S0, xs)
    return S, jnp.swapaxes(y, 0, 1)


def rwkv_branch(cols, shift0, wkv0, lw):
    f32 = jnp.float32
    B, T = cols.shape[:2]
    c = cols.astype(f32)
    prev = jnp.concatenate([shift0[:, None].astype(f32), c[:, :-1]], axis=1)
    z = c + (prev - c) * lw['mu_shift'].astype(f32)
    r, k, v, w_in, a_in, g_in = jnp.split(z, RWKV_SPLITS, axis=-1)
    w_log = lw['w0'].astype(f32) + jnp.tanh(w_in) @ lw['w2'].astype(f32)
    decay = jnp.exp(-jnp.exp(-jax.nn.softplus(-w_log) - 0.5))
    a = jax.nn.sigmoid(lw['a0'].astype(f32) + a_in @ lw['a2'].astype(f32))
    g = jax.nn.sigmoid(g_in) @ lw['g2'].astype(f32)
    hs = (RWKV_HEADS, HEAD_SIZE)
    r, k, v, decay, a = [t.reshape(B, T, RWKV_HEADS, HEAD_SIZE) for t in (r, k, v, decay, a)]
    kk = k * lw['k_k'].astype(f32).reshape(hs)
    kk = kk / jnp.maximum(jnp.sqrt(jnp.sum(kk * kk, axis=-1, keepdims=True)), 1e-12)
    k = k * (1.0 + (a - 1.0) * lw['k_a'].astype(f32).reshape(hs))
    S, y = wkv_scan(wkv0.astype(f32), r, decay, k, v, -kk, kk * a)
    mu = jnp.mean(y, axis=-1, keepdims=True)
    var = jnp.mean(jnp.square(y - mu), axis=-1, keepdims=True)
    y = (y - mu) * lax.rsqrt(var + GN_EPS) * lw['ln_w'].astype(f32).reshape(hs) + lw['ln_b'].astype(f32).reshape(hs)
    y = y + jnp.sum(r * k * lw['r_k'].astype(f32), axis=-1, keepdims=True) * v
    return y.reshape(B, T, RWKV_WIDTH) * g, S


def hybrid_layer(x, pos, attend, wkv0, shift0, lw):
    B, T = x.shape[:2]
    h = rms_norm(x, lw['g_mix'])
    q_nope, q_rope, kv_rows, rw_cols, gate_in = project_inputs(h, lw, pos)
    o_mla = attend(q_nope, q_rope, kv_rows, lw).reshape(B, T, MLA_HEADS * V_DIM).astype(x.dtype) @ lw['w_o_mla']
    o_rwkv, wkv_new = rwkv_branch(rw_cols, shift0, wkv0, lw)
    o_rwkv = o_rwkv.astype(x.dtype) @ lw['w_o_rwkv']
    g_mla, g_rwkv = jnp.split(jax.nn.sigmoid(gate_in), 2, axis=-1)
    x = x + (g_mla * o_mla + g_rwkv * o_rwkv) @ lw['w_out']
    h2 = rms_norm(x, lw['g_ffn'])
    x = x + jnp.square(jax.nn.relu(h2 @ lw['w_up'])) @ lw['w_down']
    return x, kv_rows, wkv_new, rw_cols[:, -1]


def setup_inputs(seed: int = 0) -> dict:
    key = jax.random.key(seed)
    keys = iter(jax.random.split(key, 48))

    def nrm(shape, scale):
        return scale * jax.random.normal(next(keys), shape, jnp.float32)

    def gain(shape):
        return 1.0 + nrm(shape, 0.02)

    L = DEPTH
    n_pages = PAST_LEN // PAGE_SIZE
    n_used = DEC_BATCH * n_pages
    n_pool = (n_used * 5) // 4
    x_prompt = nrm((BATCH, SEQ, D_MODEL), 1.0)
    x_sample = nrm((DEC_BATCH, DEC_SEQ, D_MODEL), 1.0)
    cache_kv = nrm((L, n_pool, PAGE_SIZE, KV_WIDTH), 1.0)
    page_table = jax.random.permutation(next(keys), n_pool)[:n_used].reshape(DEC_BATCH, n_pages).astype(jnp.int32)
    state_wkv = nrm((L, DEC_BATCH, RWKV_HEADS, HEAD_SIZE, HEAD_SIZE), 0.5)
    state_shift = nrm((L, DEC_BATCH, RWKV_COLS), 1.0)
    return {
        'x_prompt': x_prompt,
        'x_sample': x_sample,
        'cache_kv': cache_kv,
        'page_table': page_table,
        'state_wkv': state_wkv,
        'state_shift': state_shift,
        'meta_tokens': nrm((N_META, D_MODEL), 1.0),
        'g_final': gain((D_MODEL,)),
        'g_mix': gain((L, D_MODEL)),
        'w_in': nrm((L, D_MODEL, IN_COLS), D_MODEL ** -0.5),
        'g_q': gain((L, Q_LORA)),
        'w_uq': nrm((L, Q_LORA, MLA_HEADS * (QK_NOPE + QK_ROPE)), Q_LORA ** -0.5),
        'g_kv': gain((L, KV_LORA)),
        'w_uk': nrm((L, KV_LORA, MLA_HEADS * QK_NOPE), KV_LORA ** -0.5),
        'w_uv': nrm((L, KV_LORA, MLA_HEADS * V_DIM), KV_LORA ** -0.5),
        'w_o_mla': nrm((L, MLA_HEADS * V_DIM, D_MODEL), (MLA_HEADS * V_DIM) ** -0.5),
        'mu_shift': jax.random.uniform(next(keys), (L, RWKV_COLS), jnp.float32),
        'w0': -3.0 + nrm((L, RWKV_WIDTH), 1.0),
        'w2': nrm((L, W_LORA, RWKV_WIDTH), 0.5 * W_LORA ** -0.5),
        'a0': nrm((L, RWKV_WIDTH), 0.3),
        'a2': nrm((L, A_LORA, RWKV_WIDTH), 0.5 * A_LORA ** -0.5),
        'g2': nrm((L, G_LORA, RWKV_WIDTH), G_LORA ** -0.5),
        'k_k': 0.85 + nrm((L, RWKV_WIDTH), 0.02),
        'k_a': 1.0 + nrm((L, RWKV_WIDTH), 0.02),
        'r_k': nrm((L, RWKV_HEADS, HEAD_SIZE), 0.1),
        'ln_w': gain((L, RWKV_WIDTH)),
        'ln_b': nrm((L, RWKV_WIDTH), 0.02),
        'w_o_rwkv': nrm((L, RWKV_WIDTH, D_MODEL), RWKV_WIDTH ** -0.5),
        'w_out': nrm((L, D_MODEL, D_MODEL), D_MODEL ** -0.5),
        'g_ffn': gain((L, D_MODEL)),
        'w_up': nrm((L, D_MODEL, D_FF), D_MODEL ** -0.5),
        'w_down': nrm((L, D_FF, D_MODEL), D_FF ** -0.5),
    }


def reference(x_prompt, x_sample, cache_kv, page_table, state_wkv, state_shift, meta_tokens, g_final,
              g_mix, w_in, g_q, w_uq, g_kv, w_uk, w_uv, w_o_mla, mu_shift, w0, w2, a0, a2, g2,
              k_k, k_a, r_k, ln_w, ln_b, w_o_rwkv, w_out, g_ffn, w_up, w_down):
    b_p = x_prompt.shape[0]
    s_s = x_sample.shape[1]
    meta = jnp.broadcast_to(meta_tokens[None].astype(x_prompt.dtype), (b_p, N_META, D_MODEL))
    xp = jnp.concatenate([meta, x_prompt], axis=1)
    xs = x_sample
    pos_p = jnp.arange(xp.shape[1])
    pos_s = PAST_LEN + jnp.arange(s_s)
    wkv_zero = jnp.zeros((b_p, RWKV_HEADS, HEAD_SIZE, HEAD_SIZE), jnp.float32)
    shift_zero = jnp.zeros((b_p, RWKV_COLS), jnp.float32)
    kv_p, wkv_p, sh_p, kv_s, wkv_s, sh_s = [], [], [], [], [], []
    for layer in range(DEPTH):
        lw = dict(g_mix=g_mix[layer], w_in=w_in[layer], g_q=g_q[layer], w_uq=w_uq[layer], g_kv=g_kv[layer],
                  w_uk=w_uk[layer], w_uv=w_uv[layer], w_o_mla=w_o_mla[layer], mu_shift=mu_shift[layer],
                  w0=w0[layer], w2=w2[layer], a0=a0[layer], a2=a2[layer], g2=g2[layer], k_k=k_k[layer],
                  k_a=k_a[layer], r_k=r_k[layer], ln_w=ln_w[layer], ln_b=ln_b[layer],
                  w_o_rwkv=w_o_rwkv[layer], w_out=w_out[layer], g_ffn=g_ffn[layer], w_up=w_up[layer],
                  w_down=w_down[layer])
        xp, kv, wkv, sh = hybrid_layer(xp, pos_p, mla_attend_prompt, wkv_zero, shift_zero, lw)
        kv_p.append(kv)
        wkv_p.append(wkv)
        sh_p.append(sh)
        attend_s = functools.partial(mla_attend_sample, cache_kv=cache_kv[layer], page_table=page_table)
        xs, kv, wkv, sh = hybrid_layer(xs, pos_s, attend_s, state_wkv[layer], state_shift[layer], lw)
        kv_s.append(kv)
        wkv_s.append(wkv)
        sh_s.append(sh)
    y_prompt = rms_norm(xp, g_final)[:, N_META:]
    y_sample = rms_norm(xs, g_final)
    return (y_prompt, y_sample,
            jnp.stack(kv_p).astype(cache_kv.dtype),
            jnp.stack(wkv_p).astype(state_wkv.dtype),
            jnp.stack(sh_p).astype(state_shift.dtype),
            jnp.stack(kv_s).astype(cache_kv.dtype),
            jnp.stack(wkv_s).astype(state_wkv.dtype),
            jnp.stack(sh_s).astype(state_shift.dtype))
```

```python
import numpy as np
from contextlib import ExitStack
import concourse.bass as bass
import concourse.mybir as mybir
from concourse.bass_utils import run_bass_kernel_spmd

F32 = mybir.dt.float32
BF16 = mybir.dt.bfloat16
I32 = mybir.dt.int32
AF = mybir.ActivationFunctionType
ALU = mybir.AluOpType
AX = mybir.AxisListType

D = 1024
NQ, NKV, NRP = 384, 256, 32
KVW = 288
H = 8
RW = 512
RC = 1792
INC = 4512
C_Q, C_KV, C_RW, C_G = 0, 384, 672, 2464
DFF = 4096
NORM_EPS = 1e-6
GN_EPS = 64e-5
SM_SCALE = 96 ** -0.5
NMETA = 16
SPC = 16

WNAMES = ["g_final", "g_mix", "w_in", "g_q", "w_uq", "g_kv", "w_uk", "w_uv", "w_o_mla", "mu_shift", "w0", "w2",
          "a0", "a2", "g2", "k_k", "k_a", "r_k", "ln_w", "ln_b", "w_o_rwkv", "w_out", "g_ffn", "w_up", "w_down"]


def apx(base, part, free):
    return bass.AP(base.tensor, base.offset, [list(part)] + [list(f) for f in free])


class B:
    ENG = ("pe", "act", "dve", "pool", "sp")

    def __init__(self, nc, st):
        self.nc, self.st, self.st0 = nc, st, st
        self.q = {e: [] for e in self.ENG}
        if getattr(self, "_after_barrier", False):
            self.new_engine_sems()
            self._after_barrier = False
        self.sems = {}
        self.cnt = {}
        self.ek = {}
        self.phase = 0
        self.new_engine_sems()
        self.waited = {e: {} for e in self.ENG}
        self.bufs = {}
        self.nps = 0

    def new_engine_sems(self):
        self.phase += 1
        for e in ("pe", "act", "dve", "pool"):
            k = f"{e}#{self.phase}"
            self.sems[k] = self.st0.enter_context(self.nc.semaphore("s_" + k.replace("#", "_")))
            self.cnt[k] = 0
            self.ek[e] = k

    def sb(self, name, shape, dt):
        return self.st.enter_context(self.nc.sbuf_tensor(name, list(shape), dt))

    def psum(self, name, shape, dt):
        return self.st0.enter_context(self.nc.psum_tensor(name, list(shape), dt))

    def _buf(self, k):
        if k not in self.bufs:
            self.bufs[k] = {"w": None, "r": {}}
        return self.bufs[k]

    def _waits(self, eng, R, W):
        toks = {}

        def need(t):
            if t is not None:
                toks[t[0]] = max(toks.get(t[0], 0), t[1])
        for k in R:
            need(self._buf(k)["w"])
        for k in W:
            b = self._buf(k)
            need(b["w"])
            for kk, v in b["r"].items():
                need((kk, v))
        for k, v in toks.items():
            if eng == "pe" and k == self.ek["pe"]:
                continue
            if "#" not in k:
                v = self.cnt[k]
            if self.waited[eng].get(k, 0) >= v:
                continue
            self.waited[eng][k] = v
            sem = self.sems[k]
            self.q[eng].append(lambda e, sem=sem, v=v: e.wait_ge(sem, v))

    def _mark(self, tok, R, W):
        for k in R:
            b = self._buf(k)
            b["r"][tok[0]] = max(b["r"].get(tok[0], 0), tok[1])
        for k in W:
            self.bufs[k] = {"w": tok, "r": {}}

    def op(self, eng, fn, R=(), W=()):
        self._waits(eng, R, W)
        k = self.ek[eng]
        self.cnt[k] += 1
        sem = self.sems[k]
        self.q[eng].append(lambda e, fn=fn, sem=sem: fn(e).then_inc(sem, 1))
        self._mark((k, self.cnt[k]), R, W)

    DRAM_KEYS = ("ats", "ygs", "kvs", "x1s")

    def _semkey(self, R, W):
        if W and W[0] not in self.DRAM_KEYS:
            k = "i_" + W[0]
        else:
            k = "o_" + R[0]
        if k not in self.sems:
            self.sems[k] = self.st0.enter_context(self.nc.semaphore(k))
            self.cnt[k] = 0
        return k

    def dma(self, queue, semkey, out, in_, R=(), W=(), **kw):
        semkey = self._semkey(R, W)
        self._waits(queue, R, W)
        self.cnt[semkey] += 16
        sem = self.sems[semkey]
        self.q[queue].append(lambda e, sem=sem, out=out, in_=in_, kw=kw: e.dma_start(out=out, in_=in_, **kw).then_inc(sem, 16))
        self._mark((semkey, self.cnt[semkey]), R, W)

    def raw_dma(self, queue, semkey, fn, R=(), W=()):
        semkey = self._semkey(R, W)
        self._waits(queue, R, W)
        self.cnt[semkey] += 16
        sem = self.sems[semkey]
        self.q[queue].append(lambda e, sem=sem, fn=fn: fn(e).then_inc(sem, 16))
        self._mark((semkey, self.cnt[semkey]), R, W)

    def finish(self):
        for k, sem in self.sems.items():
            v = self.cnt[k]
            if v and self.waited["sp"].get(k, 0) < v:
                self.waited["sp"][k] = v
                self.q["sp"].append(lambda e, sem=sem, v=v: e.wait_ge(sem, v))

    def barrier(self):
        self._after_barrier = True
        for e in self.ENG:
            for k, sem in self.sems.items():
                v = self.cnt[k]
                if v and self.waited[e].get(k, 0) < v and not (k == self.ek.get(e)):
                    self.waited[e][k] = v
                    self.q[e].append(lambda eng, sem=sem, v=v: eng.wait_ge(sem, v))

    def emit(self):
        with self.nc.Block() as blk:
            for name, deco in (("pe", blk.tensor), ("act", blk.scalar), ("dve", blk.vector),
                               ("pool", blk.gpsimd), ("sp", blk.sync)):
                lst = self.q[name]

                def run(e, lst=lst):
                    for th in lst:
                        th(e)
                deco(run)
        self.q = {e: [] for e in self.ENG}


def host_consts(cfg):
    T, NTOK, NTL = cfg["T"], cfg["NTOK"], cfg["NTL"]
    c = {}
    idx = np.arange(128)
    c["ident"] = np.eye(128, dtype=np.float32)
    same = (idx[:, None] // 16) == (idx[None, :] // 16)
    c["maskx"] = (same & (idx[None, :] < idx[:, None])).astype(np.float32)
    mj = np.zeros((128, 2, 128), np.float32)
    mj[:, 0, :] = same & (idx[None, :] > idx[:, None])
    mj[:, 1, :] = same & (idx[None, :] >= idx[:, None])
    c["maskjt"] = mj.reshape(128, 256)
    c["cmask"] = (idx[:, None] // 16 == np.arange(8)[None, :]).astype(np.float32)
    c["blk1"] = ((idx[:, None] // 64) == (idx[None, :] // 64)).astype(np.float32)
    c["maskc"] = (idx[None, :] >= idx[:, None]).astype(np.float32)
    c["identb"] = (idx[:, None] % 64 == np.arange(64)[None, :]).astype(np.float32)
    rm = np.ones((128, 512), np.float32)
    rm[:, ::16] = 0.0
    c["scanm"] = rm
    oh = np.zeros((128, 16, 128), np.float32)
    for s in range(16):
        oh[s, s, 0:64] = 1.0
        oh[32 + s, s, 64:128] = 1.0
    c["onehot"] = oh.reshape(128, 2048)
    inv = (10000.0 ** (-np.arange(0, 32, 2, dtype=np.float32) / np.float32(32))).astype(np.float32)
    pos = np.concatenate([np.arange(T), np.full(SPC, cfg["PAST"])]).astype(np.float32)
    ang = (pos[:, None] * inv[None, :]).astype(np.float32)
    tab = np.zeros((NTL * 128, 32), np.float32)
    tab[:NTOK, :16] = np.cos(ang)
    tab[:NTOK, 16:] = np.sin(ang)
    c["rope"] = np.ascontiguousarray(tab.reshape(NTL, 128, 32).transpose(1, 0, 2)).reshape(128, NTL * 32)
    return c


def make_cfg(seq, npg, npool, past):
    T = seq + NMETA
    assert seq % 128 == 0
    NTOK = T + SPC
    NTL = (NTOK + 127) // 128
    return dict(SEQ=seq, T=T, NTOK=NTOK, NTL=NTL, NPG=npg, NPOOL=npool, PAST=past)


def build(cfg, stage=99):
    SEQ, T, NTOK, NTL, NPG, NPOOL = cfg["SEQ"], cfg["T"], cfg["NTOK"], cfg["NTL"], cfg["NPG"], cfg["NPOOL"]
    nc = bass.Bass("TRN2", target_bir_lowering=False)
    consts = host_consts(cfg)

    def din(name, shape, dt=F32):
        return nc.dram_tensor(name, list(shape), dt, kind="ExternalInput").ap()

    def dout(name, shape, dt=F32):
        return nc.dram_tensor(name, list(shape), dt, kind="ExternalOutput").ap()

    I = {}
    I["xp"] = din("xp", [SEQ, D])
    I["xs"] = din("xs", [SPC, D])
    I["meta"] = din("meta", [NMETA, D])
    I["cache"] = din("cache", [NPOOL, 128, KVW])
    I["pt"] = din("pt", [SPC, NPG], I32)
    I["swkv"] = din("swkv", [SPC, H, 64, 64])
    I["sshift"] = din("sshift", [SPC, RC])
    wshapes = dict(g_final=[1, D], g_mix=[1, D], w_in=[D, INC], g_q=[1, NQ], w_uq=[NQ, 768], g_kv=[1, NKV],
                   w_uk=[NKV, 512], w_uv=[NKV, 512], w_o_mla=[512, D], mu_shift=[1, RC], w0=[1, RW], w2=[64, RW],
                   a0=[1, RW], a2=[64, RW], g2=[128, RW], k_k=[1, RW], k_a=[1, RW], r_k=[1, RW], ln_w=[1, RW],
                   ln_b=[1, RW], w_o_rwkv=[RW, D], w_out=[D, D], g_ffn=[1, D], w_up=[D, DFF], w_down=[DFF, D])
    for n in WNAMES:
        I[n] = din(n, wshapes[n])
    for n, a in consts.items():
        I["c_" + n] = din("c_" + n, a.shape)
    O = {}
    O["y_p"] = dout("y_p", [SEQ, D])
    O["y_s"] = dout("y_s", [SPC, D])
    O["kv_p"] = dout("kv_p", [T, KVW])
    O["wkv_p"] = dout("wkv_p", [H, 64, 64])
    O["sh_p"] = dout("sh_p", [1, RC])
    O["kv_s"] = dout("kv_s", [SPC, KVW])
    O["wkv_s"] = dout("wkv_s", [SPC, H, 64, 64])
    O["sh_s"] = dout("sh_s", [SPC, RC])

    st = ExitStack()
    with st:
        b = B(nc, st)
        _program(nc, b, cfg, I, O, stage)
        b.finish()
        b.emit()
    return nc, consts


def _program(nc, b, cfg, I, O, stage):
    import os
    KA1 = int(os.environ.get('KA1', '3'))
    KQ = int(os.environ.get('KQ', '9'))
    KR = int(os.environ.get('KR', '9'))
    SEQ, T, NTOK, NTL, NPG, NPOOL = cfg["SEQ"], cfg["T"], cfg["NTOK"], cfg["NTL"], cfg["NPG"], cfg["NPOOL"]
    NFULL = NTL - 1
    LASTW = NTOK - NFULL * 128
    PP2 = 2 * NPG

    def tw(i):
        return 128 if i < NFULL else LASTW

    PS = [b.psum(f"ps{i}", [128, 512], F32) for i in range(8)]
    psk = [f"ps{i}" for i in range(8)]
    psn = [0]

    reserved = set()

    def nextps():
        while True:
            i = psn[0] % 8
            psn[0] += 1
            if i not in reserved:
                return i

    def reserve():
        i = nextps()
        reserved.add(i)
        return i

    def release(i):
        reserved.discard(i)

    def flat(name, dt=F32, n=512):
        return b.sb(name, [128, n], dt)

    def v3(t, w, g=4):
        return t[:, 0:g * w].rearrange("p (g t) -> p g t", g=g)

    ygs = nc.dram_tensor("ygs", [RW, NTL * 128], BF16, kind="Internal").ap()
    ats = nc.dram_tensor("ats", [RW, NTL * 128], BF16, kind="Internal").ap()
    kvs = nc.dram_tensor("kvs", [SPC, KVW], F32, kind="Internal").ap()
    x1s = nc.dram_tensor("x1s", [NTL * 128, D], F32, kind="Internal").ap()

    ident_f = b.sb("ident_f", [128, 128], F32)
    ident_b = b.sb("ident_b", [128, 128], BF16)
    b.dma("sp", "c0", ident_f[:], I["c_ident"], W=["ident_f"])
    b.op("dve", lambda e: e.tensor_copy(out=ident_b[:], in_=ident_f[:]), R=["ident_f"], W=["ident_b"])
    rope = b.sb("rope", [128, NTL, 32], F32)
    b.dma("sp", "c0", rope[:].rearrange("p a b -> p (a b)"), I["c_rope"], W=["rope"])

    def bcast_load(name, src, n):
        t = b.sb(name, [128, n], F32)
        b.dma("sp", "c0", t[:], src.partition_broadcast(128).rearrange("p o n -> p (o n)"), W=[name])
        return t

    gmixB = bcast_load("gmixB", I["g_mix"], D)
    xt = [b.sb(f"xt{i}", [128, D], F32) for i in range(2)]
    junk = b.sb("junk", [128, D], F32)
    hb = b.sb("hb", [128, D], BF16)
    hT = b.sb("hT", [128, 8, 128], BF16)
    st1 = b.sb("st1", [128, 8], F32)
    rt1 = b.sb("rt1", [128, H, 16], F32)
    rt2 = b.sb("rt2", [128, H, 16], F32)
    qsT = b.sb("qsT", [128, H, 16], BF16)

    def load_x(i, buf, src_scratch=False):
        w = tw(i)
        k = f"xt{buf}"
        if src_scratch:
            b.dma("sp", k, xt[buf][0:w, :], x1s[128 * i:128 * i + w, :], R=["x1s"], W=[k])
            return
        if i == 0:
            b.dma("sp", k, xt[buf][0:16, :], I["meta"], W=[k])
            b.dma("sp", k, xt[buf][16:128, :], I["xp"][0:112, :], W=[k])
        elif i < NFULL:
            b.dma("sp", k, xt[buf][:, :], I["xp"][128 * i - 16:128 * i + 112, :], W=[k])
        else:
            b.dma("sp", k, xt[buf][0:16, :], I["xp"][SEQ - 16:SEQ, :], W=[k])
            b.dma("sp", k, xt[buf][16:32, :], I["xs"], W=[k])

    def rms_rstd(src_ap, srckeys, w, n, col, eps):
        b.op("act", lambda e: e.activation(out=junk[0:w, 0:n], in_=src_ap, func=AF.Square), R=srckeys, W=["junk"])
        b.op("dve", lambda e: e.tensor_reduce(out=st1[0:w, col:col + 1], in_=junk[0:w, 0:n], axis=AX.X, op=ALU.add),
             R=["junk"], W=[f"st1_{col}"])
        b.op("act", lambda e: e.activation(out=st1[0:w, col:col + 1], in_=st1[0:w, col:col + 1], func=AF.Sqrt,
                                           bias=eps, scale=1.0 / n), R=[f"st1_{col}"], W=[f"st1_{col}"])
        b.op("dve", lambda e: e.reciprocal(out=st1[0:w, col:col + 1], in_=st1[0:w, col:col + 1]),
             R=[f"st1_{col}"], W=[f"st1_{col}"])

    def rope_tm(dst, src, w, i, nh, keysR, keysW):
        cosb = apx(rope[0:w, i, 0:16], [rope[:].ap[0][0], w], [[0, nh], [1, 16]])
        sinb = apx(rope[0:w, i, 16:32], [rope[:].ap[0][0], w], [[0, nh], [1, 16]])
        x1, x2 = src[:, :, 0:16], src[:, :, 16:32]
        t1, t2 = rt1[0:w, 0:nh, :], rt2[0:w, 0:nh, :]
        b.op("dve", lambda e: e.tensor_tensor(out=t1, in0=x1, in1=cosb, op=ALU.mult), R=keysR + ["rope"], W=["rt1"])
        b.op("dve", lambda e: e.tensor_tensor(out=t2, in0=x2, in1=sinb, op=ALU.mult), R=keysR + ["rope"], W=["rt2"])
        b.op("dve", lambda e: e.tensor_tensor(out=dst[:, :, 0:16], in0=t1, in1=t2, op=ALU.subtract), R=["rt1", "rt2"], W=keysW)
        b.op("dve", lambda e: e.tensor_tensor(out=t1, in0=x1, in1=sinb, op=ALU.mult), R=keysR + ["rope"], W=["rt1"])
        b.op("dve", lambda e: e.tensor_tensor(out=t2, in0=x2, in1=cosb, op=ALU.mult), R=keysR + ["rope"], W=["rt2"])
        b.op("dve", lambda e: e.tensor_tensor(out=dst[:, :, 16:32], in0=t1, in1=t2, op=ALU.add), R=["rt1", "rt2"], W=keysW)

    def norm_and_transpose(i, buf, gB, gkey, dst=None, dkey="hT"):
        w = tw(i)
        k = f"xt{buf}"
        dst = hT if dst is None else dst
        rms_rstd(xt[buf][0:w, :], [k], w, D, 0, NORM_EPS)
        b.op("dve", lambda e: e.scalar_tensor_tensor(out=hb[0:w, :], in0=xt[buf][0:w, :], scalar=st1[0:w, 0:1],
                                                     in1=gB[0:w, :], op0=ALU.mult, op1=ALU.mult),
             R=[k, "st1_0", gkey], W=["hb"])
        p = nextps()
        pv = PS[p][:].bitcast(BF16)
        for kc in range(8):
            b.op("pe", lambda e, kc=kc: e.transpose(out=pv[:, kc * 128:kc * 128 + w], in_=hb[0:w, kc * 128:(kc + 1) * 128],
                                                    identity=ident_b[0:w, 0:w]), R=["hb", "ident_b"], W=[psk[p]])
        src = pv[:, 0:1024].rearrange("p (k t) -> p k t", k=8)[:, :, 0:w]
        b.op("act", lambda e: e.copy(out=dst[:, :, 0:w], in_=src), R=[psk[p]], W=[dkey])

    def load_w_bf(name, shape, src, key=None):
        t = b.sb(name, shape, BF16)
        b.dma("pool", "wA", t[:], src, W=[key or name])
        return t

    b.emit()

    def phase_A1():
        NW = C_RW
        w_in_b = b.sb("w_in_a", [128, 8, NW], BF16)
        for kc in range(8):
            b.dma("pool", "wA", w_in_b[:, kc, :], I["w_in"][kc * 128:(kc + 1) * 128, 0:NW], W=["w_in_a"])
        w_uq_b = load_w_bf("w_uq_b", [128, 3, 768], I["w_uq"].rearrange("(k p) n -> p k n", p=128))
        w_uk_b = load_w_bf("w_uk_b", [128, 2, 512], I["w_uk"].rearrange("(k p) n -> p k n", p=128))
        w_uv_b = load_w_bf("w_uv_b", [128, 2, 512], I["w_uv"].rearrange("(k p) n -> p k n", p=128))
        maskc_b = load_w_bf("maskc_b", [128, 128], I["c_maskc"])
        gqB = bcast_load("gqB", I["g_q"], NQ)
        gkvB = bcast_load("gkvB", I["g_kv"], NKV)
        KT = b.sb("KT", [128, H, NFULL * 128 + 128], BF16)
        Vc = b.sb("Vc", [128, NTL, H, 65], BF16)
        b.op("pool", lambda e: e.memset(Vc[:], 1.0), W=["Vc"])
        qT = b.sb("qT", [128, H, 128], BF16)
        qn = b.sb("qn", [128, NQ], BF16)
        qf32 = b.sb("qf32", [128, 384], F32)
        qnT = b.sb("qnT", [128, 3, 128], BF16)
        qtm = b.sb("qtm", [128, H, 128], BF16)
        b.op("pool", lambda e: e.memset(qtm[:], 0.0), W=["qtm"])
        kvrow = b.sb("kvrow", [128, KVW], F32)
        kvb = b.sb("kvb", [128, 320], BF16)
        b.op("pool", lambda e: e.memset(kvb[:], 0.0), W=["kvb"])
        ckvT = b.sb("ckvT", [128, 2, 128], BF16)
        PT = [b.sb(f"PT{j}", [128, 4, 128], BF16) for j in range(2)]
        attn_tm = b.sb("attn_tm", [128, 512], BF16)
        attnT = b.sb("attnT", [128, 4, 128], BF16)
        rec = b.sb("rec", [128, 8], F32)
        osb = b.sb("osb", [128, 260], F32)

        def tile(i):
            w = tw(i)
            wk = 128 if i < NFULL else 16
            buf = i % 2
            if i + 1 < NTL:
                load_x(i + 1, (i + 1) % 2)
            norm_and_transpose(i, buf, gmixB, "gmixB")
            pq, pkv = nextps(), nextps()
            for kc in range(8):
                b.op("pe", lambda e, kc=kc: e.matmul(PS[pq][0:w, 0:NQ], lhsT=hT[:, kc, 0:w], rhs=w_in_b[:, kc, 0:NQ],
                                                     start=(kc == 0), stop=(kc == 7)), R=["hT", "w_in_a"], W=[psk[pq]])
            for kc in range(8):
                b.op("pe", lambda e, kc=kc: e.matmul(PS[pkv][0:w, 0:KVW], lhsT=hT[:, kc, 0:w], rhs=w_in_b[:, kc, C_KV:C_KV + KVW],
                                                     start=(kc == 0), stop=(kc == 7)), R=["hT", "w_in_a"], W=[psk[pkv]])
            rms_rstd(PS[pkv][0:w, 0:NKV], [psk[pkv]], w, NKV, 1, NORM_EPS)
            b.op("dve", lambda e: e.scalar_tensor_tensor(out=kvrow[0:w, 0:NKV], in0=PS[pkv][0:w, 0:NKV], scalar=st1[0:w, 1:2],
                                                         in1=gkvB[0:w, :], op0=ALU.mult, op1=ALU.mult),
                 R=[psk[pkv], "st1_1", "gkvB"], W=["kvrow"])
            rope_tm(kvrow[0:w, NKV:KVW].rearrange("p (h r) -> p h r", h=1), PS[pkv][0:w, NKV:KVW].rearrange("p (h r) -> p h r", h=1),
                    w, i, 1, [psk[pkv]], ["kvrow"])
            if i < NFULL:
                b.dma("sp", "okv", O["kv_p"][128 * i:128 * i + 128, :], kvrow[:, :], R=["kvrow"])
            else:
                b.dma("sp", "okv", O["kv_p"][128 * i:128 * i + 16, :], kvrow[0:16, :], R=["kvrow"])
                b.dma("sp", "okv", O["kv_s"][:, :], kvrow[16:32, :], R=["kvrow"])
                b.dma("sp", "okv", kvs[:, :], kvrow[16:32, :], R=["kvrow"], W=["kvs"])
            if KQ < 1:
                return
            rms_rstd(PS[pq][0:w, 0:NQ], [psk[pq]], w, NQ, 2, NORM_EPS)
            b.op("dve", lambda e: e.scalar_tensor_tensor(out=qn[0:w, :], in0=PS[pq][0:w, 0:NQ], scalar=st1[0:w, 2:3],
                                                         in1=gqB[0:w, :], op0=ALU.mult, op1=ALU.mult),
                 R=[psk[pq], "st1_2", "gqB"], W=["qn"])
            p = nextps()
            pvb = PS[p][:].bitcast(BF16)
            for kc in range(3):
                b.op("pe", lambda e, kc=kc: e.transpose(out=pvb[:, kc * 128:kc * 128 + w], in_=qn[0:w, kc * 128:(kc + 1) * 128],
                                                        identity=ident_b[0:w, 0:w]), R=["qn", "ident_b"], W=[psk[p]])
            b.op("act", lambda e: e.copy(out=qnT[:, :, 0:w], in_=pvb[:, 0:384].rearrange("p (k t) -> p k t", k=3)[:, :, 0:w]),
                 R=[psk[p]], W=["qnT"])
            if KQ < 2:
                return
            for half in range(2):
                ph = nextps()
                for kc in range(3):
                    b.op("pe", lambda e, kc=kc, ph=ph, half=half: e.matmul(PS[ph][0:w, 0:384], lhsT=qnT[:, kc, 0:w],
                                                                           rhs=w_uq_b[:, kc, half * 384:(half + 1) * 384],
                                                                           start=(kc == 0), stop=(kc == 2)), R=["qnT", "w_uq_b"], W=[psk[ph]])
                b.op("act", lambda e, ph=ph: e.copy(out=qf32[0:w, :], in_=PS[ph][0:w, 0:384]), R=[psk[ph]], W=["qf32"])
                pv4 = qf32[0:w, :].rearrange("p (h x) -> p h x", h=4)
                b.op("pool", lambda e, pv4=pv4, half=half: e.tensor_copy(out=qtm[0:w, 4 * half:4 * half + 4, 0:64], in_=pv4[:, :, 0:64]),
                     R=["qf32"], W=["qtm"])
                rope_tm(qtm[0:w, 4 * half:4 * half + 4, 64:96], pv4[:, :, 64:96], w, i, 4, ["qf32"], ["qtm"])
            if KQ < 3:
                return
            p = nextps()
            pvb = PS[p][:].bitcast(BF16)
            for h in range(H):
                b.op("pe", lambda e, h=h, pvb=pvb: e.transpose(out=pvb[:, h * 128:h * 128 + w], in_=qtm[0:w, h, :], identity=ident_b[0:w, 0:w]),
                     R=["qtm", "ident_b"], W=[psk[p]])
            b.op("act", lambda e, pvb=pvb: e.copy(out=qT[0:96, :, 0:w], in_=pvb[0:96, 0:1024].rearrange("p (h t) -> p h t", h=H)[:, :, 0:w]),
                 R=[psk[p]], W=["qT"])
            if i == NFULL:
                b.op("dve", lambda e: e.tensor_copy(out=qsT[0:96, :, :], in_=qT[0:96, :, 16:32]), R=["qT"], W=["qsT"])
            if KA1 < 2:
                return
            b.op("dve", lambda e: e.tensor_copy(out=kvb[0:w, 0:KVW], in_=kvrow[0:w, :]), R=["kvrow"], W=["kvb"])
            p = nextps()
            pvb = PS[p][:].bitcast(BF16)
            b.op("pe", lambda e, pvb=pvb: e.transpose(out=pvb[:, 0:w], in_=kvb[0:w, 0:128], identity=ident_b[0:w, 0:w]), R=["kvb", "ident_b"], W=[psk[p]])
            b.op("pe", lambda e, pvb=pvb: e.transpose(out=pvb[:, 128:128 + w], in_=kvb[0:w, 128:256], identity=ident_b[0:w, 0:w]), R=["kvb", "ident_b"], W=[psk[p]])
            b.op("pe", lambda e, pvb=pvb: e.transpose(out=pvb[:, 256:256 + w], in_=kvb[0:w, 192:320], identity=ident_b[0:w, 0:w]), R=["kvb", "ident_b"], W=[psk[p]])
            b.op("act", lambda e, pvb=pvb: e.copy(out=ckvT[:, :, 0:w], in_=pvb[:, 0:256].rearrange("p (k t) -> p k t", k=2)[:, :, 0:w]), R=[psk[p]], W=["ckvT"])
            t0 = 128 * i
            ropesrc = apx(pvb[64:96, 256:257], [pvb.ap[0][0], 32], [[0, H], [1, wk]])
            b.op("act", lambda e, ropesrc=ropesrc: e.copy(out=KT[64:96, :, t0:t0 + wk], in_=ropesrc), R=[psk[p]], W=["KT"])
            for g in range(2):
                pk = nextps()
                for hh in range(4):
                    h = 4 * g + hh
                    for kc in range(2):
                        b.op("pe", lambda e, pk=pk, hh=hh, h=h, kc=kc: e.matmul(PS[pk][0:64, hh * 128:hh * 128 + w], lhsT=w_uk_b[:, kc, h * 64:(h + 1) * 64],
                                                                                rhs=ckvT[:, kc, 0:w], start=(kc == 0), stop=(kc == 1)),
                             R=["w_uk_b", "ckvT"], W=[psk[pk]])
                b.op("act", lambda e, pk=pk, g=g: e.copy(out=KT[0:64, 4 * g:4 * g + 4, t0:t0 + wk],
                                                         in_=PS[pk][0:64, 0:512].rearrange("p (h t) -> p h t", h=4)[:, :, 0:wk]), R=[psk[pk]], W=["KT"])
            pvv = nextps()
            for kc in range(2):
                b.op("pe", lambda e, kc=kc: e.matmul(PS[pvv][0:w, 0:512], lhsT=ckvT[:, kc, 0:w], rhs=w_uv_b[:, kc, :], start=(kc == 0), stop=(kc == 1)),
                     R=["ckvT", "w_uv_b"], W=[psk[pvv]])
            b.op("dve", lambda e: e.tensor_copy(out=Vc[0:w, i, :, 0:64], in_=PS[pvv][0:w, 0:512].rearrange("p (h v) -> p h v", h=H)), R=[psk[pvv]], W=["Vc"])
            if KA1 < 3:
                return
            wq = wk
            po = [reserve(), reserve()]
            ptn = [0]
            for h in range(H):
                pog = po[h // 4]
                ocol = (h % 4) * 65
                for j0 in range(0, i + 1, 4):
                    js = list(range(j0, min(j0 + 4, i + 1)))
                    psc = nextps()
                    for jj, j in enumerate(js):
                        wkj = 128 if j < NFULL else 16
                        b.op("pe", lambda e, psc=psc, jj=jj, j=j, h=h, wkj=wkj: e.matmul(PS[psc][0:wkj, jj * 128:jj * 128 + wq], lhsT=KT[0:96, h, j * 128:j * 128 + wkj],
                                                                                          rhs=qT[0:96, h, 0:wq], start=True, stop=True),
                             R=["KT", "qT"], W=[psk[psc]])
                    pt = PT[ptn[0] % 2]
                    ptk = f"PT{ptn[0] % 2}"
                    ptn[0] += 1
                    n = len(js)
                    wkmin = 128 if js[-1] < NFULL else 16
                    if wkmin == 128 or n == 1:
                        wka = 128 if wkmin == 128 else 16
                        b.op("act", lambda e, psc=psc, pt=pt, n=n, wka=wka: e.activation(out=pt[0:wka, 0:n, 0:wq], in_=PS[psc][0:wka, 0:n * 128].rearrange("p (j t) -> p j t", j=n)[:, :, 0:wq],
                                                                                         func=AF.Exp, scale=SM_SCALE), R=[psk[psc]], W=[ptk])
                    else:
                        b.op("act", lambda e, psc=psc, pt=pt, n=n: e.activation(out=pt[0:128, 0:n - 1, 0:wq], in_=PS[psc][0:128, 0:(n - 1) * 128].rearrange("p (j t) -> p j t", j=n - 1)[:, :, 0:wq],
                                                                                func=AF.Exp, scale=SM_SCALE), R=[psk[psc]], W=[ptk])
                        b.op("act", lambda e, psc=psc, pt=pt, n=n: e.activation(out=pt[0:16, n - 1, 0:wq], in_=PS[psc][0:16, (n - 1) * 128:(n - 1) * 128 + wq],
                                                                                func=AF.Exp, scale=SM_SCALE), R=[psk[psc]], W=[ptk])
                    if js[-1] == i:
                        jj = len(js) - 1
                        b.op("dve", lambda e, pt=pt, jj=jj: e.tensor_tensor(out=pt[0:wq, jj, 0:wq], in0=pt[0:wq, jj, 0:wq], in1=maskc_b[0:wq, 0:wq], op=ALU.mult),
                             R=[ptk, "maskc_b"], W=[ptk])
                    for jj, j in enumerate(js):
                        wkj = 128 if j < NFULL else 16
                        b.op("pe", lambda e, pog=pog, ocol=ocol, pt=pt, jj=jj, j=j, h=h, wkj=wkj: e.matmul(PS[pog][0:wq, ocol:ocol + 65], lhsT=pt[0:wkj, jj, 0:wq],
                                                                                                         rhs=Vc[0:wkj, j, h, :], start=(j == 0), stop=(j == i)),
                             R=[ptk, "Vc"], W=[psk[pog]])
            for g in range(2):
                b.op("act", lambda e, g=g: e.copy(out=osb[0:wq, :], in_=PS[po[g]][0:wq, 0:260]), R=[psk[po[g]]], W=["osb"])
                ov = osb[0:wq, :].rearrange("p (h x) -> p h x", h=4)
                b.op("dve", lambda e, ov=ov, g=g: e.reciprocal(out=rec[0:wq, 4 * g:4 * g + 4], in_=ov[:, :, 64]), R=["osb"], W=["rec"])
                rb = apx(rec[0:wq, 4 * g:4 * g + 1], [rec[:].ap[0][0], wq], [[1, 4], [0, 64]])
                b.op("dve", lambda e, ov=ov, g=g, rb=rb: e.tensor_tensor(out=attn_tm[0:wq, 256 * g:256 * g + 256].rearrange("p (h v) -> p h v", h=4),
                                                                         in0=ov[:, :, 0:64], in1=rb, op=ALU.mult), R=["osb", "rec"], W=["attn_tm"])
            release(po[0])
            release(po[1])
            p = nextps()
            pvb = PS[p][:].bitcast(BF16)
            for c4 in range(4):
                b.op("pe", lambda e, c4=c4, pvb=pvb: e.transpose(out=pvb[:, c4 * 128:c4 * 128 + wq], in_=attn_tm[0:wq, c4 * 128:(c4 + 1) * 128], identity=ident_b[0:wq, 0:wq]),
                     R=["attn_tm", "ident_b"], W=[psk[p]])
            b.op("act", lambda e, pvb=pvb: e.copy(out=attnT[:, :, 0:wq], in_=pvb[:, 0:512].rearrange("p (c t) -> p c t", c=4)[:, :, 0:wq]), R=[psk[p]], W=["attnT"])
            for c4 in range(4):
                b.dma("sp", "oat", ats[c4 * 128:(c4 + 1) * 128, t0:t0 + wq], attnT[:, c4, 0:wq], R=["attnT"], W=["ats"])

        if KA1 >= 1:
            load_x(0, 0)
            for i in range(NTL):
                tile(i)

    with ExitStack() as pst:
        b.st = pst
        phase_A1()
        b.barrier()
        b.emit()
    b.st = b.st0


    def phase_S():
        w_uk_b = load_w_bf("w_uk_s", [128, 2, 512], I["w_uk"].rearrange("(k p) n -> p k n", p=128))
        w_uv_b = load_w_bf("w_uv_s", [128, 2, 512], I["w_uv"].rearrange("(k p) n -> p k n", p=128))
        w_ukT = b.sb("w_ukT", [64, H, 256], BF16)
        qfT = b.sb("qfT", [128, 3, SPC, H], BF16)
        b.op("pool", lambda e: e.memset(qfT[:], 0.0), W=["qfT"])
        ones_f = b.sb("ones_f", [1, 128], F32)
        b.op("pool", lambda e: e.memset(ones_f[:], 1.0), W=["ones_f"])
        idx = b.sb("idx", [128, 8], I32)
        idx8 = b.sb("idx8", [128, 8], I32)
        b.op("pool", lambda e: e.memset(idx[:], 0), W=["idx"])
        b.dma("sp", "c0", idx[0:PP2, :], I["pt"].rearrange("(pi s2) g -> (s2 g) pi", s2=2), W=["idx"], allow_slow_non_contiguous=True)
        b.op("dve", lambda e: e.tensor_scalar(out=idx8[:], in0=idx[:], scalar1=8, scalar2=None, op0=ALU.mult), R=["idx"], W=["idx8"])
        kvg = [b.sb(f"kvg{j}", [128, 16, KVW], F32) for j in range(2)]
        kvh = [b.sb(f"kvh{j}", [128, 16, 320], BF16) for j in range(2)]
        for j in range(2):
            b.op("pool", lambda e, j=j: e.memset(kvh[j][:], 0.0), W=[f"kvh{j}"])
            b.op("pool", lambda e, j=j: e.memset(kvh[j][:, :, 288:290], 1.0), W=[f"kvh{j}"])
        kvn = b.sb("kvn", [128, 8, KVW], F32)
        kvnb = b.sb("kvnb", [128, 8, 320], BF16)
        b.op("pool", lambda e: e.memset(kvn[:], 0.0), W=["kvn"])
        b.op("pool", lambda e: e.memset(kvnb[:], 0.0), W=["kvnb"])
        b.op("pool", lambda e: e.memset(kvnb[:, :, 288:290], 1.0), W=["kvnb"])
        kv2 = kvs.rearrange("(pi s2) c -> s2 pi c", s2=2)
        b.dma("sp", "c0", kvn[0:1, :, :], kv2[0:1], R=["kvs"], W=["kvn"])
        b.dma("sp", "c0", kvn[NPG:NPG + 1, :, :], kv2[1:2], R=["kvs"], W=["kvn"])
        b.op("dve", lambda e: e.tensor_copy(out=kvnb[:, :, 0:KVW], in_=kvn[:, :, :]), R=["kvn"], W=["kvnb"])
        KT3 = [b.sb(f"KT3_{j}", [128, 6, 128], BF16) for j in range(2)]
        PTs = [b.sb(f"PTs{j}", [128, 16, 16], BF16) for j in range(2)]
        PTn = b.sb("PTn", [128, 16], BF16)
        for j in range(2):
            b.op("pool", lambda e, j=j: e.memset(PTs[j][:], 0.0), W=[f"PTs{j}"])
        b.op("pool", lambda e: e.memset(PTn[:], 0.0), W=["PTn"])
        mloc = b.sb("mloc", [128, 1], F32)
        scs = b.sb("scs", [128, 256], F32)
        b.op("pool", lambda e: e.memset(mloc[:], 0.0), W=["mloc"])
        m2 = b.sb("m2", [1, 2], F32)
        negm = b.sb("negm", [128, 2], F32)
        olat = b.sb("olat", [16, 256], BF16)
        rl = b.sb("rl", [16, 1], F32)
        olatT = b.sb("olatT", [128, 2, 8, 16], BF16)
        attn_sT = b.sb("attn_sT", [128, 4, 16], BF16)

        for h in range(H):
            p = nextps()
            pvb = PS[p][:].bitcast(BF16)
            for kc in range(2):
                b.op("pe", lambda e, h=h, kc=kc, pvb=pvb: e.transpose(out=pvb[0:64, kc * 128:(kc + 1) * 128], in_=w_uk_b[:, kc, h * 64:(h + 1) * 64], identity=ident_b[:, :]),
                     R=["w_uk_s", "ident_b"], W=[psk[p]])
            b.op("act", lambda e, h=h, pvb=pvb: e.copy(out=w_ukT[0:64, h, :], in_=pvb[0:64, 0:256]), R=[psk[p]], W=["w_ukT"])
        for h in range(H):
            p = nextps()
            for kc in range(2):
                b.op("pe", lambda e, h=h, kc=kc, p=p: e.matmul(PS[p][:, kc * 16:(kc + 1) * 16], lhsT=w_ukT[0:64, h, kc * 128:(kc + 1) * 128], rhs=qsT[0:64, h, :],
                                                              start=True, stop=True), R=["w_ukT", "qsT"], W=[psk[p]])
            b.op("act", lambda e, h=h, p=p: e.copy(out=qfT[:, 0:2, :, h], in_=PS[p][:, 0:32].rearrange("p (k s) -> p k s", k=2)), R=[psk[p]], W=["qfT"])
        b.op("dve", lambda e: e.tensor_copy(out=qfT[64:96, 2, :, :], in_=qsT[64:96, :, :].rearrange("p h s -> p s h")), R=["qsT"], W=["qfT"])

        cache8 = I["cache"].rearrange("n (g r) c -> (n g) (r c)", r=16)

        def kt_scores(src, r0, nr, bufk, psc, col0, pi):
            for g0 in range(0, nr, 2):
                rs = list(range(g0, min(g0 + 2, nr)))
                kb = bufk[0] % 2
                bufk[0] += 1
                kt, ktk = KT3[kb], f"KT3_{kb}"
                p = nextps()
                pvb = PS[p][:].bitcast(BF16)
                for rr, r in enumerate(rs):
                    for kc, (c0, c1, m) in enumerate(((0, 128, 128), (128, 256, 128), (192, 320, 128))):
                        b.op("pe", lambda e, pvb=pvb, rr=rr, r=r, kc=kc, c0=c0, c1=c1, m=m: e.transpose(out=pvb[0:m, (rr * 3 + kc) * 128:(rr * 3 + kc) * 128 + PP2],
                                                                                                   in_=src[0:PP2, r0 + r, c0:c1], identity=ident_b[0:PP2, 0:PP2]),
                             R=[src_key[0], "ident_b"], W=[psk[p]])
                n3 = 3 * len(rs)
                eng = "act" if (bufk[0] % 2) else "dve"
                if eng == "act":
                    b.op("act", lambda e, kt=kt, pvb=pvb, n3=n3: e.copy(out=kt[:, 0:n3, 0:PP2], in_=pvb[:, 0:n3 * 128].rearrange("p (a t) -> p a t", a=n3)[:, :, 0:PP2]),
                         R=[psk[p]], W=[ktk])
                else:
                    b.op("dve", lambda e, kt=kt, pvb=pvb, n3=n3: e.tensor_copy(out=kt[:, 0:n3, 0:PP2], in_=pvb[:, 0:n3 * 128].rearrange("p (a t) -> p a t", a=n3)[:, :, 0:PP2]),
                         R=[psk[p]], W=[ktk])
                for rr, r in enumerate(rs):
                    for kc, m in enumerate((128, 128, 96)):
                        b.op("pe", lambda e, kt=kt, rr=rr, r=r, kc=kc, m=m: e.matmul(PS[psc][0:PP2, col0 + 16 * r:col0 + 16 * r + 16], lhsT=kt[0:m, rr * 3 + kc, 0:PP2],
                                                                                   rhs=qfT[0:m, kc, 2 * pi:2 * pi + 2, :].rearrange("p s h -> p (s h)"),
                                                                                   start=(kc == 0), stop=(kc == 2)), R=[ktk, "qfT"], W=[psk[psc]])

        src_key = ["kvh0"]
        bufk = [0]
        cn = [0]
        for pi in range(SPC // 2):
            pacc = reserve()
            for ck in range(8):
                cb = cn[0] % 2
                cn[0] += 1
                gk, hk = f"kvg{cb}", f"kvh{cb}"
                b.raw_dma("pool", gk, lambda e, cb=cb, ck=ck, pi=pi: e.indirect_dma_start(
                    out=kvg[cb][0:PP2, :, :].rearrange("p a b -> p (a b)"), out_offset=None, in_=cache8[:, :], element_offset=ck * 16 * KVW,
                    in_offset=bass.IndirectOffsetOnAxis(ap=idx8[0:PP2, pi:pi + 1], axis=0)), R=["idx8"], W=[gk])
                ceng = "dve" if ck % 2 == 0 else "pool"
                b.op(ceng, lambda e, cb=cb: e.tensor_copy(out=kvh[cb][0:PP2, :, 0:KVW], in_=kvg[cb][0:PP2, :, :]), R=[gk], W=[hk])
                src_key[0] = hk
                psc = reserve()
                kt_scores(kvh[cb], 0, 16, bufk, psc, 0, pi)
                release(psc)
                sc3 = PS[psc][0:PP2, 0:256].rearrange("p (r x) -> p r x", r=16)
                if ck == 0:
                    b.op("act", lambda e, psc=psc: e.copy(out=scs[0:PP2, :], in_=PS[psc][0:PP2, 0:256]), R=[psk[psc]], W=["scs"])
                    ss3 = scs[0:PP2, :].rearrange("p (r x) -> p r x", r=16)
                    b.op("dve", lambda e, ss3=ss3: e.tensor_reduce(out=mloc[0:NPG, :], in_=ss3[0:NPG, :, 0:8], axis=AX.XY, op=ALU.max), R=["scs"], W=["mloc"])
                    b.op("dve", lambda e, ss3=ss3: e.tensor_reduce(out=mloc[NPG:PP2, :], in_=ss3[NPG:PP2, :, 8:16], axis=AX.XY, op=ALU.max), R=["scs"], W=["mloc"])
                    pm_ = nextps()
                    b.op("pe", lambda e, pm_=pm_: e.transpose(out=PS[pm_][0:1, 0:PP2], in_=mloc[0:PP2, 0:1], identity=ident_f[0:PP2, 0:PP2]), R=["mloc", "ident_f"], W=[psk[pm_]])
                    b.op("dve", lambda e, pm_=pm_: e.tensor_reduce(out=m2[0:1, 0:2], in_=PS[pm_][0:1, 0:PP2].rearrange("p (a g) -> p a g", a=2), axis=AX.X, op=ALU.max),
                         R=[psk[pm_]], W=["m2"])
                    b.op("dve", lambda e: e.tensor_scalar(out=m2[0:1, 0:2], in0=m2[0:1, 0:2], scalar1=-SM_SCALE, scalar2=None, op0=ALU.mult), R=["m2"], W=["m2"])
                    pb_ = nextps()
                    b.op("pe", lambda e, pb_=pb_: e.matmul(PS[pb_][0:PP2, 0:2], lhsT=ones_f[0:1, 0:PP2], rhs=m2[0:1, 0:2], start=True, stop=True), R=["ones_f", "m2"], W=[psk[pb_]])
                    b.op("act", lambda e, pb_=pb_: e.copy(out=negm[0:PP2, :], in_=PS[pb_][0:PP2, 0:2]), R=[psk[pb_]], W=["negm"])
                pt, ptk = PTs[cb], f"PTs{cb}"
                b.op("act", lambda e, pt=pt, sc3=sc3: e.activation(out=pt[0:NPG, :, 0:8], in_=sc3[0:NPG, :, 0:8], func=AF.Exp, scale=SM_SCALE, bias=negm[0:NPG, 0:1]),
                     R=[psk[psc], "negm"], W=[ptk])
                b.op("act", lambda e, pt=pt, sc3=sc3: e.activation(out=pt[NPG:PP2, :, 8:16], in_=sc3[NPG:PP2, :, 8:16], func=AF.Exp, scale=SM_SCALE, bias=negm[NPG:PP2, 1:2]),
                     R=[psk[psc], "negm"], W=[ptk])
                for r in range(16):
                    b.op("pe", lambda e, pt=pt, cb=cb, r=r, ck=ck, pacc=pacc: e.matmul(PS[pacc][0:16, 0:290], lhsT=pt[0:PP2, r, :], rhs=kvh[cb][0:PP2, r, 0:290],
                                                                                         start=(ck == 0 and r == 0), stop=False), R=[ptk, hk], W=[psk[pacc]])
            src_key[0] = "kvnb"
            psc = reserve()
            kt_scores(kvnb, pi, 1, bufk, psc, 0, pi)
            release(psc)
            b.op("act", lambda e, psc=psc: e.activation(out=PTn[0:1, 0:8], in_=PS[psc][0:1, 0:8], func=AF.Exp, scale=SM_SCALE, bias=negm[0:1, 0:1]),
                 R=[psk[psc], "negm"], W=["PTn"])
            b.op("act", lambda e, psc=psc: e.activation(out=PTn[NPG:NPG + 1, 8:16], in_=PS[psc][NPG:NPG + 1, 8:16], func=AF.Exp, scale=SM_SCALE, bias=negm[NPG:NPG + 1, 1:2]),
                 R=[psk[psc], "negm"], W=["PTn"])
            b.op("pe", lambda e, pi=pi, pacc=pacc: e.matmul(PS[pacc][0:16, 0:290], lhsT=PTn[0:PP2, :], rhs=kvnb[0:PP2, pi, 0:290], start=False, stop=True),
                 R=["PTn", "kvnb"], W=[psk[pacc]])
            b.op("dve", lambda e, pacc=pacc: e.reciprocal(out=rl[:, :], in_=PS[pacc][0:16, 288:289]), R=[psk[pacc]], W=["rl"])
            b.op("dve", lambda e, pacc=pacc: e.tensor_scalar(out=olat[:, :], in0=PS[pacc][0:16, 0:256], scalar1=rl[:, 0:1], scalar2=None, op0=ALU.mult),
                 R=[psk[pacc], "rl"], W=["olat"])
            p = nextps()
            pvb = PS[p][:].bitcast(BF16)
            for kc in range(2):
                b.op("pe", lambda e, kc=kc, pvb=pvb: e.transpose(out=pvb[:, kc * 16:(kc + 1) * 16], in_=olat[0:16, kc * 128:(kc + 1) * 128], identity=ident_b[0:16, 0:16]),
                     R=["olat", "ident_b"], W=[psk[p]])
            release(pacc)
            b.op("act", lambda e, pi=pi, pvb=pvb: e.copy(out=olatT[:, :, pi, :], in_=pvb[:, 0:32].rearrange("p (k x) -> p k x", k=2)), R=[psk[p]], W=["olatT"])
        p = nextps()
        ol5 = olatT[:].rearrange("p k a (s h) -> p k a s h", s=2)
        for h in range(H):
            pr = slice((h % 2) * 64, (h % 2) * 64 + 64)
            for kc in range(2):
                b.op("pe", lambda e, h=h, kc=kc, pr=pr, p=p: e.matmul(PS[p][pr, h * 16:(h + 1) * 16], lhsT=w_uv_b[:, kc, h * 64:(h + 1) * 64], rhs=ol5[:, kc, :, :, h],
                                                                     start=(kc == 0), stop=(kc == 1)), R=["w_uv_s", "olatT"], W=[psk[p]])
        for h in range(H):
            pr = slice((h % 2) * 64, (h % 2) * 64 + 64)
            b.op("act", lambda e, h=h, pr=pr, p=p: e.copy(out=attn_sT[pr, h // 2, :], in_=PS[p][pr, h * 16:(h + 1) * 16]), R=[psk[p]], W=["attn_sT"])
        for c4 in range(4):
            b.dma("sp", "oat", ats[c4 * 128:(c4 + 1) * 128, NFULL * 128 + 16:NFULL * 128 + 32], attn_sT[:, c4, :], R=["attn_sT"], W=["ats"])

    if stage >= 3:
        with ExitStack() as pst:
            b.st = pst
            phase_S()
            b.barrier()
            b.emit()
        b.st = b.st0


    pst = ExitStack()
    b.st = pst
    w_in_r = b.sb("w_in_r", [128, 8, RC], BF16)
    for kc in range(8):
        b.dma("pool", "wA", w_in_r[:, kc, :], I["w_in"][kc * 128:(kc + 1) * 128, C_RW:C_G], W=["w_in_r"])
    CH = BF16

    def flat(name, dt=F32, n=512):
        return b.sb(name, [128, n], dt)

    def v3(t, w, g=4):
        return t[:, 0:g * w].rearrange("p (g t) -> p g t", g=g)

    ppar = b.sb("ppar", [128, 64], F32)

    def ld_pp(src, c0, n):
        b.dma("sp", "c0", ppar[:, c0:c0 + n], src.rearrange("o (c p) -> p (o c)", p=128), W=["ppar"],
              allow_slow_non_contiguous=True)
    PP_MU, PP_W0, PP_A0, PP_KK, PP_KA, PP_RK, PP_LNW, PP_LNB, PP_OMKA = 0, 14, 18, 22, 26, 30, 34, 38, 42
    ld_pp(I["mu_shift"], PP_MU, 14)
    ld_pp(I["w0"], PP_W0, 4)
    ld_pp(I["a0"], PP_A0, 4)
    ld_pp(I["k_k"], PP_KK, 4)
    ld_pp(I["k_a"], PP_KA, 4)
    ld_pp(I["r_k"], PP_RK, 4)
    ld_pp(I["ln_w"], PP_LNW, 4)
    ld_pp(I["ln_b"], PP_LNB, 4)
    b.op("dve", lambda e: e.tensor_scalar(out=ppar[:, PP_OMKA:PP_OMKA + 4], in0=ppar[:, PP_KA:PP_KA + 4], scalar1=-1.0,
                                          scalar2=1.0, op0=ALU.mult, op1=ALU.add), R=["ppar"], W=["ppar"])
    w2_b = b.sb("w2_b", [128, RW], BF16)
    a2_b = b.sb("a2_b", [128, RW], BF16)
    g2_b = b.sb("g2_b", [128, RW], BF16)
    b.dma("pool", "wA", w2_b[0:64, :], I["w2"], W=["w2_b"])
    b.dma("pool", "wA", a2_b[64:128, :], I["a2"], W=["a2_b"])
    b.dma("pool", "wA", g2_b[:, :], I["g2"], W=["g2_b"])
    maskx_b = b.sb("maskx_b", [128, 128], BF16)
    maskjt_b = b.sb("maskjt_b", [128, 2, 128], BF16)
    cmask_b = b.sb("cmask_b", [128, 8], BF16)
    identb_b = b.sb("identb_b", [128, 64], BF16)
    onehot_b = b.sb("onehot_b", [128, 16, 128], BF16)
    b.dma("pool", "wA", maskx_b[:], I["c_maskx"], W=["maskx_b"])
    b.dma("pool", "wA", maskjt_b[:].rearrange("p a b -> p (a b)"), I["c_maskjt"], W=["maskjt_b"])
    b.dma("pool", "wA", cmask_b[:], I["c_cmask"], W=["cmask_b"])
    b.dma("pool", "wA", identb_b[:], I["c_identb"], W=["identb_b"])
    b.dma("pool", "wA", onehot_b[:].rearrange("p a b -> p (a b)"), I["c_onehot"], W=["onehot_b"])
    blk1_f = b.sb("blk1_f", [128, 128], F32)
    blk64_f = b.sb("blk64_f", [128, 128], F32)
    scanm = b.sb("scanm", [128, 512], F32)
    b.dma("sp", "c0", blk1_f[:], I["c_blk1"], W=["blk1_f"])
    b.dma("sp", "c0", scanm[:], I["c_scanm"], W=["scanm"])
    b.op("dve", lambda e: e.tensor_scalar(out=blk64_f[:], in0=blk1_f[:], scalar1=1.0 / 64, scalar2=None, op0=ALU.mult),
         R=["blk1_f"], W=["blk64_f"])

    cT = b.sb("cT", [128, 14, 129], F32)
    zt = b.sb("zt", [128, 14, 128], F32)
    b.op("pool", lambda e: e.memset(cT[:], 0.0), W=["cT"])
    sshT = b.sb("sshT", [128, 14, 16], F32)
    sshtm = b.sb("sshtm", [16, RC], F32)
    f = {n: flat("f_" + n) for n in ["ld", "asig", "kk", "t", "bb", "kp", "L", "Lx", "eL", "enL", "eLC"]}
    gsb = flat("gsb", BF16)
    tanhw = flat("tanhw", BF16, 128)
    alb = flat("alb", BF16, 128)
    sgb = flat("sgb", BF16, 128)
    vb = flat("vb", BF16)
    AR = b.sb("AR", [128, 1024], BF16)
    Kt, Bt, Kb, Bb = (flat(n, BF16) for n in ["Kt", "Bt", "Kb", "Bb"])
    Vtm = b.sb("Vtm", [128, 4, 128], BF16)
    YA = b.sb("YA", [128, 2, 128], BF16)
    NK = b.sb("NK", [128, 2, 128], BF16)
    Xm = b.sb("Xm", [128, 128], BF16)
    XY = [b.sb(f"XY{j}", [128, 2, 128], BF16) for j in range(2)]
    Y8 = b.sb("Y8", [128, 128], BF16)
    Zs = [b.sb(f"Z{j}", [128, 128], BF16) for j in range(2)]
    KBtm = b.sb("KBtm", [128, 2, 64], BF16)
    Bexp = b.sb("Bexp", [128, 8, 64], BF16)
    Vexp = b.sb("Vexp", [128, 8, 64], BF16)
    Vhexp = b.sb("Vhexp", [128, 8, 64], BF16)
    Wd = b.sb("Wd", [128, 4, 8, 64], BF16)
    RhT = flat("RhT", CH)
    MT = [b.sb(f"MT{hp}", [128, 8, 128], CH) for hp in range(4)]
    GB = [b.sb(f"GB{hp}", [128, 8, 128], CH) for hp in range(4)]
    Sbd = [[b.sb(f"S{hp}_{j}", [128, 128], CH) for j in range(2)] for hp in range(4)]
    scur = [0, 0, 0, 0]
    for hp in range(4):
        b.op("pool", lambda e, hp=hp: e.memset(MT[hp][:], 0.0), W=[f"MT{hp}"])
        b.op("pool", lambda e, hp=hp: e.memset(GB[hp][:], 0.0), W=[f"GB{hp}"])
        b.op("pool", lambda e, hp=hp: e.memset(Sbd[hp][0][:], 0.0), W=[f"S{hp}_0"])
        b.op("pool", lambda e, hp=hp: e.memset(Sbd[hp][1][:], 0.0), W=[f"S{hp}_1"])
    ygT = flat("ygT", BF16)
    tm5 = b.sb("tm5", [48, 5, 256], BF16)
    xb = b.sb("xb", [128, 5, 4, 16], BF16)
    b.op("pool", lambda e: e.memset(tm5[:], 0.0), W=["tm5"])
    Sx = [b.sb(f"Sx{j}", [128, 4, 64], F32) for j in range(2)]
    stmp = b.sb("stmp", [128, 4, 64], F32)
    ssa = b.sb("ssa", [128, 4], F32)
    wkvo = b.sb("wkvo", [128, 4, 128], F32)

    b.dma("sp", "c0", sshtm[:], I["sshift"], W=["sshtm"])

    def prep_ssh():
        for g in range(4):
            js = list(range(4 * g, min(4 * g + 4, 14)))
            p = nextps()
            for jj, j in enumerate(js):
                b.op("pe", lambda e, jj=jj, j=j, p=p: e.transpose(out=PS[p][:, jj * 16:(jj + 1) * 16], in_=sshtm[0:16, j * 128:(j + 1) * 128],
                                                              identity=ident_f[0:16, 0:16]), R=["sshtm", "ident_f"], W=[psk[p]])
            n = len(js)
            b.op("act", lambda e, p=p, n=n, j0=js[0]: e.copy(out=sshT[:, j0:j0 + n, :], in_=PS[p][:, 0:n * 16].rearrange("p (j t) -> p j t", j=n)),
                 R=[psk[p]], W=["sshT"])
    prep_ssh()

    def pp(c0, hp):
        return ppar[:, c0 + hp:c0 + hp + 1]

    def rwkv_tile(i):
        w = tw(i)
        wu = 128 if i < NFULL else 16
        nch = wu // 16
        last = (i == NFULL)
        if i > 0:
            b.op("dve", lambda e: e.tensor_copy(out=cT[:, :, 0:1], in_=cT[:, :, 128:129]), R=["cT"], W=["cT"])
        for g in range(4):
            js = list(range(4 * g, min(4 * g + 4, 14)))
            p = nextps()
            for jj, j in enumerate(js):
                for kc in range(8):
                    b.op("pe", lambda e, jj=jj, j=j, kc=kc, p=p: e.matmul(PS[p][:, jj * 128:jj * 128 + w], lhsT=w_in_r[:, kc, 128 * j:128 * (j + 1)],
                                                                           rhs=hT[:, kc, 0:w], start=(kc == 0), stop=(kc == 7)),
                         R=["hT", "w_in_r"], W=[psk[p]])
            n = len(js)
            b.op("act", lambda e, p=p, n=n, j0=js[0]: e.copy(out=cT[:, j0:j0 + n, 1:1 + w],
                                                            in_=PS[p][:, 0:n * 128].rearrange("p (j t) -> p j t", j=n)[:, :, 0:w]),
                 R=[psk[p]], W=["cT"])
        wp = wu if last else w
        b.op("dve", lambda e: e.tensor_tensor(out=zt[:, :, 0:wp], in0=cT[:, :, 0:wp], in1=cT[:, :, 1:1 + wp], op=ALU.subtract),
             R=["cT"], W=["zt"])
        if last:
            b.op("dve", lambda e: e.tensor_tensor(out=zt[:, :, 16:32], in0=sshT[:, :, :], in1=cT[:, :, 17:33], op=ALU.subtract),
                 R=["cT", "sshT"], W=["zt"])
        for j in range(14):
            b.op("dve", lambda e, j=j: e.scalar_tensor_tensor(out=zt[:, j, 0:w], in0=zt[:, j, 0:w], scalar=ppar[:, PP_MU + j:PP_MU + j + 1],
                                                              in1=cT[:, j, 1:1 + w], op0=ALU.mult, op1=ALU.add),
                 R=["zt", "cT", "ppar"], W=["zt"])
        r3, k3, v3_ = zt[:, 0:4, 0:w], zt[:, 4:8, 0:w], zt[:, 8:12, 0:w]
        W4 = 4 * w
        ld3, as3, kk3, t3, b3, kp3 = (v3(f[n], w) for n in ["ld", "asig", "kk", "t", "bb", "kp"])
        b.op("act", lambda e: e.activation(out=tanhw[0:64, 0:w], in_=zt[0:64, 12, 0:w], func=AF.Tanh), R=["zt"], W=["tanhw"])
        b.op("pool", lambda e: e.tensor_copy(out=alb[64:128, 0:w], in_=zt[64:128, 12, 0:w]), R=["zt"], W=["alb"])
        b.op("act", lambda e: e.activation(out=sgb[:, 0:w], in_=zt[:, 13, 0:w], func=AF.Sigmoid), R=["zt"], W=["sgb"])
        b.op("pool", lambda e: e.tensor_copy(out=v3(vb, w), in_=v3_), R=["zt"], W=["vb"])
        pW, pA, pG = nextps(), nextps(), nextps()
        for hp in range(4):
            b.op("pe", lambda e, hp=hp: e.matmul(PS[pW][:, hp * w:(hp + 1) * w], lhsT=w2_b[0:64, hp * 128:(hp + 1) * 128], rhs=tanhw[0:64, 0:w],
                                                 start=True, stop=True), R=["w2_b", "tanhw"], W=[psk[pW]])
        for hp in range(4):
            b.op("pe", lambda e, hp=hp: e.matmul(PS[pA][:, hp * w:(hp + 1) * w], lhsT=a2_b[64:128, hp * 128:(hp + 1) * 128], rhs=alb[64:128, 0:w],
                                                 start=True, stop=True), R=["a2_b", "alb"], W=[psk[pA]])
        for hp in range(4):
            b.op("pe", lambda e, hp=hp: e.matmul(PS[pG][:, hp * w:(hp + 1) * w], lhsT=g2_b[:, hp * 128:(hp + 1) * 128], rhs=sgb[:, 0:w],
                                                 start=True, stop=True), R=["g2_b", "sgb"], W=[psk[pG]])
        for hp in range(4):
            b.op("act", lambda e, hp=hp: e.activation(out=ld3[:, hp, :], in_=PS[pW][:, hp * w:(hp + 1) * w], func=AF.Sigmoid, bias=pp(PP_W0, hp)),
                 R=[psk[pW], "ppar"], W=["f_ld"])
            b.op("act", lambda e, hp=hp: e.activation(out=as3[:, hp, :], in_=PS[pA][:, hp * w:(hp + 1) * w], func=AF.Sigmoid, bias=pp(PP_A0, hp)),
                 R=[psk[pA], "ppar"], W=["f_asig"])
        b.op("act", lambda e: e.copy(out=gsb[:, 0:W4], in_=PS[pG][:, 0:W4]), R=[psk[pG]], W=["gsb"])
        b.op("dve", lambda e: e.tensor_scalar(out=f["ld"][:, 0:W4], in0=f["ld"][:, 0:W4], scalar1=-0.6065306597126334, scalar2=None, op0=ALU.mult),
             R=["f_ld"], W=["f_ld"])
        for hp in range(4):
            b.op("act", lambda e, hp=hp: e.activation(out=kk3[:, hp, :], in_=zt[:, 4 + hp, 0:w], func=AF.Copy, scale=pp(PP_KK, hp)),
                 R=["zt", "ppar"], W=["f_kk"])
        b.op("act", lambda e: e.activation(out=f["t"][:, 0:W4], in_=f["kk"][:, 0:W4], func=AF.Square), R=["f_kk"], W=["f_t"])
        pN = nextps()
        b.op("pe", lambda e: e.matmul(PS[pN][:, 0:W4], lhsT=blk1_f[:], rhs=f["t"][:, 0:W4], start=True, stop=True),
             R=["blk1_f", "f_t"], W=[psk[pN]])
        b.op("act", lambda e: e.activation(out=f["t"][:, 0:W4], in_=PS[pN][:, 0:W4], func=AF.Sqrt), R=[psk[pN]], W=["f_t"])
        b.op("dve", lambda e: e.tensor_scalar(out=f["t"][:, 0:W4], in0=f["t"][:, 0:W4], scalar1=1e-12, scalar2=None, op0=ALU.max),
             R=["f_t"], W=["f_t"])
        b.op("dve", lambda e: e.reciprocal(out=f["t"][:, 0:W4], in_=f["t"][:, 0:W4]), R=["f_t"], W=["f_t"])
        b.op("dve", lambda e: e.tensor_tensor(out=f["kk"][:, 0:W4], in0=f["kk"][:, 0:W4], in1=f["t"][:, 0:W4], op=ALU.mult),
             R=["f_kk", "f_t"], W=["f_kk"])
        b.op("dve", lambda e: e.tensor_tensor(out=f["bb"][:, 0:W4], in0=f["kk"][:, 0:W4], in1=f["asig"][:, 0:W4], op=ALU.mult),
             R=["f_kk", "f_asig"], W=["f_bb"])
        for hp in range(4):
            b.op("dve", lambda e, hp=hp: e.tensor_scalar(out=t3[:, hp, :], in0=as3[:, hp, :], scalar1=pp(PP_KA, hp), scalar2=pp(PP_OMKA, hp),
                                                         op0=ALU.mult, op1=ALU.add), R=["f_asig", "ppar"], W=["f_t"])
        b.op("dve", lambda e: e.tensor_tensor(out=kp3, in0=k3, in1=t3, op=ALU.mult), R=["zt", "f_t"], W=["f_kp"])
        b.op("dve", lambda e: e.tensor_tensor_scan(out=f["L"][:, 0:W4], data0=scanm[:, 0:W4], data1=f["ld"][:, 0:W4], initial=0.0,
                                                   op0=ALU.mult, op1=ALU.add), R=["scanm", "f_ld"], W=["f_L"])
        b.op("dve", lambda e: e.tensor_tensor(out=f["Lx"][:, 0:W4], in0=f["L"][:, 0:W4], in1=f["ld"][:, 0:W4], op=ALU.subtract),
             R=["f_L", "f_ld"], W=["f_Lx"])
        ng = W4 // 16
        Lg = f["L"][:, 0:W4].rearrange("p (g t) -> p g t", t=16)
        Lend = apx(f["L"][:, 15:16], [f["L"][:].ap[0][0], 128], [[16, ng], [0, 16]])
        b.op("dve", lambda e: e.tensor_tensor(out=f["eLC"][:, 0:W4].rearrange("p (g t) -> p g t", t=16), in0=Lend, in1=Lg, op=ALU.subtract),
             R=["f_L"], W=["f_eLC"])
        b.op("act", lambda e: e.activation(out=f["eL"][:, 0:W4], in_=f["L"][:, 0:W4], func=AF.Exp), R=["f_L"], W=["f_eL"])
        b.op("act", lambda e: e.activation(out=f["enL"][:, 0:W4], in_=f["L"][:, 0:W4], func=AF.Exp, scale=-1.0), R=["f_L"], W=["f_enL"])
        b.op("act", lambda e: e.activation(out=f["Lx"][:, 0:W4], in_=f["Lx"][:, 0:W4], func=AF.Exp), R=["f_Lx"], W=["f_Lx"])
        b.op("act", lambda e: e.activation(out=f["eLC"][:, 0:W4], in_=f["eLC"][:, 0:W4], func=AF.Exp), R=["f_eLC"], W=["f_eLC"])
        AR4 = AR[:, 0:2 * W4].rearrange("p (h two t) -> p h two t", h=4, two=2)
        b.op("dve", lambda e: e.scalar_tensor_tensor(out=AR4[:, :, 0, :], in0=kk3, scalar=-1.0, in1=v3(f["Lx"], w), op0=ALU.mult, op1=ALU.mult),
             R=["f_kk", "f_Lx"], W=["AR"])
        b.op("dve", lambda e: e.tensor_tensor(out=AR4[:, :, 1, :], in0=r3, in1=v3(f["eL"], w), op=ALU.mult), R=["zt", "f_eL"], W=["AR"])
        b.op("dve", lambda e: e.tensor_tensor(out=Kt[:, 0:W4], in0=f["kp"][:, 0:W4], in1=f["enL"][:, 0:W4], op=ALU.mult), R=["f_kp", "f_enL"], W=["Kt"])
        b.op("dve", lambda e: e.tensor_tensor(out=Bt[:, 0:W4], in0=f["bb"][:, 0:W4], in1=f["enL"][:, 0:W4], op=ALU.mult), R=["f_bb", "f_enL"], W=["Bt"])
        b.op("pool", lambda e: e.tensor_tensor(out=Kb[:, 0:W4], in0=f["kp"][:, 0:W4], in1=f["eLC"][:, 0:W4], op=ALU.mult), R=["f_kp", "f_eLC"], W=["Kb"])
        b.op("pool", lambda e: e.tensor_tensor(out=Bb[:, 0:W4], in0=f["bb"][:, 0:W4], in1=f["eLC"][:, 0:W4], op=ALU.mult), R=["f_bb", "f_eLC"], W=["Bb"])
        for hp in range(4):
            wc = apx(f["eL"][:, hp * w + 15:hp * w + 16], [f["eL"][:].ap[0][0], 128], [[16, nch], [0, 64]])
            idb = apx(identb_b[:, 0:1], [identb_b[:].ap[0][0], 128], [[0, nch], [1, 64]])
            b.op("dve", lambda e, hp=hp, wc=wc, idb=idb: e.tensor_tensor(out=Wd[:, hp, 0:nch, :], in0=idb, in1=wc, op=ALU.mult),
                 R=["f_eL", "identb_b"], W=["Wd"])
        p = nextps()
        pvb = PS[p][:].bitcast(BF16)
        for hp in range(4):
            b.op("pe", lambda e, hp=hp, p=p: e.transpose(out=pvb[0:w, hp * 128:(hp + 1) * 128], in_=vb[:, hp * w:(hp + 1) * w], identity=ident_b[:, :]),
                 R=["vb", "ident_b"], W=[psk[p]])
        b.op("act", lambda e, p=p: e.copy(out=Vtm[0:w, :, :], in_=pvb[0:w, 0:512].rearrange("p (h x) -> p h x", h=4)), R=[psk[p]], W=["Vtm"])

        ARf = AR4
        for hp in range(4):
            for h2 in range(2):
                unit(i, hp, h2, wu, nch, w)
        pY = reserve()
        for hp in range(4):
            for c in range(nch):
                cur = scur[hp]
                S0, S1 = Sbd[hp][cur], Sbd[hp][1 - cur]
                k0, k1 = f"S{hp}_{cur}", f"S{hp}_{1 - cur}"
                b.op("pe", lambda e, hp=hp, c=c, S0=S0: e.matmul(PS[pY][:, hp * 128 + c * 16:hp * 128 + c * 16 + 16], lhsT=S0[:, :],
                                                                rhs=RhT[:, hp * 128 + c * 16:hp * 128 + c * 16 + 16], start=True, stop=True),
                     R=[k0, "RhT"], W=[psk[pY]])
                ps_ = nextps()
                b.op("pe", lambda e, hp=hp, c=c, S0=S0, ps_=ps_: e.matmul(PS[ps_][:, 0:128], lhsT=MT[hp][:, c, :], rhs=S0[:, :], start=True, stop=True),
                     R=[k0, f"MT{hp}"], W=[psk[ps_]])
                b.op("dve", lambda e, hp=hp, c=c, S1=S1, ps_=ps_: e.tensor_tensor(out=S1[:, :], in0=PS[ps_][:, 0:128], in1=GB[hp][:, c, :], op=ALU.add),
                     R=[psk[ps_], f"GB{hp}"], W=[k1])
                scur[hp] = 1 - cur
        ysb = f["L"]
        y3 = v3(ysb, w)
        for hp in range(4):
            b.op("dve", lambda e, hp=hp: e.tensor_tensor(out=y3[:, hp, 0:wu], in0=PS[pY][:, hp * 128:hp * 128 + wu], in1=f["enL"][:, hp * 128:hp * 128 + wu], op=ALU.add),
                 R=[psk[pY], "f_enL"], W=["f_L"])
        release(pY)
        if last:
            sample_rwkv(w, y3)
            wkv_prompt_out()
        finalize(i, w, y3)

    def unit(i, hp, h2, wu, nch, w):
        pr = slice(h2 * 64, h2 * 64 + 64)
        pRY, pM, pGm = nextps(), nextps(), nextps()
        AR4 = AR[:, 0:8 * w].rearrange("p (h two t) -> p h two t", h=4, two=2)
        At = AR4[pr, hp, 0, 0:wu]
        ARr = AR[pr, hp * 2 * w:(hp + 1) * 2 * w].rearrange("p (two t) -> p two t", two=2)[:, :, 0:wu]
        Kt_ = Kt[pr, hp * w:hp * w + wu]
        Bt_ = Bt[pr, hp * w:hp * w + wu]
        Kb_ = Kb[pr, hp * w:hp * w + wu]
        Bb_ = Bb[pr, hp * w:hp * w + wu]
        Vh = Vtm[0:wu, hp, h2 * 64:(h2 + 1) * 64]
        pa, pb = nextps(), nextps()
        o1 = PS[pa][0:wu, 0:2 * wu].rearrange("p (two t) -> p two t", two=2)
        o2 = PS[pa][0:wu, 256:256 + 2 * wu].rearrange("p (two t) -> p two t", two=2)
        b.op("pe", lambda e: e.matmul(o1, lhsT=Bt_, rhs=ARr, start=True, stop=True), R=["Bt", "AR"], W=[psk[pa]])
        b.op("pe", lambda e: e.matmul(o2, lhsT=Kt_, rhs=ARr, start=True, stop=True), R=["Kt", "AR"], W=[psk[pa]])
        b.op("pe", lambda e: e.matmul(PS[pb][0:wu, 0:wu], lhsT=At, rhs=Bt_, start=True, stop=True), R=["Bt", "AR"], W=[psk[pb]])
        mj = maskjt_b[0:wu, :, 0:wu]
        b.op("dve", lambda e: e.tensor_tensor(out=YA[0:wu, :, 0:wu], in0=o1, in1=mj, op=ALU.mult), R=[psk[pa], "maskjt_b"], W=["YA"])
        b.op("dve", lambda e: e.tensor_tensor(out=NK[0:wu, :, 0:wu], in0=o2, in1=mj, op=ALU.mult), R=[psk[pa], "maskjt_b"], W=["NK"])
        b.op("dve", lambda e: e.tensor_tensor(out=Xm[0:wu, 0:wu], in0=PS[pb][0:wu, 0:wu], in1=maskx_b[0:wu, 0:wu], op=ALU.mult),
             R=[psk[pb], "maskx_b"], W=["Xm"])
        Y1, Arb, Nak, Ark = YA[0:wu, 0, 0:wu], YA[0:wu, 1, 0:wu], NK[0:wu, 0, 0:wu], NK[0:wu, 1, 0:wu]
        X1 = Xm[0:wu, 0:wu]
        Xp, Yp, Ypow = X1, Y1, [Y1]
        for lvl in range(2):
            pc = nextps()
            oc = PS[pc][0:wu, 0:2 * wu].rearrange("p (two t) -> p two t", two=2)
            rk = ["YA", "Xm"] if lvl == 0 else [f"XY{lvl - 1}"]
            b.op("pe", lambda e, oc=oc, Xp=Xp, Yp=Yp: e.matmul(oc[:, 0, :], lhsT=Yp, rhs=Xp, start=True, stop=True), R=rk, W=[psk[pc]])
            b.op("pe", lambda e, oc=oc, Xp=Xp, Yp=Yp: e.matmul(oc[:, 1, :], lhsT=Xp, rhs=Yp, start=True, stop=True), R=rk, W=[psk[pc]])
            b.op("act", lambda e, oc=oc, lvl=lvl: e.copy(out=XY[lvl][0:wu, :, 0:wu], in_=oc), R=[psk[pc]], W=[f"XY{lvl}"])
            Xp, Yp = XY[lvl][0:wu, 0, 0:wu], XY[lvl][0:wu, 1, 0:wu]
            Ypow.append(Yp)
        pc = nextps()
        b.op("pe", lambda e, Xp=Xp, Yp=Yp, pc=pc: e.matmul(PS[pc][0:wu, 0:wu], lhsT=Xp, rhs=Yp, start=True, stop=True), R=["XY1"], W=[psk[pc]])
        b.op("act", lambda e, pc=pc: e.copy(out=Y8[0:wu, 0:wu], in_=PS[pc][0:wu, 0:wu]), R=[psk[pc]], W=["Y8"])
        Ypow.append(Y8[0:wu, 0:wu])
        ykeys = [["YA"], ["XY0"], ["XY1"], ["Y8"]]
        pz = nextps()
        pzb = PS[pz][:].bitcast(BF16)
        b.op("pe", lambda e: e.transpose(out=pzb[0:wu, 0:64], in_=At, identity=ident_b[pr, pr]), R=["AR", "ident_b"], W=[psk[pz]])
        b.op("pe", lambda e: e.matmul(PS[pz][0:wu, 64:128], lhsT=Nak, rhs=Vh, start=True, stop=True), R=["NK", "Vtm"], W=[psk[pz]])
        b.op("act", lambda e: e.copy(out=Zs[0][0:wu, 0:64], in_=pzb[0:wu, 0:64]), R=[psk[pz]], W=["Z0"])
        b.op("act", lambda e: e.copy(out=Zs[0][0:wu, 64:128], in_=PS[pz][0:wu, 64:128]), R=[psk[pz]], W=["Z0"])
        zc = 0
        for lvl in range(4):
            pq_ = nextps()
            b.op("pe", lambda e, lvl=lvl, zc=zc, pq_=pq_: e.matmul(PS[pq_][0:wu, 0:128], lhsT=Ypow[lvl], rhs=Zs[zc][0:wu, :], start=True, stop=True),
                 R=ykeys[lvl] + [f"Z{zc}"], W=[psk[pq_]])
            b.op("dve", lambda e, zc=zc, pq_=pq_: e.tensor_tensor(out=Zs[1 - zc][0:wu, :], in0=PS[pq_][0:wu, 0:128], in1=Zs[zc][0:wu, :], op=ALU.add),
                 R=[psk[pq_], f"Z{zc}"], W=[f"Z{1 - zc}"])
            zc = 1 - zc
        Z4 = Zs[zc]
        zk = f"Z{zc}"
        Ah, Vhh = Z4[0:wu, 0:64], Z4[0:wu, 64:128]
        b.op("pe", lambda e: e.matmul(PS[pRY][pr, 0:wu], lhsT=Ah, rhs=Arb, start=True, stop=True), R=[zk, "YA"], W=[psk[pRY]])
        b.op("pe", lambda e: e.matmul(PS[pRY][pr, 128:128 + wu], lhsT=Vh, rhs=Ark, start=True, stop=False), R=["Vtm", "NK"], W=[psk[pRY]])
        b.op("pe", lambda e: e.matmul(PS[pRY][pr, 128:128 + wu], lhsT=Vhh, rhs=Arb, start=False, stop=True), R=[zk, "YA"], W=[psk[pRY]])
        pt_ = nextps()
        ptb = PS[pt_][:].bitcast(BF16)
        b.op("pe", lambda e: e.transpose(out=ptb[0:wu, 0:64], in_=Bb_, identity=ident_b[pr, pr]), R=["Bb", "ident_b"], W=[psk[pt_]])
        b.op("pe", lambda e: e.transpose(out=ptb[0:wu, 64:128], in_=Kb_, identity=ident_b[pr, pr]), R=["Kb", "ident_b"], W=[psk[pt_]])
        b.op("act", lambda e: e.copy(out=KBtm[0:wu, :, :], in_=ptb[0:wu, 0:128].rearrange("p (a k) -> p a k", a=2)), R=[psk[pt_]], W=["KBtm"])
        pbase = KBtm[:].ap[0][0]

        def bc_c(ap2):
            return apx(ap2, [ap2.ap[0][0], wu], [[0, nch], [1, 64]])
        cmb = apx(cmask_b[0:wu, 0:1], [cmask_b[:].ap[0][0], wu], [[1, nch], [0, 64]])
        b.op("dve", lambda e: e.tensor_tensor(out=Bexp[0:wu, 0:nch, :], in0=bc_c(KBtm[0:wu, 0, :]), in1=cmb, op=ALU.mult), R=["KBtm", "cmask_b"], W=["Bexp"])
        b.op("pool", lambda e: e.tensor_tensor(out=Vexp[0:wu, 0:nch, :], in0=bc_c(Vh), in1=cmb, op=ALU.mult), R=["Vtm", "cmask_b"], W=["Vexp"])
        b.op("dve", lambda e: e.tensor_tensor(out=Vhexp[0:wu, 0:nch, :], in0=bc_c(Vhh), in1=cmb, op=ALU.mult), R=[zk, "cmask_b"], W=["Vhexp"])
        N = nch * 64
        b.op("pe", lambda e: e.matmul(PS[pM][pr, 0:N], lhsT=Ah, rhs=Bexp[0:wu, 0:nch, :].rearrange("p c k -> p (c k)"), start=True, stop=False),
             R=[zk, "Bexp"], W=[psk[pM]])
        b.op("pe", lambda e: e.matmul(PS[pM][pr, 0:N], lhsT=ident_b[pr, pr], rhs=Wd[pr, hp, 0:nch, :].rearrange("p c k -> p (c k)"), start=False, stop=True),
             R=["ident_b", "Wd"], W=[psk[pM]])
        b.op("pe", lambda e: e.matmul(PS[pGm][pr, 0:N], lhsT=KBtm[0:wu, 1, :], rhs=Vexp[0:wu, 0:nch, :].rearrange("p c k -> p (c k)"), start=True, stop=False),
             R=["KBtm", "Vexp"], W=[psk[pGm]])
        b.op("pe", lambda e: e.matmul(PS[pGm][pr, 0:N], lhsT=KBtm[0:wu, 0, :], rhs=Vhexp[0:wu, 0:nch, :].rearrange("p c k -> p (c k)"), start=False, stop=True),
             R=["KBtm", "Vhexp"], W=[psk[pGm]])

        ARf = AR[:, 0:8 * w].rearrange("p (h two t) -> p h two t", h=4, two=2)
        b.op("dve", lambda e: e.tensor_tensor(out=RhT[pr, hp * 128:hp * 128 + wu], in0=PS[pRY][pr, 0:wu], in1=ARf[pr, hp, 1, 0:wu], op=ALU.add),
             R=[psk[pRY], "AR"], W=["RhT"])
        b.op("act", lambda e: e.copy(out=f["enL"][pr, hp * 128:hp * 128 + wu], in_=PS[pRY][pr, 128:128 + wu]),
             R=[psk[pRY]], W=["f_enL"])
        b.op("act", lambda e: e.copy(out=MT[hp][pr, 0:nch, pr], in_=PS[pM][pr, 0:nch * 64].rearrange("p (c k) -> p c k", c=nch)),
             R=[psk[pM]], W=[f"MT{hp}"])
        b.op("dve", lambda e: e.tensor_copy(out=GB[hp][pr, 0:nch, pr], in_=PS[pGm][pr, 0:nch * 64].rearrange("p (c k) -> p c k", c=nch)),
             R=[psk[pGm]], W=[f"GB{hp}"])

    def finalize(i, w, y3):
        W4 = 4 * w
        ysb = f["L"]
        dd, sq, tt = f["Lx"], f["eL"], f["eLC"]
        pm = nextps()
        b.op("pe", lambda e: e.matmul(PS[pm][:, 0:W4], lhsT=blk64_f[:], rhs=ysb[:, 0:W4], start=True, stop=True), R=["blk64_f", "f_L"], W=[psk[pm]])
        b.op("dve", lambda e: e.tensor_tensor(out=dd[:, 0:W4], in0=ysb[:, 0:W4], in1=PS[pm][:, 0:W4], op=ALU.subtract), R=["f_L", psk[pm]], W=["f_Lx"])
        b.op("act", lambda e: e.activation(out=sq[:, 0:W4], in_=dd[:, 0:W4], func=AF.Square), R=["f_Lx"], W=["f_eL"])
        pv_ = nextps()
        b.op("pe", lambda e: e.matmul(PS[pv_][:, 0:W4], lhsT=blk64_f[:], rhs=sq[:, 0:W4], start=True, stop=True), R=["blk64_f", "f_eL"], W=[psk[pv_]])
        b.op("act", lambda e: e.activation(out=sq[:, 0:W4], in_=PS[pv_][:, 0:W4], func=AF.Sqrt, bias=GN_EPS), R=[psk[pv_]], W=["f_eL"])
        b.op("dve", lambda e: e.reciprocal(out=sq[:, 0:W4], in_=sq[:, 0:W4]), R=["f_eL"], W=["f_eL"])
        b.op("dve", lambda e: e.tensor_tensor(out=dd[:, 0:W4], in0=dd[:, 0:W4], in1=sq[:, 0:W4], op=ALU.mult), R=["f_Lx", "f_eL"], W=["f_Lx"])
        d3, t3, kp3 = v3(dd, w), v3(tt, w), v3(f["kp"], w)
        for hp in range(4):
            b.op("dve", lambda e, hp=hp: e.tensor_scalar(out=d3[:, hp, :], in0=d3[:, hp, :], scalar1=pp(PP_LNW, hp), scalar2=pp(PP_LNB, hp),
                                                         op0=ALU.mult, op1=ALU.add), R=["f_Lx", "ppar"], W=["f_Lx"])
            b.op("dve", lambda e, hp=hp: e.scalar_tensor_tensor(out=t3[:, hp, :], in0=zt[:, hp, 0:w], scalar=pp(PP_RK, hp), in1=kp3[:, hp, :],
                                                                op0=ALU.mult, op1=ALU.mult), R=["zt", "f_kp", "ppar"], W=["f_eLC"])
        pb_ = nextps()
        b.op("pe", lambda e: e.matmul(PS[pb_][:, 0:W4], lhsT=blk1_f[:], rhs=tt[:, 0:W4], start=True, stop=True), R=["blk1_f", "f_eLC"], W=[psk[pb_]])
        b.op("dve", lambda e: e.tensor_tensor(out=t3, in0=PS[pb_][:, 0:W4].rearrange("p (h t) -> p h t", h=4), in1=zt[:, 8:12, 0:w], op=ALU.mult),
             R=[psk[pb_], "zt"], W=["f_eLC"])
        b.op("dve", lambda e: e.tensor_tensor(out=dd[:, 0:W4], in0=dd[:, 0:W4], in1=tt[:, 0:W4], op=ALU.add), R=["f_Lx", "f_eLC"], W=["f_Lx"])
        b.op("dve", lambda e: e.tensor_tensor(out=ygT[:, 0:W4], in0=dd[:, 0:W4], in1=gsb[:, 0:W4], op=ALU.mult), R=["f_Lx", "gsb"], W=["ygT"])
        for hp in range(4):
            b.dma("sp", "oyg", ygs[hp * 128:(hp + 1) * 128, i * 128:i * 128 + w], ygT[:, hp * w:(hp + 1) * w], R=["ygT"], W=["ygs"])

    def sample_rwkv(w, y3):
        W4 = 4 * w
        b.op("act", lambda e: e.activation(out=f["t"][:, 0:W4], in_=f["ld"][:, 0:W4], func=AF.Exp), R=["f_ld"], W=["f_t"])
        b.op("dve", lambda e: e.tensor_scalar(out=f["asig"][:, 0:W4], in0=f["kk"][:, 0:W4], scalar1=-1.0, scalar2=None, op0=ALU.mult),
             R=["f_kk"], W=["f_asig"])
        srcs = [(v3(f["t"], w), "f_t"), (v3(f["asig"], w), "f_asig"), (v3(f["bb"], w), "f_bb"), (v3(f["kp"], w), "f_kp"), (zt[:, 0:4, 0:w], "zt")]
        for qi, (src, key) in enumerate(srcs):
            b.op("dve", lambda e, qi=qi, src=src: e.tensor_copy(out=xb[:, qi, :, :], in_=src[:, :, 16:32]), R=[key], W=["xb"])
            p = nextps()
            pvb_ = PS[p][:].bitcast(BF16)
            for hp in range(4):
                for h2 in range(2):
                    pr = slice(h2 * 64, h2 * 64 + 64)
                    b.op("pe", lambda e, pvb_=pvb_, qi=qi, hp=hp, h2=h2, pr=pr: e.transpose(out=pvb_[h2 * 32:h2 * 32 + 16, hp * 64:(hp + 1) * 64], in_=xb[pr, qi, hp, :],
                                                                                          identity=ident_b[pr, pr]),
                         R=["xb", "ident_b"], W=[psk[p]])
            for h2 in range(2):
                b.op("act", lambda e, pvb_=pvb_, h2=h2, qi=qi: e.copy(out=tm5[h2 * 32:h2 * 32 + 16, qi, :], in_=pvb_[h2 * 32:h2 * 32 + 16, 0:256]),
                     R=[psk[p]], W=["tm5"])
        for s in range(SPC):
            sx = Sx[s % 2]
            sk = f"Sx{s % 2}"
            for h2 in range(2):
                src = I["swkv"][s].rearrange("(hp two) v k -> two v hp k", two=2)[h2]
                b.dma("sp", sk, sx[h2 * 64:(h2 + 1) * 64, :, :], src, W=[sk])
            pbs = [nextps() for _ in range(5)]
            for qi in range(5):
                b.op("pe", lambda e, qi=qi, s=s, pbs=pbs: e.matmul(PS[pbs[qi]][:, 0:256], lhsT=onehot_b[0:48, s, :], rhs=tm5[0:48, qi, :], start=True, stop=True),
                     R=["onehot_b", "tm5"], W=[psk[pbs[qi]]])

            def bq(qi, pbs=pbs):
                return PS[pbs[qi]][:, 0:256].rearrange("p (h k) -> p h k", h=4)
            col = 16 + s
            vcol = apx(zt[:, 8, col:col + 1], [zt[:].ap[0][0], 128], [[128, 4], [0, 64]])
            sab = apx(ssa[:, 0:1], [ssa[:].ap[0][0], 128], [[1, 4], [0, 64]])
            b.op("dve", lambda e, sx=sx, bq=bq: e.tensor_tensor(out=stmp[:], in0=sx[:], in1=bq(1), op=ALU.mult), R=[sk, psk[pbs[1]]], W=["stmp"])
            b.op("dve", lambda e: e.tensor_reduce(out=ssa[:, :], in_=stmp[:], axis=AX.X, op=ALU.add), R=["stmp"], W=["ssa"])
            b.op("dve", lambda e, sx=sx, bq=bq: e.tensor_tensor(out=sx[:], in0=sx[:], in1=bq(0), op=ALU.mult), R=[sk, psk[pbs[0]]], W=[sk])
            b.op("dve", lambda e, bq=bq, sab=sab: e.tensor_tensor(out=stmp[:], in0=sab, in1=bq(2), op=ALU.mult), R=["ssa", psk[pbs[2]]], W=["stmp"])
            b.op("dve", lambda e, sx=sx: e.tensor_tensor(out=sx[:], in0=sx[:], in1=stmp[:], op=ALU.add), R=[sk, "stmp"], W=[sk])
            b.op("dve", lambda e, bq=bq, vcol=vcol: e.tensor_tensor(out=stmp[:], in0=vcol, in1=bq(3), op=ALU.mult), R=["zt", psk[pbs[3]]], W=["stmp"])
            b.op("dve", lambda e, sx=sx: e.tensor_tensor(out=sx[:], in0=sx[:], in1=stmp[:], op=ALU.add), R=[sk, "stmp"], W=[sk])
            b.op("dve", lambda e, sx=sx, bq=bq: e.tensor_tensor(out=stmp[:], in0=sx[:], in1=bq(4), op=ALU.mult), R=[sk, psk[pbs[4]]], W=["stmp"])
            b.op("dve", lambda e, col=col: e.tensor_reduce(out=y3[:, :, col], in_=stmp[:], axis=AX.X, op=ALU.add), R=["stmp"], W=["f_L"])
            for h2 in range(2):
                dst = O["wkv_s"][s].rearrange("(hp two) v k -> two v hp k", two=2)[h2]
                b.dma("sp", "owkvs", dst, sx[h2 * 64:(h2 + 1) * 64, :, :], R=[sk])

    def wkv_prompt_out():
        for hp in range(4):
            S0 = Sbd[hp][scur[hp]]
            k0 = f"S{hp}_{scur[hp]}"
            p = nextps()
            pb_ = PS[p][:].bitcast(BF16) if CH == BF16 else PS[p][:]
            idn = ident_b if CH == BF16 else ident_f
            b.op("pe", lambda e, S0=S0, pb_=pb_, idn=idn: e.transpose(out=pb_[:, 0:128], in_=S0[:, :], identity=idn[:, :]), R=[k0, "ident_b", "ident_f"], W=[psk[p]])
            b.op("act", lambda e, hp=hp, pb_=pb_: e.copy(out=wkvo[:, hp, :], in_=pb_[:, 0:128]), R=[psk[p]], W=["wkvo"])
            for h2 in range(2):
                b.dma("sp", "owkvp", O["wkv_p"][2 * hp + h2], wkvo[h2 * 64:(h2 + 1) * 64, hp, h2 * 64:(h2 + 1) * 64], R=["wkvo"])


    def tile_A2(i):
        w = tw(i)
        buf = i % 2
        if i + 1 < NTL:
            load_x(i + 1, (i + 1) % 2)
        norm_and_transpose(i, buf, gmixB, "gmixB")
        if i == NFULL:
            shrow = zt[:].rearrange("p a b -> p (a b)")
            for piece in range(4):
                p = nextps()
                c0 = piece * 448
                for kc in range(8):
                    b.op("pe", lambda e, kc=kc, p=p, c0=c0: e.matmul(PS[p][0:w, 0:448], lhsT=hT[:, kc, 0:w], rhs=w_in_r[:, kc, c0:c0 + 448],
                                                                     start=(kc == 0), stop=(kc == 7)), R=["hT", "w_in_r"], W=[psk[p]])
                b.op("act", lambda e, p=p, piece=piece: e.copy(out=shrow[0:w, piece * 448:(piece + 1) * 448], in_=PS[p][0:w, 0:448]),
                     R=[psk[p]], W=["zt"])
            b.dma("sp", "osh", O["sh_p"][:, :], shrow[15:16, :], R=["zt"])
            b.dma("sp", "osh", O["sh_s"][:, :], shrow[16:32, :], R=["zt"])
        rwkv_tile(i)

    if stage >= 2:
        load_x(0, 0)
        for i in range(NTL):
            tile_A2(i)
    b.barrier()
    b.emit()
    pst.close()
    b.st = b.st0

    def phase_B():
        wg_b = b.sb("wg_b", [128, 8, 2048], BF16)
        for kc in range(8):
            b.dma("pool", "wA", wg_b[:, kc, :], I["w_in"][kc * 128:(kc + 1) * 128, C_G:INC], W=["wg_b"])
        w_om = load_w_bf("w_om", [128, 4, D], I["w_o_mla"].rearrange("(k p) n -> p k n", p=128))
        w_or = load_w_bf("w_or", [128, 4, D], I["w_o_rwkv"].rearrange("(k p) n -> p k n", p=128))
        w_ob = load_w_bf("w_ob", [128, 8, D], I["w_out"].rearrange("(k p) n -> p k n", p=128))
        gsg = b.sb("gsg", [128, 16, 128], BF16)
        b.op("pool", lambda e: e.memset(gsg[:], 0.0), W=["gsg"])
        atT = b.sb("atT", [128, 4, 128], BF16)
        ygTt = b.sb("ygTt", [128, 4, 128], BF16)
        mT = b.sb("mT", [128, 8, 128], BF16)
        tmpa = b.sb("tmpa", [128, 512], F32)
        tmpb = b.sb("tmpb", [128, 512], F32)
        x1 = b.sb("x1", [128, D], F32)

        def tile(i):
            w = tw(i)
            buf = i % 2
            t0 = 128 * i
            if i + 1 < NTL:
                load_x(i + 1, (i + 1) % 2)
            b.dma("sp", "ldat", atT[:, :, 0:w], ats[:, t0:t0 + w].rearrange("(c p) t -> p c t", p=128), R=["ats"], W=["atT"])
            b.dma("sp", "ldyg", ygTt[:, :, 0:w], ygs[:, t0:t0 + w].rearrange("(c p) t -> p c t", p=128), R=["ygs"], W=["ygTt"])
            norm_and_transpose(i, buf, gmixB, "gmixB")
            for g in range(4):
                p = nextps()
                for jj in range(4):
                    j = 4 * g + jj
                    for kc in range(8):
                        b.op("pe", lambda e, p=p, jj=jj, j=j, kc=kc: e.matmul(PS[p][:, jj * 128:jj * 128 + w], lhsT=wg_b[:, kc, j * 128:(j + 1) * 128], rhs=hT[:, kc, 0:w],
                                                                               start=(kc == 0), stop=(kc == 7)), R=["wg_b", "hT"], W=[psk[p]])
                b.op("act", lambda e, p=p, g=g: e.activation(out=gsg[:, 4 * g:4 * g + 4, 0:w], in_=PS[p][:, 0:512].rearrange("p (j t) -> p j t", j=4)[:, :, 0:w], func=AF.Sigmoid),
                     R=[psk[p]], W=["gsg"])
            for g in range(2):
                pm_, pr_ = nextps(), nextps()
                for jj in range(4):
                    j = 4 * g + jj
                    for kc in range(4):
                        b.op("pe", lambda e, pm_=pm_, jj=jj, j=j, kc=kc: e.matmul(PS[pm_][:, jj * 128:jj * 128 + w], lhsT=w_om[:, kc, j * 128:(j + 1) * 128], rhs=atT[:, kc, 0:w],
                                                                                   start=(kc == 0), stop=(kc == 3)), R=["w_om", "atT"], W=[psk[pm_]])
                    for kc in range(4):
                        b.op("pe", lambda e, pr_=pr_, jj=jj, j=j, kc=kc: e.matmul(PS[pr_][:, jj * 128:jj * 128 + w], lhsT=w_or[:, kc, j * 128:(j + 1) * 128], rhs=ygTt[:, kc, 0:w],
                                                                                   start=(kc == 0), stop=(kc == 3)), R=["w_or", "ygTt"], W=[psk[pr_]])
                v1 = PS[pm_][:, 0:512].rearrange("p (j t) -> p j t", j=4)
                v2 = PS[pr_][:, 0:512].rearrange("p (j t) -> p j t", j=4)
                ta = tmpa[:, 0:512].rearrange("p (j t) -> p j t", j=4)
                tb = tmpb[:, 0:512].rearrange("p (j t) -> p j t", j=4)
                b.op("dve", lambda e, v1=v1, ta=ta, g=g: e.tensor_tensor(out=ta, in0=v1, in1=gsg[:, 4 * g:4 * g + 4, :], op=ALU.mult), R=[psk[pm_], "gsg"], W=["tmpa"])
                b.op("dve", lambda e, v2=v2, tb=tb, g=g: e.tensor_tensor(out=tb, in0=v2, in1=gsg[:, 8 + 4 * g:8 + 4 * g + 4, :], op=ALU.mult), R=[psk[pr_], "gsg"], W=["tmpb"])
                b.op("pool", lambda e, ta=ta, tb=tb, g=g: e.tensor_tensor(out=mT[:, 4 * g:4 * g + 4, :], in0=ta, in1=tb, op=ALU.add), R=["tmpa", "tmpb"], W=["mT"])
            for half in range(2):
                po_ = nextps()
                for kc in range(8):
                    b.op("pe", lambda e, po_=po_, kc=kc, half=half: e.matmul(PS[po_][0:w, 0:512], lhsT=mT[:, kc, 0:w], rhs=w_ob[:, kc, half * 512:(half + 1) * 512],
                                                                             start=(kc == 0), stop=(kc == 7)), R=["mT", "w_ob"], W=[psk[po_]])
                b.op("dve", lambda e, po_=po_, half=half: e.tensor_tensor(out=x1[0:w, half * 512:(half + 1) * 512], in0=PS[po_][0:w, 0:512], in1=xt[buf][0:w, half * 512:(half + 1) * 512], op=ALU.add),
                     R=[psk[po_], f"xt{buf}"], W=["x1"])
            b.dma("sp", "ox1", x1s[t0:t0 + w, :], x1[0:w, :], R=["x1"], W=["x1s"])

        load_x(0, 0)
        for i in range(NTL):
            tile(i)

    if stage >= 4:
        with ExitStack() as pst:
            b.st = pst
            phase_B()
            b.barrier()
            b.emit()
        b.st = b.st0

    def phase_C():
        w_up_b = b.sb("w_up_b", [128, 8, DFF], BF16)
        for kc in range(8):
            b.dma("pool", "wA", w_up_b[:, kc, :], I["w_up"][kc * 128:(kc + 1) * 128, :], W=["w_up_b"])
        w_dn_b = b.sb("w_dn_b", [128, 32, D], BF16)
        for k4 in range(8):
            b.dma("pool", "wA", w_dn_b[:, 4 * k4:4 * k4 + 4, :], I["w_down"][512 * k4:512 * (k4 + 1), :].rearrange("(k p) n -> p k n", p=128), W=["w_dn_b"])
        gffnB = bcast_load("gffnB", I["g_ffn"], D)
        gfinB = bcast_load("gfinB", I["g_final"], D)
        uT = b.sb("uT", [128, 32, 128], BF16)
        rl_ = b.sb("relu_t", [128, 512], BF16)
        x2 = b.sb("x2", [128, D], F32)
        yo = b.sb("yo", [128, D], F32)

        def tile(i):
            w = tw(i)
            buf = i % 2
            t0 = 128 * i
            if i + 1 < NTL:
                load_x(i + 1, (i + 1) % 2, src_scratch=True)
            norm_and_transpose(i, buf, gffnB, "gffnB")
            for g in range(8):
                p = nextps()
                for jj in range(4):
                    j = 4 * g + jj
                    for kc in range(8):
                        b.op("pe", lambda e, p=p, jj=jj, j=j, kc=kc: e.matmul(PS[p][:, jj * 128:jj * 128 + w], lhsT=w_up_b[:, kc, j * 128:(j + 1) * 128], rhs=hT[:, kc, 0:w],
                                                                               start=(kc == 0), stop=(kc == 7)), R=["w_up_b", "hT"], W=[psk[p]])
                rv = rl_[:, 0:4 * w].rearrange("p (j t) -> p j t", j=4)
                b.op("act", lambda e, p=p, rv=rv: e.activation(out=rv, in_=PS[p][:, 0:512].rearrange("p (j t) -> p j t", j=4)[:, :, 0:w], func=AF.Relu), R=[psk[p]], W=["relu_t"])
                b.op("pool", lambda e, g=g, rv=rv: e.tensor_tensor(out=uT[:, 4 * g:4 * g + 4, 0:w], in0=rv, in1=rv, op=ALU.mult), R=["relu_t"], W=["uT"])
            for half in range(2):
                po_ = nextps()
                for kc in range(32):
                    b.op("pe", lambda e, po_=po_, kc=kc, half=half: e.matmul(PS[po_][0:w, 0:512], lhsT=uT[:, kc, 0:w], rhs=w_dn_b[:, kc, half * 512:(half + 1) * 512],
                                                                             start=(kc == 0), stop=(kc == 31)), R=["uT", "w_dn_b"], W=[psk[po_]])
                b.op("dve", lambda e, po_=po_, half=half: e.tensor_tensor(out=x2[0:w, half * 512:(half + 1) * 512], in0=PS[po_][0:w, 0:512], in1=xt[buf][0:w, half * 512:(half + 1) * 512], op=ALU.add),
                     R=[psk[po_], f"xt{buf}"], W=["x2"])
            rms_rstd(x2[0:w, :], ["x2"], w, D, 3, NORM_EPS)
            b.op("dve", lambda e: e.scalar_tensor_tensor(out=yo[0:w, :], in0=x2[0:w, :], scalar=st1[0:w, 3:4], in1=gfinB[0:w, :], op0=ALU.mult, op1=ALU.mult),
                 R=["x2", "st1_3", "gfinB"], W=["yo"])
            if i == 0:
                b.dma("sp", "oy", O["y_p"][0:112, :], yo[16:128, :], R=["yo"])
            elif i < NFULL:
                b.dma("sp", "oy", O["y_p"][128 * i - 16:128 * i + 112, :], yo[:, :], R=["yo"])
            else:
                b.dma("sp", "oy", O["y_p"][SEQ - 16:SEQ, :], yo[0:16, :], R=["yo"])
                b.dma("sp", "oy", O["y_s"][:, :], yo[16:32, :], R=["yo"])

        load_x(0, 0, src_scratch=True)
        for i in range(NTL):
            tile(i)

    if stage >= 4:
        with ExitStack() as pst:
            b.st = pst
            phase_C()
            b.barrier()
            b.emit()
        b.st = b.st0


def kernel(**inputs):
    x_prompt = np.asarray(inputs["x_prompt"])
    x_sample = np.asarray(inputs["x_sample"])
    ncores, seq = x_prompt.shape[0], x_prompt.shape[1]
    cache = np.asarray(inputs["cache_kv"])[0]
    pt = np.asarray(inputs["page_table"]).astype(np.int32)
    npool, npg = cache.shape[0], pt.shape[1]
    cfg = make_cfg(seq, npg, npool, npg * 128)
    import os
    nc, consts = build(cfg, stage=int(os.environ.get('KSTAGE', '99')))
    wsh = dict(g_final=[1, D], g_mix=[1, D], w_in=[D, INC], g_q=[1, NQ], w_uq=[NQ, 768], g_kv=[1, NKV],
               w_uk=[NKV, 512], w_uv=[NKV, 512], w_o_mla=[512, D], mu_shift=[1, RC], w0=[1, RW], w2=[64, RW],
               a0=[1, RW], a2=[64, RW], g2=[128, RW], k_k=[1, RW], k_a=[1, RW], r_k=[1, RW], ln_w=[1, RW],
               ln_b=[1, RW], w_o_rwkv=[RW, D], w_out=[D, D], g_ffn=[1, D], w_up=[D, DFF], w_down=[DFF, D])
    shared = {}
    for n in WNAMES:
        shared[n] = np.ascontiguousarray(np.asarray(inputs[n], dtype=np.float32).reshape(wsh[n]))
    for n, a in consts.items():
        shared["c_" + n] = a
    shared["meta"] = np.ascontiguousarray(np.asarray(inputs["meta_tokens"], dtype=np.float32))
    shared["cache"] = np.ascontiguousarray(cache)
    swkv = np.asarray(inputs["state_wkv"])[0]
    sshift = np.asarray(inputs["state_shift"])[0]
    in_maps = []
    for c in range(ncores):
        m = dict(shared)
        m["xp"] = np.ascontiguousarray(x_prompt[c])
        m["xs"] = np.ascontiguousarray(x_sample[SPC * c:SPC * (c + 1), 0])
        m["pt"] = np.ascontiguousarray(pt[SPC * c:SPC * (c + 1)])
        m["swkv"] = np.ascontiguousarray(swkv[SPC * c:SPC * (c + 1)])
        m["sshift"] = np.ascontiguousarray(sshift[SPC * c:SPC * (c + 1)])
        in_maps.append(m)
    res = run_bass_kernel_spmd(nc, in_maps, core_ids=list(range(ncores)))
    r = res.results
    y_p = np.stack([r[c]["y_p"] for c in range(ncores)])
    y_s = np.concatenate([r[c]["y_s"] for c in range(ncores)])[:, None, :]
    kv_p = np.stack([r[c]["kv_p"] for c in range(ncores)])[None]
    wkv_p = np.stack([r[c]["wkv_p"] for c in range(ncores)])[None]
    sh_p = np.concatenate([r[c]["sh_p"] for c in range(ncores)])[None]
    kv_s = np.concatenate([r[c]["kv_s"] for c in range(ncores)])[None, :, None, :]
    wkv_s = np.concatenate([r[c]["wkv_s"] for c in range(ncores)])[None]
    sh_s = np.concatenate([r[c]["sh_s"] for c in range(ncores)])[None]
    return tuple(np.ascontiguousarray(a, dtype=np.float32) for a in (y_p, y_s, kv_p, wkv_p, sh_p, kv_s, wkv_s, sh_s))
```

```python
import numpy as np
from contextlib import ExitStack
import concourse.bass as bass
import concourse.mybir as mybir
from concourse.bass_utils import run_bass_kernel_spmd

F32 = mybir.dt.float32
BF16 = mybir.dt.bfloat16
I32 = mybir.dt.int32
AF = mybir.ActivationFunctionType
ALU = mybir.AluOpType
AX = mybir.AxisListType

D = 1024
NQ, NKV, NRP = 384, 256, 32
KVW = 288
H = 8
RW = 512
RC = 1792
INC = 4512
C_Q, C_KV, C_RW, C_G = 0, 384, 672, 2464
DFF = 4096
NORM_EPS = 1e-6
GN_EPS = 64e-5
SM_SCALE = 96 ** -0.5
NMETA = 16
SPC = 16

WNAMES = ["g_final", "g_mix", "w_in", "g_q", "w_uq", "g_kv", "w_uk", "w_uv", "w_o_mla", "mu_shift", "w0", "w2",
          "a0", "a2", "g2", "k_k", "k_a", "r_k", "ln_w", "ln_b", "w_o_rwkv", "w_out", "g_ffn", "w_up", "w_down"]


def apx(base, part, free):
    return bass.AP(base.tensor, base.offset, [list(part)] + [list(f) for f in free])


class B:
    ENG = ("pe", "act", "dve", "pool", "sp")

    def __init__(self, nc, st):
        self.nc, self.st, self.st0 = nc, st, st
        self.q = {e: [] for e in self.ENG}
        if getattr(self, "_after_barrier", False):
            self.new_engine_sems()
            self._after_barrier = False
        self.sems = {}
        self.cnt = {}
        self.ek = {}
        self.phase = 0
        self.new_engine_sems()
        self.waited = {e: {} for e in self.ENG}
        self.bufs = {}
        self.nps = 0

    def new_engine_sems(self):
        self.phase += 1
        for e in ("pe", "act", "dve", "pool"):
            k = f"{e}#{self.phase}"
            self.sems[k] = self.st0.enter_context(self.nc.semaphore("s_" + k.replace("#", "_")))
            self.cnt[k] = 0
            self.ek[e] = k

    def sb(self, name, shape, dt):
        return self.st.enter_context(self.nc.sbuf_tensor(name, list(shape), dt))

    def psum(self, name, shape, dt):
        return self.st0.enter_context(self.nc.psum_tensor(name, list(shape), dt))

    def _buf(self, k):
        if k not in self.bufs:
            self.bufs[k] = {"w": None, "r": {}}
        return self.bufs[k]

    def _waits(self, eng, R, W):
        toks = {}

        def need(t):
            if t is not None:
                toks[t[0]] = max(toks.get(t[0], 0), t[1])
        for k in R:
            need(self._buf(k)["w"])
        for k in W:
            b = self._buf(k)
            need(b["w"])
            for kk, v in b["r"].items():
                need((kk, v))
        for k, v in toks.items():
            if eng == "pe" and k == self.ek["pe"]:
                continue
            if "#" not in k:
                v = self.cnt[k]
            if self.waited[eng].get(k, 0) >= v:
                continue
            self.waited[eng][k] = v
            sem = self.sems[k]
            self.q[eng].append(lambda e, sem=sem, v=v: e.wait_ge(sem, v))

    def _mark(self, tok, R, W):
        for k in R:
            b = self._buf(k)
            b["r"][tok[0]] = max(b["r"].get(tok[0], 0), tok[1])
        for k in W:
            self.bufs[k] = {"w": tok, "r": {}}

    def op(self, eng, fn, R=(), W=()):
        self._waits(eng, R, W)
        k = self.ek[eng]
        self.cnt[k] += 1
        sem = self.sems[k]
        self.q[eng].append(lambda e, fn=fn, sem=sem: fn(e).then_inc(sem, 1))
        self._mark((k, self.cnt[k]), R, W)

    DRAM_KEYS = ("ats", "ygs", "kvs", "x1s")

    def _semkey(self, R, W):
        if W and W[0] not in self.DRAM_KEYS:
            k = "i_" + W[0]
        else:
            k = "o_" + R[0]
        if k not in self.sems:
            self.sems[k] = self.st0.enter_context(self.nc.semaphore(k))
            self.cnt[k] = 0
        return k

    def dma(self, queue, semkey, out, in_, R=(), W=(), **kw):
        semkey = self._semkey(R, W)
        self._waits(queue, R, W)
        self.cnt[semkey] += 16
        sem = self.sems[semkey]
        self.q[queue].append(lambda e, sem=sem, out=out, in_=in_, kw=kw: e.dma_start(out=out, in_=in_, **kw).then_inc(sem, 16))
        self._mark((semkey, self.cnt[semkey]), R, W)

    def raw_dma(self, queue, semkey, fn, R=(), W=()):
        semkey = self._semkey(R, W)
        self._waits(queue, R, W)
        self.cnt[semkey] += 16
        sem = self.sems[semkey]
        self.q[queue].append(lambda e, sem=sem, fn=fn: fn(e).then_inc(sem, 16))
        self._mark((semkey, self.cnt[semkey]), R, W)

    def finish(self):
        for k, sem in self.sems.items():
            v = self.cnt[k]
            if v and self.waited["sp"].get(k, 0) < v:
                self.waited["sp"][k] = v
                self.q["sp"].append(lambda e, sem=sem, v=v: e.wait_ge(sem, v))

    def barrier(self):
        self._after_barrier = True
        for e in self.ENG:
            for k, sem in self.sems.items():
                v = self.cnt[k]
                if v and self.waited[e].get(k, 0) < v and not (k == self.ek.get(e)):
                    self.waited[e][k] = v
                    self.q[e].append(lambda eng, sem=sem, v=v: eng.wait_ge(sem, v))

    def emit(self):
        with self.nc.Block() as blk:
            for name, deco in (("pe", blk.tensor), ("act", blk.scalar), ("dve", blk.vector),
                               ("pool", blk.gpsimd), ("sp", blk.sync)):
                lst = self.q[name]

                def run(e, lst=lst):
                    for th in lst:
                        th(e)
                deco(run)
        self.q = {e: [] for e in self.ENG}


def host_consts(cfg):
    T, NTOK, NTL = cfg["T"], cfg["NTOK"], cfg["NTL"]
    c = {}
    idx = np.arange(128)
    c["ident"] = np.eye(128, dtype=np.float32)
    same = (idx[:, None] // 16) == (idx[None, :] // 16)
    c["maskx"] = (same & (idx[None, :] < idx[:, None])).astype(np.float32)
    mj = np.zeros((128, 2, 128), np.float32)
    mj[:, 0, :] = same & (idx[None, :] > idx[:, None])
    mj[:, 1, :] = same & (idx[None, :] >= idx[:, None])
    c["maskjt"] = mj.reshape(128, 256)
    c["cmask"] = (idx[:, None] // 16 == np.arange(8)[None, :]).astype(np.float32)
    c["blk1"] = ((idx[:, None] // 64) == (idx[None, :] // 64)).astype(np.float32)
    c["maskc"] = (idx[None, :] >= idx[:, None]).astype(np.float32)
    c["identb"] = (idx[:, None] % 64 == np.arange(64)[None, :]).astype(np.float32)
    rm = np.ones((128, 512), np.float32)
    rm[:, ::16] = 0.0
    c["scanm"] = rm
    oh = np.zeros((128, 16, 128), np.float32)
    for s in range(16):
        oh[s, s, 0:64] = 1.0
        oh[32 + s, s, 64:128] = 1.0
    c["onehot"] = oh.reshape(128, 2048)
    inv = (10000.0 ** (-np.arange(0, 32, 2, dtype=np.float32) / np.float32(32))).astype(np.float32)
    pos = np.concatenate([np.arange(T), np.full(SPC, cfg["PAST"])]).astype(np.float32)
    ang = (pos[:, None] * inv[None, :]).astype(np.float32)
    tab = np.zeros((NTL * 128, 32), np.float32)
    tab[:NTOK, :16] = np.cos(ang)
    tab[:NTOK, 16:] = np.sin(ang)
    c["rope"] = np.ascontiguousarray(tab.reshape(NTL, 128, 32).transpose(1, 0, 2)).reshape(128, NTL * 32)
    return c


def make_cfg(seq, npg, npool, past):
    T = seq + NMETA
    assert seq % 128 == 0
    NTOK = T + SPC
    NTL = (NTOK + 127) // 128
    return dict(SEQ=seq, T=T, NTOK=NTOK, NTL=NTL, NPG=npg, NPOOL=npool, PAST=past)


def build(cfg, stage=99):
    SEQ, T, NTOK, NTL, NPG, NPOOL = cfg["SEQ"], cfg["T"], cfg["NTOK"], cfg["NTL"], cfg["NPG"], cfg["NPOOL"]
    nc = bass.Bass("TRN2", target_bir_lowering=False)
    consts = host_consts(cfg)

    def din(name, shape, dt=F32):
        return nc.dram_tensor(name, list(shape), dt, kind="ExternalInput").ap()

    def dout(name, shape, dt=F32):
        return nc.dram_tensor(name, list(shape), dt, kind="ExternalOutput").ap()

    I = {}
    I["xp"] = din("xp", [SEQ, D])
    I["xs"] = din("xs", [SPC, D])
    I["meta"] = din("meta", [NMETA, D])
    I["cache"] = din("cache", [NPOOL, 128, KVW])
    I["pt"] = din("pt", [SPC, NPG], I32)
    I["swkv"] = din("swkv", [SPC, H, 64, 64])
    I["sshift"] = din("sshift", [SPC, RC])
    wshapes = dict(g_final=[1, D], g_mix=[1, D], w_in=[D, INC], g_q=[1, NQ], w_uq=[NQ, 768], g_kv=[1, NKV],
                   w_uk=[NKV, 512], w_uv=[NKV, 512], w_o_mla=[512, D], mu_shift=[1, RC], w0=[1, RW], w2=[64, RW],
                   a0=[1, RW], a2=[64, RW], g2=[128, RW], k_k=[1, RW], k_a=[1, RW], r_k=[1, RW], ln_w=[1, RW],
                   ln_b=[1, RW], w_o_rwkv=[RW, D], w_out=[D, D], g_ffn=[1, D], w_up=[D, DFF], w_down=[DFF, D])
    for n in WNAMES:
        I[n] = din(n, wshapes[n])
    for n, a in consts.items():
        I["c_" + n] = din("c_" + n, a.shape)
    O = {}
    O["y_p"] = dout("y_p", [SEQ, D])
    O["y_s"] = dout("y_s", [SPC, D])
    O["kv_p"] = dout("kv_p", [T, KVW])
    O["wkv_p"] = dout("wkv_p", [H, 64, 64])
    O["sh_p"] = dout("sh_p", [1, RC])
    O["kv_s"] = dout("kv_s", [SPC, KVW])
    O["wkv_s"] = dout("wkv_s", [SPC, H, 64, 64])
    O["sh_s"] = dout("sh_s", [SPC, RC])

    st = ExitStack()
    with st:
        b = B(nc, st)
        _program(nc, b, cfg, I, O, stage)
        b.finish()
        b.emit()
    return nc, consts


def _program(nc, b, cfg, I, O, stage):
    import os
    KA1 = int(os.environ.get('KA1', '3'))
    KQ = int(os.environ.get('KQ', '9'))
    KR = int(os.environ.get('KR', '9'))
    SEQ, T, NTOK, NTL, NPG, NPOOL = cfg["SEQ"], cfg["T"], cfg["NTOK"], cfg["NTL"], cfg["NPG"], cfg["NPOOL"]
    NFULL = NTL - 1
    LASTW = NTOK - NFULL * 128
    PP2 = 2 * NPG

    def tw(i):
        return 128 if i < NFULL else LASTW

    PS = [b.psum(f"ps{i}", [128, 512], F32) for i in range(8)]
    psk = [f"ps{i}" for i in range(8)]
    psn = [0]

    reserved = set()

    def nextps():
        while True:
            i = psn[0] % 8
            psn[0] += 1
            if i not in reserved:
                return i

    def reserve():
        i = nextps()
        reserved.add(i)
        return i

    def release(i):
        reserved.discard(i)

    def flat(name, dt=F32, n=512):
        return b.sb(name, [128, n], dt)

    def v3(t, w, g=4):
        return t[:, 0:g * w].rearrange("p (g t) -> p g t", g=g)

    ygs = nc.dram_tensor("ygs", [RW, NTL * 128], BF16, kind="Internal").ap()
    ats = nc.dram_tensor("ats", [RW, NTL * 128], BF16, kind="Internal").ap()
    kvs = nc.dram_tensor("kvs", [SPC, KVW], F32, kind="Internal").ap()
    x1s = nc.dram_tensor("x1s", [NTL * 128, D], F32, kind="Internal").ap()

    ident_f = b.sb("ident_f", [128, 128], F32)
    ident_b = b.sb("ident_b", [128, 128], BF16)
    b.dma("sp", "c0", ident_f[:], I["c_ident"], W=["ident_f"])
    b.op("dve", lambda e: e.tensor_copy(out=ident_b[:], in_=ident_f[:]), R=["ident_f"], W=["ident_b"])
    rope = b.sb("rope", [128, NTL, 32], F32)
    b.dma("sp", "c0", rope[:].rearrange("p a b -> p (a b)"), I["c_rope"], W=["rope"])

    def bcast_load(name, src, n):
        t = b.sb(name, [128, n], F32)
        b.dma("sp", "c0", t[:], src.partition_broadcast(128).rearrange("p o n -> p (o n)"), W=[name])
        return t

    gmixB = bcast_load("gmixB", I["g_mix"], D)
    xt = [b.sb(f"xt{i}", [128, D], F32) for i in range(2)]
    junk = b.sb("junk", [128, D], F32)
    hb = b.sb("hb", [128, D], BF16)
    hT = b.sb("hT", [128, 8, 128], BF16)
    st1 = b.sb("st1", [128, 8], F32)
    rt1 = b.sb("rt1", [128, H, 16], F32)
    rt2 = b.sb("rt2", [128, H, 16], F32)
    qsT = b.sb("qsT", [128, H, 16], BF16)

    def load_x(i, buf, src_scratch=False):
        w = tw(i)
        k = f"xt{buf}"
        if src_scratch:
            b.dma("sp", k, xt[buf][0:w, :], x1s[128 * i:128 * i + w, :], R=["x1s"], W=[k])
            return
        if i == 0:
            b.dma("sp", k, xt[buf][0:16, :], I["meta"], W=[k])
            b.dma("sp", k, xt[buf][16:128, :], I["xp"][0:112, :], W=[k])
        elif i < NFULL:
            b.dma("sp", k, xt[buf][:, :], I["xp"][128 * i - 16:128 * i + 112, :], W=[k])
        else:
            b.dma("sp", k, xt[buf][0:16, :], I["xp"][SEQ - 16:SEQ, :], W=[k])
            b.dma("sp", k, xt[buf][16:32, :], I["xs"], W=[k])

    def rms_rstd(src_ap, srckeys, w, n, col, eps):
        b.op("act", lambda e: e.activation(out=junk[0:w, 0:n], in_=src_ap, func=AF.Square), R=srckeys, W=["junk"])
        b.op("dve", lambda e: e.tensor_reduce(out=st1[0:w, col:col + 1], in_=junk[0:w, 0:n], axis=AX.X, op=ALU.add),
             R=["junk"], W=[f"st1_{col}"])
        b.op("act", lambda e: e.activation(out=st1[0:w, col:col + 1], in_=st1[0:w, col:col + 1], func=AF.Sqrt,
                                           bias=eps, scale=1.0 / n), R=[f"st1_{col}"], W=[f"st1_{col}"])
        b.op("dve", lambda e: e.reciprocal(out=st1[0:w, col:col + 1], in_=st1[0:w, col:col + 1]),
             R=[f"st1_{col}"], W=[f"st1_{col}"])

    def rope_tm(dst, src, w, i, nh, keysR, keysW):
        cosb = apx(rope[0:w, i, 0:16], [rope[:].ap[0][0], w], [[0, nh], [1, 16]])
        sinb = apx(rope[0:w, i, 16:32], [rope[:].ap[0][0], w], [[0, nh], [1, 16]])
        x1, x2 = src[:, :, 0:16], src[:, :, 16:32]
        t1, t2 = rt1[0:w, 0:nh, :], rt2[0:w, 0:nh, :]
        b.op("dve", lambda e: e.tensor_tensor(out=t1, in0=x1, in1=cosb, op=ALU.mult), R=keysR + ["rope"], W=["rt1"])
        b.op("dve", lambda e: e.tensor_tensor(out=t2, in0=x2, in1=sinb, op=ALU.mult), R=keysR + ["rope"], W=["rt2"])
        b.op("dve", lambda e: e.tensor_tensor(out=dst[:, :, 0:16], in0=t1, in1=t2, op=ALU.subtract), R=["rt1", "rt2"], W=keysW)
        b.op("dve", lambda e: e.tensor_tensor(out=t1, in0=x1, in1=sinb, op=ALU.mult), R=keysR + ["rope"], W=["rt1"])
        b.op("dve", lambda e: e.tensor_tensor(out=t2, in0=x2, in1=cosb, op=ALU.mult), R=keysR + ["rope"], W=["rt2"])
        b.op("dve", lambda e: e.tensor_tensor(out=dst[:, :, 16:32], in0=t1, in1=t2, op=ALU.add), R=["rt1", "rt2"], W=keysW)

    def norm_and_transpose(i, buf, gB, gkey, dst=None, dkey="hT"):
        w = tw(i)
        k = f"xt{buf}"
        dst = hT if dst is None else dst
        rms_rstd(xt[buf][0:w, :], [k], w, D, 0, NORM_EPS)
        b.op("dve", lambda e: e.scalar_tensor_tensor(out=hb[0:w, :], in0=xt[buf][0:w, :], scalar=st1[0:w, 0:1],
                                                     in1=gB[0:w, :], op0=ALU.mult, op1=ALU.mult),
             R=[k, "st1_0", gkey], W=["hb"])
        p = nextps()
        pv = PS[p][:].bitcast(BF16)
        for kc in range(8):
            b.op("pe", lambda e, kc=kc: e.transpose(out=pv[:, kc * 128:kc * 128 + w], in_=hb[0:w, kc * 128:(kc + 1) * 128],
                                                    identity=ident_b[0:w, 0:w]), R=["hb", "ident_b"], W=[psk[p]])
        src = pv[:, 0:1024].rearrange("p (k t) -> p k t", k=8)[:, :, 0:w]
        b.op("act", lambda e: e.copy(out=dst[:, :, 0:w], in_=src), R=[psk[p]], W=[dkey])

    def load_w_bf(name, shape, src, key=None):
        t = b.sb(name, shape, BF16)
        b.dma("pool", "wA", t[:], src, W=[key or name])
        return t

    b.emit()

    def phase_A1():
        NW = C_RW
        w_in_b = b.sb("w_in_a", [128, 8, NW], BF16)
        for kc in range(8):
            b.dma("pool", "wA", w_in_b[:, kc, :], I["w_in"][kc * 128:(kc + 1) * 128, 0:NW], W=["w_in_a"])
        w_uq_b = load_w_bf("w_uq_b", [128, 3, 768], I["w_uq"].rearrange("(k p) n -> p k n", p=128))
        w_uk_b = load_w_bf("w_uk_b", [128, 2, 512], I["w_uk"].rearrange("(k p) n -> p k n", p=128))
        w_uv_b = load_w_bf("w_uv_b", [128, 2, 512], I["w_uv"].rearrange("(k p) n -> p k n", p=128))
        maskc_b = load_w_bf("maskc_b", [128, 128], I["c_maskc"])
        gqB = bcast_load("gqB", I["g_q"], NQ)
        gkvB = bcast_load("gkvB", I["g_kv"], NKV)
        KT = b.sb("KT", [128, H, NFULL * 128 + 128], BF16)
        Vc = b.sb("Vc", [128, NTL, H, 65], BF16)
        b.op("pool", lambda e: e.memset(Vc[:], 1.0), W=["Vc"])
        qT = b.sb("qT", [128, H, 128], BF16)
        qn = b.sb("qn", [128, NQ], BF16)
        qf32 = b.sb("qf32", [128, 384], F32)
        qnT = b.sb("qnT", [128, 3, 128], BF16)
        qtm = b.sb("qtm", [128, H, 128], BF16)
        b.op("pool", lambda e: e.memset(qtm[:], 0.0), W=["qtm"])
        kvrow = b.sb("kvrow", [128, KVW], F32)
        kvb = b.sb("kvb", [128, 320], BF16)
        b.op("pool", lambda e: e.memset(kvb[:], 0.0), W=["kvb"])
        ckvT = b.sb("ckvT", [128, 2, 128], BF16)
        PT = [b.sb(f"PT{j}", [128, 4, 128], BF16) for j in range(4)]
        attn_tm = b.sb("attn_tm", [128, 512], BF16)
        attnT = b.sb("attnT", [128, 4, 128], BF16)
        rec = b.sb("rec", [128, 8], F32)
        osb = b.sb("osb", [128, 260], F32)

        def tile(i):
            w = tw(i)
            wk = 128 if i < NFULL else 16
            buf = i % 2
            if i + 1 < NTL:
                load_x(i + 1, (i + 1) % 2)
            norm_and_transpose(i, buf, gmixB, "gmixB")
            pq, pkv = nextps(), nextps()
            for kc in range(8):
                b.op("pe", lambda e, kc=kc: e.matmul(PS[pq][0:w, 0:NQ], lhsT=hT[:, kc, 0:w], rhs=w_in_b[:, kc, 0:NQ],
                                                     start=(kc == 0), stop=(kc == 7)), R=["hT", "w_in_a"], W=[psk[pq]])
            for kc in range(8):
                b.op("pe", lambda e, kc=kc: e.matmul(PS[pkv][0:w, 0:KVW], lhsT=hT[:, kc, 0:w], rhs=w_in_b[:, kc, C_KV:C_KV + KVW],
                                                     start=(kc == 0), stop=(kc == 7)), R=["hT", "w_in_a"], W=[psk[pkv]])
            rms_rstd(PS[pkv][0:w, 0:NKV], [psk[pkv]], w, NKV, 1, NORM_EPS)
            b.op("dve", lambda e: e.scalar_tensor_tensor(out=kvrow[0:w, 0:NKV], in0=PS[pkv][0:w, 0:NKV], scalar=st1[0:w, 1:2],
                                                         in1=gkvB[0:w, :], op0=ALU.mult, op1=ALU.mult),
                 R=[psk[pkv], "st1_1", "gkvB"], W=["kvrow"])
            rope_tm(kvrow[0:w, NKV:KVW].rearrange("p (h r) -> p h r", h=1), PS[pkv][0:w, NKV:KVW].rearrange("p (h r) -> p h r", h=1),
                    w, i, 1, [psk[pkv]], ["kvrow"])
            if i < NFULL:
                b.dma("sp", "okv", O["kv_p"][128 * i:128 * i + 128, :], kvrow[:, :], R=["kvrow"])
            else:
                b.dma("sp", "okv", O["kv_p"][128 * i:128 * i + 16, :], kvrow[0:16, :], R=["kvrow"])
                b.dma("sp", "okv", O["kv_s"][:, :], kvrow[16:32, :], R=["kvrow"])
                b.dma("sp", "okv", kvs[:, :], kvrow[16:32, :], R=["kvrow"], W=["kvs"])
            if KQ < 1:
                return
            rms_rstd(PS[pq][0:w, 0:NQ], [psk[pq]], w, NQ, 2, NORM_EPS)
            b.op("dve", lambda e: e.scalar_tensor_tensor(out=qn[0:w, :], in0=PS[pq][0:w, 0:NQ], scalar=st1[0:w, 2:3],
                                                         in1=gqB[0:w, :], op0=ALU.mult, op1=ALU.mult),
                 R=[psk[pq], "st1_2", "gqB"], W=["qn"])
            p = nextps()
            pvb = PS[p][:].bitcast(BF16)
            for kc in range(3):
                b.op("pe", lambda e, kc=kc: e.transpose(out=pvb[:, kc * 128:kc * 128 + w], in_=qn[0:w, kc * 128:(kc + 1) * 128],
                                                        identity=ident_b[0:w, 0:w]), R=["qn", "ident_b"], W=[psk[p]])
            b.op("act", lambda e: e.copy(out=qnT[:, :, 0:w], in_=pvb[:, 0:384].rearrange("p (k t) -> p k t", k=3)[:, :, 0:w]),
                 R=[psk[p]], W=["qnT"])
            if KQ < 2:
                return
            for half in range(2):
                ph = nextps()
                for kc in range(3):
                    b.op("pe", lambda e, kc=kc, ph=ph, half=half: e.matmul(PS[ph][0:w, 0:384], lhsT=qnT[:, kc, 0:w],
                                                                           rhs=w_uq_b[:, kc, half * 384:(half + 1) * 384],
                                                                           start=(kc == 0), stop=(kc == 2)), R=["qnT", "w_uq_b"], W=[psk[ph]])
                b.op("act", lambda e, ph=ph: e.copy(out=qf32[0:w, :], in_=PS[ph][0:w, 0:384]), R=[psk[ph]], W=["qf32"])
                pv4 = qf32[0:w, :].rearrange("p (h x) -> p h x", h=4)
                b.op("pool", lambda e, pv4=pv4, half=half: e.tensor_copy(out=qtm[0:w, 4 * half:4 * half + 4, 0:64], in_=pv4[:, :, 0:64]),
                     R=["qf32"], W=["qtm"])
                rope_tm(qtm[0:w, 4 * half:4 * half + 4, 64:96], pv4[:, :, 64:96], w, i, 4, ["qf32"], ["qtm"])
            if KQ < 3:
                return
            p = nextps()
            pvb = PS[p][:].bitcast(BF16)
            for h in range(H):
                b.op("pe", lambda e, h=h, pvb=pvb: e.transpose(out=pvb[:, h * 128:h * 128 + w], in_=qtm[0:w, h, :], identity=ident_b[0:w, 0:w]),
                     R=["qtm", "ident_b"], W=[psk[p]])
            b.op("act", lambda e, pvb=pvb: e.copy(out=qT[0:96, :, 0:w], in_=pvb[0:96, 0:1024].rearrange("p (h t) -> p h t", h=H)[:, :, 0:w]),
                 R=[psk[p]], W=["qT"])
            if i == NFULL:
                b.op("dve", lambda e: e.tensor_copy(out=qsT[0:96, :, :], in_=qT[0:96, :, 16:32]), R=["qT"], W=["qsT"])
            if KA1 < 2:
                return
            b.op("dve", lambda e: e.tensor_copy(out=kvb[0:w, 0:KVW], in_=kvrow[0:w, :]), R=["kvrow"], W=["kvb"])
            p = nextps()
            pvb = PS[p][:].bitcast(BF16)
            b.op("pe", lambda e, pvb=pvb: e.transpose(out=pvb[:, 0:w], in_=kvb[0:w, 0:128], identity=ident_b[0:w, 0:w]), R=["kvb", "ident_b"], W=[psk[p]])
            b.op("pe", lambda e, pvb=pvb: e.transpose(out=pvb[:, 128:128 + w], in_=kvb[0:w, 128:256], identity=ident_b[0:w, 0:w]), R=["kvb", "ident_b"], W=[psk[p]])
            b.op("pe", lambda e, pvb=pvb: e.transpose(out=pvb[:, 256:256 + w], in_=kvb[0:w, 192:320], identity=ident_b[0:w, 0:w]), R=["kvb", "ident_b"], W=[psk[p]])
            b.op("act", lambda e, pvb=pvb: e.copy(out=ckvT[:, :, 0:w], in_=pvb[:, 0:256].rearrange("p (k t) -> p k t", k=2)[:, :, 0:w]), R=[psk[p]], W=["ckvT"])
            t0 = 128 * i
            ropesrc = apx(pvb[64:96, 256:257], [pvb.ap[0][0], 32], [[0, H], [1, wk]])
            b.op("act", lambda e, ropesrc=ropesrc: e.copy(out=KT[64:96, :, t0:t0 + wk], in_=ropesrc), R=[psk[p]], W=["KT"])
            for g in range(2):
                pk = nextps()
                for hh in range(4):
                    h = 4 * g + hh
                    for kc in range(2):
                        b.op("pe", lambda e, pk=pk, hh=hh, h=h, kc=kc: e.matmul(PS[pk][0:64, hh * 128:hh * 128 + w], lhsT=w_uk_b[:, kc, h * 64:(h + 1) * 64],
                                                                                rhs=ckvT[:, kc, 0:w], start=(kc == 0), stop=(kc == 1)),
                             R=["w_uk_b", "ckvT"], W=[psk[pk]])
                b.op("act", lambda e, pk=pk, g=g: e.copy(out=KT[0:64, 4 * g:4 * g + 4, t0:t0 + wk],
                                                         in_=PS[pk][0:64, 0:512].rearrange("p (h t) -> p h t", h=4)[:, :, 0:wk]), R=[psk[pk]], W=["KT"])
            pvv = nextps()
            for kc in range(2):
                b.op("pe", lambda e, kc=kc: e.matmul(PS[pvv][0:w, 0:512], lhsT=ckvT[:, kc, 0:w], rhs=w_uv_b[:, kc, :], start=(kc == 0), stop=(kc == 1)),
                     R=["ckvT", "w_uv_b"], W=[psk[pvv]])
            b.op("dve", lambda e: e.tensor_copy(out=Vc[0:w, i, :, 0:64], in_=PS[pvv][0:w, 0:512].rearrange("p (h v) -> p h v", h=H)), R=[psk[pvv]], W=["Vc"])
            if KA1 < 3:
                return
            wq = wk
            po = [reserve(), reserve()]
            ptn = [0]
            def stage_a(h, j0):
                pog = po[h // 4]
                ocol = (h % 4) * 65
                js = list(range(j0, min(j0 + 4, i + 1)))
                psc = nextps()
                for jj, j in enumerate(js):
                    wkj = 128 if j < NFULL else 16
                    b.op("pe", lambda e, psc=psc, jj=jj, j=j, h=h, wkj=wkj: e.matmul(PS[psc][0:wkj, jj * 128:jj * 128 + wq], lhsT=KT[0:96, h, j * 128:j * 128 + wkj],
                                                                                      rhs=qT[0:96, h, 0:wq], start=True, stop=True),
                         R=["KT", "qT"], W=[psk[psc]])
                pt = PT[ptn[0] % 4]
                ptk = f"PT{ptn[0] % 4}"
                ptn[0] += 1
                n = len(js)
                wkmin = 128 if js[-1] < NFULL else 16
                if wkmin == 128 or n == 1:
                    wka = 128 if wkmin == 128 else 16
                    b.op("act", lambda e, psc=psc, pt=pt, n=n, wka=wka: e.activation(out=pt[0:wka, 0:n, 0:wq], in_=PS[psc][0:wka, 0:n * 128].rearrange("p (j t) -> p j t", j=n)[:, :, 0:wq],
                                                                                     func=AF.Exp, scale=SM_SCALE), R=[psk[psc]], W=[ptk])
                else:
                    b.op("act", lambda e, psc=psc, pt=pt, n=n: e.activation(out=pt[0:128, 0:n - 1, 0:wq], in_=PS[psc][0:128, 0:(n - 1) * 128].rearrange("p (j t) -> p j t", j=n - 1)[:, :, 0:wq],
                                                                            func=AF.Exp, scale=SM_SCALE), R=[psk[psc]], W=[ptk])
                    b.op("act", lambda e, psc=psc, pt=pt, n=n: e.activation(out=pt[0:16, n - 1, 0:wq], in_=PS[psc][0:16, (n - 1) * 128:(n - 1) * 128 + wq],
                                                                            func=AF.Exp, scale=SM_SCALE), R=[psk[psc]], W=[ptk])
                if js[-1] == i:
                    jj = len(js) - 1
                    b.op("dve", lambda e, pt=pt, jj=jj: e.tensor_tensor(out=pt[0:wq, jj, 0:wq], in0=pt[0:wq, jj, 0:wq], in1=maskc_b[0:wq, 0:wq], op=ALU.mult),
                         R=[ptk, "maskc_b"], W=[ptk])

                def stage_b():
                    for jj, j in enumerate(js):
                        wkj = 128 if j < NFULL else 16
                        b.op("pe", lambda e, jj=jj, j=j, wkj=wkj: e.matmul(PS[pog][0:wq, ocol:ocol + 65], lhsT=pt[0:wkj, jj, 0:wq],
                                                                           rhs=Vc[0:wkj, j, h, :], start=(j == 0), stop=(j == i)),
                             R=[ptk, "Vc"], W=[psk[pog]])
                return stage_b

            pend = []
            for h in range(H):
                for j0 in range(0, i + 1, 4):
                    pend.append(stage_a(h, j0))
                    if len(pend) > 2:
                        pend.pop(0)()
            for fn in pend:
                fn()
            for g in range(2):
                b.op("act", lambda e, g=g: e.copy(out=osb[0:wq, :], in_=PS[po[g]][0:wq, 0:260]), R=[psk[po[g]]], W=["osb"])
                ov = osb[0:wq, :].rearrange("p (h x) -> p h x", h=4)
                b.op("dve", lambda e, ov=ov, g=g: e.reciprocal(out=rec[0:wq, 4 * g:4 * g + 4], in_=ov[:, :, 64]), R=["osb"], W=["rec"])
                rb = apx(rec[0:wq, 4 * g:4 * g + 1], [rec[:].ap[0][0], wq], [[1, 4], [0, 64]])
                b.op("dve", lambda e, ov=ov, g=g, rb=rb: e.tensor_tensor(out=attn_tm[0:wq, 256 * g:256 * g + 256].rearrange("p (h v) -> p h v", h=4),
                                                                         in0=ov[:, :, 0:64], in1=rb, op=ALU.mult), R=["osb", "rec"], W=["attn_tm"])
            release(po[0])
            release(po[1])
            p = nextps()
            pvb = PS[p][:].bitcast(BF16)
            for c4 in range(4):
                b.op("pe", lambda e, c4=c4, pvb=pvb: e.transpose(out=pvb[:, c4 * 128:c4 * 128 + wq], in_=attn_tm[0:wq, c4 * 128:(c4 + 1) * 128], identity=ident_b[0:wq, 0:wq]),
                     R=["attn_tm", "ident_b"], W=[psk[p]])
            b.op("act", lambda e, pvb=pvb: e.copy(out=attnT[:, :, 0:wq], in_=pvb[:, 0:512].rearrange("p (c t) -> p c t", c=4)[:, :, 0:wq]), R=[psk[p]], W=["attnT"])
            for c4 in range(4):
                b.dma("sp", "oat", ats[c4 * 128:(c4 + 1) * 128, t0:t0 + wq], attnT[:, c4, 0:wq], R=["attnT"], W=["ats"])

        if KA1 >= 1:
            load_x(0, 0)
            for i in range(NTL):
                tile(i)

    with ExitStack() as pst:
        b.st = pst
        phase_A1()
        b.barrier()
        b.emit()
    b.st = b.st0


    def phase_S():
        w_uk_b = load_w_bf("w_uk_s", [128, 2, 512], I["w_uk"].rearrange("(k p) n -> p k n", p=128))
        w_uv_b = load_w_bf("w_uv_s", [128, 2, 512], I["w_uv"].rearrange("(k p) n -> p k n", p=128))
        w_ukT = b.sb("w_ukT", [64, H, 256], BF16)
        qfT = b.sb("qfT", [128, 3, SPC, H], BF16)
        b.op("pool", lambda e: e.memset(qfT[:], 0.0), W=["qfT"])
        ones_f = b.sb("ones_f", [1, 128], F32)
        b.op("pool", lambda e: e.memset(ones_f[:], 1.0), W=["ones_f"])
        idx = b.sb("idx", [128, 8], I32)
        idx8 = b.sb("idx8", [128, 8], I32)
        b.op("pool", lambda e: e.memset(idx[:], 0), W=["idx"])
        b.dma("sp", "c0", idx[0:PP2, :], I["pt"].rearrange("(pi s2) g -> (s2 g) pi", s2=2), W=["idx"], allow_slow_non_contiguous=True)
        b.op("dve", lambda e: e.tensor_scalar(out=idx8[:], in0=idx[:], scalar1=8, scalar2=None, op0=ALU.mult), R=["idx"], W=["idx8"])
        kvg = [b.sb(f"kvg{j}", [128, 16, KVW], F32) for j in range(2)]
        kvh = [b.sb(f"kvh{j}", [128, 16, 320], BF16) for j in range(2)]
        for j in range(2):
            b.op("pool", lambda e, j=j: e.memset(kvh[j][:], 0.0), W=[f"kvh{j}"])
            b.op("pool", lambda e, j=j: e.memset(kvh[j][:, :, 288:290], 1.0), W=[f"kvh{j}"])
        kvn = b.sb("kvn", [128, 8, KVW], F32)
        kvnb = b.sb("kvnb", [128, 8, 320], BF16)
        b.op("pool", lambda e: e.memset(kvn[:], 0.0), W=["kvn"])
        b.op("pool", lambda e: e.memset(kvnb[:], 0.0), W=["kvnb"])
        b.op("pool", lambda e: e.memset(kvnb[:, :, 288:290], 1.0), W=["kvnb"])
        kv2 = kvs.rearrange("(pi s2) c -> s2 pi c", s2=2)
        b.dma("sp", "c0", kvn[0:1, :, :], kv2[0:1], R=["kvs"], W=["kvn"])
        b.dma("sp", "c0", kvn[NPG:NPG + 1, :, :], kv2[1:2], R=["kvs"], W=["kvn"])
        b.op("dve", lambda e: e.tensor_copy(out=kvnb[:, :, 0:KVW], in_=kvn[:, :, :]), R=["kvn"], W=["kvnb"])
        KT3 = [b.sb(f"KT3_{j}", [128, 6, 128], BF16) for j in range(4)]
        PTs = [b.sb(f"PTs{j}", [128, 16, 16], BF16) for j in range(2)]
        PTn = b.sb("PTn", [128, 16], BF16)
        for j in range(2):
            b.op("pool", lambda e, j=j: e.memset(PTs[j][:], 0.0), W=[f"PTs{j}"])
        b.op("pool", lambda e: e.memset(PTn[:], 0.0), W=["PTn"])
        mloc = b.sb("mloc", [128, 1], F32)
        scs = b.sb("scs", [128, 256], F32)
        b.op("pool", lambda e: e.memset(mloc[:], 0.0), W=["mloc"])
        m2 = b.sb("m2", [1, 2], F32)
        negm = b.sb("negm", [128, 2], F32)
        olat = b.sb("olat", [16, 256], BF16)
        rl = b.sb("rl", [16, 1], F32)
        olatT = b.sb("olatT", [128, 2, 8, 16], BF16)
        attn_sT = b.sb("attn_sT", [128, 4, 16], BF16)

        for h in range(H):
            p = nextps()
            pvb = PS[p][:].bitcast(BF16)
            for kc in range(2):
                b.op("pe", lambda e, h=h, kc=kc, pvb=pvb: e.transpose(out=pvb[0:64, kc * 128:(kc + 1) * 128], in_=w_uk_b[:, kc, h * 64:(h + 1) * 64], identity=ident_b[:, :]),
                     R=["w_uk_s", "ident_b"], W=[psk[p]])
            b.op("act", lambda e, h=h, pvb=pvb: e.copy(out=w_ukT[0:64, h, :], in_=pvb[0:64, 0:256]), R=[psk[p]], W=["w_ukT"])
        for h in range(H):
            p = nextps()
            for kc in range(2):
                b.op("pe", lambda e, h=h, kc=kc, p=p: e.matmul(PS[p][:, kc * 16:(kc + 1) * 16], lhsT=w_ukT[0:64, h, kc * 128:(kc + 1) * 128], rhs=qsT[0:64, h, :],
                                                              start=True, stop=True), R=["w_ukT", "qsT"], W=[psk[p]])
            b.op("act", lambda e, h=h, p=p: e.copy(out=qfT[:, 0:2, :, h], in_=PS[p][:, 0:32].rearrange("p (k s) -> p k s", k=2)), R=[psk[p]], W=["qfT"])
        b.op("dve", lambda e: e.tensor_copy(out=qfT[64:96, 2, :, :], in_=qsT[64:96, :, :].rearrange("p h s -> p s h")), R=["qsT"], W=["qfT"])

        cache8 = I["cache"].rearrange("n (g r) c -> (n g) (r c)", r=16)

        def kt_scores(src, r0, nr, bufk, psc, col0, pi):
            for g0 in range(0, nr, 2):
                rs = list(range(g0, min(g0 + 2, nr)))
                kb = bufk[0] % 4
                bufk[0] += 1
                kt, ktk = KT3[kb], f"KT3_{kb}"
                p = nextps()
                pvb = PS[p][:].bitcast(BF16)
                for rr, r in enumerate(rs):
                    for kc, (c0, c1, m) in enumerate(((0, 128, 128), (128, 256, 128), (192, 320, 128))):
                        b.op("pe", lambda e, pvb=pvb, rr=rr, r=r, kc=kc, c0=c0, c1=c1, m=m: e.transpose(out=pvb[0:m, (rr * 3 + kc) * 128:(rr * 3 + kc) * 128 + PP2],
                                                                                                   in_=src[0:PP2, r0 + r, c0:c1], identity=ident_b[0:PP2, 0:PP2]),
                             R=[src_key[0], "ident_b"], W=[psk[p]])
                n3 = 3 * len(rs)
                eng = "act" if (bufk[0] % 2) else "dve"
                if eng == "act":
                    b.op("act", lambda e, kt=kt, pvb=pvb, n3=n3: e.copy(out=kt[:, 0:n3, 0:PP2], in_=pvb[:, 0:n3 * 128].rearrange("p (a t) -> p a t", a=n3)[:, :, 0:PP2]),
                         R=[psk[p]], W=[ktk])
                else:
                    b.op("dve", lambda e, kt=kt, pvb=pvb, n3=n3: e.tensor_copy(out=kt[:, 0:n3, 0:PP2], in_=pvb[:, 0:n3 * 128].rearrange("p (a t) -> p a t", a=n3)[:, :, 0:PP2]),
                         R=[psk[p]], W=[ktk])
                for rr, r in enumerate(rs):
                    for kc, m in enumerate((128, 128, 96)):
                        b.op("pe", lambda e, kt=kt, rr=rr, r=r, kc=kc, m=m: e.matmul(PS[psc][0:PP2, col0 + 16 * r:col0 + 16 * r + 16], lhsT=kt[0:m, rr * 3 + kc, 0:PP2],
                                                                                   rhs=qfT[0:m, kc, 2 * pi:2 * pi + 2, :].rearrange("p s h -> p (s h)"),
                                                                                   start=(kc == 0), stop=(kc == 2)), R=[ktk, "qfT"], W=[psk[psc]])

        src_key = ["kvh0"]
        bufk = [0]
        cn = [0]
        for pi in range(SPC // 2):
            pacc = reserve()
            for ck in range(8):
                cb = cn[0] % 2
                cn[0] += 1
                gk, hk = f"kvg{cb}", f"kvh{cb}"
                b.raw_dma("pool", gk, lambda e, cb=cb, ck=ck, pi=pi: e.indirect_dma_start(
                    out=kvg[cb][0:PP2, :, :].rearrange("p a b -> p (a b)"), out_offset=None, in_=cache8[:, :], element_offset=ck * 16 * KVW,
                    in_offset=bass.IndirectOffsetOnAxis(ap=idx8[0:PP2, pi:pi + 1], axis=0)), R=["idx8"], W=[gk])
                ceng = "dve" if ck % 2 == 0 else "pool"
                b.op(ceng, lambda e, cb=cb: e.tensor_copy(out=kvh[cb][0:PP2, :, 0:KVW], in_=kvg[cb][0:PP2, :, :]), R=[gk], W=[hk])
                src_key[0] = hk
                psc = reserve()
                kt_scores(kvh[cb], 0, 16, bufk, psc, 0, pi)
                release(psc)
                sc3 = PS[psc][0:PP2, 0:256].rearrange("p (r x) -> p r x", r=16)
                if ck == 0:
                    b.op("act", lambda e, psc=psc: e.copy(out=scs[0:PP2, :], in_=PS[psc][0:PP2, 0:256]), R=[psk[psc]], W=["scs"])
                    ss3 = scs[0:PP2, :].rearrange("p (r x) -> p r x", r=16)
                    b.op("dve", lambda e, ss3=ss3: e.tensor_reduce(out=mloc[0:NPG, :], in_=ss3[0:NPG, :, 0:8], axis=AX.XY, op=ALU.max), R=["scs"], W=["mloc"])
                    b.op("dve", lambda e, ss3=ss3: e.tensor_reduce(out=mloc[NPG:PP2, :], in_=ss3[NPG:PP2, :, 8:16], axis=AX.XY, op=ALU.max), R=["scs"], W=["mloc"])
                    pm_ = nextps()
                    b.op("pe", lambda e, pm_=pm_: e.transpose(out=PS[pm_][0:1, 0:PP2], in_=mloc[0:PP2, 0:1], identity=ident_f[0:PP2, 0:PP2]), R=["mloc", "ident_f"], W=[psk[pm_]])
                    b.op("dve", lambda e, pm_=pm_: e.tensor_reduce(out=m2[0:1, 0:2], in_=PS[pm_][0:1, 0:PP2].rearrange("p (a g) -> p a g", a=2), axis=AX.X, op=ALU.max),
                         R=[psk[pm_]], W=["m2"])
                    b.op("dve", lambda e: e.tensor_scalar(out=m2[0:1, 0:2], in0=m2[0:1, 0:2], scalar1=-SM_SCALE, scalar2=None, op0=ALU.mult), R=["m2"], W=["m2"])
                    pb_ = nextps()
                    b.op("pe", lambda e, pb_=pb_: e.matmul(PS[pb_][0:PP2, 0:2], lhsT=ones_f[0:1, 0:PP2], rhs=m2[0:1, 0:2], start=True, stop=True), R=["ones_f", "m2"], W=[psk[pb_]])
                    b.op("act", lambda e, pb_=pb_: e.copy(out=negm[0:PP2, :], in_=PS[pb_][0:PP2, 0:2]), R=[psk[pb_]], W=["negm"])
                pt, ptk = PTs[cb], f"PTs{cb}"
                b.op("act", lambda e, pt=pt, sc3=sc3: e.activation(out=pt[0:NPG, :, 0:8], in_=sc3[0:NPG, :, 0:8], func=AF.Exp, scale=SM_SCALE, bias=negm[0:NPG, 0:1]),
                     R=[psk[psc], "negm"], W=[ptk])
                b.op("act", lambda e, pt=pt, sc3=sc3: e.activation(out=pt[NPG:PP2, :, 8:16], in_=sc3[NPG:PP2, :, 8:16], func=AF.Exp, scale=SM_SCALE, bias=negm[NPG:PP2, 1:2]),
                     R=[psk[psc], "negm"], W=[ptk])
                for r in range(16):
                    b.op("pe", lambda e, pt=pt, cb=cb, r=r, ck=ck, pacc=pacc: e.matmul(PS[pacc][0:16, 0:290], lhsT=pt[0:PP2, r, :], rhs=kvh[cb][0:PP2, r, 0:290],
                                                                                         start=(ck == 0 and r == 0), stop=False), R=[ptk, hk], W=[psk[pacc]])
            src_key[0] = "kvnb"
            psc = reserve()
            kt_scores(kvnb, pi, 1, bufk, psc, 0, pi)
            release(psc)
            b.op("act", lambda e, psc=psc: e.activation(out=PTn[0:1, 0:8], in_=PS[psc][0:1, 0:8], func=AF.Exp, scale=SM_SCALE, bias=negm[0:1, 0:1]),
                 R=[psk[psc], "negm"], W=["PTn"])
            b.op("act", lambda e, psc=psc: e.activation(out=PTn[NPG:NPG + 1, 8:16], in_=PS[psc][NPG:NPG + 1, 8:16], func=AF.Exp, scale=SM_SCALE, bias=negm[NPG:NPG + 1, 1:2]),
                 R=[psk[psc], "negm"], W=["PTn"])
            b.op("pe", lambda e, pi=pi, pacc=pacc: e.matmul(PS[pacc][0:16, 0:290], lhsT=PTn[0:PP2, :], rhs=kvnb[0:PP2, pi, 0:290], start=False, stop=True),
                 R=["PTn", "kvnb"], W=[psk[pacc]])
            b.op("dve", lambda e, pacc=pacc: e.reciprocal(out=rl[:, :], in_=PS[pacc][0:16, 288:289]), R=[psk[pacc]], W=["rl"])
            b.op("dve", lambda e, pacc=pacc: e.tensor_scalar(out=olat[:, :], in0=PS[pacc][0:16, 0:256], scalar1=rl[:, 0:1], scalar2=None, op0=ALU.mult),
                 R=[psk[pacc], "rl"], W=["olat"])
            p = nextps()
            pvb = PS[p][:].bitcast(BF16)
            for kc in range(2):
                b.op("pe", lambda e, kc=kc, pvb=pvb: e.transpose(out=pvb[:, kc * 16:(kc + 1) * 16], in_=olat[0:16, kc * 128:(kc + 1) * 128], identity=ident_b[0:16, 0:16]),
                     R=["olat", "ident_b"], W=[psk[p]])
            release(pacc)
            b.op("act", lambda e, pi=pi, pvb=pvb: e.copy(out=olatT[:, :, pi, :], in_=pvb[:, 0:32].rearrange("p (k x) -> p k x", k=2)), R=[psk[p]], W=["olatT"])
        p = nextps()
        ol5 = olatT[:].rearrange("p k a (s h) -> p k a s h", s=2)
        for h in range(H):
            pr = slice((h % 2) * 64, (h % 2) * 64 + 64)
            for kc in range(2):
                b.op("pe", lambda e, h=h, kc=kc, pr=pr, p=p: e.matmul(PS[p][pr, h * 16:(h + 1) * 16], lhsT=w_uv_b[:, kc, h * 64:(h + 1) * 64], rhs=ol5[:, kc, :, :, h],
                                                                     start=(kc == 0), stop=(kc == 1)), R=["w_uv_s", "olatT"], W=[psk[p]])
        for h in range(H):
            pr = slice((h % 2) * 64, (h % 2) * 64 + 64)
            b.op("act", lambda e, h=h, pr=pr, p=p: e.copy(out=attn_sT[pr, h // 2, :], in_=PS[p][pr, h * 16:(h + 1) * 16]), R=[psk[p]], W=["attn_sT"])
        for c4 in range(4):
            b.dma("sp", "oat", ats[c4 * 128:(c4 + 1) * 128, NFULL * 128 + 16:NFULL * 128 + 32], attn_sT[:, c4, :], R=["attn_sT"], W=["ats"])

    if stage >= 3:
        with ExitStack() as pst:
            b.st = pst
            phase_S()
            b.barrier()
            b.emit()
        b.st = b.st0


    pst = ExitStack()
    b.st = pst
    w_in_r = b.sb("w_in_r", [128, 8, RC], BF16)
    for kc in range(8):
        b.dma("pool", "wA", w_in_r[:, kc, :], I["w_in"][kc * 128:(kc + 1) * 128, C_RW:C_G], W=["w_in_r"])
    CH = BF16

    def flat(name, dt=F32, n=512):
        return b.sb(name, [128, n], dt)

    def v3(t, w, g=4):
        return t[:, 0:g * w].rearrange("p (g t) -> p g t", g=g)

    ppar = b.sb("ppar", [128, 64], F32)

    def ld_pp(src, c0, n):
        b.dma("sp", "c0", ppar[:, c0:c0 + n], src.rearrange("o (c p) -> p (o c)", p=128), W=["ppar"],
              allow_slow_non_contiguous=True)
    PP_MU, PP_W0, PP_A0, PP_KK, PP_KA, PP_RK, PP_LNW, PP_LNB, PP_OMKA = 0, 14, 18, 22, 26, 30, 34, 38, 42
    ld_pp(I["mu_shift"], PP_MU, 14)
    ld_pp(I["w0"], PP_W0, 4)
    ld_pp(I["a0"], PP_A0, 4)
    ld_pp(I["k_k"], PP_KK, 4)
    ld_pp(I["k_a"], PP_KA, 4)
    ld_pp(I["r_k"], PP_RK, 4)
    ld_pp(I["ln_w"], PP_LNW, 4)
    ld_pp(I["ln_b"], PP_LNB, 4)
    b.op("dve", lambda e: e.tensor_scalar(out=ppar[:, PP_OMKA:PP_OMKA + 4], in0=ppar[:, PP_KA:PP_KA + 4], scalar1=-1.0,
                                          scalar2=1.0, op0=ALU.mult, op1=ALU.add), R=["ppar"], W=["ppar"])
    w2_b = b.sb("w2_b", [128, RW], BF16)
    a2_b = b.sb("a2_b", [128, RW], BF16)
    g2_b = b.sb("g2_b", [128, RW], BF16)
    b.dma("pool", "wA", w2_b[0:64, :], I["w2"], W=["w2_b"])
    b.dma("pool", "wA", a2_b[64:128, :], I["a2"], W=["a2_b"])
    b.dma("pool", "wA", g2_b[:, :], I["g2"], W=["g2_b"])
    maskx_b = b.sb("maskx_b", [128, 128], BF16)
    maskjt_b = b.sb("maskjt_b", [128, 2, 128], BF16)
    cmask_b = b.sb("cmask_b", [128, 8], BF16)
    identb_b = b.sb("identb_b", [128, 64], BF16)
    onehot_b = b.sb("onehot_b", [128, 16, 128], BF16)
    b.dma("pool", "wA", maskx_b[:], I["c_maskx"], W=["maskx_b"])
    b.dma("pool", "wA", maskjt_b[:].rearrange("p a b -> p (a b)"), I["c_maskjt"], W=["maskjt_b"])
    b.dma("pool", "wA", cmask_b[:], I["c_cmask"], W=["cmask_b"])
    b.dma("pool", "wA", identb_b[:], I["c_identb"], W=["identb_b"])
    b.dma("pool", "wA", onehot_b[:].rearrange("p a b -> p (a b)"), I["c_onehot"], W=["onehot_b"])
    blk1_f = b.sb("blk1_f", [128, 128], F32)
    blk64_f = b.sb("blk64_f", [128, 128], F32)
    scanm = b.sb("scanm", [128, 512], F32)
    b.dma("sp", "c0", blk1_f[:], I["c_blk1"], W=["blk1_f"])
    b.dma("sp", "c0", scanm[:], I["c_scanm"], W=["scanm"])
    b.op("dve", lambda e: e.tensor_scalar(out=blk64_f[:], in0=blk1_f[:], scalar1=1.0 / 64, scalar2=None, op0=ALU.mult),
         R=["blk1_f"], W=["blk64_f"])

    cT = b.sb("cT", [128, 14, 129], F32)
    zt = b.sb("zt", [128, 14, 128], F32)
    b.op("pool", lambda e: e.memset(cT[:], 0.0), W=["cT"])
    sshT = b.sb("sshT", [128, 14, 16], F32)
    sshtm = b.sb("sshtm", [16, RC], F32)
    f = {n: flat("f_" + n) for n in ["ld", "asig", "kk", "t", "bb", "kp", "L", "Lx", "eL", "enL", "eLC"]}
    gsb = flat("gsb", BF16)
    tanhw = flat("tanhw", BF16, 128)
    alb = flat("alb", BF16, 128)
    sgb = flat("sgb", BF16, 128)
    vb = flat("vb", BF16)
    AR = b.sb("AR", [128, 1024], BF16)
    Kt, Bt, Kb, Bb = (flat(n, BF16) for n in ["Kt", "Bt", "Kb", "Bb"])
    Vtm = b.sb("Vtm", [128, 4, 128], BF16)
    YA = b.sb("YA", [128, 2, 128], BF16)
    NK = b.sb("NK", [128, 2, 128], BF16)
    Xm = b.sb("Xm", [128, 128], BF16)
    XY = [b.sb(f"XY{j}", [128, 2, 128], BF16) for j in range(2)]
    Y8 = b.sb("Y8", [128, 128], BF16)
    Zs = [b.sb(f"Z{j}", [128, 128], BF16) for j in range(2)]
    KBtm = b.sb("KBtm", [128, 2, 64], BF16)
    Bexp = b.sb("Bexp", [128, 8, 64], BF16)
    Vexp = b.sb("Vexp", [128, 8, 64], BF16)
    Vhexp = b.sb("Vhexp", [128, 8, 64], BF16)
    Wd = b.sb("Wd", [128, 4, 8, 64], BF16)
    RhT = flat("RhT", CH)
    MT = [b.sb(f"MT{hp}", [128, 8, 128], CH) for hp in range(4)]
    GB = [b.sb(f"GB{hp}", [128, 8, 128], CH) for hp in range(4)]
    Sbd = [[b.sb(f"S{hp}_{j}", [128, 128], CH) for j in range(2)] for hp in range(4)]
    scur = [0, 0, 0, 0]
    for hp in range(4):
        b.op("pool", lambda e, hp=hp: e.memset(MT[hp][:], 0.0), W=[f"MT{hp}"])
        b.op("pool", lambda e, hp=hp: e.memset(GB[hp][:], 0.0), W=[f"GB{hp}"])
        b.op("pool", lambda e, hp=hp: e.memset(Sbd[hp][0][:], 0.0), W=[f"S{hp}_0"])
        b.op("pool", lambda e, hp=hp: e.memset(Sbd[hp][1][:], 0.0), W=[f"S{hp}_1"])
    ygT = flat("ygT", BF16)
    tm5 = b.sb("tm5", [48, 5, 256], BF16)
    xb = b.sb("xb", [128, 5, 4, 16], BF16)
    b.op("pool", lambda e: e.memset(tm5[:], 0.0), W=["tm5"])
    Sx = [b.sb(f"Sx{j}", [128, 4, 64], F32) for j in range(2)]
    stmp = b.sb("stmp", [128, 4, 64], F32)
    ssa = b.sb("ssa", [128, 4], F32)
    wkvo = b.sb("wkvo", [128, 4, 128], F32)

    b.dma("sp", "c0", sshtm[:], I["sshift"], W=["sshtm"])

    def prep_ssh():
        for g in range(4):
            js = list(range(4 * g, min(4 * g + 4, 14)))
            p = nextps()
            for jj, j in enumerate(js):
                b.op("pe", lambda e, jj=jj, j=j, p=p: e.transpose(out=PS[p][:, jj * 16:(jj + 1) * 16], in_=sshtm[0:16, j * 128:(j + 1) * 128],
                                                              identity=ident_f[0:16, 0:16]), R=["sshtm", "ident_f"], W=[psk[p]])
            n = len(js)
            b.op("act", lambda e, p=p, n=n, j0=js[0]: e.copy(out=sshT[:, j0:j0 + n, :], in_=PS[p][:, 0:n * 16].rearrange("p (j t) -> p j t", j=n)),
                 R=[psk[p]], W=["sshT"])
    prep_ssh()

    def pp(c0, hp):
        return ppar[:, c0 + hp:c0 + hp + 1]

    def rwkv_tile(i):
        w = tw(i)
        wu = 128 if i < NFULL else 16
        nch = wu // 16
        last = (i == NFULL)
        if i > 0:
            b.op("dve", lambda e: e.tensor_copy(out=cT[:, :, 0:1], in_=cT[:, :, 128:129]), R=["cT"], W=["cT"])
        for g in range(4):
            js = list(range(4 * g, min(4 * g + 4, 14)))
            p = nextps()
            for jj, j in enumerate(js):
                for kc in range(8):
                    b.op("pe", lambda e, jj=jj, j=j, kc=kc, p=p: e.matmul(PS[p][:, jj * 128:jj * 128 + w], lhsT=w_in_r[:, kc, 128 * j:128 * (j + 1)],
                                                                           rhs=hT[:, kc, 0:w], start=(kc == 0), stop=(kc == 7)),
                         R=["hT", "w_in_r"], W=[psk[p]])
            n = len(js)
            b.op("act", lambda e, p=p, n=n, j0=js[0]: e.copy(out=cT[:, j0:j0 + n, 1:1 + w],
                                                            in_=PS[p][:, 0:n * 128].rearrange("p (j t) -> p j t", j=n)[:, :, 0:w]),
                 R=[psk[p]], W=["cT"])
        wp = wu if last else w
        b.op("dve", lambda e: e.tensor_tensor(out=zt[:, :, 0:wp], in0=cT[:, :, 0:wp], in1=cT[:, :, 1:1 + wp], op=ALU.subtract),
             R=["cT"], W=["zt"])
        if last:
            b.op("dve", lambda e: e.tensor_tensor(out=zt[:, :, 16:32], in0=sshT[:, :, :], in1=cT[:, :, 17:33], op=ALU.subtract),
                 R=["cT", "sshT"], W=["zt"])
        for j in range(14):
            b.op("dve", lambda e, j=j: e.scalar_tensor_tensor(out=zt[:, j, 0:w], in0=zt[:, j, 0:w], scalar=ppar[:, PP_MU + j:PP_MU + j + 1],
                                                              in1=cT[:, j, 1:1 + w], op0=ALU.mult, op1=ALU.add),
                 R=["zt", "cT", "ppar"], W=["zt"])
        r3, k3, v3_ = zt[:, 0:4, 0:w], zt[:, 4:8, 0:w], zt[:, 8:12, 0:w]
        W4 = 4 * w
        ld3, as3, kk3, t3, b3, kp3 = (v3(f[n], w) for n in ["ld", "asig", "kk", "t", "bb", "kp"])
        b.op("act", lambda e: e.activation(out=tanhw[0:64, 0:w], in_=zt[0:64, 12, 0:w], func=AF.Tanh), R=["zt"], W=["tanhw"])
        b.op("pool", lambda e: e.tensor_copy(out=alb[64:128, 0:w], in_=zt[64:128, 12, 0:w]), R=["zt"], W=["alb"])
        b.op("act", lambda e: e.activation(out=sgb[:, 0:w], in_=zt[:, 13, 0:w], func=AF.Sigmoid), R=["zt"], W=["sgb"])
        b.op("pool", lambda e: e.tensor_copy(out=v3(vb, w), in_=v3_), R=["zt"], W=["vb"])
        pW, pA, pG = nextps(), nextps(), nextps()
        for hp in range(4):
            b.op("pe", lambda e, hp=hp: e.matmul(PS[pW][:, hp * w:(hp + 1) * w], lhsT=w2_b[0:64, hp * 128:(hp + 1) * 128], rhs=tanhw[0:64, 0:w],
                                                 start=True, stop=True), R=["w2_b", "tanhw"], W=[psk[pW]])
        for hp in range(4):
            b.op("pe", lambda e, hp=hp: e.matmul(PS[pA][:, hp * w:(hp + 1) * w], lhsT=a2_b[64:128, hp * 128:(hp + 1) * 128], rhs=alb[64:128, 0:w],
                                                 start=True, stop=True), R=["a2_b", "alb"], W=[psk[pA]])
        for hp in range(4):
            b.op("pe", lambda e, hp=hp: e.matmul(PS[pG][:, hp * w:(hp + 1) * w], lhsT=g2_b[:, hp * 128:(hp + 1) * 128], rhs=sgb[:, 0:w],
                                                 start=True, stop=True), R=["g2_b", "sgb"], W=[psk[pG]])
        for hp in range(4):
            b.op("act", lambda e, hp=hp: e.activation(out=ld3[:, hp, :], in_=PS[pW][:, hp * w:(hp + 1) * w], func=AF.Sigmoid, bias=pp(PP_W0, hp)),
                 R=[psk[pW], "ppar"], W=["f_ld"])
            b.op("act", lambda e, hp=hp: e.activation(out=as3[:, hp, :], in_=PS[pA][:, hp * w:(hp + 1) * w], func=AF.Sigmoid, bias=pp(PP_A0, hp)),
                 R=[psk[pA], "ppar"], W=["f_asig"])
        b.op("act", lambda e: e.copy(out=gsb[:, 0:W4], in_=PS[pG][:, 0:W4]), R=[psk[pG]], W=["gsb"])
        b.op("dve", lambda e: e.tensor_scalar(out=f["ld"][:, 0:W4], in0=f["ld"][:, 0:W4], scalar1=-0.6065306597126334, scalar2=None, op0=ALU.mult),
             R=["f_ld"], W=["f_ld"])
        for hp in range(4):
            b.op("act", lambda e, hp=hp: e.activation(out=kk3[:, hp, :], in_=zt[:, 4 + hp, 0:w], func=AF.Copy, scale=pp(PP_KK, hp)),
                 R=["zt", "ppar"], W=["f_kk"])
        b.op("act", lambda e: e.activation(out=f["t"][:, 0:W4], in_=f["kk"][:, 0:W4], func=AF.Square), R=["f_kk"], W=["f_t"])
        pN = nextps()
        b.op("pe", lambda e: e.matmul(PS[pN][:, 0:W4], lhsT=blk1_f[:], rhs=f["t"][:, 0:W4], start=True, stop=True),
             R=["blk1_f", "f_t"], W=[psk[pN]])
        b.op("act", lambda e: e.activation(out=f["t"][:, 0:W4], in_=PS[pN][:, 0:W4], func=AF.Sqrt), R=[psk[pN]], W=["f_t"])
        b.op("dve", lambda e: e.tensor_scalar(out=f["t"][:, 0:W4], in0=f["t"][:, 0:W4], scalar1=1e-12, scalar2=None, op0=ALU.max),
             R=["f_t"], W=["f_t"])
        b.op("dve", lambda e: e.reciprocal(out=f["t"][:, 0:W4], in_=f["t"][:, 0:W4]), R=["f_t"], W=["f_t"])
        b.op("dve", lambda e: e.tensor_tensor(out=f["kk"][:, 0:W4], in0=f["kk"][:, 0:W4], in1=f["t"][:, 0:W4], op=ALU.mult),
             R=["f_kk", "f_t"], W=["f_kk"])
        b.op("dve", lambda e: e.tensor_tensor(out=f["bb"][:, 0:W4], in0=f["kk"][:, 0:W4], in1=f["asig"][:, 0:W4], op=ALU.mult),
             R=["f_kk", "f_asig"], W=["f_bb"])
        for hp in range(4):
            b.op("dve", lambda e, hp=hp: e.tensor_scalar(out=t3[:, hp, :], in0=as3[:, hp, :], scalar1=pp(PP_KA, hp), scalar2=pp(PP_OMKA, hp),
                                                         op0=ALU.mult, op1=ALU.add), R=["f_asig", "ppar"], W=["f_t"])
        b.op("dve", lambda e: e.tensor_tensor(out=kp3, in0=k3, in1=t3, op=ALU.mult), R=["zt", "f_t"], W=["f_kp"])
        b.op("dve", lambda e: e.tensor_tensor_scan(out=f["L"][:, 0:W4], data0=scanm[:, 0:W4], data1=f["ld"][:, 0:W4], initial=0.0,
                                                   op0=ALU.mult, op1=ALU.add), R=["scanm", "f_ld"], W=["f_L"])
        b.op("dve", lambda e: e.tensor_tensor(out=f["Lx"][:, 0:W4], in0=f["L"][:, 0:W4], in1=f["ld"][:, 0:W4], op=ALU.subtract),
             R=["f_L", "f_ld"], W=["f_Lx"])
        ng = W4 // 16
        Lg = f["L"][:, 0:W4].rearrange("p (g t) -> p g t", t=16)
        Lend = apx(f["L"][:, 15:16], [f["L"][:].ap[0][0], 128], [[16, ng], [0, 16]])
        b.op("dve", lambda e: e.tensor_tensor(out=f["eLC"][:, 0:W4].rearrange("p (g t) -> p g t", t=16), in0=Lend, in1=Lg, op=ALU.subtract),
             R=["f_L"], W=["f_eLC"])
        b.op("act", lambda e: e.activation(out=f["eL"][:, 0:W4], in_=f["L"][:, 0:W4], func=AF.Exp), R=["f_L"], W=["f_eL"])
        b.op("act", lambda e: e.activation(out=f["enL"][:, 0:W4], in_=f["L"][:, 0:W4], func=AF.Exp, scale=-1.0), R=["f_L"], W=["f_enL"])
        b.op("act", lambda e: e.activation(out=f["Lx"][:, 0:W4], in_=f["Lx"][:, 0:W4], func=AF.Exp), R=["f_Lx"], W=["f_Lx"])
        b.op("act", lambda e: e.activation(out=f["eLC"][:, 0:W4], in_=f["eLC"][:, 0:W4], func=AF.Exp), R=["f_eLC"], W=["f_eLC"])
        AR4 = AR[:, 0:2 * W4].rearrange("p (h two t) -> p h two t", h=4, two=2)
        b.op("dve", lambda e: e.scalar_tensor_tensor(out=AR4[:, :, 0, :], in0=kk3, scalar=-1.0, in1=v3(f["Lx"], w), op0=ALU.mult, op1=ALU.mult),
             R=["f_kk", "f_Lx"], W=["AR"])
        b.op("dve", lambda e: e.tensor_tensor(out=AR4[:, :, 1, :], in0=r3, in1=v3(f["eL"], w), op=ALU.mult), R=["zt", "f_eL"], W=["AR"])
        b.op("dve", lambda e: e.tensor_tensor(out=Kt[:, 0:W4], in0=f["kp"][:, 0:W4], in1=f["enL"][:, 0:W4], op=ALU.mult), R=["f_kp", "f_enL"], W=["Kt"])
        b.op("dve", lambda e: e.tensor_tensor(out=Bt[:, 0:W4], in0=f["bb"][:, 0:W4], in1=f["enL"][:, 0:W4], op=ALU.mult), R=["f_bb", "f_enL"], W=["Bt"])
        b.op("pool", lambda e: e.tensor_tensor(out=Kb[:, 0:W4], in0=f["kp"][:, 0:W4], in1=f["eLC"][:, 0:W4], op=ALU.mult), R=["f_kp", "f_eLC"], W=["Kb"])
        b.op("pool", lambda e: e.tensor_tensor(out=Bb[:, 0:W4], in0=f["bb"][:, 0:W4], in1=f["eLC"][:, 0:W4], op=ALU.mult), R=["f_bb", "f_eLC"], W=["Bb"])
        for hp in range(4):
            wc = apx(f["eL"][:, hp * w + 15:hp * w + 16], [f["eL"][:].ap[0][0], 128], [[16, nch], [0, 64]])
            idb = apx(identb_b[:, 0:1], [identb_b[:].ap[0][0], 128], [[0, nch], [1, 64]])
            b.op("dve", lambda e, hp=hp, wc=wc, idb=idb: e.tensor_tensor(out=Wd[:, hp, 0:nch, :], in0=idb, in1=wc, op=ALU.mult),
                 R=["f_eL", "identb_b"], W=["Wd"])
        p = nextps()
        pvb = PS[p][:].bitcast(BF16)
        for hp in range(4):
            b.op("pe", lambda e, hp=hp, p=p: e.transpose(out=pvb[0:w, hp * 128:(hp + 1) * 128], in_=vb[:, hp * w:(hp + 1) * w], identity=ident_b[:, :]),
                 R=["vb", "ident_b"], W=[psk[p]])
        b.op("act", lambda e, p=p: e.copy(out=Vtm[0:w, :, :], in_=pvb[0:w, 0:512].rearrange("p (h x) -> p h x", h=4)), R=[psk[p]], W=["Vtm"])

        ARf = AR4
        for hp in range(4):
            for h2 in range(2):
                unit(i, hp, h2, wu, nch, w)
        pY = reserve()
        for c in range(nch):
            for hp in range(4):
                cur = scur[hp]
                S0, S1 = Sbd[hp][cur], Sbd[hp][1 - cur]
                k0, k1 = f"S{hp}_{cur}", f"S{hp}_{1 - cur}"
                b.op("pe", lambda e, hp=hp, c=c, S0=S0: e.matmul(PS[pY][:, hp * 128 + c * 16:hp * 128 + c * 16 + 16], lhsT=S0[:, :],
                                                                rhs=RhT[:, hp * 128 + c * 16:hp * 128 + c * 16 + 16], start=True, stop=True),
                     R=[k0, "RhT"], W=[psk[pY]])
                ps_ = nextps()
                b.op("pe", lambda e, hp=hp, c=c, S0=S0, ps_=ps_: e.matmul(PS[ps_][:, 0:128], lhsT=MT[hp][:, c, :], rhs=S0[:, :], start=True, stop=True),
                     R=[k0, f"MT{hp}"], W=[psk[ps_]])
                b.op("dve", lambda e, hp=hp, c=c, S1=S1, ps_=ps_: e.tensor_tensor(out=S1[:, :], in0=PS[ps_][:, 0:128], in1=GB[hp][:, c, :], op=ALU.add),
                     R=[psk[ps_], f"GB{hp}"], W=[k1])
                scur[hp] = 1 - cur
        ysb = f["L"]
        y3 = v3(ysb, w)
        for hp in range(4):
            b.op("dve", lambda e, hp=hp: e.tensor_tensor(out=y3[:, hp, 0:wu], in0=PS[pY][:, hp * 128:hp * 128 + wu], in1=f["enL"][:, hp * 128:hp * 128 + wu], op=ALU.add),
                 R=[psk[pY], "f_enL"], W=["f_L"])
        release(pY)
        if last:
            sample_rwkv(w, y3)
            wkv_prompt_out()
        finalize(i, w, y3)

    def unit(i, hp, h2, wu, nch, w):
        pr = slice(h2 * 64, h2 * 64 + 64)
        pRY, pM, pGm = nextps(), nextps(), nextps()
        AR4 = AR[:, 0:8 * w].rearrange("p (h two t) -> p h two t", h=4, two=2)
        At = AR4[pr, hp, 0, 0:wu]
        ARr = AR[pr, hp * 2 * w:(hp + 1) * 2 * w].rearrange("p (two t) -> p two t", two=2)[:, :, 0:wu]
        Kt_ = Kt[pr, hp * w:hp * w + wu]
        Bt_ = Bt[pr, hp * w:hp * w + wu]
        Kb_ = Kb[pr, hp * w:hp * w + wu]
        Bb_ = Bb[pr, hp * w:hp * w + wu]
        Vh = Vtm[0:wu, hp, h2 * 64:(h2 + 1) * 64]
        pa, pb = nextps(), nextps()
        o1 = PS[pa][0:wu, 0:2 * wu].rearrange("p (two t) -> p two t", two=2)
        o2 = PS[pa][0:wu, 256:256 + 2 * wu].rearrange("p (two t) -> p two t", two=2)
        b.op("pe", lambda e: e.matmul(o1, lhsT=Bt_, rhs=ARr, start=True, stop=True), R=["Bt", "AR"], W=[psk[pa]])
        b.op("pe", lambda e: e.matmul(o2, lhsT=Kt_, rhs=ARr, start=True, stop=True), R=["Kt", "AR"], W=[psk[pa]])
        b.op("pe", lambda e: e.matmul(PS[pb][0:wu, 0:wu], lhsT=At, rhs=Bt_, start=True, stop=True), R=["Bt", "AR"], W=[psk[pb]])
        mj = maskjt_b[0:wu, :, 0:wu]
        b.op("dve", lambda e: e.tensor_tensor(out=YA[0:wu, :, 0:wu], in0=o1, in1=mj, op=ALU.mult), R=[psk[pa], "maskjt_b"], W=["YA"])
        b.op("dve", lambda e: e.tensor_tensor(out=NK[0:wu, :, 0:wu], in0=o2, in1=mj, op=ALU.mult), R=[psk[pa], "maskjt_b"], W=["NK"])
        b.op("dve", lambda e: e.tensor_tensor(out=Xm[0:wu, 0:wu], in0=PS[pb][0:wu, 0:wu], in1=maskx_b[0:wu, 0:wu], op=ALU.mult),
             R=[psk[pb], "maskx_b"], W=["Xm"])
        Y1, Arb, Nak, Ark = YA[0:wu, 0, 0:wu], YA[0:wu, 1, 0:wu], NK[0:wu, 0, 0:wu], NK[0:wu, 1, 0:wu]
        X1 = Xm[0:wu, 0:wu]
        Xp, Yp, Ypow = X1, Y1, [Y1]
        for lvl in range(2):
            pc = nextps()
            oc = PS[pc][0:wu, 0:2 * wu].rearrange("p (two t) -> p two t", two=2)
            rk = ["YA", "Xm"] if lvl == 0 else [f"XY{lvl - 1}"]
            b.op("pe", lambda e, oc=oc, Xp=Xp, Yp=Yp: e.matmul(oc[:, 0, :], lhsT=Yp, rhs=Xp, start=True, stop=True), R=rk, W=[psk[pc]])
            b.op("pe", lambda e, oc=oc, Xp=Xp, Yp=Yp: e.matmul(oc[:, 1, :], lhsT=Xp, rhs=Yp, start=True, stop=True), R=rk, W=[psk[pc]])
            b.op("act", lambda e, oc=oc, lvl=lvl: e.copy(out=XY[lvl][0:wu, :, 0:wu], in_=oc), R=[psk[pc]], W=[f"XY{lvl}"])
            Xp, Yp = XY[lvl][0:wu, 0, 0:wu], XY[lvl][0:wu, 1, 0:wu]
            Ypow.append(Yp)
        pc = nextps()
        b.op("pe", lambda e, Xp=Xp, Yp=Yp, pc=pc: e.matmul(PS[pc][0:wu, 0:wu], lhsT=Xp, rhs=Yp, start=True, stop=True), R=["XY1"], W=[psk[pc]])
        b.op("act", lambda e, pc=pc: e.copy(out=Y8[0:wu, 0:wu], in_=PS[pc][0:wu, 0:wu]), R=[psk[pc]], W=["Y8"])
        Ypow.append(Y8[0:wu, 0:wu])
        ykeys = [["YA"], ["XY0"], ["XY1"], ["Y8"]]
        pz = nextps()
        pzb = PS[pz][:].bitcast(BF16)
        b.op("pe", lambda e: e.transpose(out=pzb[0:wu, 0:64], in_=At, identity=ident_b[pr, pr]), R=["AR", "ident_b"], W=[psk[pz]])
        b.op("pe", lambda e: e.matmul(PS[pz][0:wu, 64:128], lhsT=Nak, rhs=Vh, start=True, stop=True), R=["NK", "Vtm"], W=[psk[pz]])
        b.op("act", lambda e: e.copy(out=Zs[0][0:wu, 0:64], in_=pzb[0:wu, 0:64]), R=[psk[pz]], W=["Z0"])
        b.op("act", lambda e: e.copy(out=Zs[0][0:wu, 64:128], in_=PS[pz][0:wu, 64:128]), R=[psk[pz]], W=["Z0"])
        zc = 0
        for lvl in range(4):
            pq_ = nextps()
            b.op("pe", lambda e, lvl=lvl, zc=zc, pq_=pq_: e.matmul(PS[pq_][0:wu, 0:128], lhsT=Ypow[lvl], rhs=Zs[zc][0:wu, :], start=True, stop=True),
                 R=ykeys[lvl] + [f"Z{zc}"], W=[psk[pq_]])
            b.op("dve", lambda e, zc=zc, pq_=pq_: e.tensor_tensor(out=Zs[1 - zc][0:wu, :], in0=PS[pq_][0:wu, 0:128], in1=Zs[zc][0:wu, :], op=ALU.add),
                 R=[psk[pq_], f"Z{zc}"], W=[f"Z{1 - zc}"])
            zc = 1 - zc
        Z4 = Zs[zc]
        zk = f"Z{zc}"
        Ah, Vhh = Z4[0:wu, 0:64], Z4[0:wu, 64:128]
        b.op("pe", lambda e: e.matmul(PS[pRY][pr, 0:wu], lhsT=Ah, rhs=Arb, start=True, stop=True), R=[zk, "YA"], W=[psk[pRY]])
        b.op("pe", lambda e: e.matmul(PS[pRY][pr, 128:128 + wu], lhsT=Vh, rhs=Ark, start=True, stop=False), R=["Vtm", "NK"], W=[psk[pRY]])
        b.op("pe", lambda e: e.matmul(PS[pRY][pr, 128:128 + wu], lhsT=Vhh, rhs=Arb, start=False, stop=True), R=[zk, "YA"], W=[psk[pRY]])
        pt_ = nextps()
        ptb = PS[pt_][:].bitcast(BF16)
        b.op("pe", lambda e: e.transpose(out=ptb[0:wu, 0:64], in_=Bb_, identity=ident_b[pr, pr]), R=["Bb", "ident_b"], W=[psk[pt_]])
        b.op("pe", lambda e: e.transpose(out=ptb[0:wu, 64:128], in_=Kb_, identity=ident_b[pr, pr]), R=["Kb", "ident_b"], W=[psk[pt_]])
        b.op("act", lambda e: e.copy(out=KBtm[0:wu, :, :], in_=ptb[0:wu, 0:128].rearrange("p (a k) -> p a k", a=2)), R=[psk[pt_]], W=["KBtm"])
        pbase = KBtm[:].ap[0][0]

        def bc_c(ap2):
            return apx(ap2, [ap2.ap[0][0], wu], [[0, nch], [1, 64]])
        cmb = apx(cmask_b[0:wu, 0:1], [cmask_b[:].ap[0][0], wu], [[1, nch], [0, 64]])
        b.op("dve", lambda e: e.tensor_tensor(out=Bexp[0:wu, 0:nch, :], in0=bc_c(KBtm[0:wu, 0, :]), in1=cmb, op=ALU.mult), R=["KBtm", "cmask_b"], W=["Bexp"])
        b.op("pool", lambda e: e.tensor_tensor(out=Vexp[0:wu, 0:nch, :], in0=bc_c(Vh), in1=cmb, op=ALU.mult), R=["Vtm", "cmask_b"], W=["Vexp"])
        b.op("dve", lambda e: e.tensor_tensor(out=Vhexp[0:wu, 0:nch, :], in0=bc_c(Vhh), in1=cmb, op=ALU.mult), R=[zk, "cmask_b"], W=["Vhexp"])
        N = nch * 64
        b.op("pe", lambda e: e.matmul(PS[pM][pr, 0:N], lhsT=Ah, rhs=Bexp[0:wu, 0:nch, :].rearrange("p c k -> p (c k)"), start=True, stop=False),
             R=[zk, "Bexp"], W=[psk[pM]])
        b.op("pe", lambda e: e.matmul(PS[pM][pr, 0:N], lhsT=ident_b[pr, pr], rhs=Wd[pr, hp, 0:nch, :].rearrange("p c k -> p (c k)"), start=False, stop=True),
             R=["ident_b", "Wd"], W=[psk[pM]])
        b.op("pe", lambda e: e.matmul(PS[pGm][pr, 0:N], lhsT=KBtm[0:wu, 1, :], rhs=Vexp[0:wu, 0:nch, :].rearrange("p c k -> p (c k)"), start=True, stop=False),
             R=["KBtm", "Vexp"], W=[psk[pGm]])
        b.op("pe", lambda e: e.matmul(PS[pGm][pr, 0:N], lhsT=KBtm[0:wu, 0, :], rhs=Vhexp[0:wu, 0:nch, :].rearrange("p c k -> p (c k)"), start=False, stop=True),
             R=["KBtm", "Vhexp"], W=[psk[pGm]])

        ARf = AR[:, 0:8 * w].rearrange("p (h two t) -> p h two t", h=4, two=2)
        b.op("dve", lambda e: e.tensor_tensor(out=RhT[pr, hp * 128:hp * 128 + wu], in0=PS[pRY][pr, 0:wu], in1=ARf[pr, hp, 1, 0:wu], op=ALU.add),
             R=[psk[pRY], "AR"], W=["RhT"])
        b.op("act", lambda e: e.copy(out=f["enL"][pr, hp * 128:hp * 128 + wu], in_=PS[pRY][pr, 128:128 + wu]),
             R=[psk[pRY]], W=["f_enL"])
        b.op("act", lambda e: e.copy(out=MT[hp][pr, 0:nch, pr], in_=PS[pM][pr, 0:nch * 64].rearrange("p (c k) -> p c k", c=nch)),
             R=[psk[pM]], W=[f"MT{hp}"])
        b.op("dve", lambda e: e.tensor_copy(out=GB[hp][pr, 0:nch, pr], in_=PS[pGm][pr, 0:nch * 64].rearrange("p (c k) -> p c k", c=nch)),
             R=[psk[pGm]], W=[f"GB{hp}"])

    def finalize(i, w, y3):
        W4 = 4 * w
        ysb = f["L"]
        dd, sq, tt = f["Lx"], f["eL"], f["eLC"]
        pm = nextps()
        b.op("pe", lambda e: e.matmul(PS[pm][:, 0:W4], lhsT=blk64_f[:], rhs=ysb[:, 0:W4], start=True, stop=True), R=["blk64_f", "f_L"], W=[psk[pm]])
        b.op("dve", lambda e: e.tensor_tensor(out=dd[:, 0:W4], in0=ysb[:, 0:W4], in1=PS[pm][:, 0:W4], op=ALU.subtract), R=["f_L", psk[pm]], W=["f_Lx"])
        b.op("act", lambda e: e.activation(out=sq[:, 0:W4], in_=dd[:, 0:W4], func=AF.Square), R=["f_Lx"], W=["f_eL"])
        pv_ = nextps()
        b.op("pe", lambda e: e.matmul(PS[pv_][:, 0:W4], lhsT=blk64_f[:], rhs=sq[:, 0:W4], start=True, stop=True), R=["blk64_f", "f_eL"], W=[psk[pv_]])
        b.op("act", lambda e: e.activation(out=sq[:, 0:W4], in_=PS[pv_][:, 0:W4], func=AF.Sqrt, bias=GN_EPS), R=[psk[pv_]], W=["f_eL"])
        b.op("dve", lambda e: e.reciprocal(out=sq[:, 0:W4], in_=sq[:, 0:W4]), R=["f_eL"], W=["f_eL"])
        b.op("dve", lambda e: e.tensor_tensor(out=dd[:, 0:W4], in0=dd[:, 0:W4], in1=sq[:, 0:W4], op=ALU.mult), R=["f_Lx", "f_eL"], W=["f_Lx"])
        d3, t3, kp3 = v3(dd, w), v3(tt, w), v3(f["kp"], w)
        for hp in range(4):
            b.op("dve", lambda e, hp=hp: e.tensor_scalar(out=d3[:, hp, :], in0=d3[:, hp, :], scalar1=pp(PP_LNW, hp), scalar2=pp(PP_LNB, hp),
                                                         op0=ALU.mult, op1=ALU.add), R=["f_Lx", "ppar"], W=["f_Lx"])
            b.op("dve", lambda e, hp=hp: e.scalar_tensor_tensor(out=t3[:, hp, :], in0=zt[:, hp, 0:w], scalar=pp(PP_RK, hp), in1=kp3[:, hp, :],
                                                                op0=ALU.mult, op1=ALU.mult), R=["zt", "f_kp", "ppar"], W=["f_eLC"])
        pb_ = nextps()
        b.op("pe", lambda e: e.matmul(PS[pb_][:, 0:W4], lhsT=blk1_f[:], rhs=tt[:, 0:W4], start=True, stop=True), R=["blk1_f", "f_eLC"], W=[psk[pb_]])
        b.op("dve", lambda e: e.tensor_tensor(out=t3, in0=PS[pb_][:, 0:W4].rearrange("p (h t) -> p h t", h=4), in1=zt[:, 8:12, 0:w], op=ALU.mult),
             R=[psk[pb_], "zt"], W=["f_eLC"])
        b.op("dve", lambda e: e.tensor_tensor(out=dd[:, 0:W4], in0=dd[:, 0:W4], in1=tt[:, 0:W4], op=ALU.add), R=["f_Lx", "f_eLC"], W=["f_Lx"])
        b.op("dve", lambda e: e.tensor_tensor(out=ygT[:, 0:W4], in0=dd[:, 0:W4], in1=gsb[:, 0:W4], op=ALU.mult), R=["f_Lx", "gsb"], W=["ygT"])
        for hp in range(4):
            b.dma("sp", "oyg", ygs[hp * 128:(hp + 1) * 128, i * 128:i * 128 + w], ygT[:, hp * w:(hp + 1) * w], R=["ygT"], W=["ygs"])

    def sample_rwkv(w, y3):
        W4 = 4 * w
        b.op("act", lambda e: e.activation(out=f["t"][:, 0:W4], in_=f["ld"][:, 0:W4], func=AF.Exp), R=["f_ld"], W=["f_t"])
        b.op("dve", lambda e: e.tensor_scalar(out=f["asig"][:, 0:W4], in0=f["kk"][:, 0:W4], scalar1=-1.0, scalar2=None, op0=ALU.mult),
             R=["f_kk"], W=["f_asig"])
        srcs = [(v3(f["t"], w), "f_t"), (v3(f["asig"], w), "f_asig"), (v3(f["bb"], w), "f_bb"), (v3(f["kp"], w), "f_kp"), (zt[:, 0:4, 0:w], "zt")]
        for qi, (src, key) in enumerate(srcs):
            b.op("dve", lambda e, qi=qi, src=src: e.tensor_copy(out=xb[:, qi, :, :], in_=src[:, :, 16:32]), R=[key], W=["xb"])
            p = nextps()
            pvb_ = PS[p][:].bitcast(BF16)
            for hp in range(4):
                for h2 in range(2):
                    pr = slice(h2 * 64, h2 * 64 + 64)
                    b.op("pe", lambda e, pvb_=pvb_, qi=qi, hp=hp, h2=h2, pr=pr: e.transpose(out=pvb_[h2 * 32:h2 * 32 + 16, hp * 64:(hp + 1) * 64], in_=xb[pr, qi, hp, :],
                                                                                          identity=ident_b[pr, pr]),
                         R=["xb", "ident_b"], W=[psk[p]])
            for h2 in range(2):
                b.op("act", lambda e, pvb_=pvb_, h2=h2, qi=qi: e.copy(out=tm5[h2 * 32:h2 * 32 + 16, qi, :], in_=pvb_[h2 * 32:h2 * 32 + 16, 0:256]),
                     R=[psk[p]], W=["tm5"])
        for s in range(SPC):
            sx = Sx[s % 2]
            sk = f"Sx{s % 2}"
            for h2 in range(2):
                src = I["swkv"][s].rearrange("(hp two) v k -> two v hp k", two=2)[h2]
                b.dma("sp", sk, sx[h2 * 64:(h2 + 1) * 64, :, :], src, W=[sk])
            pbs = [nextps() for _ in range(5)]
            for qi in range(5):
                b.op("pe", lambda e, qi=qi, s=s, pbs=pbs: e.matmul(PS[pbs[qi]][:, 0:256], lhsT=onehot_b[0:48, s, :], rhs=tm5[0:48, qi, :], start=True, stop=True),
                     R=["onehot_b", "tm5"], W=[psk[pbs[qi]]])

            def bq(qi, pbs=pbs):
                return PS[pbs[qi]][:, 0:256].rearrange("p (h k) -> p h k", h=4)
            col = 16 + s
            vcol = apx(zt[:, 8, col:col + 1], [zt[:].ap[0][0], 128], [[128, 4], [0, 64]])
            sab = apx(ssa[:, 0:1], [ssa[:].ap[0][0], 128], [[1, 4], [0, 64]])
            b.op("dve", lambda e, sx=sx, bq=bq: e.tensor_tensor(out=stmp[:], in0=sx[:], in1=bq(1), op=ALU.mult), R=[sk, psk[pbs[1]]], W=["stmp"])
            b.op("dve", lambda e: e.tensor_reduce(out=ssa[:, :], in_=stmp[:], axis=AX.X, op=ALU.add), R=["stmp"], W=["ssa"])
            b.op("dve", lambda e, sx=sx, bq=bq: e.tensor_tensor(out=sx[:], in0=sx[:], in1=bq(0), op=ALU.mult), R=[sk, psk[pbs[0]]], W=[sk])
            b.op("dve", lambda e, bq=bq, sab=sab: e.tensor_tensor(out=stmp[:], in0=sab, in1=bq(2), op=ALU.mult), R=["ssa", psk[pbs[2]]], W=["stmp"])
            b.op("dve", lambda e, sx=sx: e.tensor_tensor(out=sx[:], in0=sx[:], in1=stmp[:], op=ALU.add), R=[sk, "stmp"], W=[sk])
            b.op("dve", lambda e, bq=bq, vcol=vcol: e.tensor_tensor(out=stmp[:], in0=vcol, in1=bq(3), op=ALU.mult), R=["zt", psk[pbs[3]]], W=["stmp"])
            b.op("dve", lambda e, sx=sx: e.tensor_tensor(out=sx[:], in0=sx[:], in1=stmp[:], op=ALU.add), R=[sk, "stmp"], W=[sk])
            b.op("dve", lambda e, sx=sx, bq=bq: e.tensor_tensor(out=stmp[:], in0=sx[:], in1=bq(4), op=ALU.mult), R=[sk, psk[pbs[4]]], W=["stmp"])
            b.op("dve", lambda e, col=col: e.tensor_reduce(out=y3[:, :, col], in_=stmp[:], axis=AX.X, op=ALU.add), R=["stmp"], W=["f_L"])
            for h2 in range(2):
                dst = O["wkv_s"][s].rearrange("(hp two) v k -> two v hp k", two=2)[h2]
                b.dma("sp", "owkvs", dst, sx[h2 * 64:(h2 + 1) * 64, :, :], R=[sk])

    def wkv_prompt_out():
        for hp in range(4):
            S0 = Sbd[hp][scur[hp]]
            k0 = f"S{hp}_{scur[hp]}"
            p = nextps()
            pb_ = PS[p][:].bitcast(BF16) if CH == BF16 else PS[p][:]
            idn = ident_b if CH == BF16 else ident_f
            b.op("pe", lambda e, S0=S0, pb_=pb_, idn=idn: e.transpose(out=pb_[:, 0:128], in_=S0[:, :], identity=idn[:, :]), R=[k0, "ident_b", "ident_f"], W=[psk[p]])
            b.op("act", lambda e, hp=hp, pb_=pb_: e.copy(out=wkvo[:, hp, :], in_=pb_[:, 0:128]), R=[psk[p]], W=["wkvo"])
            for h2 in range(2):
                b.dma("sp", "owkvp", O["wkv_p"][2 * hp + h2], wkvo[h2 * 64:(h2 + 1) * 64, hp, h2 * 64:(h2 + 1) * 64], R=["wkvo"])


    def tile_A2(i):
        w = tw(i)
        buf = i % 2
        if i + 1 < NTL:
            load_x(i + 1, (i + 1) % 2)
        norm_and_transpose(i, buf, gmixB, "gmixB")
        if i == NFULL:
            shrow = zt[:].rearrange("p a b -> p (a b)")
            for piece in range(4):
                p = nextps()
                c0 = piece * 448
                for kc in range(8):
                    b.op("pe", lambda e, kc=kc, p=p, c0=c0: e.matmul(PS[p][0:w, 0:448], lhsT=hT[:, kc, 0:w], rhs=w_in_r[:, kc, c0:c0 + 448],
                                                                     start=(kc == 0), stop=(kc == 7)), R=["hT", "w_in_r"], W=[psk[p]])
                b.op("act", lambda e, p=p, piece=piece: e.copy(out=shrow[0:w, piece * 448:(piece + 1) * 448], in_=PS[p][0:w, 0:448]),
                     R=[psk[p]], W=["zt"])
            b.dma("sp", "osh", O["sh_p"][:, :], shrow[15:16, :], R=["zt"])
            b.dma("sp", "osh", O["sh_s"][:, :], shrow[16:32, :], R=["zt"])
        rwkv_tile(i)

    if stage >= 2:
        load_x(0, 0)
        for i in range(NTL):
            tile_A2(i)
    b.barrier()
    b.emit()
    pst.close()
    b.st = b.st0

    def phase_B():
        wg_b = b.sb("wg_b", [128, 8, 2048], BF16)
        for kc in range(8):
            b.dma("pool", "wA", wg_b[:, kc, :], I["w_in"][kc * 128:(kc + 1) * 128, C_G:INC], W=["wg_b"])
        w_om = load_w_bf("w_om", [128, 4, D], I["w_o_mla"].rearrange("(k p) n -> p k n", p=128))
        w_or = load_w_bf("w_or", [128, 4, D], I["w_o_rwkv"].rearrange("(k p) n -> p k n", p=128))
        w_ob = load_w_bf("w_ob", [128, 8, D], I["w_out"].rearrange("(k p) n -> p k n", p=128))
        gsg = b.sb("gsg", [128, 16, 128], BF16)
        b.op("pool", lambda e: e.memset(gsg[:], 0.0), W=["gsg"])
        atT = b.sb("atT", [128, 4, 128], BF16)
        ygTt = b.sb("ygTt", [128, 4, 128], BF16)
        mT = b.sb("mT", [128, 8, 128], BF16)
        tmpa = b.sb("tmpa", [128, 512], F32)
        tmpb = b.sb("tmpb", [128, 512], F32)
        x1 = b.sb("x1", [128, D], F32)

        def tile(i):
            w = tw(i)
            buf = i % 2
            t0 = 128 * i
            if i + 1 < NTL:
                load_x(i + 1, (i + 1) % 2)
            b.dma("sp", "ldat", atT[:, :, 0:w], ats[:, t0:t0 + w].rearrange("(c p) t -> p c t", p=128), R=["ats"], W=["atT"])
            b.dma("sp", "ldyg", ygTt[:, :, 0:w], ygs[:, t0:t0 + w].rearrange("(c p) t -> p c t", p=128), R=["ygs"], W=["ygTt"])
            norm_and_transpose(i, buf, gmixB, "gmixB")
            for g in range(4):
                p = nextps()
                for jj in range(4):
                    j = 4 * g + jj
                    for kc in range(8):
                        b.op("pe", lambda e, p=p, jj=jj, j=j, kc=kc: e.matmul(PS[p][:, jj * 128:jj * 128 + w], lhsT=wg_b[:, kc, j * 128:(j + 1) * 128], rhs=hT[:, kc, 0:w],
                                                                               start=(kc == 0), stop=(kc == 7)), R=["wg_b", "hT"], W=[psk[p]])
                b.op("act", lambda e, p=p, g=g: e.activation(out=gsg[:, 4 * g:4 * g + 4, 0:w], in_=PS[p][:, 0:512].rearrange("p (j t) -> p j t", j=4)[:, :, 0:w], func=AF.Sigmoid),
                     R=[psk[p]], W=["gsg"])
            for g in range(2):
                pm_, pr_ = nextps(), nextps()
                for jj in range(4):
                    j = 4 * g + jj
                    for kc in range(4):
                        b.op("pe", lambda e, pm_=pm_, jj=jj, j=j, kc=kc: e.matmul(PS[pm_][:, jj * 128:jj * 128 + w], lhsT=w_om[:, kc, j * 128:(j + 1) * 128], rhs=atT[:, kc, 0:w],
                                                                                   start=(kc == 0), stop=(kc == 3)), R=["w_om", "atT"], W=[psk[pm_]])
                    for kc in range(4):
                        b.op("pe", lambda e, pr_=pr_, jj=jj, j=j, kc=kc: e.matmul(PS[pr_][:, jj * 128:jj * 128 + w], lhsT=w_or[:, kc, j * 128:(j + 1) * 128], rhs=ygTt[:, kc, 0:w],
                                                                                   start=(kc == 0), stop=(kc == 3)), R=["w_or", "ygTt"], W=[psk[pr_]])
                v1 = PS[pm_][:, 0:512].rearrange("p (j t) -> p j t", j=4)
                v2 = PS[pr_][:, 0:512].rearrange("p (j t) -> p j t", j=4)
                ta = tmpa[:, 0:512].rearrange("p (j t) -> p j t", j=4)
                tb = tmpb[:, 0:512].rearrange("p (j t) -> p j t", j=4)
                b.op("dve", lambda e, v1=v1, ta=ta, g=g: e.tensor_tensor(out=ta, in0=v1, in1=gsg[:, 4 * g:4 * g + 4, :], op=ALU.mult), R=[psk[pm_], "gsg"], W=["tmpa"])
                b.op("dve", lambda e, v2=v2, tb=tb, g=g: e.tensor_tensor(out=tb, in0=v2, in1=gsg[:, 8 + 4 * g:8 + 4 * g + 4, :], op=ALU.mult), R=[psk[pr_], "gsg"], W=["tmpb"])
                b.op("pool", lambda e, ta=ta, tb=tb, g=g: e.tensor_tensor(out=mT[:, 4 * g:4 * g + 4, :], in0=ta, in1=tb, op=ALU.add), R=["tmpa", "tmpb"], W=["mT"])
            for half in range(2):
                po_ = nextps()
                for kc in range(8):
                    b.op("pe", lambda e, po_=po_, kc=kc, half=half: e.matmul(PS[po_][0:w, 0:512], lhsT=mT[:, kc, 0:w], rhs=w_ob[:, kc, half * 512:(half + 1) * 512],
                                                                             start=(kc == 0), stop=(kc == 7)), R=["mT", "w_ob"], W=[psk[po_]])
                b.op("dve", lambda e, po_=po_, half=half: e.tensor_tensor(out=x1[0:w, half * 512:(half + 1) * 512], in0=PS[po_][0:w, 0:512], in1=xt[buf][0:w, half * 512:(half + 1) * 512], op=ALU.add),
                     R=[psk[po_], f"xt{buf}"], W=["x1"])
            b.dma("sp", "ox1", x1s[t0:t0 + w, :], x1[0:w, :], R=["x1"], W=["x1s"])

        load_x(0, 0)
        for i in range(NTL):
            tile(i)

    if stage >= 4:
        with ExitStack() as pst:
            b.st = pst
            phase_B()
            b.barrier()
            b.emit()
        b.st = b.st0

    def phase_C():
        w_up_b = b.sb("w_up_b", [128, 8, DFF], BF16)
        for kc in range(8):
            b.dma("pool", "wA", w_up_b[:, kc, :], I["w_up"][kc * 128:(kc + 1) * 128, :], W=["w_up_b"])
        w_dn_b = b.sb("w_dn_b", [128, 32, D], BF16)
        for k4 in range(8):
            b.dma("pool", "wA", w_dn_b[:, 4 * k4:4 * k4 + 4, :], I["w_down"][512 * k4:512 * (k4 + 1), :].rearrange("(k p) n -> p k n", p=128), W=["w_dn_b"])
        gffnB = bcast_load("gffnB", I["g_ffn"], D)
        gfinB = bcast_load("gfinB", I["g_final"], D)
        uT = b.sb("uT", [128, 32, 128], BF16)
        rl_ = b.sb("relu_t", [128, 512], BF16)
        x2 = b.sb("x2", [128, D], F32)
        yo = b.sb("yo", [128, D], F32)

        def tile(i):
            w = tw(i)
            buf = i % 2
            t0 = 128 * i
            if i + 1 < NTL:
                load_x(i + 1, (i + 1) % 2, src_scratch=True)
            norm_and_transpose(i, buf, gffnB, "gffnB")
            for g in range(8):
                p = nextps()
                for jj in range(4):
                    j = 4 * g + jj
                    for kc in range(8):
                        b.op("pe", lambda e, p=p, jj=jj, j=j, kc=kc: e.matmul(PS[p][:, jj * 128:jj * 128 + w], lhsT=w_up_b[:, kc, j * 128:(j + 1) * 128], rhs=hT[:, kc, 0:w],
                                                                               start=(kc == 0), stop=(kc == 7)), R=["w_up_b", "hT"], W=[psk[p]])
                rv = rl_[:, 0:4 * w].rearrange("p (j t) -> p j t", j=4)
                b.op("act", lambda e, p=p, rv=rv: e.activation(out=rv, in_=PS[p][:, 0:512].rearrange("p (j t) -> p j t", j=4)[:, :, 0:w], func=AF.Relu), R=[psk[p]], W=["relu_t"])
                b.op("pool", lambda e, g=g, rv=rv: e.tensor_tensor(out=uT[:, 4 * g:4 * g + 4, 0:w], in0=rv, in1=rv, op=ALU.mult), R=["relu_t"], W=["uT"])
            for half in range(2):
                po_ = nextps()
                for kc in range(32):
                    b.op("pe", lambda e, po_=po_, kc=kc, half=half: e.matmul(PS[po_][0:w, 0:512], lhsT=uT[:, kc, 0:w], rhs=w_dn_b[:, kc, half * 512:(half + 1) * 512],
                                                                             start=(kc == 0), stop=(kc == 31)), R=["uT", "w_dn_b"], W=[psk[po_]])
                b.op("dve", lambda e, po_=po_, half=half: e.tensor_tensor(out=x2[0:w, half * 512:(half + 1) * 512], in0=PS[po_][0:w, 0:512], in1=xt[buf][0:w, half * 512:(half + 1) * 512], op=ALU.add),
                     R=[psk[po_], f"xt{buf}"], W=["x2"])
            rms_rstd(x2[0:w, :], ["x2"], w, D, 3, NORM_EPS)
            b.op("dve", lambda e: e.scalar_tensor_tensor(out=yo[0:w, :], in0=x2[0:w, :], scalar=st1[0:w, 3:4], in1=gfinB[0:w, :], op0=ALU.mult, op1=ALU.mult),
                 R=["x2", "st1_3", "gfinB"], W=["yo"])
            if i == 0:
                b.dma("sp", "oy", O["y_p"][0:112, :], yo[16:128, :], R=["yo"])
            elif i < NFULL:
                b.dma("sp", "oy", O["y_p"][128 * i - 16:128 * i + 112, :], yo[:, :], R=["yo"])
            else:
                b.dma("sp", "oy", O["y_p"][SEQ - 16:SEQ, :], yo[0:16, :], R=["yo"])
                b.dma("sp", "oy", O["y_s"][:, :], yo[16:32, :], R=["yo"])

        load_x(0, 0, src_scratch=True)
        for i in range(NTL):
            tile(i)

    if stage >= 4:
        with ExitStack() as pst:
            b.st = pst
            phase_C()
            b.barrier()
            b.emit()
        b.st = b.st0


def kernel(**inputs):
    x_prompt = np.asarray(inputs["x_prompt"])
    x_sample = np.asarray(inputs["x_sample"])
    ncores, seq = x_prompt.shape[0], x_prompt.shape[1]
    cache = np.asarray(inputs["cache_kv"])[0]
    pt = np.asarray(inputs["page_table"]).astype(np.int32)
    npool, npg = cache.shape[0], pt.shape[1]
    cfg = make_cfg(seq, npg, npool, npg * 128)
    import os
    nc, consts = build(cfg, stage=int(os.environ.get('KSTAGE', '99')))
    wsh = dict(g_final=[1, D], g_mix=[1, D], w_in=[D, INC], g_q=[1, NQ], w_uq=[NQ, 768], g_kv=[1, NKV],
               w_uk=[NKV, 512], w_uv=[NKV, 512], w_o_mla=[512, D], mu_shift=[1, RC], w0=[1, RW], w2=[64, RW],
               a0=[1, RW], a2=[64, RW], g2=[128, RW], k_k=[1, RW], k_a=[1, RW], r_k=[1, RW], ln_w=[1, RW],
               ln_b=[1, RW], w_o_rwkv=[RW, D], w_out=[D, D], g_ffn=[1, D], w_up=[D, DFF], w_down=[DFF, D])
    shared = {}
    for n in WNAMES:
        shared[n] = np.ascontiguousarray(np.asarray(inputs[n], dtype=np.float32).reshape(wsh[n]))
    for n, a in consts.items():
        shared["c_" + n] = a
    shared["meta"] = np.ascontiguousarray(np.asarray(inputs["meta_tokens"], dtype=np.float32))
    shared["cache"] = np.ascontiguousarray(cache)
    swkv = np.asarray(inputs["state_wkv"])[0]
    sshift = np.asarray(inputs["state_shift"])[0]
    in_maps = []
    for c in range(ncores):
        m = dict(shared)
        m["xp"] = np.ascontiguousarray(x_prompt[c])
        m["xs"] = np.ascontiguousarray(x_sample[SPC * c:SPC * (c + 1), 0])
        m["pt"] = np.ascontiguousarray(pt[SPC * c:SPC * (c + 1)])
        m["swkv"] = np.ascontiguousarray(swkv[SPC * c:SPC * (c + 1)])
        m["sshift"] = np.ascontiguousarray(sshift[SPC * c:SPC * (c + 1)])
        in_maps.append(m)
    res = run_bass_kernel_spmd(nc, in_maps, core_ids=list(range(ncores)))
    r = res.results
    y_p = np.stack([r[c]["y_p"] for c in range(ncores)])
    y_s = np.concatenate([r[c]["y_s"] for c in range(ncores)])[:, None, :]
    kv_p = np.stack([r[c]["kv_p"] for c in range(ncores)])[None]
    wkv_p = np.stack([r[c]["wkv_p"] for c in range(ncores)])[None]
    sh_p = np.concatenate([r[c]["sh_p"] for c in range(ncores)])[None]
    kv_s = np.concatenate([r[c]["kv_s"] for c in range(ncores)])[None, :, None, :]
    wkv_s = np.concatenate([r[c]["wkv_s"] for c in range(ncores)])[None]
    sh_s = np.concatenate([r[c]["sh_s"] for c in range(ncores)])[None]
    return tuple(np.ascontiguousarray(a, dtype=np.float32) for a in (y_p, y_s, kv_p, wkv_p, sh_p, kv_s, wkv_s, sh_s))
```

```python
import numpy as np
from contextlib import ExitStack
import concourse.bass as bass
import concourse.mybir as mybir
from concourse.bass_utils import run_bass_kernel_spmd

F32 = mybir.dt.float32
BF16 = mybir.dt.bfloat16
I32 = mybir.dt.int32
AF = mybir.ActivationFunctionType
ALU = mybir.AluOpType
AX = mybir.AxisListType

D = 1024
NQ, NKV, NRP = 384, 256, 32
KVW = 288
H = 8
RW = 512
RC = 1792
INC = 4512
C_Q, C_KV, C_RW, C_G = 0, 384, 672, 2464
DFF = 4096
NORM_EPS = 1e-6
GN_EPS = 64e-5
SM_SCALE = 96 ** -0.5
NMETA = 16
SPC = 16

WNAMES = ["g_final", "g_mix", "w_in", "g_q", "w_uq", "g_kv", "w_uk", "w_uv", "w_o_mla", "mu_shift", "w0", "w2",
          "a0", "a2", "g2", "k_k", "k_a", "r_k", "ln_w", "ln_b", "w_o_rwkv", "w_out", "g_ffn", "w_up", "w_down"]


def apx(base, part, free):
    return bass.AP(base.tensor, base.offset, [list(part)] + [list(f) for f in free])


class B:
    ENG = ("pe", "act", "dve", "pool", "sp")

    def __init__(self, nc, st):
        self.nc, self.st, self.st0 = nc, st, st
        self.q = {e: [] for e in self.ENG}
        if getattr(self, "_after_barrier", False):
            self.new_engine_sems()
            self._after_barrier = False
        self.sems = {}
        self.cnt = {}
        self.ek = {}
        self.phase = 0
        self.new_engine_sems()
        self.waited = {e: {} for e in self.ENG}
        self.bufs = {}
        self.nps = 0

    def new_engine_sems(self):
        self.phase += 1
        for e in ("pe", "act", "dve", "pool"):
            k = f"{e}#{self.phase}"
            self.sems[k] = self.st0.enter_context(self.nc.semaphore("s_" + k.replace("#", "_")))
            self.cnt[k] = 0
            self.ek[e] = k

    def sb(self, name, shape, dt):
        return self.st.enter_context(self.nc.sbuf_tensor(name, list(shape), dt))

    def psum(self, name, shape, dt):
        return self.st0.enter_context(self.nc.psum_tensor(name, list(shape), dt))

    def _buf(self, k):
        if k not in self.bufs:
            self.bufs[k] = {"w": None, "r": {}}
        return self.bufs[k]

    def _waits(self, eng, R, W):
        toks = {}

        def need(t):
            if t is not None:
                toks[t[0]] = max(toks.get(t[0], 0), t[1])
        for k in R:
            need(self._buf(k)["w"])
        for k in W:
            b = self._buf(k)
            need(b["w"])
            for kk, v in b["r"].items():
                need((kk, v))
        for k, v in toks.items():
            if eng == "pe" and k == self.ek["pe"]:
                continue
            if "#" not in k:
                v = self.cnt[k]
            if self.waited[eng].get(k, 0) >= v:
                continue
            self.waited[eng][k] = v
            sem = self.sems[k]
            self.q[eng].append(lambda e, sem=sem, v=v: e.wait_ge(sem, v))

    def _mark(self, tok, R, W):
        for k in R:
            b = self._buf(k)
            b["r"][tok[0]] = max(b["r"].get(tok[0], 0), tok[1])
        for k in W:
            self.bufs[k] = {"w": tok, "r": {}}

    def op(self, eng, fn, R=(), W=()):
        self._waits(eng, R, W)
        k = self.ek[eng]
        self.cnt[k] += 1
        sem = self.sems[k]
        self.q[eng].append(lambda e, fn=fn, sem=sem: fn(e).then_inc(sem, 1))
        self._mark((k, self.cnt[k]), R, W)

    DRAM_KEYS = ("ats", "ygs", "kvs", "x1s")

    def _semkey(self, R, W):
        if W and W[0] not in self.DRAM_KEYS:
            k = "i_" + W[0]
        else:
            k = "o_" + R[0]
        if k not in self.sems:
            self.sems[k] = self.st0.enter_context(self.nc.semaphore(k))
            self.cnt[k] = 0
        return k

    def dma(self, queue, semkey, out, in_, R=(), W=(), **kw):
        semkey = self._semkey(R, W)
        self._waits(queue, R, W)
        self.cnt[semkey] += 16
        sem = self.sems[semkey]
        self.q[queue].append(lambda e, sem=sem, out=out, in_=in_, kw=kw: e.dma_start(out=out, in_=in_, **kw).then_inc(sem, 16))
        self._mark((semkey, self.cnt[semkey]), R, W)

    def raw_dma(self, queue, semkey, fn, R=(), W=()):
        semkey = self._semkey(R, W)
        self._waits(queue, R, W)
        self.cnt[semkey] += 16
        sem = self.sems[semkey]
        self.q[queue].append(lambda e, sem=sem, fn=fn: fn(e).then_inc(sem, 16))
        self._mark((semkey, self.cnt[semkey]), R, W)

    def finish(self):
        for k, sem in self.sems.items():
            v = self.cnt[k]
            if v and self.waited["sp"].get(k, 0) < v:
                self.waited["sp"][k] = v
                self.q["sp"].append(lambda e, sem=sem, v=v: e.wait_ge(sem, v))

    def barrier(self):
        self._after_barrier = True
        for e in self.ENG:
            for k, sem in self.sems.items():
                v = self.cnt[k]
                if v and self.waited[e].get(k, 0) < v and not (k == self.ek.get(e)):
                    self.waited[e][k] = v
                    self.q[e].append(lambda eng, sem=sem, v=v: eng.wait_ge(sem, v))

    def emit(self):
        with self.nc.Block() as blk:
            for name, deco in (("pe", blk.tensor), ("act", blk.scalar), ("dve", blk.vector),
                               ("pool", blk.gpsimd), ("sp", blk.sync)):
                lst = self.q[name]

                def run(e, lst=lst):
                    for th in lst:
                        th(e)
                deco(run)
        self.q = {e: [] for e in self.ENG}


def host_consts(cfg):
    T, NTOK, NTL = cfg["T"], cfg["NTOK"], cfg["NTL"]
    c = {}
    idx = np.arange(128)
    c["ident"] = np.eye(128, dtype=np.float32)
    same = (idx[:, None] // 16) == (idx[None, :] // 16)
    c["maskx"] = (same & (idx[None, :] < idx[:, None])).astype(np.float32)
    mj = np.zeros((128, 2, 128), np.float32)
    mj[:, 0, :] = same & (idx[None, :] > idx[:, None])
    mj[:, 1, :] = same & (idx[None, :] >= idx[:, None])
    c["maskjt"] = mj.reshape(128, 256)
    c["cmask"] = (idx[:, None] // 16 == np.arange(8)[None, :]).astype(np.float32)
    c["blk1"] = ((idx[:, None] // 64) == (idx[None, :] // 64)).astype(np.float32)
    c["maskc"] = (idx[None, :] >= idx[:, None]).astype(np.float32)
    c["identb"] = (idx[:, None] % 64 == np.arange(64)[None, :]).astype(np.float32)
    rm = np.ones((128, 512), np.float32)
    rm[:, ::16] = 0.0
    c["scanm"] = rm
    oh = np.zeros((128, 16, 128), np.float32)
    for s in range(16):
        oh[s, s, 0:64] = 1.0
        oh[32 + s, s, 64:128] = 1.0
    c["onehot"] = oh.reshape(128, 2048)
    inv = (10000.0 ** (-np.arange(0, 32, 2, dtype=np.float32) / np.float32(32))).astype(np.float32)
    pos = np.concatenate([np.arange(T), np.full(SPC, cfg["PAST"])]).astype(np.float32)
    ang = (pos[:, None] * inv[None, :]).astype(np.float32)
    tab = np.zeros((NTL * 128, 32), np.float32)
    tab[:NTOK, :16] = np.cos(ang)
    tab[:NTOK, 16:] = np.sin(ang)
    c["rope"] = np.ascontiguousarray(tab.reshape(NTL, 128, 32).transpose(1, 0, 2)).reshape(128, NTL * 32)
    return c


def make_cfg(seq, npg, npool, past):
    T = seq + NMETA
    assert seq % 128 == 0
    NTOK = T + SPC
    NTL = (NTOK + 127) // 128
    return dict(SEQ=seq, T=T, NTOK=NTOK, NTL=NTL, NPG=npg, NPOOL=npool, PAST=past)


def build(cfg, stage=99):
    SEQ, T, NTOK, NTL, NPG, NPOOL = cfg["SEQ"], cfg["T"], cfg["NTOK"], cfg["NTL"], cfg["NPG"], cfg["NPOOL"]
    nc = bass.Bass("TRN2", target_bir_lowering=False)
    consts = host_consts(cfg)

    def din(name, shape, dt=F32):
        return nc.dram_tensor(name, list(shape), dt, kind="ExternalInput").ap()

    def dout(name, shape, dt=F32):
        return nc.dram_tensor(name, list(shape), dt, kind="ExternalOutput").ap()

    I = {}
    I["xp"] = din("xp", [SEQ, D])
    I["xs"] = din("xs", [SPC, D])
    I["meta"] = din("meta", [NMETA, D])
    I["cache"] = din("cache", [NPOOL, 128, KVW])
    I["pt"] = din("pt", [SPC, NPG], I32)
    I["swkv"] = din("swkv", [SPC, H, 64, 64])
    I["sshift"] = din("sshift", [SPC, RC])
    wshapes = dict(g_final=[1, D], g_mix=[1, D], w_in=[D, INC], g_q=[1, NQ], w_uq=[NQ, 768], g_kv=[1, NKV],
                   w_uk=[NKV, 512], w_uv=[NKV, 512], w_o_mla=[512, D], mu_shift=[1, RC], w0=[1, RW], w2=[64, RW],
                   a0=[1, RW], a2=[64, RW], g2=[128, RW], k_k=[1, RW], k_a=[1, RW], r_k=[1, RW], ln_w=[1, RW],
                   ln_b=[1, RW], w_o_rwkv=[RW, D], w_out=[D, D], g_ffn=[1, D], w_up=[D, DFF], w_down=[DFF, D])
    for n in WNAMES:
        I[n] = din(n, wshapes[n])
    for n, a in consts.items():
        I["c_" + n] = din("c_" + n, a.shape)
    O = {}
    O["y_p"] = dout("y_p", [SEQ, D])
    O["y_s"] = dout("y_s", [SPC, D])
    O["kv_p"] = dout("kv_p", [T, KVW])
    O["wkv_p"] = dout("wkv_p", [H, 64, 64])
    O["sh_p"] = dout("sh_p", [1, RC])
    O["kv_s"] = dout("kv_s", [SPC, KVW])
    O["wkv_s"] = dout("wkv_s", [SPC, H, 64, 64])
    O["sh_s"] = dout("sh_s", [SPC, RC])

    st = ExitStack()
    with st:
        b = B(nc, st)
        _program(nc, b, cfg, I, O, stage)
        b.finish()
        b.emit()
    return nc, consts


def _program(nc, b, cfg, I, O, stage):
    import os
    KA1 = int(os.environ.get('KA1', '3'))
    KQ = int(os.environ.get('KQ', '9'))
    KR = int(os.environ.get('KR', '9'))
    SEQ, T, NTOK, NTL, NPG, NPOOL = cfg["SEQ"], cfg["T"], cfg["NTOK"], cfg["NTL"], cfg["NPG"], cfg["NPOOL"]
    NFULL = NTL - 1
    LASTW = NTOK - NFULL * 128
    PP2 = 2 * NPG

    def tw(i):
        return 128 if i < NFULL else LASTW

    PS = [b.psum(f"ps{i}", [128, 512], F32) for i in range(8)]
    psk = [f"ps{i}" for i in range(8)]
    psn = [0]

    reserved = set()

    def nextps():
        while True:
            i = psn[0] % 8
            psn[0] += 1
            if i not in reserved:
                return i

    def reserve():
        i = nextps()
        reserved.add(i)
        return i

    def release(i):
        reserved.discard(i)

    def flat(name, dt=F32, n=512):
        return b.sb(name, [128, n], dt)

    def v3(t, w, g=4):
        return t[:, 0:g * w].rearrange("p (g t) -> p g t", g=g)

    ygs = nc.dram_tensor("ygs", [RW, NTL * 128], BF16, kind="Internal").ap()
    ats = nc.dram_tensor("ats", [RW, NTL * 128], BF16, kind="Internal").ap()
    kvs = nc.dram_tensor("kvs", [SPC, KVW], F32, kind="Internal").ap()
    x1s = nc.dram_tensor("x1s", [NTL * 128, D], F32, kind="Internal").ap()

    ident_f = b.sb("ident_f", [128, 128], F32)
    ident_b = b.sb("ident_b", [128, 128], BF16)
    b.dma("sp", "c0", ident_f[:], I["c_ident"], W=["ident_f"])
    b.op("dve", lambda e: e.tensor_copy(out=ident_b[:], in_=ident_f[:]), R=["ident_f"], W=["ident_b"])
    rope = b.sb("rope", [128, NTL, 32], F32)
    b.dma("sp", "c0", rope[:].rearrange("p a b -> p (a b)"), I["c_rope"], W=["rope"])

    def bcast_load(name, src, n):
        t = b.sb(name, [128, n], F32)
        b.dma("sp", "c0", t[:], src.partition_broadcast(128).rearrange("p o n -> p (o n)"), W=[name])
        return t

    gmixB = bcast_load("gmixB", I["g_mix"], D)
    xt = [b.sb(f"xt{i}", [128, D], F32) for i in range(2)]
    junk = b.sb("junk", [128, D], F32)
    hb = b.sb("hb", [128, D], BF16)
    hT = b.sb("hT", [128, 8, 128], BF16)
    st1 = b.sb("st1", [128, 8], F32)
    rt1 = b.sb("rt1", [128, H, 16], F32)
    rt2 = b.sb("rt2", [128, H, 16], F32)
    qsT = b.sb("qsT", [128, H, 16], BF16)

    def load_x(i, buf, src_scratch=False):
        w = tw(i)
        k = f"xt{buf}"
        if src_scratch:
            b.dma("sp", k, xt[buf][0:w, :], x1s[128 * i:128 * i + w, :], R=["x1s"], W=[k])
            return
        if i == 0:
            b.dma("sp", k, xt[buf][0:16, :], I["meta"], W=[k])
            b.dma("sp", k, xt[buf][16:128, :], I["xp"][0:112, :], W=[k])
        elif i < NFULL:
            b.dma("sp", k, xt[buf][:, :], I["xp"][128 * i - 16:128 * i + 112, :], W=[k])
        else:
            b.dma("sp", k, xt[buf][0:16, :], I["xp"][SEQ - 16:SEQ, :], W=[k])
            b.dma("sp", k, xt[buf][16:32, :], I["xs"], W=[k])

    def rms_rstd(src_ap, srckeys, w, n, col, eps):
        b.op("act", lambda e: e.activation(out=junk[0:w, 0:n], in_=src_ap, func=AF.Square), R=srckeys, W=["junk"])
        b.op("dve", lambda e: e.tensor_reduce(out=st1[0:w, col:col + 1], in_=junk[0:w, 0:n], axis=AX.X, op=ALU.add),
             R=["junk"], W=[f"st1_{col}"])
        b.op("act", lambda e: e.activation(out=st1[0:w, col:col + 1], in_=st1[0:w, col:col + 1], func=AF.Sqrt,
                                           bias=eps, scale=1.0 / n), R=[f"st1_{col}"], W=[f"st1_{col}"])
        b.op("dve", lambda e: e.reciprocal(out=st1[0:w, col:col + 1], in_=st1[0:w, col:col + 1]),
             R=[f"st1_{col}"], W=[f"st1_{col}"])

    def rope_tm(dst, src, w, i, nh, keysR, keysW):
        cosb = apx(rope[0:w, i, 0:16], [rope[:].ap[0][0], w], [[0, nh], [1, 16]])
        sinb = apx(rope[0:w, i, 16:32], [rope[:].ap[0][0], w], [[0, nh], [1, 16]])
        x1, x2 = src[:, :, 0:16], src[:, :, 16:32]
        t1, t2 = rt1[0:w, 0:nh, :], rt2[0:w, 0:nh, :]
        b.op("dve", lambda e: e.tensor_tensor(out=t1, in0=x1, in1=cosb, op=ALU.mult), R=keysR + ["rope"], W=["rt1"])
        b.op("dve", lambda e: e.tensor_tensor(out=t2, in0=x2, in1=sinb, op=ALU.mult), R=keysR + ["rope"], W=["rt2"])
        b.op("dve", lambda e: e.tensor_tensor(out=dst[:, :, 0:16], in0=t1, in1=t2, op=ALU.subtract), R=["rt1", "rt2"], W=keysW)
        b.op("dve", lambda e: e.tensor_tensor(out=t1, in0=x1, in1=sinb, op=ALU.mult), R=keysR + ["rope"], W=["rt1"])
        b.op("dve", lambda e: e.tensor_tensor(out=t2, in0=x2, in1=cosb, op=ALU.mult), R=keysR + ["rope"], W=["rt2"])
        b.op("dve", lambda e: e.tensor_tensor(out=dst[:, :, 16:32], in0=t1, in1=t2, op=ALU.add), R=["rt1", "rt2"], W=keysW)

    def norm_and_transpose(i, buf, gB, gkey, dst=None, dkey="hT"):
        w = tw(i)
        k = f"xt{buf}"
        dst = hT if dst is None else dst
        rms_rstd(xt[buf][0:w, :], [k], w, D, 0, NORM_EPS)
        b.op("dve", lambda e: e.scalar_tensor_tensor(out=hb[0:w, :], in0=xt[buf][0:w, :], scalar=st1[0:w, 0:1],
                                                     in1=gB[0:w, :], op0=ALU.mult, op1=ALU.mult),
             R=[k, "st1_0", gkey], W=["hb"])
        p = nextps()
        pv = PS[p][:].bitcast(BF16)
        for kc in range(8):
            b.op("pe", lambda e, kc=kc: e.transpose(out=pv[:, kc * 128:kc * 128 + w], in_=hb[0:w, kc * 128:(kc + 1) * 128],
                                                    identity=ident_b[0:w, 0:w]), R=["hb", "ident_b"], W=[psk[p]])
        src = pv[:, 0:1024].rearrange("p (k t) -> p k t", k=8)[:, :, 0:w]
        b.op("act", lambda e: e.copy(out=dst[:, :, 0:w], in_=src), R=[psk[p]], W=[dkey])

    def load_w_bf(name, shape, src, key=None):
        t = b.sb(name, shape, BF16)
        b.dma("pool", "wA", t[:], src, W=[key or name])
        return t

    b.emit()

    def phase_A1():
        NW = C_RW
        w_in_b = b.sb("w_in_a", [128, 8, NW], BF16)
        for kc in range(8):
            b.dma("pool", "wA", w_in_b[:, kc, :], I["w_in"][kc * 128:(kc + 1) * 128, 0:NW], W=["w_in_a"])
        w_uq_b = load_w_bf("w_uq_b", [128, 3, 768], I["w_uq"].rearrange("(k p) n -> p k n", p=128))
        w_uk_b = load_w_bf("w_uk_b", [128, 2, 512], I["w_uk"].rearrange("(k p) n -> p k n", p=128))
        w_uv_b = load_w_bf("w_uv_b", [128, 2, 512], I["w_uv"].rearrange("(k p) n -> p k n", p=128))
        maskc_b = load_w_bf("maskc_b", [128, 128], I["c_maskc"])
        gqB = bcast_load("gqB", I["g_q"], NQ)
        gkvB = bcast_load("gkvB", I["g_kv"], NKV)
        KT = b.sb("KT", [128, H, NFULL * 128 + 128], BF16)
        Vc = b.sb("Vc", [128, NTL, H, 65], BF16)
        b.op("pool", lambda e: e.memset(Vc[:], 1.0), W=["Vc"])
        qT = b.sb("qT", [128, H, 128], BF16)
        qn = b.sb("qn", [128, NQ], BF16)
        qf32 = b.sb("qf32", [128, 384], F32)
        qnT = b.sb("qnT", [128, 3, 128], BF16)
        qtm = b.sb("qtm", [128, H, 128], BF16)
        b.op("pool", lambda e: e.memset(qtm[:], 0.0), W=["qtm"])
        kvrow = b.sb("kvrow", [128, KVW], F32)
        kvb = b.sb("kvb", [128, 320], BF16)
        b.op("pool", lambda e: e.memset(kvb[:], 0.0), W=["kvb"])
        ckvT = b.sb("ckvT", [128, 2, 128], BF16)
        PT = [b.sb(f"PT{j}", [128, 4, 128], BF16) for j in range(4)]
        attn_tm = b.sb("attn_tm", [128, 512], BF16)
        attnT = b.sb("attnT", [128, 4, 128], BF16)
        rec = b.sb("rec", [128, 8], F32)
        osb = b.sb("osb", [128, 260], F32)

        def tile(i):
            w = tw(i)
            wk = 128 if i < NFULL else 16
            buf = i % 2
            if i + 1 < NTL:
                load_x(i + 1, (i + 1) % 2)
            norm_and_transpose(i, buf, gmixB, "gmixB")
            pq, pkv = nextps(), nextps()
            for kc in range(8):
                b.op("pe", lambda e, kc=kc: e.matmul(PS[pq][0:w, 0:NQ], lhsT=hT[:, kc, 0:w], rhs=w_in_b[:, kc, 0:NQ],
                                                     start=(kc == 0), stop=(kc == 7)), R=["hT", "w_in_a"], W=[psk[pq]])
            for kc in range(8):
                b.op("pe", lambda e, kc=kc: e.matmul(PS[pkv][0:w, 0:KVW], lhsT=hT[:, kc, 0:w], rhs=w_in_b[:, kc, C_KV:C_KV + KVW],
                                                     start=(kc == 0), stop=(kc == 7)), R=["hT", "w_in_a"], W=[psk[pkv]])
            rms_rstd(PS[pkv][0:w, 0:NKV], [psk[pkv]], w, NKV, 1, NORM_EPS)
            b.op("dve", lambda e: e.scalar_tensor_tensor(out=kvrow[0:w, 0:NKV], in0=PS[pkv][0:w, 0:NKV], scalar=st1[0:w, 1:2],
                                                         in1=gkvB[0:w, :], op0=ALU.mult, op1=ALU.mult),
                 R=[psk[pkv], "st1_1", "gkvB"], W=["kvrow"])
            rope_tm(kvrow[0:w, NKV:KVW].rearrange("p (h r) -> p h r", h=1), PS[pkv][0:w, NKV:KVW].rearrange("p (h r) -> p h r", h=1),
                    w, i, 1, [psk[pkv]], ["kvrow"])
            if i < NFULL:
                b.dma("sp", "okv", O["kv_p"][128 * i:128 * i + 128, :], kvrow[:, :], R=["kvrow"])
            else:
                b.dma("sp", "okv", O["kv_p"][128 * i:128 * i + 16, :], kvrow[0:16, :], R=["kvrow"])
                b.dma("sp", "okv", O["kv_s"][:, :], kvrow[16:32, :], R=["kvrow"])
                b.dma("sp", "okv", kvs[:, :], kvrow[16:32, :], R=["kvrow"], W=["kvs"])
            if KQ < 1:
                return
            rms_rstd(PS[pq][0:w, 0:NQ], [psk[pq]], w, NQ, 2, NORM_EPS)
            b.op("dve", lambda e: e.scalar_tensor_tensor(out=qn[0:w, :], in0=PS[pq][0:w, 0:NQ], scalar=st1[0:w, 2:3],
                                                         in1=gqB[0:w, :], op0=ALU.mult, op1=ALU.mult),
                 R=[psk[pq], "st1_2", "gqB"], W=["qn"])
            p = nextps()
            pvb = PS[p][:].bitcast(BF16)
            for kc in range(3):
                b.op("pe", lambda e, kc=kc: e.transpose(out=pvb[:, kc * 128:kc * 128 + w], in_=qn[0:w, kc * 128:(kc + 1) * 128],
                                                        identity=ident_b[0:w, 0:w]), R=["qn", "ident_b"], W=[psk[p]])
            b.op("act", lambda e: e.copy(out=qnT[:, :, 0:w], in_=pvb[:, 0:384].rearrange("p (k t) -> p k t", k=3)[:, :, 0:w]),
                 R=[psk[p]], W=["qnT"])
            if KQ < 2:
                return
            for half in range(2):
                ph = nextps()
                for kc in range(3):
                    b.op("pe", lambda e, kc=kc, ph=ph, half=half: e.matmul(PS[ph][0:w, 0:384], lhsT=qnT[:, kc, 0:w],
                                                                           rhs=w_uq_b[:, kc, half * 384:(half + 1) * 384],
                                                                           start=(kc == 0), stop=(kc == 2)), R=["qnT", "w_uq_b"], W=[psk[ph]])
                b.op("act", lambda e, ph=ph: e.copy(out=qf32[0:w, :], in_=PS[ph][0:w, 0:384]), R=[psk[ph]], W=["qf32"])
                pv4 = qf32[0:w, :].rearrange("p (h x) -> p h x", h=4)
                b.op("pool", lambda e, pv4=pv4, half=half: e.tensor_copy(out=qtm[0:w, 4 * half:4 * half + 4, 0:64], in_=pv4[:, :, 0:64]),
                     R=["qf32"], W=["qtm"])
                rope_tm(qtm[0:w, 4 * half:4 * half + 4, 64:96], pv4[:, :, 64:96], w, i, 4, ["qf32"], ["qtm"])
            if KQ < 3:
                return
            p = nextps()
            pvb = PS[p][:].bitcast(BF16)
            for h in range(H):
                b.op("pe", lambda e, h=h, pvb=pvb: e.transpose(out=pvb[:, h * 128:h * 128 + w], in_=qtm[0:w, h, :], identity=ident_b[0:w, 0:w]),
                     R=["qtm", "ident_b"], W=[psk[p]])
            b.op("act", lambda e, pvb=pvb: e.copy(out=qT[0:96, :, 0:w], in_=pvb[0:96, 0:1024].rearrange("p (h t) -> p h t", h=H)[:, :, 0:w]),
                 R=[psk[p]], W=["qT"])
            if i == NFULL:
                b.op("dve", lambda e: e.tensor_copy(out=qsT[0:96, :, :], in_=qT[0:96, :, 16:32]), R=["qT"], W=["qsT"])
            if KA1 < 2:
                return
            b.op("dve", lambda e: e.tensor_copy(out=kvb[0:w, 0:KVW], in_=kvrow[0:w, :]), R=["kvrow"], W=["kvb"])
            p = nextps()
            pvb = PS[p][:].bitcast(BF16)
            b.op("pe", lambda e, pvb=pvb: e.transpose(out=pvb[:, 0:w], in_=kvb[0:w, 0:128], identity=ident_b[0:w, 0:w]), R=["kvb", "ident_b"], W=[psk[p]])
            b.op("pe", lambda e, pvb=pvb: e.transpose(out=pvb[:, 128:128 + w], in_=kvb[0:w, 128:256], identity=ident_b[0:w, 0:w]), R=["kvb", "ident_b"], W=[psk[p]])
            b.op("pe", lambda e, pvb=pvb: e.transpose(out=pvb[:, 256:256 + w], in_=kvb[0:w, 192:320], identity=ident_b[0:w, 0:w]), R=["kvb", "ident_b"], W=[psk[p]])
            b.op("act", lambda e, pvb=pvb: e.copy(out=ckvT[:, :, 0:w], in_=pvb[:, 0:256].rearrange("p (k t) -> p k t", k=2)[:, :, 0:w]), R=[psk[p]], W=["ckvT"])
            t0 = 128 * i
            ropesrc = apx(pvb[64:96, 256:257], [pvb.ap[0][0], 32], [[0, H], [1, wk]])
            b.op("act", lambda e, ropesrc=ropesrc: e.copy(out=KT[64:96, :, t0:t0 + wk], in_=ropesrc), R=[psk[p]], W=["KT"])
            for g in range(2):
                pk = nextps()
                for hh in range(4):
                    h = 4 * g + hh
                    for kc in range(2):
                        b.op("pe", lambda e, pk=pk, hh=hh, h=h, kc=kc: e.matmul(PS[pk][0:64, hh * 128:hh * 128 + w], lhsT=w_uk_b[:, kc, h * 64:(h + 1) * 64],
                                                                                rhs=ckvT[:, kc, 0:w], start=(kc == 0), stop=(kc == 1)),
                             R=["w_uk_b", "ckvT"], W=[psk[pk]])
                b.op("act", lambda e, pk=pk, g=g: e.copy(out=KT[0:64, 4 * g:4 * g + 4, t0:t0 + wk],
                                                         in_=PS[pk][0:64, 0:512].rearrange("p (h t) -> p h t", h=4)[:, :, 0:wk]), R=[psk[pk]], W=["KT"])
            pvv = nextps()
            for kc in range(2):
                b.op("pe", lambda e, kc=kc: e.matmul(PS[pvv][0:w, 0:512], lhsT=ckvT[:, kc, 0:w], rhs=w_uv_b[:, kc, :], start=(kc == 0), stop=(kc == 1)),
                     R=["ckvT", "w_uv_b"], W=[psk[pvv]])
            b.op("dve", lambda e: e.tensor_copy(out=Vc[0:w, i, :, 0:64], in_=PS[pvv][0:w, 0:512].rearrange("p (h v) -> p h v", h=H)), R=[psk[pvv]], W=["Vc"])
            if KA1 < 3:
                return
            wq = wk
            po = [reserve(), reserve()]
            ptn = [0]
            def stage_a(h, j0):
                pog = po[h // 4]
                ocol = (h % 4) * 65
                js = list(range(j0, min(j0 + 4, i + 1)))
                psc = nextps()
                for jj, j in enumerate(js):
                    wkj = 128 if j < NFULL else 16
                    b.op("pe", lambda e, psc=psc, jj=jj, j=j, h=h, wkj=wkj: e.matmul(PS[psc][0:wkj, jj * 128:jj * 128 + wq], lhsT=KT[0:96, h, j * 128:j * 128 + wkj],
                                                                                      rhs=qT[0:96, h, 0:wq], start=True, stop=True),
                         R=["KT", "qT"], W=[psk[psc]])
                pt = PT[ptn[0] % 4]
                ptk = f"PT{ptn[0] % 4}"
                ptn[0] += 1
                n = len(js)
                wkmin = 128 if js[-1] < NFULL else 16
                if wkmin == 128 or n == 1:
                    wka = 128 if wkmin == 128 else 16
                    b.op("act", lambda e, psc=psc, pt=pt, n=n, wka=wka: e.activation(out=pt[0:wka, 0:n, 0:wq], in_=PS[psc][0:wka, 0:n * 128].rearrange("p (j t) -> p j t", j=n)[:, :, 0:wq],
                                                                                     func=AF.Exp, scale=SM_SCALE), R=[psk[psc]], W=[ptk])
                else:
                    b.op("act", lambda e, psc=psc, pt=pt, n=n: e.activation(out=pt[0:128, 0:n - 1, 0:wq], in_=PS[psc][0:128, 0:(n - 1) * 128].rearrange("p (j t) -> p j t", j=n - 1)[:, :, 0:wq],
                                                                            func=AF.Exp, scale=SM_SCALE), R=[psk[psc]], W=[ptk])
                    b.op("act", lambda e, psc=psc, pt=pt, n=n: e.activation(out=pt[0:16, n - 1, 0:wq], in_=PS[psc][0:16, (n - 1) * 128:(n - 1) * 128 + wq],
                                                                            func=AF.Exp, scale=SM_SCALE), R=[psk[psc]], W=[ptk])
                if js[-1] == i:
                    jj = len(js) - 1
                    b.op("dve", lambda e, pt=pt, jj=jj: e.tensor_tensor(out=pt[0:wq, jj, 0:wq], in0=pt[0:wq, jj, 0:wq], in1=maskc_b[0:wq, 0:wq], op=ALU.mult),
                         R=[ptk, "maskc_b"], W=[ptk])

                def stage_b():
                    for jj, j in enumerate(js):
                        wkj = 128 if j < NFULL else 16
                        b.op("pe", lambda e, jj=jj, j=j, wkj=wkj: e.matmul(PS[pog][0:wq, ocol:ocol + 65], lhsT=pt[0:wkj, jj, 0:wq],
                                                                           rhs=Vc[0:wkj, j, h, :], start=(j == 0), stop=(j == i)),
                             R=[ptk, "Vc"], W=[psk[pog]])
                return stage_b

            pend = []
            for h in range(H):
                for j0 in range(0, i + 1, 4):
                    pend.append(stage_a(h, j0))
                    if len(pend) > 2:
                        pend.pop(0)()
            for fn in pend:
                fn()
            for g in range(2):
                b.op("act", lambda e, g=g: e.copy(out=osb[0:wq, :], in_=PS[po[g]][0:wq, 0:260]), R=[psk[po[g]]], W=["osb"])
                ov = osb[0:wq, :].rearrange("p (h x) -> p h x", h=4)
                b.op("dve", lambda e, ov=ov, g=g: e.reciprocal(out=rec[0:wq, 4 * g:4 * g + 4], in_=ov[:, :, 64]), R=["osb"], W=["rec"])
                rb = apx(rec[0:wq, 4 * g:4 * g + 1], [rec[:].ap[0][0], wq], [[1, 4], [0, 64]])
                b.op("dve", lambda e, ov=ov, g=g, rb=rb: e.tensor_tensor(out=attn_tm[0:wq, 256 * g:256 * g + 256].rearrange("p (h v) -> p h v", h=4),
                                                                         in0=ov[:, :, 0:64], in1=rb, op=ALU.mult), R=["osb", "rec"], W=["attn_tm"])
            release(po[0])
            release(po[1])
            p = nextps()
            pvb = PS[p][:].bitcast(BF16)
            for c4 in range(4):
                b.op("pe", lambda e, c4=c4, pvb=pvb: e.transpose(out=pvb[:, c4 * 128:c4 * 128 + wq], in_=attn_tm[0:wq, c4 * 128:(c4 + 1) * 128], identity=ident_b[0:wq, 0:wq]),
                     R=["attn_tm", "ident_b"], W=[psk[p]])
            b.op("act", lambda e, pvb=pvb: e.copy(out=attnT[:, :, 0:wq], in_=pvb[:, 0:512].rearrange("p (c t) -> p c t", c=4)[:, :, 0:wq]), R=[psk[p]], W=["attnT"])
            for c4 in range(4):
                b.dma("sp", "oat", ats[c4 * 128:(c4 + 1) * 128, t0:t0 + wq], attnT[:, c4, 0:wq], R=["attnT"], W=["ats"])

        if KA1 >= 1:
            load_x(0, 0)
            for i in range(NTL):
                tile(i)

    with ExitStack() as pst:
        b.st = pst
        phase_A1()
        b.barrier()
        b.emit()
    b.st = b.st0


    def phase_S():
        w_uk_b = load_w_bf("w_uk_s", [128, 2, 512], I["w_uk"].rearrange("(k p) n -> p k n", p=128))
        w_uv_b = load_w_bf("w_uv_s", [128, 2, 512], I["w_uv"].rearrange("(k p) n -> p k n", p=128))
        w_ukT = b.sb("w_ukT", [64, H, 256], BF16)
        qfT = b.sb("qfT", [128, 3, SPC, H], BF16)
        b.op("pool", lambda e: e.memset(qfT[:], 0.0), W=["qfT"])
        ones_f = b.sb("ones_f", [1, 128], F32)
        b.op("pool", lambda e: e.memset(ones_f[:], 1.0), W=["ones_f"])
        idx = b.sb("idx", [128, 8], I32)
        idx8 = b.sb("idx8", [128, 8], I32)
        b.op("pool", lambda e: e.memset(idx[:], 0), W=["idx"])
        b.dma("sp", "c0", idx[0:PP2, :], I["pt"].rearrange("(pi s2) g -> (s2 g) pi", s2=2), W=["idx"], allow_slow_non_contiguous=True)
        b.op("dve", lambda e: e.tensor_scalar(out=idx8[:], in0=idx[:], scalar1=8, scalar2=None, op0=ALU.mult), R=["idx"], W=["idx8"])
        kvg = [b.sb(f"kvg{j}", [128, 16, KVW], F32) for j in range(2)]
        kvh = [b.sb(f"kvh{j}", [128, 16, 320], BF16) for j in range(2)]
        for j in range(2):
            b.op("pool", lambda e, j=j: e.memset(kvh[j][:], 0.0), W=[f"kvh{j}"])
            b.op("pool", lambda e, j=j: e.memset(kvh[j][:, :, 288:290], 1.0), W=[f"kvh{j}"])
        kvn = b.sb("kvn", [128, 8, KVW], F32)
        kvnb = b.sb("kvnb", [128, 8, 320], BF16)
        b.op("pool", lambda e: e.memset(kvn[:], 0.0), W=["kvn"])
        b.op("pool", lambda e: e.memset(kvnb[:], 0.0), W=["kvnb"])
        b.op("pool", lambda e: e.memset(kvnb[:, :, 288:290], 1.0), W=["kvnb"])
        kv2 = kvs.rearrange("(pi s2) c -> s2 pi c", s2=2)
        b.dma("sp", "c0", kvn[0:1, :, :], kv2[0:1], R=["kvs"], W=["kvn"])
        b.dma("sp", "c0", kvn[NPG:NPG + 1, :, :], kv2[1:2], R=["kvs"], W=["kvn"])
        b.op("dve", lambda e: e.tensor_copy(out=kvnb[:, :, 0:KVW], in_=kvn[:, :, :]), R=["kvn"], W=["kvnb"])
        KT3 = [b.sb(f"KT3_{j}", [128, 6, 128], BF16) for j in range(4)]
        PTs = [b.sb(f"PTs{j}", [128, 16, 16], BF16) for j in range(2)]
        PTn = b.sb("PTn", [128, 16], BF16)
        for j in range(2):
            b.op("pool", lambda e, j=j: e.memset(PTs[j][:], 0.0), W=[f"PTs{j}"])
        b.op("pool", lambda e: e.memset(PTn[:], 0.0), W=["PTn"])
        mloc = b.sb("mloc", [128, 1], F32)
        scs = b.sb("scs", [128, 256], F32)
        b.op("pool", lambda e: e.memset(mloc[:], 0.0), W=["mloc"])
        m2 = b.sb("m2", [1, 2], F32)
        negm = b.sb("negm", [128, 2], F32)
        olat = b.sb("olat", [16, 256], BF16)
        rl = b.sb("rl", [16, 1], F32)
        olatT = b.sb("olatT", [128, 2, 8, 16], BF16)
        attn_sT = b.sb("attn_sT", [128, 4, 16], BF16)

        for h in range(H):
            p = nextps()
            pvb = PS[p][:].bitcast(BF16)
            for kc in range(2):
                b.op("pe", lambda e, h=h, kc=kc, pvb=pvb: e.transpose(out=pvb[0:64, kc * 128:(kc + 1) * 128], in_=w_uk_b[:, kc, h * 64:(h + 1) * 64], identity=ident_b[:, :]),
                     R=["w_uk_s", "ident_b"], W=[psk[p]])
            b.op("act", lambda e, h=h, pvb=pvb: e.copy(out=w_ukT[0:64, h, :], in_=pvb[0:64, 0:256]), R=[psk[p]], W=["w_ukT"])
        for h in range(H):
            p = nextps()
            for kc in range(2):
                b.op("pe", lambda e, h=h, kc=kc, p=p: e.matmul(PS[p][:, kc * 16:(kc + 1) * 16], lhsT=w_ukT[0:64, h, kc * 128:(kc + 1) * 128], rhs=qsT[0:64, h, :],
                                                              start=True, stop=True), R=["w_ukT", "qsT"], W=[psk[p]])
            b.op("act", lambda e, h=h, p=p: e.copy(out=qfT[:, 0:2, :, h], in_=PS[p][:, 0:32].rearrange("p (k s) -> p k s", k=2)), R=[psk[p]], W=["qfT"])
        b.op("dve", lambda e: e.tensor_copy(out=qfT[64:96, 2, :, :], in_=qsT[64:96, :, :].rearrange("p h s -> p s h")), R=["qsT"], W=["qfT"])

        cache8 = I["cache"].rearrange("n (g r) c -> (n g) (r c)", r=16)

        def kt_scores(src, r0, nr, bufk, psc, col0, pi):
            for g0 in range(0, nr, 2):
                rs = list(range(g0, min(g0 + 2, nr)))
                kb = bufk[0] % 4
                bufk[0] += 1
                kt, ktk = KT3[kb], f"KT3_{kb}"
                p = nextps()
                pvb = PS[p][:].bitcast(BF16)
                for rr, r in enumerate(rs):
                    for kc, (c0, c1, m) in enumerate(((0, 128, 128), (128, 256, 128), (192, 320, 128))):
                        b.op("pe", lambda e, pvb=pvb, rr=rr, r=r, kc=kc, c0=c0, c1=c1, m=m: e.transpose(out=pvb[0:m, (rr * 3 + kc) * 128:(rr * 3 + kc) * 128 + PP2],
                                                                                                   in_=src[0:PP2, r0 + r, c0:c1], identity=ident_b[0:PP2, 0:PP2]),
                             R=[src_key[0], "ident_b"], W=[psk[p]])
                n3 = 3 * len(rs)
                eng = "act" if (bufk[0] % 2) else "dve"
                if eng == "act":
                    b.op("act", lambda e, kt=kt, pvb=pvb, n3=n3: e.copy(out=kt[:, 0:n3, 0:PP2], in_=pvb[:, 0:n3 * 128].rearrange("p (a t) -> p a t", a=n3)[:, :, 0:PP2]),
                         R=[psk[p]], W=[ktk])
                else:
                    b.op("dve", lambda e, kt=kt, pvb=pvb, n3=n3: e.tensor_copy(out=kt[:, 0:n3, 0:PP2], in_=pvb[:, 0:n3 * 128].rearrange("p (a t) -> p a t", a=n3)[:, :, 0:PP2]),
                         R=[psk[p]], W=[ktk])
                for rr, r in enumerate(rs):
                    for kc, m in enumerate((128, 128, 96)):
                        b.op("pe", lambda e, kt=kt, rr=rr, r=r, kc=kc, m=m: e.matmul(PS[psc][0:PP2, col0 + 16 * r:col0 + 16 * r + 16], lhsT=kt[0:m, rr * 3 + kc, 0:PP2],
                                                                                   rhs=qfT[0:m, kc, 2 * pi:2 * pi + 2, :].rearrange("p s h -> p (s h)"),
                                                                                   start=(kc == 0), stop=(kc == 2)), R=[ktk, "qfT"], W=[psk[psc]])

        src_key = ["kvh0"]
        bufk = [0]
        cn = [0]
        for pi in range(SPC // 2):
            pacc = reserve()
            for ck in range(8):
                cb = cn[0] % 2
                cn[0] += 1
                gk, hk = f"kvg{cb}", f"kvh{cb}"
                b.raw_dma("pool", gk, lambda e, cb=cb, ck=ck, pi=pi: e.indirect_dma_start(
                    out=kvg[cb][0:PP2, :, :].rearrange("p a b -> p (a b)"), out_offset=None, in_=cache8[:, :], element_offset=ck * 16 * KVW,
                    in_offset=bass.IndirectOffsetOnAxis(ap=idx8[0:PP2, pi:pi + 1], axis=0)), R=["idx8"], W=[gk])
                ceng = "dve" if ck % 2 == 0 else "pool"
                b.op(ceng, lambda e, cb=cb: e.tensor_copy(out=kvh[cb][0:PP2, :, 0:KVW], in_=kvg[cb][0:PP2, :, :]), R=[gk], W=[hk])
                src_key[0] = hk
                psc = reserve()
                kt_scores(kvh[cb], 0, 16, bufk, psc, 0, pi)
                release(psc)
                sc3 = PS[psc][0:PP2, 0:256].rearrange("p (r x) -> p r x", r=16)
                if ck == 0:
                    b.op("act", lambda e, psc=psc: e.copy(out=scs[0:PP2, :], in_=PS[psc][0:PP2, 0:256]), R=[psk[psc]], W=["scs"])
                    ss3 = scs[0:PP2, :].rearrange("p (r x) -> p r x", r=16)
                    b.op("dve", lambda e, ss3=ss3: e.tensor_reduce(out=mloc[0:NPG, :], in_=ss3[0:NPG, :, 0:8], axis=AX.XY, op=ALU.max), R=["scs"], W=["mloc"])
                    b.op("dve", lambda e, ss3=ss3: e.tensor_reduce(out=mloc[NPG:PP2, :], in_=ss3[NPG:PP2, :, 8:16], axis=AX.XY, op=ALU.max), R=["scs"], W=["mloc"])
                    pm_ = nextps()
                    b.op("pe", lambda e, pm_=pm_: e.transpose(out=PS[pm_][0:1, 0:PP2], in_=mloc[0:PP2, 0:1], identity=ident_f[0:PP2, 0:PP2]), R=["mloc", "ident_f"], W=[psk[pm_]])
                    b.op("dve", lambda e, pm_=pm_: e.tensor_reduce(out=m2[0:1, 0:2], in_=PS[pm_][0:1, 0:PP2].rearrange("p (a g) -> p a g", a=2), axis=AX.X, op=ALU.max),
                         R=[psk[pm_]], W=["m2"])
                    b.op("dve", lambda e: e.tensor_scalar(out=m2[0:1, 0:2], in0=m2[0:1, 0:2], scalar1=-SM_SCALE, scalar2=None, op0=ALU.mult), R=["m2"], W=["m2"])
                    pb_ = nextps()
                    b.op("pe", lambda e, pb_=pb_: e.matmul(PS[pb_][0:PP2, 0:2], lhsT=ones_f[0:1, 0:PP2], rhs=m2[0:1, 0:2], start=True, stop=True), R=["ones_f", "m2"], W=[psk[pb_]])
                    b.op("act", lambda e, pb_=pb_: e.copy(out=negm[0:PP2, :], in_=PS[pb_][0:PP2, 0:2]), R=[psk[pb_]], W=["negm"])
                pt, ptk = PTs[cb], f"PTs{cb}"
                b.op("act", lambda e, pt=pt, sc3=sc3: e.activation(out=pt[0:NPG, :, 0:8], in_=sc3[0:NPG, :, 0:8], func=AF.Exp, scale=SM_SCALE, bias=negm[0:NPG, 0:1]),
                     R=[psk[psc], "negm"], W=[ptk])
                b.op("act", lambda e, pt=pt, sc3=sc3: e.activation(out=pt[NPG:PP2, :, 8:16], in_=sc3[NPG:PP2, :, 8:16], func=AF.Exp, scale=SM_SCALE, bias=negm[NPG:PP2, 1:2]),
                     R=[psk[psc], "negm"], W=[ptk])
                for r in range(16):
                    b.op("pe", lambda e, pt=pt, cb=cb, r=r, ck=ck, pacc=pacc: e.matmul(PS[pacc][0:16, 0:290], lhsT=pt[0:PP2, r, :], rhs=kvh[cb][0:PP2, r, 0:290],
                                                                                         start=(ck == 0 and r == 0), stop=False), R=[ptk, hk], W=[psk[pacc]])
            src_key[0] = "kvnb"
            psc = reserve()
            kt_scores(kvnb, pi, 1, bufk, psc, 0, pi)
            release(psc)
            b.op("act", lambda e, psc=psc: e.activation(out=PTn[0:1, 0:8], in_=PS[psc][0:1, 0:8], func=AF.Exp, scale=SM_SCALE, bias=negm[0:1, 0:1]),
                 R=[psk[psc], "negm"], W=["PTn"])
            b.op("act", lambda e, psc=psc: e.activation(out=PTn[NPG:NPG + 1, 8:16], in_=PS[psc][NPG:NPG + 1, 8:16], func=AF.Exp, scale=SM_SCALE, bias=negm[NPG:NPG + 1, 1:2]),
                 R=[psk[psc], "negm"], W=["PTn"])
            b.op("pe", lambda e, pi=pi, pacc=pacc: e.matmul(PS[pacc][0:16, 0:290], lhsT=PTn[0:PP2, :], rhs=kvnb[0:PP2, pi, 0:290], start=False, stop=True),
                 R=["PTn", "kvnb"], W=[psk[pacc]])
            b.op("dve", lambda e, pacc=pacc: e.reciprocal(out=rl[:, :], in_=PS[pacc][0:16, 288:289]), R=[psk[pacc]], W=["rl"])
            b.op("dve", lambda e, pacc=pacc: e.tensor_scalar(out=olat[:, :], in0=PS[pacc][0:16, 0:256], scalar1=rl[:, 0:1], scalar2=None, op0=ALU.mult),
                 R=[psk[pacc], "rl"], W=["olat"])
            p = nextps()
            pvb = PS[p][:].bitcast(BF16)
            for kc in range(2):
                b.op("pe", lambda e, kc=kc, pvb=pvb: e.transpose(out=pvb[:, kc * 16:(kc + 1) * 16], in_=olat[0:16, kc * 128:(kc + 1) * 128], identity=ident_b[0:16, 0:16]),
                     R=["olat", "ident_b"], W=[psk[p]])
            release(pacc)
            b.op("act", lambda e, pi=pi, pvb=pvb: e.copy(out=olatT[:, :, pi, :], in_=pvb[:, 0:32].rearrange("p (k x) -> p k x", k=2)), R=[psk[p]], W=["olatT"])
        p = nextps()
        ol5 = olatT[:].rearrange("p k a (s h) -> p k a s h", s=2)
        for h in range(H):
            pr = slice((h % 2) * 64, (h % 2) * 64 + 64)
            for kc in range(2):
                b.op("pe", lambda e, h=h, kc=kc, pr=pr, p=p: e.matmul(PS[p][pr, h * 16:(h + 1) * 16], lhsT=w_uv_b[:, kc, h * 64:(h + 1) * 64], rhs=ol5[:, kc, :, :, h],
                                                                     start=(kc == 0), stop=(kc == 1)), R=["w_uv_s", "olatT"], W=[psk[p]])
        for h in range(H):
            pr = slice((h % 2) * 64, (h % 2) * 64 + 64)
            b.op("act", lambda e, h=h, pr=pr, p=p: e.copy(out=attn_sT[pr, h // 2, :], in_=PS[p][pr, h * 16:(h + 1) * 16]), R=[psk[p]], W=["attn_sT"])
        for c4 in range(4):
            b.dma("sp", "oat", ats[c4 * 128:(c4 + 1) * 128, NFULL * 128 + 16:NFULL * 128 + 32], attn_sT[:, c4, :], R=["attn_sT"], W=["ats"])

    if stage >= 3:
        with ExitStack() as pst:
            b.st = pst
            phase_S()
            b.barrier()
            b.emit()
        b.st = b.st0


    pst = ExitStack()
    b.st = pst
    w_in_r = b.sb("w_in_r", [128, 8, RC], BF16)
    for kc in range(8):
        b.dma("pool", "wA", w_in_r[:, kc, :], I["w_in"][kc * 128:(kc + 1) * 128, C_RW:C_G], W=["w_in_r"])
    CH = BF16

    def flat(name, dt=F32, n=512):
        return b.sb(name, [128, n], dt)

    def v3(t, w, g=4):
        return t[:, 0:g * w].rearrange("p (g t) -> p g t", g=g)

    ppar = b.sb("ppar", [128, 64], F32)

    def ld_pp(src, c0, n):
        b.dma("sp", "c0", ppar[:, c0:c0 + n], src.rearrange("o (c p) -> p (o c)", p=128), W=["ppar"],
              allow_slow_non_contiguous=True)
    PP_MU, PP_W0, PP_A0, PP_KK, PP_KA, PP_RK, PP_LNW, PP_LNB, PP_OMKA = 0, 14, 18, 22, 26, 30, 34, 38, 42
    ld_pp(I["mu_shift"], PP_MU, 14)
    ld_pp(I["w0"], PP_W0, 4)
    ld_pp(I["a0"], PP_A0, 4)
    ld_pp(I["k_k"], PP_KK, 4)
    ld_pp(I["k_a"], PP_KA, 4)
    ld_pp(I["r_k"], PP_RK, 4)
    ld_pp(I["ln_w"], PP_LNW, 4)
    ld_pp(I["ln_b"], PP_LNB, 4)
    b.op("dve", lambda e: e.tensor_scalar(out=ppar[:, PP_OMKA:PP_OMKA + 4], in0=ppar[:, PP_KA:PP_KA + 4], scalar1=-1.0,
                                          scalar2=1.0, op0=ALU.mult, op1=ALU.add), R=["ppar"], W=["ppar"])
    w2_b = b.sb("w2_b", [128, RW], BF16)
    a2_b = b.sb("a2_b", [128, RW], BF16)
    g2_b = b.sb("g2_b", [128, RW], BF16)
    b.dma("pool", "wA", w2_b[0:64, :], I["w2"], W=["w2_b"])
    b.dma("pool", "wA", a2_b[64:128, :], I["a2"], W=["a2_b"])
    b.dma("pool", "wA", g2_b[:, :], I["g2"], W=["g2_b"])
    maskx_b = b.sb("maskx_b", [128, 128], BF16)
    maskjt_b = b.sb("maskjt_b", [128, 2, 128], BF16)
    cmask_b = b.sb("cmask_b", [128, 8], BF16)
    identb_b = b.sb("identb_b", [128, 64], BF16)
    onehot_b = b.sb("onehot_b", [128, 16, 128], BF16)
    b.dma("pool", "wA", maskx_b[:], I["c_maskx"], W=["maskx_b"])
    b.dma("pool", "wA", maskjt_b[:].rearrange("p a b -> p (a b)"), I["c_maskjt"], W=["maskjt_b"])
    b.dma("pool", "wA", cmask_b[:], I["c_cmask"], W=["cmask_b"])
    b.dma("pool", "wA", identb_b[:], I["c_identb"], W=["identb_b"])
    b.dma("pool", "wA", onehot_b[:].rearrange("p a b -> p (a b)"), I["c_onehot"], W=["onehot_b"])
    blk1_f = b.sb("blk1_f", [128, 128], F32)
    blk64_f = b.sb("blk64_f", [128, 128], F32)
    scanm = b.sb("scanm", [128, 512], F32)
    b.dma("sp", "c0", blk1_f[:], I["c_blk1"], W=["blk1_f"])
    b.dma("sp", "c0", scanm[:], I["c_scanm"], W=["scanm"])
    b.op("dve", lambda e: e.tensor_scalar(out=blk64_f[:], in0=blk1_f[:], scalar1=1.0 / 64, scalar2=None, op0=ALU.mult),
         R=["blk1_f"], W=["blk64_f"])

    cT = b.sb("cT", [128, 14, 129], F32)
    zt = b.sb("zt", [128, 14, 128], F32)
    b.op("pool", lambda e: e.memset(cT[:], 0.0), W=["cT"])
    sshT = b.sb("sshT", [128, 14, 16], F32)
    sshtm = b.sb("sshtm", [16, RC], F32)
    f = {n: flat("f_" + n) for n in ["ld", "asig", "kk", "t", "bb", "kp", "L", "Lx", "eL", "enL", "eLC"]}
    gsb = flat("gsb", BF16)
    tanhw = flat("tanhw", BF16, 128)
    alb = flat("alb", BF16, 128)
    sgb = flat("sgb", BF16, 128)
    vb = flat("vb", BF16)
    AR = b.sb("AR", [128, 1024], BF16)
    Kt, Bt, Kb, Bb = (flat(n, BF16) for n in ["Kt", "Bt", "Kb", "Bb"])
    Vtm = b.sb("Vtm", [128, 4, 128], BF16)
    NU = 4
    UB = []
    for uu in range(NU):
        UB.append((b.sb(f"YA{uu}", [128, 2, 128], BF16), b.sb(f"NK{uu}", [128, 2, 128], BF16), b.sb(f"Xm{uu}", [128, 128], BF16),
                   [b.sb(f"XY{j}_{uu}", [128, 2, 128], BF16) for j in range(2)], b.sb(f"Y8_{uu}", [128, 128], BF16),
                   [b.sb(f"Z{j}_{uu}", [128, 128], BF16) for j in range(2)], b.sb(f"KBtm{uu}", [128, 2, 64], BF16),
                   b.sb(f"Bexp{uu}", [128, 8, 64], BF16), b.sb(f"Vexp{uu}", [128, 8, 64], BF16), b.sb(f"Vhexp{uu}", [128, 8, 64], BF16)))
    Wd = b.sb("Wd", [128, 4, 8, 64], BF16)
    RhT = flat("RhT", CH)
    MT = [b.sb(f"MT{hp}", [128, 8, 128], CH) for hp in range(4)]
    GB = [b.sb(f"GB{hp}", [128, 8, 128], CH) for hp in range(4)]
    Sbd = [[b.sb(f"S{hp}_{j}", [128, 128], CH) for j in range(2)] for hp in range(4)]
    scur = [0, 0, 0, 0]
    for hp in range(4):
        b.op("pool", lambda e, hp=hp: e.memset(MT[hp][:], 0.0), W=[f"MT{hp}"])
        b.op("pool", lambda e, hp=hp: e.memset(GB[hp][:], 0.0), W=[f"GB{hp}"])
        b.op("pool", lambda e, hp=hp: e.memset(Sbd[hp][0][:], 0.0), W=[f"S{hp}_0"])
        b.op("pool", lambda e, hp=hp: e.memset(Sbd[hp][1][:], 0.0), W=[f"S{hp}_1"])
    ygT = flat("ygT", BF16)
    tm5 = b.sb("tm5", [48, 5, 256], BF16)
    xb = b.sb("xb", [128, 5, 4, 16], BF16)
    b.op("pool", lambda e: e.memset(tm5[:], 0.0), W=["tm5"])
    Sx = [b.sb(f"Sx{j}", [128, 4, 64], F32) for j in range(2)]
    stmp = b.sb("stmp", [128, 4, 64], F32)
    ssa = b.sb("ssa", [128, 4], F32)
    wkvo = b.sb("wkvo", [128, 4, 128], F32)

    b.dma("sp", "c0", sshtm[:], I["sshift"], W=["sshtm"])

    def prep_ssh():
        for g in range(4):
            js = list(range(4 * g, min(4 * g + 4, 14)))
            p = nextps()
            for jj, j in enumerate(js):
                b.op("pe", lambda e, jj=jj, j=j, p=p: e.transpose(out=PS[p][:, jj * 16:(jj + 1) * 16], in_=sshtm[0:16, j * 128:(j + 1) * 128],
                                                              identity=ident_f[0:16, 0:16]), R=["sshtm", "ident_f"], W=[psk[p]])
            n = len(js)
            b.op("act", lambda e, p=p, n=n, j0=js[0]: e.copy(out=sshT[:, j0:j0 + n, :], in_=PS[p][:, 0:n * 16].rearrange("p (j t) -> p j t", j=n)),
                 R=[psk[p]], W=["sshT"])
    prep_ssh()

    def pp(c0, hp):
        return ppar[:, c0 + hp:c0 + hp + 1]

    def rwkv_tile(i):
        w = tw(i)
        wu = 128 if i < NFULL else 16
        nch = wu // 16
        last = (i == NFULL)
        if i > 0:
            b.op("dve", lambda e: e.tensor_copy(out=cT[:, :, 0:1], in_=cT[:, :, 128:129]), R=["cT"], W=["cT"])
        for g in range(4):
            js = list(range(4 * g, min(4 * g + 4, 14)))
            p = nextps()
            for jj, j in enumerate(js):
                for kc in range(8):
                    b.op("pe", lambda e, jj=jj, j=j, kc=kc, p=p: e.matmul(PS[p][:, jj * 128:jj * 128 + w], lhsT=w_in_r[:, kc, 128 * j:128 * (j + 1)],
                                                                           rhs=hT[:, kc, 0:w], start=(kc == 0), stop=(kc == 7)),
                         R=["hT", "w_in_r"], W=[psk[p]])
            n = len(js)
            b.op("act", lambda e, p=p, n=n, j0=js[0]: e.copy(out=cT[:, j0:j0 + n, 1:1 + w],
                                                            in_=PS[p][:, 0:n * 128].rearrange("p (j t) -> p j t", j=n)[:, :, 0:w]),
                 R=[psk[p]], W=["cT"])
        wp = wu if last else w
        b.op("dve", lambda e: e.tensor_tensor(out=zt[:, :, 0:wp], in0=cT[:, :, 0:wp], in1=cT[:, :, 1:1 + wp], op=ALU.subtract),
             R=["cT"], W=["zt"])
        if last:
            b.op("dve", lambda e: e.tensor_tensor(out=zt[:, :, 16:32], in0=sshT[:, :, :], in1=cT[:, :, 17:33], op=ALU.subtract),
                 R=["cT", "sshT"], W=["zt"])
        for j in range(14):
            b.op("dve", lambda e, j=j: e.scalar_tensor_tensor(out=zt[:, j, 0:w], in0=zt[:, j, 0:w], scalar=ppar[:, PP_MU + j:PP_MU + j + 1],
                                                              in1=cT[:, j, 1:1 + w], op0=ALU.mult, op1=ALU.add),
                 R=["zt", "cT", "ppar"], W=["zt"])
        r3, k3, v3_ = zt[:, 0:4, 0:w], zt[:, 4:8, 0:w], zt[:, 8:12, 0:w]
        W4 = 4 * w
        ld3, as3, kk3, t3, b3, kp3 = (v3(f[n], w) for n in ["ld", "asig", "kk", "t", "bb", "kp"])
        b.op("act", lambda e: e.activation(out=tanhw[0:64, 0:w], in_=zt[0:64, 12, 0:w], func=AF.Tanh), R=["zt"], W=["tanhw"])
        b.op("pool", lambda e: e.tensor_copy(out=alb[64:128, 0:w], in_=zt[64:128, 12, 0:w]), R=["zt"], W=["alb"])
        b.op("act", lambda e: e.activation(out=sgb[:, 0:w], in_=zt[:, 13, 0:w], func=AF.Sigmoid), R=["zt"], W=["sgb"])
        b.op("pool", lambda e: e.tensor_copy(out=v3(vb, w), in_=v3_), R=["zt"], W=["vb"])
        pW, pA, pG = nextps(), nextps(), nextps()
        for hp in range(4):
            b.op("pe", lambda e, hp=hp: e.matmul(PS[pW][:, hp * w:(hp + 1) * w], lhsT=w2_b[0:64, hp * 128:(hp + 1) * 128], rhs=tanhw[0:64, 0:w],
                                                 start=True, stop=True), R=["w2_b", "tanhw"], W=[psk[pW]])
        for hp in range(4):
            b.op("pe", lambda e, hp=hp: e.matmul(PS[pA][:, hp * w:(hp + 1) * w], lhsT=a2_b[64:128, hp * 128:(hp + 1) * 128], rhs=alb[64:128, 0:w],
                                                 start=True, stop=True), R=["a2_b", "alb"], W=[psk[pA]])
        for hp in range(4):
            b.op("pe", lambda e, hp=hp: e.matmul(PS[pG][:, hp * w:(hp + 1) * w], lhsT=g2_b[:, hp * 128:(hp + 1) * 128], rhs=sgb[:, 0:w],
                                                 start=True, stop=True), R=["g2_b", "sgb"], W=[psk[pG]])
        for hp in range(4):
            b.op("act", lambda e, hp=hp: e.activation(out=ld3[:, hp, :], in_=PS[pW][:, hp * w:(hp + 1) * w], func=AF.Sigmoid, bias=pp(PP_W0, hp)),
                 R=[psk[pW], "ppar"], W=["f_ld"])
            b.op("act", lambda e, hp=hp: e.activation(out=as3[:, hp, :], in_=PS[pA][:, hp * w:(hp + 1) * w], func=AF.Sigmoid, bias=pp(PP_A0, hp)),
                 R=[psk[pA], "ppar"], W=["f_asig"])
        b.op("act", lambda e: e.copy(out=gsb[:, 0:W4], in_=PS[pG][:, 0:W4]), R=[psk[pG]], W=["gsb"])
        b.op("dve", lambda e: e.tensor_scalar(out=f["ld"][:, 0:W4], in0=f["ld"][:, 0:W4], scalar1=-0.6065306597126334, scalar2=None, op0=ALU.mult),
             R=["f_ld"], W=["f_ld"])
        for hp in range(4):
            b.op("act", lambda e, hp=hp: e.activation(out=kk3[:, hp, :], in_=zt[:, 4 + hp, 0:w], func=AF.Copy, scale=pp(PP_KK, hp)),
                 R=["zt", "ppar"], W=["f_kk"])
        b.op("act", lambda e: e.activation(out=f["t"][:, 0:W4], in_=f["kk"][:, 0:W4], func=AF.Square), R=["f_kk"], W=["f_t"])
        pN = nextps()
        b.op("pe", lambda e: e.matmul(PS[pN][:, 0:W4], lhsT=blk1_f[:], rhs=f["t"][:, 0:W4], start=True, stop=True),
             R=["blk1_f", "f_t"], W=[psk[pN]])
        b.op("act", lambda e: e.activation(out=f["t"][:, 0:W4], in_=PS[pN][:, 0:W4], func=AF.Sqrt), R=[psk[pN]], W=["f_t"])
        b.op("dve", lambda e: e.tensor_scalar(out=f["t"][:, 0:W4], in0=f["t"][:, 0:W4], scalar1=1e-12, scalar2=None, op0=ALU.max),
             R=["f_t"], W=["f_t"])
        b.op("dve", lambda e: e.reciprocal(out=f["t"][:, 0:W4], in_=f["t"][:, 0:W4]), R=["f_t"], W=["f_t"])
        b.op("dve", lambda e: e.tensor_tensor(out=f["kk"][:, 0:W4], in0=f["kk"][:, 0:W4], in1=f["t"][:, 0:W4], op=ALU.mult),
             R=["f_kk", "f_t"], W=["f_kk"])
        b.op("dve", lambda e: e.tensor_tensor(out=f["bb"][:, 0:W4], in0=f["kk"][:, 0:W4], in1=f["asig"][:, 0:W4], op=ALU.mult),
             R=["f_kk", "f_asig"], W=["f_bb"])
        for hp in range(4):
            b.op("dve", lambda e, hp=hp: e.tensor_scalar(out=t3[:, hp, :], in0=as3[:, hp, :], scalar1=pp(PP_KA, hp), scalar2=pp(PP_OMKA, hp),
                                                         op0=ALU.mult, op1=ALU.add), R=["f_asig", "ppar"], W=["f_t"])
        b.op("dve", lambda e: e.tensor_tensor(out=kp3, in0=k3, in1=t3, op=ALU.mult), R=["zt", "f_t"], W=["f_kp"])
        b.op("dve", lambda e: e.tensor_tensor_scan(out=f["L"][:, 0:W4], data0=scanm[:, 0:W4], data1=f["ld"][:, 0:W4], initial=0.0,
                                                   op0=ALU.mult, op1=ALU.add), R=["scanm", "f_ld"], W=["f_L"])
        b.op("dve", lambda e: e.tensor_tensor(out=f["Lx"][:, 0:W4], in0=f["L"][:, 0:W4], in1=f["ld"][:, 0:W4], op=ALU.subtract),
             R=["f_L", "f_ld"], W=["f_Lx"])
        ng = W4 // 16
        Lg = f["L"][:, 0:W4].rearrange("p (g t) -> p g t", t=16)
        Lend = apx(f["L"][:, 15:16], [f["L"][:].ap[0][0], 128], [[16, ng], [0, 16]])
        b.op("dve", lambda e: e.tensor_tensor(out=f["eLC"][:, 0:W4].rearrange("p (g t) -> p g t", t=16), in0=Lend, in1=Lg, op=ALU.subtract),
             R=["f_L"], W=["f_eLC"])
        b.op("act", lambda e: e.activation(out=f["eL"][:, 0:W4], in_=f["L"][:, 0:W4], func=AF.Exp), R=["f_L"], W=["f_eL"])
        b.op("act", lambda e: e.activation(out=f["enL"][:, 0:W4], in_=f["L"][:, 0:W4], func=AF.Exp, scale=-1.0), R=["f_L"], W=["f_enL"])
        b.op("act", lambda e: e.activation(out=f["Lx"][:, 0:W4], in_=f["Lx"][:, 0:W4], func=AF.Exp), R=["f_Lx"], W=["f_Lx"])
        b.op("act", lambda e: e.activation(out=f["eLC"][:, 0:W4], in_=f["eLC"][:, 0:W4], func=AF.Exp), R=["f_eLC"], W=["f_eLC"])
        AR4 = AR[:, 0:2 * W4].rearrange("p (h two t) -> p h two t", h=4, two=2)
        b.op("dve", lambda e: e.scalar_tensor_tensor(out=AR4[:, :, 0, :], in0=kk3, scalar=-1.0, in1=v3(f["Lx"], w), op0=ALU.mult, op1=ALU.mult),
             R=["f_kk", "f_Lx"], W=["AR"])
        b.op("dve", lambda e: e.tensor_tensor(out=AR4[:, :, 1, :], in0=r3, in1=v3(f["eL"], w), op=ALU.mult), R=["zt", "f_eL"], W=["AR"])
        b.op("dve", lambda e: e.tensor_tensor(out=Kt[:, 0:W4], in0=f["kp"][:, 0:W4], in1=f["enL"][:, 0:W4], op=ALU.mult), R=["f_kp", "f_enL"], W=["Kt"])
        b.op("dve", lambda e: e.tensor_tensor(out=Bt[:, 0:W4], in0=f["bb"][:, 0:W4], in1=f["enL"][:, 0:W4], op=ALU.mult), R=["f_bb", "f_enL"], W=["Bt"])
        b.op("pool", lambda e: e.tensor_tensor(out=Kb[:, 0:W4], in0=f["kp"][:, 0:W4], in1=f["eLC"][:, 0:W4], op=ALU.mult), R=["f_kp", "f_eLC"], W=["Kb"])
        b.op("pool", lambda e: e.tensor_tensor(out=Bb[:, 0:W4], in0=f["bb"][:, 0:W4], in1=f["eLC"][:, 0:W4], op=ALU.mult), R=["f_bb", "f_eLC"], W=["Bb"])
        for hp in range(4):
            wc = apx(f["eL"][:, hp * w + 15:hp * w + 16], [f["eL"][:].ap[0][0], 128], [[16, nch], [0, 64]])
            idb = apx(identb_b[:, 0:1], [identb_b[:].ap[0][0], 128], [[0, nch], [1, 64]])
            b.op("dve", lambda e, hp=hp, wc=wc, idb=idb: e.tensor_tensor(out=Wd[:, hp, 0:nch, :], in0=idb, in1=wc, op=ALU.mult),
                 R=["f_eL", "identb_b"], W=["Wd"])
        p = nextps()
        pvb = PS[p][:].bitcast(BF16)
        for hp in range(4):
            b.op("pe", lambda e, hp=hp, p=p: e.transpose(out=pvb[0:w, hp * 128:(hp + 1) * 128], in_=vb[:, hp * w:(hp + 1) * w], identity=ident_b[:, :]),
                 R=["vb", "ident_b"], W=[psk[p]])
        b.op("act", lambda e, p=p: e.copy(out=Vtm[0:w, :, :], in_=pvb[0:w, 0:512].rearrange("p (h x) -> p h x", h=4)), R=[psk[p]], W=["Vtm"])

        ARf = AR4
        units = [(hp, h2) for hp in range(4) for h2 in range(2)]
        for w0 in range(0, 8, NU):
            gens = [unit(i, hp, h2, wu, nch, w, uu) for uu, (hp, h2) in enumerate(units[w0:w0 + NU])]
            while gens:
                for g in list(gens):
                    try:
                        next(g)
                    except StopIteration:
                        gens.remove(g)
        pY = reserve()
        for c in range(nch):
            for hp in range(4):
                cur = scur[hp]
                S0, S1 = Sbd[hp][cur], Sbd[hp][1 - cur]
                k0, k1 = f"S{hp}_{cur}", f"S{hp}_{1 - cur}"
                b.op("pe", lambda e, hp=hp, c=c, S0=S0: e.matmul(PS[pY][:, hp * 128 + c * 16:hp * 128 + c * 16 + 16], lhsT=S0[:, :],
                                                                rhs=RhT[:, hp * 128 + c * 16:hp * 128 + c * 16 + 16], start=True, stop=True),
                     R=[k0, "RhT"], W=[psk[pY]])
                ps_ = nextps()
                b.op("pe", lambda e, hp=hp, c=c, S0=S0, ps_=ps_: e.matmul(PS[ps_][:, 0:128], lhsT=MT[hp][:, c, :], rhs=S0[:, :], start=True, stop=True),
                     R=[k0, f"MT{hp}"], W=[psk[ps_]])
                b.op("dve", lambda e, hp=hp, c=c, S1=S1, ps_=ps_: e.tensor_tensor(out=S1[:, :], in0=PS[ps_][:, 0:128], in1=GB[hp][:, c, :], op=ALU.add),
                     R=[psk[ps_], f"GB{hp}"], W=[k1])
                scur[hp] = 1 - cur
        ysb = f["L"]
        y3 = v3(ysb, w)
        for hp in range(4):
            b.op("dve", lambda e, hp=hp: e.tensor_tensor(out=y3[:, hp, 0:wu], in0=PS[pY][:, hp * 128:hp * 128 + wu], in1=f["enL"][:, hp * 128:hp * 128 + wu], op=ALU.add),
                 R=[psk[pY], "f_enL"], W=["f_L"])
        release(pY)
        if last:
            sample_rwkv(w, y3)
            wkv_prompt_out()
        finalize(i, w, y3)

    def unit(i, hp, h2, wu, nch, w, u):
        pr = slice(h2 * 64, h2 * 64 + 64)
        YA, NK, Xm, XY, Y8, Zs, KBtm, Bexp, Vexp, Vhexp = UB[u]
        sf = f"_{u}"
        AR4 = AR[:, 0:8 * w].rearrange("p (h two t) -> p h two t", h=4, two=2)
        At = AR4[pr, hp, 0, 0:wu]
        ARr = AR[pr, hp * 2 * w:(hp + 1) * 2 * w].rearrange("p (two t) -> p two t", two=2)[:, :, 0:wu]
        Kt_ = Kt[pr, hp * w:hp * w + wu]
        Bt_ = Bt[pr, hp * w:hp * w + wu]
        Kb_ = Kb[pr, hp * w:hp * w + wu]
        Bb_ = Bb[pr, hp * w:hp * w + wu]
        Vh = Vtm[0:wu, hp, h2 * 64:(h2 + 1) * 64]
        pa, pb = nextps(), nextps()
        o1 = PS[pa][0:wu, 0:2 * wu].rearrange("p (two t) -> p two t", two=2)
        o2 = PS[pa][0:wu, 256:256 + 2 * wu].rearrange("p (two t) -> p two t", two=2)
        b.op("pe", lambda e: e.matmul(o1, lhsT=Bt_, rhs=ARr, start=True, stop=True), R=["Bt", "AR"], W=[psk[pa]])
        b.op("pe", lambda e: e.matmul(o2, lhsT=Kt_, rhs=ARr, start=True, stop=True), R=["Kt", "AR"], W=[psk[pa]])
        b.op("pe", lambda e: e.matmul(PS[pb][0:wu, 0:wu], lhsT=At, rhs=Bt_, start=True, stop=True), R=["Bt", "AR"], W=[psk[pb]])
        mj = maskjt_b[0:wu, :, 0:wu]
        b.op("dve", lambda e: e.tensor_tensor(out=YA[0:wu, :, 0:wu], in0=o1, in1=mj, op=ALU.mult), R=[psk[pa], "maskjt_b"], W=["YA" + sf])
        b.op("dve", lambda e: e.tensor_tensor(out=NK[0:wu, :, 0:wu], in0=o2, in1=mj, op=ALU.mult), R=[psk[pa], "maskjt_b"], W=["NK" + sf])
        b.op("dve", lambda e: e.tensor_tensor(out=Xm[0:wu, 0:wu], in0=PS[pb][0:wu, 0:wu], in1=maskx_b[0:wu, 0:wu], op=ALU.mult),
             R=[psk[pb], "maskx_b"], W=["Xm" + sf])
        yield
        Y1, Arb, Nak, Ark = YA[0:wu, 0, 0:wu], YA[0:wu, 1, 0:wu], NK[0:wu, 0, 0:wu], NK[0:wu, 1, 0:wu]
        X1 = Xm[0:wu, 0:wu]
        Xp, Yp, Ypow = X1, Y1, [Y1]
        for lvl in range(2):
            pc = nextps()
            oc = PS[pc][0:wu, 0:2 * wu].rearrange("p (two t) -> p two t", two=2)
            rk = ["YA" + sf, "Xm" + sf] if lvl == 0 else [f"XY{lvl - 1}" + sf]
            b.op("pe", lambda e, oc=oc, Xp=Xp, Yp=Yp: e.matmul(oc[:, 0, :], lhsT=Yp, rhs=Xp, start=True, stop=True), R=rk, W=[psk[pc]])
            b.op("pe", lambda e, oc=oc, Xp=Xp, Yp=Yp: e.matmul(oc[:, 1, :], lhsT=Xp, rhs=Yp, start=True, stop=True), R=rk, W=[psk[pc]])
            b.op("act", lambda e, oc=oc, lvl=lvl: e.copy(out=XY[lvl][0:wu, :, 0:wu], in_=oc), R=[psk[pc]], W=[f"XY{lvl}" + sf])
            yield
            Xp, Yp = XY[lvl][0:wu, 0, 0:wu], XY[lvl][0:wu, 1, 0:wu]
            Ypow.append(Yp)
        pc = nextps()
        b.op("pe", lambda e, Xp=Xp, Yp=Yp, pc=pc: e.matmul(PS[pc][0:wu, 0:wu], lhsT=Xp, rhs=Yp, start=True, stop=True), R=["XY1" + sf], W=[psk[pc]])
        b.op("act", lambda e, pc=pc: e.copy(out=Y8[0:wu, 0:wu], in_=PS[pc][0:wu, 0:wu]), R=[psk[pc]], W=["Y8" + sf])
        Ypow.append(Y8[0:wu, 0:wu])
        yield
        ykeys = [["YA" + sf], ["XY0" + sf], ["XY1" + sf], ["Y8" + sf]]
        pz = nextps()
        pzb = PS[pz][:].bitcast(BF16)
        b.op("pe", lambda e: e.transpose(out=pzb[0:wu, 0:64], in_=At, identity=ident_b[pr, pr]), R=["AR", "ident_b"], W=[psk[pz]])
        b.op("pe", lambda e: e.matmul(PS[pz][0:wu, 64:128], lhsT=Nak, rhs=Vh, start=True, stop=True), R=["NK" + sf, "Vtm"], W=[psk[pz]])
        b.op("act", lambda e: e.copy(out=Zs[0][0:wu, 0:64], in_=pzb[0:wu, 0:64]), R=[psk[pz]], W=["Z0" + sf])
        b.op("act", lambda e: e.copy(out=Zs[0][0:wu, 64:128], in_=PS[pz][0:wu, 64:128]), R=[psk[pz]], W=["Z0" + sf])
        yield
        zc = 0
        for lvl in range(4):
            pq_ = nextps()
            b.op("pe", lambda e, lvl=lvl, zc=zc, pq_=pq_: e.matmul(PS[pq_][0:wu, 0:128], lhsT=Ypow[lvl], rhs=Zs[zc][0:wu, :], start=True, stop=True),
                 R=ykeys[lvl] + [f"Z{zc}" + sf], W=[psk[pq_]])
            b.op("dve", lambda e, zc=zc, pq_=pq_: e.tensor_tensor(out=Zs[1 - zc][0:wu, :], in0=PS[pq_][0:wu, 0:128], in1=Zs[zc][0:wu, :], op=ALU.add),
                 R=[psk[pq_], f"Z{zc}" + sf], W=[f"Z{1 - zc}" + sf])
            zc = 1 - zc
            yield
        Z4 = Zs[zc]
        zk = f"Z{zc}" + sf
        Ah, Vhh = Z4[0:wu, 0:64], Z4[0:wu, 64:128]
        pRY = reserve()
        b.op("pe", lambda e: e.matmul(PS[pRY][pr, 0:wu], lhsT=Ah, rhs=Arb, start=True, stop=True), R=[zk, "YA" + sf], W=[psk[pRY]])
        b.op("pe", lambda e: e.matmul(PS[pRY][pr, 128:128 + wu], lhsT=Vh, rhs=Ark, start=True, stop=False), R=["Vtm", "NK" + sf], W=[psk[pRY]])
        b.op("pe", lambda e: e.matmul(PS[pRY][pr, 128:128 + wu], lhsT=Vhh, rhs=Arb, start=False, stop=True), R=[zk, "YA" + sf], W=[psk[pRY]])
        yield
        pt_ = nextps()
        ptb = PS[pt_][:].bitcast(BF16)
        b.op("pe", lambda e: e.transpose(out=ptb[0:wu, 0:64], in_=Bb_, identity=ident_b[pr, pr]), R=["Bb", "ident_b"], W=[psk[pt_]])
        b.op("pe", lambda e: e.transpose(out=ptb[0:wu, 64:128], in_=Kb_, identity=ident_b[pr, pr]), R=["Kb", "ident_b"], W=[psk[pt_]])
        b.op("act", lambda e: e.copy(out=KBtm[0:wu, :, :], in_=ptb[0:wu, 0:128].rearrange("p (a k) -> p a k", a=2)), R=[psk[pt_]], W=["KBtm" + sf])
        yield

        def bc_c(ap2):
            return apx(ap2, [ap2.ap[0][0], wu], [[0, nch], [1, 64]])
        cmb = apx(cmask_b[0:wu, 0:1], [cmask_b[:].ap[0][0], wu], [[1, nch], [0, 64]])
        b.op("dve", lambda e: e.tensor_tensor(out=Bexp[0:wu, 0:nch, :], in0=bc_c(KBtm[0:wu, 0, :]), in1=cmb, op=ALU.mult), R=["KBtm" + sf, "cmask_b"], W=["Bexp" + sf])
        b.op("pool", lambda e: e.tensor_tensor(out=Vexp[0:wu, 0:nch, :], in0=bc_c(Vh), in1=cmb, op=ALU.mult), R=["Vtm", "cmask_b"], W=["Vexp" + sf])
        b.op("dve", lambda e: e.tensor_tensor(out=Vhexp[0:wu, 0:nch, :], in0=bc_c(Vhh), in1=cmb, op=ALU.mult), R=[zk, "cmask_b"], W=["Vhexp" + sf])
        yield
        N = nch * 64
        pM, pGm = nextps(), nextps()
        b.op("pe", lambda e: e.matmul(PS[pM][pr, 0:N], lhsT=Ah, rhs=Bexp[0:wu, 0:nch, :].rearrange("p c k -> p (c k)"), start=True, stop=False),
             R=[zk, "Bexp" + sf], W=[psk[pM]])
        b.op("pe", lambda e: e.matmul(PS[pM][pr, 0:N], lhsT=ident_b[pr, pr], rhs=Wd[pr, hp, 0:nch, :].rearrange("p c k -> p (c k)"), start=False, stop=True),
             R=["ident_b", "Wd"], W=[psk[pM]])
        b.op("pe", lambda e: e.matmul(PS[pGm][pr, 0:N], lhsT=KBtm[0:wu, 1, :], rhs=Vexp[0:wu, 0:nch, :].rearrange("p c k -> p (c k)"), start=True, stop=False),
             R=["KBtm" + sf, "Vexp" + sf], W=[psk[pGm]])
        b.op("pe", lambda e: e.matmul(PS[pGm][pr, 0:N], lhsT=KBtm[0:wu, 0, :], rhs=Vhexp[0:wu, 0:nch, :].rearrange("p c k -> p (c k)"), start=False, stop=True),
             R=["KBtm" + sf, "Vhexp" + sf], W=[psk[pGm]])
        b.op("act", lambda e: e.copy(out=MT[hp][pr, 0:nch, pr], in_=PS[pM][pr, 0:nch * 64].rearrange("p (c k) -> p c k", c=nch)),
             R=[psk[pM]], W=[f"MT{hp}"])
        b.op("dve", lambda e: e.tensor_copy(out=GB[hp][pr, 0:nch, pr], in_=PS[pGm][pr, 0:nch * 64].rearrange("p (c k) -> p c k", c=nch)),
             R=[psk[pGm]], W=[f"GB{hp}"])
        ARf = AR[:, 0:8 * w].rearrange("p (h two t) -> p h two t", h=4, two=2)
        b.op("dve", lambda e: e.tensor_tensor(out=RhT[pr, hp * 128:hp * 128 + wu], in0=PS[pRY][pr, 0:wu], in1=ARf[pr, hp, 1, 0:wu], op=ALU.add),
             R=[psk[pRY], "AR"], W=["RhT"])
        b.op("act", lambda e: e.copy(out=f["enL"][pr, hp * 128:hp * 128 + wu], in_=PS[pRY][pr, 128:128 + wu]),
             R=[psk[pRY]], W=["f_enL"])
        release(pRY)

    def finalize(i, w, y3):
        W4 = 4 * w
        ysb = f["L"]
        dd, sq, tt = f["Lx"], f["eL"], f["eLC"]
        pm = nextps()
        b.op("pe", lambda e: e.matmul(PS[pm][:, 0:W4], lhsT=blk64_f[:], rhs=ysb[:, 0:W4], start=True, stop=True), R=["blk64_f", "f_L"], W=[psk[pm]])
        b.op("dve", lambda e: e.tensor_tensor(out=dd[:, 0:W4], in0=ysb[:, 0:W4], in1=PS[pm][:, 0:W4], op=ALU.subtract), R=["f_L", psk[pm]], W=["f_Lx"])
        b.op("act", lambda e: e.activation(out=sq[:, 0:W4], in_=dd[:, 0:W4], func=AF.Square), R=["f_Lx"], W=["f_eL"])
        pv_ = nextps()
        b.op("pe", lambda e: e.matmul(PS[pv_][:, 0:W4], lhsT=blk64_f[:], rhs=sq[:, 0:W4], start=True, stop=True), R=["blk64_f", "f_eL"], W=[psk[pv_]])
        b.op("act", lambda e: e.activation(out=sq[:, 0:W4], in_=PS[pv_][:, 0:W4], func=AF.Sqrt, bias=GN_EPS), R=[psk[pv_]], W=["f_eL"])
        b.op("dve", lambda e: e.reciprocal(out=sq[:, 0:W4], in_=sq[:, 0:W4]), R=["f_eL"], W=["f_eL"])
        b.op("dve", lambda e: e.tensor_tensor(out=dd[:, 0:W4], in0=dd[:, 0:W4], in1=sq[:, 0:W4], op=ALU.mult), R=["f_Lx", "f_eL"], W=["f_Lx"])
        d3, t3, kp3 = v3(dd, w), v3(tt, w), v3(f["kp"], w)
        for hp in range(4):
            b.op("dve", lambda e, hp=hp: e.tensor_scalar(out=d3[:, hp, :], in0=d3[:, hp, :], scalar1=pp(PP_LNW, hp), scalar2=pp(PP_LNB, hp),
                                                         op0=ALU.mult, op1=ALU.add), R=["f_Lx", "ppar"], W=["f_Lx"])
            b.op("dve", lambda e, hp=hp: e.scalar_tensor_tensor(out=t3[:, hp, :], in0=zt[:, hp, 0:w], scalar=pp(PP_RK, hp), in1=kp3[:, hp, :],
                                                                op0=ALU.mult, op1=ALU.mult), R=["zt", "f_kp", "ppar"], W=["f_eLC"])
        pb_ = nextps()
        b.op("pe", lambda e: e.matmul(PS[pb_][:, 0:W4], lhsT=blk1_f[:], rhs=tt[:, 0:W4], start=True, stop=True), R=["blk1_f", "f_eLC"], W=[psk[pb_]])
        b.op("dve", lambda e: e.tensor_tensor(out=t3, in0=PS[pb_][:, 0:W4].rearrange("p (h t) -> p h t", h=4), in1=zt[:, 8:12, 0:w], op=ALU.mult),
             R=[psk[pb_], "zt"], W=["f_eLC"])
        b.op("dve", lambda e: e.tensor_tensor(out=dd[:, 0:W4], in0=dd[:, 0:W4], in1=tt[:, 0:W4], op=ALU.add), R=["f_Lx", "f_eLC"], W=["f_Lx"])
        b.op("dve", lambda e: e.tensor_tensor(out=ygT[:, 0:W4], in0=dd[:, 0:W4], in1=gsb[:, 0:W4], op=ALU.mult), R=["f_Lx", "gsb"], W=["ygT"])
        for hp in range(4):
            b.dma("sp", "oyg", ygs[hp * 128:(hp + 1) * 128, i * 128:i * 128 + w], ygT[:, hp * w:(hp + 1) * w], R=["ygT"], W=["ygs"])

    def sample_rwkv(w, y3):
        W4 = 4 * w
        b.op("act", lambda e: e.activation(out=f["t"][:, 0:W4], in_=f["ld"][:, 0:W4], func=AF.Exp), R=["f_ld"], W=["f_t"])
        b.op("dve", lambda e: e.tensor_scalar(out=f["asig"][:, 0:W4], in0=f["kk"][:, 0:W4], scalar1=-1.0, scalar2=None, op0=ALU.mult),
             R=["f_kk"], W=["f_asig"])
        srcs = [(v3(f["t"], w), "f_t"), (v3(f["asig"], w), "f_asig"), (v3(f["bb"], w), "f_bb"), (v3(f["kp"], w), "f_kp"), (zt[:, 0:4, 0:w], "zt")]
        for qi, (src, key) in enumerate(srcs):
            b.op("dve", lambda e, qi=qi, src=src: e.tensor_copy(out=xb[:, qi, :, :], in_=src[:, :, 16:32]), R=[key], W=["xb"])
            p = nextps()
            pvb_ = PS[p][:].bitcast(BF16)
            for hp in range(4):
                for h2 in range(2):
                    pr = slice(h2 * 64, h2 * 64 + 64)
                    b.op("pe", lambda e, pvb_=pvb_, qi=qi, hp=hp, h2=h2, pr=pr: e.transpose(out=pvb_[h2 * 32:h2 * 32 + 16, hp * 64:(hp + 1) * 64], in_=xb[pr, qi, hp, :],
                                                                                          identity=ident_b[pr, pr]),
                         R=["xb", "ident_b"], W=[psk[p]])
            for h2 in range(2):
                b.op("act", lambda e, pvb_=pvb_, h2=h2, qi=qi: e.copy(out=tm5[h2 * 32:h2 * 32 + 16, qi, :], in_=pvb_[h2 * 32:h2 * 32 + 16, 0:256]),
                     R=[psk[p]], W=["tm5"])
        for s in range(SPC):
            sx = Sx[s % 2]
            sk = f"Sx{s % 2}"
            for h2 in range(2):
                src = I["swkv"][s].rearrange("(hp two) v k -> two v hp k", two=2)[h2]
                b.dma("sp", sk, sx[h2 * 64:(h2 + 1) * 64, :, :], src, W=[sk])
            pbs = [nextps() for _ in range(5)]
            for qi in range(5):
                b.op("pe", lambda e, qi=qi, s=s, pbs=pbs: e.matmul(PS[pbs[qi]][:, 0:256], lhsT=onehot_b[0:48, s, :], rhs=tm5[0:48, qi, :], start=True, stop=True),
                     R=["onehot_b", "tm5"], W=[psk[pbs[qi]]])

            def bq(qi, pbs=pbs):
                return PS[pbs[qi]][:, 0:256].rearrange("p (h k) -> p h k", h=4)
            col = 16 + s
            vcol = apx(zt[:, 8, col:col + 1], [zt[:].ap[0][0], 128], [[128, 4], [0, 64]])
            sab = apx(ssa[:, 0:1], [ssa[:].ap[0][0], 128], [[1, 4], [0, 64]])
            b.op("dve", lambda e, sx=sx, bq=bq: e.tensor_tensor(out=stmp[:], in0=sx[:], in1=bq(1), op=ALU.mult), R=[sk, psk[pbs[1]]], W=["stmp"])
            b.op("dve", lambda e: e.tensor_reduce(out=ssa[:, :], in_=stmp[:], axis=AX.X, op=ALU.add), R=["stmp"], W=["ssa"])
            b.op("dve", lambda e, sx=sx, bq=bq: e.tensor_tensor(out=sx[:], in0=sx[:], in1=bq(0), op=ALU.mult), R=[sk, psk[pbs[0]]], W=[sk])
            b.op("dve", lambda e, bq=bq, sab=sab: e.tensor_tensor(out=stmp[:], in0=sab, in1=bq(2), op=ALU.mult), R=["ssa", psk[pbs[2]]], W=["stmp"])
            b.op("dve", lambda e, sx=sx: e.tensor_tensor(out=sx[:], in0=sx[:], in1=stmp[:], op=ALU.add), R=[sk, "stmp"], W=[sk])
            b.op("dve", lambda e, bq=bq, vcol=vcol: e.tensor_tensor(out=stmp[:], in0=vcol, in1=bq(3), op=ALU.mult), R=["zt", psk[pbs[3]]], W=["stmp"])
            b.op("dve", lambda e, sx=sx: e.tensor_tensor(out=sx[:], in0=sx[:], in1=stmp[:], op=ALU.add), R=[sk, "stmp"], W=[sk])
            b.op("dve", lambda e, sx=sx, bq=bq: e.tensor_tensor(out=stmp[:], in0=sx[:], in1=bq(4), op=ALU.mult), R=[sk, psk[pbs[4]]], W=["stmp"])
            b.op("dve", lambda e, col=col: e.tensor_reduce(out=y3[:, :, col], in_=stmp[:], axis=AX.X, op=ALU.add), R=["stmp"], W=["f_L"])
            for h2 in range(2):
                dst = O["wkv_s"][s].rearrange("(hp two) v k -> two v hp k", two=2)[h2]
                b.dma("sp", "owkvs", dst, sx[h2 * 64:(h2 + 1) * 64, :, :], R=[sk])

    def wkv_prompt_out():
        for hp in range(4):
            S0 = Sbd[hp][scur[hp]]
            k0 = f"S{hp}_{scur[hp]}"
            p = nextps()
            pb_ = PS[p][:].bitcast(BF16) if CH == BF16 else PS[p][:]
            idn = ident_b if CH == BF16 else ident_f
            b.op("pe", lambda e, S0=S0, pb_=pb_, idn=idn: e.transpose(out=pb_[:, 0:128], in_=S0[:, :], identity=idn[:, :]), R=[k0, "ident_b", "ident_f"], W=[psk[p]])
            b.op("act", lambda e, hp=hp, pb_=pb_: e.copy(out=wkvo[:, hp, :], in_=pb_[:, 0:128]), R=[psk[p]], W=["wkvo"])
            for h2 in range(2):
                b.dma("sp", "owkvp", O["wkv_p"][2 * hp + h2], wkvo[h2 * 64:(h2 + 1) * 64, hp, h2 * 64:(h2 + 1) * 64], R=["wkvo"])


    def tile_A2(i):
        w = tw(i)
        buf = i % 2
        if i + 1 < NTL:
            load_x(i + 1, (i + 1) % 2)
        norm_and_transpose(i, buf, gmixB, "gmixB")
        if i == NFULL:
            shrow = zt[:].rearrange("p a b -> p (a b)")
            for piece in range(4):
                p = nextps()
                c0 = piece * 448
                for kc in range(8):
                    b.op("pe", lambda e, kc=kc, p=p, c0=c0: e.matmul(PS[p][0:w, 0:448], lhsT=hT[:, kc, 0:w], rhs=w_in_r[:, kc, c0:c0 + 448],
                                                                     start=(kc == 0), stop=(kc == 7)), R=["hT", "w_in_r"], W=[psk[p]])
                b.op("act", lambda e, p=p, piece=piece: e.copy(out=shrow[0:w, piece * 448:(piece + 1) * 448], in_=PS[p][0:w, 0:448]),
                     R=[psk[p]], W=["zt"])
            b.dma("sp", "osh", O["sh_p"][:, :], shrow[15:16, :], R=["zt"])
            b.dma("sp", "osh", O["sh_s"][:, :], shrow[16:32, :], R=["zt"])
        rwkv_tile(i)

    if stage >= 2:
        load_x(0, 0)
        for i in range(NTL):
            tile_A2(i)
    b.barrier()
    b.emit()
    pst.close()
    b.st = b.st0

    def phase_B():
        wg_b = b.sb("wg_b", [128, 8, 2048], BF16)
        for kc in range(8):
            b.dma("pool", "wA", wg_b[:, kc, :], I["w_in"][kc * 128:(kc + 1) * 128, C_G:INC], W=["wg_b"])
        w_om = load_w_bf("w_om", [128, 4, D], I["w_o_mla"].rearrange("(k p) n -> p k n", p=128))
        w_or = load_w_bf("w_or", [128, 4, D], I["w_o_rwkv"].rearrange("(k p) n -> p k n", p=128))
        w_ob = load_w_bf("w_ob", [128, 8, D], I["w_out"].rearrange("(k p) n -> p k n", p=128))
        gsg = b.sb("gsg", [128, 16, 128], BF16)
        b.op("pool", lambda e: e.memset(gsg[:], 0.0), W=["gsg"])
        atT = b.sb("atT", [128, 4, 128], BF16)
        ygTt = b.sb("ygTt", [128, 4, 128], BF16)
        mT = b.sb("mT", [128, 8, 128], BF16)
        tmpa = b.sb("tmpa", [128, 512], F32)
        tmpb = b.sb("tmpb", [128, 512], F32)
        x1 = b.sb("x1", [128, D], F32)

        def tile(i):
            w = tw(i)
            buf = i % 2
            t0 = 128 * i
            if i + 1 < NTL:
                load_x(i + 1, (i + 1) % 2)
            b.dma("sp", "ldat", atT[:, :, 0:w], ats[:, t0:t0 + w].rearrange("(c p) t -> p c t", p=128), R=["ats"], W=["atT"])
            b.dma("sp", "ldyg", ygTt[:, :, 0:w], ygs[:, t0:t0 + w].rearrange("(c p) t -> p c t", p=128), R=["ygs"], W=["ygTt"])
            norm_and_transpose(i, buf, gmixB, "gmixB")
            for g in range(4):
                p = nextps()
                for jj in range(4):
                    j = 4 * g + jj
                    for kc in range(8):
                        b.op("pe", lambda e, p=p, jj=jj, j=j, kc=kc: e.matmul(PS[p][:, jj * 128:jj * 128 + w], lhsT=wg_b[:, kc, j * 128:(j + 1) * 128], rhs=hT[:, kc, 0:w],
                                                                               start=(kc == 0), stop=(kc == 7)), R=["wg_b", "hT"], W=[psk[p]])
                b.op("act", lambda e, p=p, g=g: e.activation(out=gsg[:, 4 * g:4 * g + 4, 0:w], in_=PS[p][:, 0:512].rearrange("p (j t) -> p j t", j=4)[:, :, 0:w], func=AF.Sigmoid),
                     R=[psk[p]], W=["gsg"])
            for g in range(2):
                pm_, pr_ = nextps(), nextps()
                for jj in range(4):
                    j = 4 * g + jj
                    for kc in range(4):
                        b.op("pe", lambda e, pm_=pm_, jj=jj, j=j, kc=kc: e.matmul(PS[pm_][:, jj * 128:jj * 128 + w], lhsT=w_om[:, kc, j * 128:(j + 1) * 128], rhs=atT[:, kc, 0:w],
                                                                                   start=(kc == 0), stop=(kc == 3)), R=["w_om", "atT"], W=[psk[pm_]])
                    for kc in range(4):
                        b.op("pe", lambda e, pr_=pr_, jj=jj, j=j, kc=kc: e.matmul(PS[pr_][:, jj * 128:jj * 128 + w], lhsT=w_or[:, kc, j * 128:(j + 1) * 128], rhs=ygTt[:, kc, 0:w],
                                                                                   start=(kc == 0), stop=(kc == 3)), R=["w_or", "ygTt"], W=[psk[pr_]])
                v1 = PS[pm_][:, 0:512].rearrange("p (j t) -> p j t", j=4)
                v2 = PS[pr_][:, 0:512].rearrange("p (j t) -> p j t", j=4)
                ta = tmpa[:, 0:512].rearrange("p (j t) -> p j t", j=4)
                tb = tmpb[:, 0:512].rearrange("p (j t) -> p j t", j=4)
                b.op("dve", lambda e, v1=v1, ta=ta, g=g: e.tensor_tensor(out=ta, in0=v1, in1=gsg[:, 4 * g:4 * g + 4, :], op=ALU.mult), R=[psk[pm_], "gsg"], W=["tmpa"])
                b.op("dve", lambda e, v2=v2, tb=tb, g=g: e.tensor_tensor(out=tb, in0=v2, in1=gsg[:, 8 + 4 * g:8 + 4 * g + 4, :], op=ALU.mult), R=[psk[pr_], "gsg"], W=["tmpb"])
                b.op("pool", lambda e, ta=ta, tb=tb, g=g: e.tensor_tensor(out=mT[:, 4 * g:4 * g + 4, :], in0=ta, in1=tb, op=ALU.add), R=["tmpa", "tmpb"], W=["mT"])
            for half in range(2):
                po_ = nextps()
                for kc in range(8):
                    b.op("pe", lambda e, po_=po_, kc=kc, half=half: e.matmul(PS[po_][0:w, 0:512], lhsT=mT[:, kc, 0:w], rhs=w_ob[:, kc, half * 512:(half + 1) * 512],
                                                                             start=(kc == 0), stop=(kc == 7)), R=["mT", "w_ob"], W=[psk[po_]])
                b.op("dve", lambda e, po_=po_, half=half: e.tensor_tensor(out=x1[0:w, half * 512:(half + 1) * 512], in0=PS[po_][0:w, 0:512], in1=xt[buf][0:w, half * 512:(half + 1) * 512], op=ALU.add),
                     R=[psk[po_], f"xt{buf}"], W=["x1"])
            b.dma("sp", "ox1", x1s[t0:t0 + w, :], x1[0:w, :], R=["x1"], W=["x1s"])

        load_x(0, 0)
        for i in range(NTL):
            tile(i)

    if stage >= 4:
        with ExitStack() as pst:
            b.st = pst
            phase_B()
            b.barrier()
            b.emit()
        b.st = b.st0

    def phase_C():
        w_up_b = b.sb("w_up_b", [128, 8, DFF], BF16)
        for kc in range(8):
            b.dma("pool", "wA", w_up_b[:, kc, :], I["w_up"][kc * 128:(kc + 1) * 128, :], W=["w_up_b"])
        w_dn_b = b.sb("w_dn_b", [128, 32, D], BF16)
        for k4 in range(8):
            b.dma("pool", "wA", w_dn_b[:, 4 * k4:4 * k4 + 4, :], I["w_down"][512 * k4:512 * (k4 + 1), :].rearrange("(k p) n -> p k n", p=128), W=["w_dn_b"])
        gffnB = bcast_load("gffnB", I["g_ffn"], D)
        gfinB = bcast_load("gfinB", I["g_final"], D)
        uT = b.sb("uT", [128, 32, 128], BF16)
        rl_ = b.sb("relu_t", [128, 512], BF16)
        x2 = b.sb("x2", [128, D], F32)
        yo = b.sb("yo", [128, D], F32)

        def tile(i):
            w = tw(i)
            buf = i % 2
            t0 = 128 * i
            if i + 1 < NTL:
                load_x(i + 1, (i + 1) % 2, src_scratch=True)
            norm_and_transpose(i, buf, gffnB, "gffnB")
            for g in range(8):
                p = nextps()
                for jj in range(4):
                    j = 4 * g + jj
                    for kc in range(8):
                        b.op("pe", lambda e, p=p, jj=jj, j=j, kc=kc: e.matmul(PS[p][:, jj * 128:jj * 128 + w], lhsT=w_up_b[:, kc, j * 128:(j + 1) * 128], rhs=hT[:, kc, 0:w],
                                                                               start=(kc == 0), stop=(kc == 7)), R=["w_up_b", "hT"], W=[psk[p]])
                rv = rl_[:, 0:4 * w].rearrange("p (j t) -> p j t", j=4)
                b.op("act", lambda e, p=p, rv=rv: e.activation(out=rv, in_=PS[p][:, 0:512].rearrange("p (j t) -> p j t", j=4)[:, :, 0:w], func=AF.Relu), R=[psk[p]], W=["relu_t"])
                b.op("pool", lambda e, g=g, rv=rv: e.tensor_tensor(out=uT[:, 4 * g:4 * g + 4, 0:w], in0=rv, in1=rv, op=ALU.mult), R=["relu_t"], W=["uT"])
            for half in range(2):
                po_ = nextps()
                for kc in range(32):
                    b.op("pe", lambda e, po_=po_, kc=kc, half=half: e.matmul(PS[po_][0:w, 0:512], lhsT=uT[:, kc, 0:w], rhs=w_dn_b[:, kc, half * 512:(half + 1) * 512],
                                                                             start=(kc == 0), stop=(kc == 31)), R=["uT", "w_dn_b"], W=[psk[po_]])
                b.op("dve", lambda e, po_=po_, half=half: e.tensor_tensor(out=x2[0:w, half * 512:(half + 1) * 512], in0=PS[po_][0:w, 0:512], in1=xt[buf][0:w, half * 512:(half + 1) * 512], op=ALU.add),
                     R=[psk[po_], f"xt{buf}"], W=["x2"])
            rms_rstd(x2[0:w, :], ["x2"], w, D, 3, NORM_EPS)
            b.op("dve", lambda e: e.scalar_tensor_tensor(out=yo[0:w, :], in0=x2[0:w, :], scalar=st1[0:w, 3:4], in1=gfinB[0:w, :], op0=ALU.mult, op1=ALU.mult),
                 R=["x2", "st1_3", "gfinB"], W=["yo"])
            if i == 0:
                b.dma("sp", "oy", O["y_p"][0:112, :], yo[16:128, :], R=["yo"])
            elif i < NFULL:
                b.dma("sp", "oy", O["y_p"][128 * i - 16:128 * i + 112, :], yo[:, :], R=["yo"])
            else:
                b.dma("sp", "oy", O["y_p"][SEQ - 16:SEQ, :], yo[0:16, :], R=["yo"])
                b.dma("sp", "oy", O["y_s"][:, :], yo[16:32, :], R=["yo"])

        load_x(0, 0, src_scratch=True)
        for i in range(NTL):
            tile(i)

    if stage >= 4:
        with ExitStack() as pst:
            b.st = pst
            phase_C()
            b.barrier()
            b.emit()
        b.st = b.st0


def kernel(**inputs):
    x_prompt = np.asarray(inputs["x_prompt"])
    x_sample = np.asarray(inputs["x_sample"])
    ncores, seq = x_prompt.shape[0], x_prompt.shape[1]
    cache = np.asarray(inputs["cache_kv"])[0]
    pt = np.asarray(inputs["page_table"]).astype(np.int32)
    npool, npg = cache.shape[0], pt.shape[1]
    cfg = make_cfg(seq, npg, npool, npg * 128)
    import os
    nc, consts = build(cfg, stage=int(os.environ.get('KSTAGE', '99')))
    wsh = dict(g_final=[1, D], g_mix=[1, D], w_in=[D, INC], g_q=[1, NQ], w_uq=[NQ, 768], g_kv=[1, NKV],
               w_uk=[NKV, 512], w_uv=[NKV, 512], w_o_mla=[512, D], mu_shift=[1, RC], w0=[1, RW], w2=[64, RW],
               a0=[1, RW], a2=[64, RW], g2=[128, RW], k_k=[1, RW], k_a=[1, RW], r_k=[1, RW], ln_w=[1, RW],
               ln_b=[1, RW], w_o_rwkv=[RW, D], w_out=[D, D], g_ffn=[1, D], w_up=[D, DFF], w_down=[DFF, D])
    shared = {}
    for n in WNAMES:
        shared[n] = np.ascontiguousarray(np.asarray(inputs[n], dtype=np.float32).reshape(wsh[n]))
    for n, a in consts.items():
        shared["c_" + n] = a
    shared["meta"] = np.ascontiguousarray(np.asarray(inputs["meta_tokens"], dtype=np.float32))
    shared["cache"] = np.ascontiguousarray(cache)
    swkv = np.asarray(inputs["state_wkv"])[0]
    sshift = np.asarray(inputs["state_shift"])[0]
    in_maps = []
    for c in range(ncores):
        m = dict(shared)
        m["xp"] = np.ascontiguousarray(x_prompt[c])
        m["xs"] = np.ascontiguousarray(x_sample[SPC * c:SPC * (c + 1), 0])
        m["pt"] = np.ascontiguousarray(pt[SPC * c:SPC * (c + 1)])
        m["swkv"] = np.ascontiguousarray(swkv[SPC * c:SPC * (c + 1)])
        m["sshift"] = np.ascontiguousarray(sshift[SPC * c:SPC * (c + 1)])
        in_maps.append(m)
    res = run_bass_kernel_spmd(nc, in_maps, core_ids=list(range(ncores)))
    r = res.results
    y_p = np.stack([r[c]["y_p"] for c in range(ncores)])
    y_s = np.concatenate([r[c]["y_s"] for c in range(ncores)])[:, None, :]
    kv_p = np.stack([r[c]["kv_p"] for c in range(ncores)])[None]
    wkv_p = np.stack([r[c]["wkv_p"] for c in range(ncores)])[None]
    sh_p = np.concatenate([r[c]["sh_p"] for c in range(ncores)])[None]
    kv_s = np.concatenate([r[c]["kv_s"] for c in range(ncores)])[None, :, None, :]
    wkv_s = np.concatenate([r[c]["wkv_s"] for c in range(ncores)])[None]
    sh_s = np.concatenate([r[c]["sh_s"] for c in range(ncores)])[None]
    return tuple(np.ascontiguousarray(a, dtype=np.float32) for a in (y_p, y_s, kv_p, wkv_p, sh_p, kv_s, wkv_s, sh_s))
```

```python
import numpy as np
from contextlib import ExitStack
import concourse.bass as bass
import concourse.mybir as mybir
from concourse.bass_utils import run_bass_kernel_spmd

F32 = mybir.dt.float32
BF16 = mybir.dt.bfloat16
I32 = mybir.dt.int32
AF = mybir.ActivationFunctionType
ALU = mybir.AluOpType
AX = mybir.AxisListType

D = 1024
NQ, NKV, NRP = 384, 256, 32
KVW = 288
H = 8
RW = 512
RC = 1792
INC = 4512
C_Q, C_KV, C_RW, C_G = 0, 384, 672, 2464
DFF = 4096
NORM_EPS = 1e-6
GN_EPS = 64e-5
SM_SCALE = 96 ** -0.5
NMETA = 16
SPC = 16

WNAMES = ["g_final", "g_mix", "w_in", "g_q", "w_uq", "g_kv", "w_uk", "w_uv", "w_o_mla", "mu_shift", "w0", "w2",
          "a0", "a2", "g2", "k_k", "k_a", "r_k", "ln_w", "ln_b", "w_o_rwkv", "w_out", "g_ffn", "w_up", "w_down"]


def apx(base, part, free):
    return bass.AP(base.tensor, base.offset, [list(part)] + [list(f) for f in free])


class B:
    ENG = ("pe", "act", "dve", "pool", "sp")

    def __init__(self, nc, st):
        self.nc, self.st, self.st0 = nc, st, st
        self.q = {e: [] for e in self.ENG}
        if getattr(self, "_after_barrier", False):
            self.new_engine_sems()
            self._after_barrier = False
        self.sems = {}
        self.cnt = {}
        self.ek = {}
        self.phase = 0
        self.new_engine_sems()
        self.waited = {e: {} for e in self.ENG}
        self.bufs = {}
        self.nps = 0

    def new_engine_sems(self):
        self.phase += 1
        for e in ("pe", "act", "dve", "pool"):
            k = f"{e}#{self.phase}"
            self.sems[k] = self.st0.enter_context(self.nc.semaphore("s_" + k.replace("#", "_")))
            self.cnt[k] = 0
            self.ek[e] = k

    def sb(self, name, shape, dt):
        return self.st.enter_context(self.nc.sbuf_tensor(name, list(shape), dt))

    def psum(self, name, shape, dt):
        return self.st0.enter_context(self.nc.psum_tensor(name, list(shape), dt))

    def _buf(self, k):
        if k not in self.bufs:
            self.bufs[k] = {"w": None, "r": {}}
        return self.bufs[k]

    def _waits(self, eng, R, W):
        toks = {}

        def need(t):
            if t is not None:
                toks[t[0]] = max(toks.get(t[0], 0), t[1])
        for k in R:
            need(self._buf(k)["w"])
        for k in W:
            b = self._buf(k)
            need(b["w"])
            for kk, v in b["r"].items():
                need((kk, v))
        for k, v in toks.items():
            if eng == "pe" and k == self.ek["pe"]:
                continue
            if "#" not in k:
                v = self.cnt[k]
            if self.waited[eng].get(k, 0) >= v:
                continue
            self.waited[eng][k] = v
            sem = self.sems[k]
            self.q[eng].append(lambda e, sem=sem, v=v: e.wait_ge(sem, v))

    def _mark(self, tok, R, W):
        for k in R:
            b = self._buf(k)
            b["r"][tok[0]] = max(b["r"].get(tok[0], 0), tok[1])
        for k in W:
            self.bufs[k] = {"w": tok, "r": {}}

    def op(self, eng, fn, R=(), W=()):
        self._waits(eng, R, W)
        k = self.ek[eng]
        self.cnt[k] += 1
        sem = self.sems[k]
        self.q[eng].append(lambda e, fn=fn, sem=sem: fn(e).then_inc(sem, 1))
        self._mark((k, self.cnt[k]), R, W)

    DRAM_KEYS = ("ats", "ygs", "kvs", "x1s")

    def _semkey(self, R, W):
        if W and W[0] not in self.DRAM_KEYS:
            k = "i_" + W[0]
        else:
            k = "o_" + R[0]
        if k not in self.sems:
            self.sems[k] = self.st0.enter_context(self.nc.semaphore(k))
            self.cnt[k] = 0
        return k

    def dma(self, queue, semkey, out, in_, R=(), W=(), **kw):
        semkey = self._semkey(R, W)
        self._waits(queue, R, W)
        self.cnt[semkey] += 16
        sem = self.sems[semkey]
        self.q[queue].append(lambda e, sem=sem, out=out, in_=in_, kw=kw: e.dma_start(out=out, in_=in_, **kw).then_inc(sem, 16))
        self._mark((semkey, self.cnt[semkey]), R, W)

    def raw_dma(self, queue, semkey, fn, R=(), W=()):
        semkey = self._semkey(R, W)
        self._waits(queue, R, W)
        self.cnt[semkey] += 16
        sem = self.sems[semkey]
        self.q[queue].append(lambda e, sem=sem, fn=fn: fn(e).then_inc(sem, 16))
        self._mark((semkey, self.cnt[semkey]), R, W)

    def finish(self):
        for k, sem in self.sems.items():
            v = self.cnt[k]
            if v and self.waited["sp"].get(k, 0) < v:
                self.waited["sp"][k] = v
                self.q["sp"].append(lambda e, sem=sem, v=v: e.wait_ge(sem, v))

    def barrier(self):
        self._after_barrier = True
        for e in self.ENG:
            for k, sem in self.sems.items():
                v = self.cnt[k]
                if v and self.waited[e].get(k, 0) < v and not (k == self.ek.get(e)):
                    self.waited[e][k] = v
                    self.q[e].append(lambda eng, sem=sem, v=v: eng.wait_ge(sem, v))

    def emit(self):
        with self.nc.Block() as blk:
            for name, deco in (("pe", blk.tensor), ("act", blk.scalar), ("dve", blk.vector),
                               ("pool", blk.gpsimd), ("sp", blk.sync)):
                lst = self.q[name]

                def run(e, lst=lst):
                    for th in lst:
                        th(e)
                deco(run)
        self.q = {e: [] for e in self.ENG}


def host_consts(cfg):
    T, NTOK, NTL = cfg["T"], cfg["NTOK"], cfg["NTL"]
    c = {}
    idx = np.arange(128)
    c["ident"] = np.eye(128, dtype=np.float32)
    same = (idx[:, None] // 16) == (idx[None, :] // 16)
    c["maskx"] = (same & (idx[None, :] < idx[:, None])).astype(np.float32)
    mj = np.zeros((128, 2, 128), np.float32)
    mj[:, 0, :] = same & (idx[None, :] > idx[:, None])
    mj[:, 1, :] = same & (idx[None, :] >= idx[:, None])
    c["maskjt"] = mj.reshape(128, 256)
    c["cmask"] = (idx[:, None] // 16 == np.arange(8)[None, :]).astype(np.float32)
    c["blk1"] = ((idx[:, None] // 64) == (idx[None, :] // 64)).astype(np.float32)
    c["maskc"] = (idx[None, :] >= idx[:, None]).astype(np.float32)
    c["identb"] = (idx[:, None] % 64 == np.arange(64)[None, :]).astype(np.float32)
    rm = np.ones((128, 512), np.float32)
    rm[:, ::16] = 0.0
    c["scanm"] = rm
    oh = np.zeros((128, 16, 128), np.float32)
    for s in range(16):
        oh[s, s, 0:64] = 1.0
        oh[32 + s, s, 64:128] = 1.0
    c["onehot"] = oh.reshape(128, 2048)
    inv = (10000.0 ** (-np.arange(0, 32, 2, dtype=np.float32) / np.float32(32))).astype(np.float32)
    pos = np.concatenate([np.arange(T), np.full(SPC, cfg["PAST"])]).astype(np.float32)
    ang = (pos[:, None] * inv[None, :]).astype(np.float32)
    tab = np.zeros((NTL * 128, 32), np.float32)
    tab[:NTOK, :16] = np.cos(ang)
    tab[:NTOK, 16:] = np.sin(ang)
    c["rope"] = np.ascontiguousarray(tab.reshape(NTL, 128, 32).transpose(1, 0, 2)).reshape(128, NTL * 32)
    return c


def make_cfg(seq, npg, npool, past):
    T = seq + NMETA
    assert seq % 128 == 0
    NTOK = T + SPC
    NTL = (NTOK + 127) // 128
    return dict(SEQ=seq, T=T, NTOK=NTOK, NTL=NTL, NPG=npg, NPOOL=npool, PAST=past)


def build(cfg, stage=99):
    SEQ, T, NTOK, NTL, NPG, NPOOL = cfg["SEQ"], cfg["T"], cfg["NTOK"], cfg["NTL"], cfg["NPG"], cfg["NPOOL"]
    nc = bass.Bass("TRN2", target_bir_lowering=False)
    consts = host_consts(cfg)

    def din(name, shape, dt=F32):
        return nc.dram_tensor(name, list(shape), dt, kind="ExternalInput").ap()

    def dout(name, shape, dt=F32):
        return nc.dram_tensor(name, list(shape), dt, kind="ExternalOutput").ap()

    I = {}
    I["xp"] = din("xp", [SEQ, D])
    I["xs"] = din("xs", [SPC, D])
    I["meta"] = din("meta", [NMETA, D])
    I["cache"] = din("cache", [NPOOL, 128, KVW])
    I["pt"] = din("pt", [SPC, NPG], I32)
    I["swkv"] = din("swkv", [SPC, H, 64, 64])
    I["sshift"] = din("sshift", [SPC, RC])
    wshapes = dict(g_final=[1, D], g_mix=[1, D], w_in=[D, INC], g_q=[1, NQ], w_uq=[NQ, 768], g_kv=[1, NKV],
                   w_uk=[NKV, 512], w_uv=[NKV, 512], w_o_mla=[512, D], mu_shift=[1, RC], w0=[1, RW], w2=[64, RW],
                   a0=[1, RW], a2=[64, RW], g2=[128, RW], k_k=[1, RW], k_a=[1, RW], r_k=[1, RW], ln_w=[1, RW],
                   ln_b=[1, RW], w_o_rwkv=[RW, D], w_out=[D, D], g_ffn=[1, D], w_up=[D, DFF], w_down=[DFF, D])
    for n in WNAMES:
        I[n] = din(n, wshapes[n])
    for n, a in consts.items():
        I["c_" + n] = din("c_" + n, a.shape)
    O = {}
    O["y_p"] = dout("y_p", [SEQ, D])
    O["y_s"] = dout("y_s", [SPC, D])
    O["kv_p"] = dout("kv_p", [T, KVW])
    O["wkv_p"] = dout("wkv_p", [H, 64, 64])
    O["sh_p"] = dout("sh_p", [1, RC])
    O["kv_s"] = dout("kv_s", [SPC, KVW])
    O["wkv_s"] = dout("wkv_s", [SPC, H, 64, 64])
    O["sh_s"] = dout("sh_s", [SPC, RC])

    st = ExitStack()
    with st:
        b = B(nc, st)
        _program(nc, b, cfg, I, O, stage)
        b.finish()
        b.emit()
    return nc, consts


def _program(nc, b, cfg, I, O, stage):
    import os
    KA1 = int(os.environ.get('KA1', '3'))
    KQ = int(os.environ.get('KQ', '9'))
    KR = int(os.environ.get('KR', '9'))
    SEQ, T, NTOK, NTL, NPG, NPOOL = cfg["SEQ"], cfg["T"], cfg["NTOK"], cfg["NTL"], cfg["NPG"], cfg["NPOOL"]
    NFULL = NTL - 1
    LASTW = NTOK - NFULL * 128
    PP2 = 2 * NPG

    def tw(i):
        return 128 if i < NFULL else LASTW

    PS = [b.psum(f"ps{i}", [128, 512], F32) for i in range(8)]
    psk = [f"ps{i}" for i in range(8)]
    psn = [0]

    reserved = set()

    def nextps():
        while True:
            i = psn[0] % 8
            psn[0] += 1
            if i not in reserved:
                return i

    def reserve():
        i = nextps()
        reserved.add(i)
        return i

    def release(i):
        reserved.discard(i)

    def flat(name, dt=F32, n=512):
        return b.sb(name, [128, n], dt)

    def v3(t, w, g=4):
        return t[:, 0:g * w].rearrange("p (g t) -> p g t", g=g)

    ygs = nc.dram_tensor("ygs", [RW, NTL * 128], BF16, kind="Internal").ap()
    ats = nc.dram_tensor("ats", [RW, NTL * 128], BF16, kind="Internal").ap()
    kvs = nc.dram_tensor("kvs", [SPC, KVW], F32, kind="Internal").ap()
    x1s = nc.dram_tensor("x1s", [NTL * 128, D], F32, kind="Internal").ap()

    ident_f = b.sb("ident_f", [128, 128], F32)
    ident_b = b.sb("ident_b", [128, 128], BF16)
    b.dma("sp", "c0", ident_f[:], I["c_ident"], W=["ident_f"])
    b.op("dve", lambda e: e.tensor_copy(out=ident_b[:], in_=ident_f[:]), R=["ident_f"], W=["ident_b"])
    rope = b.sb("rope", [128, NTL, 32], F32)
    b.dma("sp", "c0", rope[:].rearrange("p a b -> p (a b)"), I["c_rope"], W=["rope"])

    def bcast_load(name, src, n):
        t = b.sb(name, [128, n], F32)
        b.dma("sp", "c0", t[:], src.partition_broadcast(128).rearrange("p o n -> p (o n)"), W=[name])
        return t

    gmixB = bcast_load("gmixB", I["g_mix"], D)
    xt = [b.sb(f"xt{i}", [128, D], F32) for i in range(2)]
    junk = b.sb("junk", [128, D], F32)
    hb = b.sb("hb", [128, D], BF16)
    hT = b.sb("hT", [128, 8, 128], BF16)
    st1 = b.sb("st1", [128, 8], F32)
    rt1 = b.sb("rt1", [128, H, 16], F32)
    rt2 = b.sb("rt2", [128, H, 16], F32)
    qsT = b.sb("qsT", [128, H, 16], BF16)

    def load_x(i, buf, src_scratch=False):
        w = tw(i)
        k = f"xt{buf}"
        if src_scratch:
            b.dma("sp", k, xt[buf][0:w, :], x1s[128 * i:128 * i + w, :], R=["x1s"], W=[k])
            return
        if i == 0:
            b.dma("sp", k, xt[buf][0:16, :], I["meta"], W=[k])
            b.dma("sp", k, xt[buf][16:128, :], I["xp"][0:112, :], W=[k])
        elif i < NFULL:
            b.dma("sp", k, xt[buf][:, :], I["xp"][128 * i - 16:128 * i + 112, :], W=[k])
        else:
            b.dma("sp", k, xt[buf][0:16, :], I["xp"][SEQ - 16:SEQ, :], W=[k])
            b.dma("sp", k, xt[buf][16:32, :], I["xs"], W=[k])

    def rms_rstd(src_ap, srckeys, w, n, col, eps):
        b.op("act", lambda e: e.activation(out=junk[0:w, 0:n], in_=src_ap, func=AF.Square), R=srckeys, W=["junk"])
        b.op("dve", lambda e: e.tensor_reduce(out=st1[0:w, col:col + 1], in_=junk[0:w, 0:n], axis=AX.X, op=ALU.add),
             R=["junk"], W=[f"st1_{col}"])
        b.op("act", lambda e: e.activation(out=st1[0:w, col:col + 1], in_=st1[0:w, col:col + 1], func=AF.Sqrt,
                                           bias=eps, scale=1.0 / n), R=[f"st1_{col}"], W=[f"st1_{col}"])
        b.op("dve", lambda e: e.reciprocal(out=st1[0:w, col:col + 1], in_=st1[0:w, col:col + 1]),
             R=[f"st1_{col}"], W=[f"st1_{col}"])

    def rope_tm(dst, src, w, i, nh, keysR, keysW):
        cosb = apx(rope[0:w, i, 0:16], [rope[:].ap[0][0], w], [[0, nh], [1, 16]])
        sinb = apx(rope[0:w, i, 16:32], [rope[:].ap[0][0], w], [[0, nh], [1, 16]])
        x1, x2 = src[:, :, 0:16], src[:, :, 16:32]
        t1, t2 = rt1[0:w, 0:nh, :], rt2[0:w, 0:nh, :]
        b.op("dve", lambda e: e.tensor_tensor(out=t1, in0=x1, in1=cosb, op=ALU.mult), R=keysR + ["rope"], W=["rt1"])
        b.op("dve", lambda e: e.tensor_tensor(out=t2, in0=x2, in1=sinb, op=ALU.mult), R=keysR + ["rope"], W=["rt2"])
        b.op("dve", lambda e: e.tensor_tensor(out=dst[:, :, 0:16], in0=t1, in1=t2, op=ALU.subtract), R=["rt1", "rt2"], W=keysW)
        b.op("dve", lambda e: e.tensor_tensor(out=t1, in0=x1, in1=sinb, op=ALU.mult), R=keysR + ["rope"], W=["rt1"])
        b.op("dve", lambda e: e.tensor_tensor(out=t2, in0=x2, in1=cosb, op=ALU.mult), R=keysR + ["rope"], W=["rt2"])
        b.op("dve", lambda e: e.tensor_tensor(out=dst[:, :, 16:32], in0=t1, in1=t2, op=ALU.add), R=["rt1", "rt2"], W=keysW)

    def norm_and_transpose(i, buf, gB, gkey, dst=None, dkey="hT"):
        w = tw(i)
        k = f"xt{buf}"
        dst = hT if dst is None else dst
        rms_rstd(xt[buf][0:w, :], [k], w, D, 0, NORM_EPS)
        b.op("dve", lambda e: e.scalar_tensor_tensor(out=hb[0:w, :], in0=xt[buf][0:w, :], scalar=st1[0:w, 0:1],
                                                     in1=gB[0:w, :], op0=ALU.mult, op1=ALU.mult),
             R=[k, "st1_0", gkey], W=["hb"])
        p = nextps()
        pv = PS[p][:].bitcast(BF16)
        for kc in range(8):
            b.op("pe", lambda e, kc=kc: e.transpose(out=pv[:, kc * 128:kc * 128 + w], in_=hb[0:w, kc * 128:(kc + 1) * 128],
                                                    identity=ident_b[0:w, 0:w]), R=["hb", "ident_b"], W=[psk[p]])
        src = pv[:, 0:1024].rearrange("p (k t) -> p k t", k=8)[:, :, 0:w]
        b.op("act", lambda e: e.copy(out=dst[:, :, 0:w], in_=src), R=[psk[p]], W=[dkey])

    def load_w_bf(name, shape, src, key=None):
        t = b.sb(name, shape, BF16)
        b.dma("pool", "wA", t[:], src, W=[key or name])
        return t

    b.emit()

    def phase_A1():
        NW = C_RW
        w_in_b = b.sb("w_in_a", [128, 8, NW], BF16)
        for kc in range(8):
            b.dma("pool", "wA", w_in_b[:, kc, :], I["w_in"][kc * 128:(kc + 1) * 128, 0:NW], W=["w_in_a"])
        w_uq_b = load_w_bf("w_uq_b", [128, 3, 768], I["w_uq"].rearrange("(k p) n -> p k n", p=128))
        w_uk_b = load_w_bf("w_uk_b", [128, 2, 512], I["w_uk"].rearrange("(k p) n -> p k n", p=128))
        w_uv_b = load_w_bf("w_uv_b", [128, 2, 512], I["w_uv"].rearrange("(k p) n -> p k n", p=128))
        maskc_b = load_w_bf("maskc_b", [128, 128], I["c_maskc"])
        gqB = bcast_load("gqB", I["g_q"], NQ)
        gkvB = bcast_load("gkvB", I["g_kv"], NKV)
        KT = b.sb("KT", [128, H, NFULL * 128 + 128], BF16)
        Vc = b.sb("Vc", [128, NTL, H, 65], BF16)
        b.op("pool", lambda e: e.memset(Vc[:], 1.0), W=["Vc"])
        qT = b.sb("qT", [128, H, 128], BF16)
        qn = b.sb("qn", [128, NQ], BF16)
        qf32 = b.sb("qf32", [128, 384], F32)
        qnT = b.sb("qnT", [128, 3, 128], BF16)
        qtm = b.sb("qtm", [128, H, 128], BF16)
        b.op("pool", lambda e: e.memset(qtm[:], 0.0), W=["qtm"])
        kvrow = b.sb("kvrow", [128, KVW], F32)
        kvb = b.sb("kvb", [128, 320], BF16)
        b.op("pool", lambda e: e.memset(kvb[:], 0.0), W=["kvb"])
        ckvT = b.sb("ckvT", [128, 2, 128], BF16)
        PT = [b.sb(f"PT{j}", [128, 4, 128], BF16) for j in range(4)]
        attn_tm = b.sb("attn_tm", [128, 512], BF16)
        attnT = b.sb("attnT", [128, 4, 128], BF16)
        rec = b.sb("rec", [128, 8], F32)
        osb = b.sb("osb", [128, 260], F32)

        def tile(i):
            w = tw(i)
            wk = 128 if i < NFULL else 16
            buf = i % 2
            if i + 1 < NTL:
                load_x(i + 1, (i + 1) % 2)
            norm_and_transpose(i, buf, gmixB, "gmixB")
            pq, pkv = nextps(), nextps()
            for kc in range(8):
                b.op("pe", lambda e, kc=kc: e.matmul(PS[pq][0:w, 0:NQ], lhsT=hT[:, kc, 0:w], rhs=w_in_b[:, kc, 0:NQ],
                                                     start=(kc == 0), stop=(kc == 7)), R=["hT", "w_in_a"], W=[psk[pq]])
            for kc in range(8):
                b.op("pe", lambda e, kc=kc: e.matmul(PS[pkv][0:w, 0:KVW], lhsT=hT[:, kc, 0:w], rhs=w_in_b[:, kc, C_KV:C_KV + KVW],
                                                     start=(kc == 0), stop=(kc == 7)), R=["hT", "w_in_a"], W=[psk[pkv]])
            rms_rstd(PS[pkv][0:w, 0:NKV], [psk[pkv]], w, NKV, 1, NORM_EPS)
            b.op("dve", lambda e: e.scalar_tensor_tensor(out=kvrow[0:w, 0:NKV], in0=PS[pkv][0:w, 0:NKV], scalar=st1[0:w, 1:2],
                                                         in1=gkvB[0:w, :], op0=ALU.mult, op1=ALU.mult),
                 R=[psk[pkv], "st1_1", "gkvB"], W=["kvrow"])
            rope_tm(kvrow[0:w, NKV:KVW].rearrange("p (h r) -> p h r", h=1), PS[pkv][0:w, NKV:KVW].rearrange("p (h r) -> p h r", h=1),
                    w, i, 1, [psk[pkv]], ["kvrow"])
            if i < NFULL:
                b.dma("sp", "okv", O["kv_p"][128 * i:128 * i + 128, :], kvrow[:, :], R=["kvrow"])
            else:
                b.dma("sp", "okv", O["kv_p"][128 * i:128 * i + 16, :], kvrow[0:16, :], R=["kvrow"])
                b.dma("sp", "okv", O["kv_s"][:, :], kvrow[16:32, :], R=["kvrow"])
                b.dma("sp", "okv", kvs[:, :], kvrow[16:32, :], R=["kvrow"], W=["kvs"])
            if KQ < 1:
                return
            rms_rstd(PS[pq][0:w, 0:NQ], [psk[pq]], w, NQ, 2, NORM_EPS)
            b.op("dve", lambda e: e.scalar_tensor_tensor(out=qn[0:w, :], in0=PS[pq][0:w, 0:NQ], scalar=st1[0:w, 2:3],
                                                         in1=gqB[0:w, :], op0=ALU.mult, op1=ALU.mult),
                 R=[psk[pq], "st1_2", "gqB"], W=["qn"])
            p = nextps()
            pvb = PS[p][:].bitcast(BF16)
            for kc in range(3):
                b.op("pe", lambda e, kc=kc: e.transpose(out=pvb[:, kc * 128:kc * 128 + w], in_=qn[0:w, kc * 128:(kc + 1) * 128],
                                                        identity=ident_b[0:w, 0:w]), R=["qn", "ident_b"], W=[psk[p]])
            b.op("act", lambda e: e.copy(out=qnT[:, :, 0:w], in_=pvb[:, 0:384].rearrange("p (k t) -> p k t", k=3)[:, :, 0:w]),
                 R=[psk[p]], W=["qnT"])
            if KQ < 2:
                return
            for half in range(2):
                ph = nextps()
                for kc in range(3):
                    b.op("pe", lambda e, kc=kc, ph=ph, half=half: e.matmul(PS[ph][0:w, 0:384], lhsT=qnT[:, kc, 0:w],
                                                                           rhs=w_uq_b[:, kc, half * 384:(half + 1) * 384],
                                                                           start=(kc == 0), stop=(kc == 2)), R=["qnT", "w_uq_b"], W=[psk[ph]])
                b.op("act", lambda e, ph=ph: e.copy(out=qf32[0:w, :], in_=PS[ph][0:w, 0:384]), R=[psk[ph]], W=["qf32"])
                pv4 = qf32[0:w, :].rearrange("p (h x) -> p h x", h=4)
                b.op("pool", lambda e, pv4=pv4, half=half: e.tensor_copy(out=qtm[0:w, 4 * half:4 * half + 4, 0:64], in_=pv4[:, :, 0:64]),
                     R=["qf32"], W=["qtm"])
                rope_tm(qtm[0:w, 4 * half:4 * half + 4, 64:96], pv4[:, :, 64:96], w, i, 4, ["qf32"], ["qtm"])
            if KQ < 3:
                return
            p = nextps()
            pvb = PS[p][:].bitcast(BF16)
            for h in range(H):
                b.op("pe", lambda e, h=h, pvb=pvb: e.transpose(out=pvb[:, h * 128:h * 128 + w], in_=qtm[0:w, h, :], identity=ident_b[0:w, 0:w]),
                     R=["qtm", "ident_b"], W=[psk[p]])
            b.op("act", lambda e, pvb=pvb: e.copy(out=qT[0:96, :, 0:w], in_=pvb[0:96, 0:1024].rearrange("p (h t) -> p h t", h=H)[:, :, 0:w]),
                 R=[psk[p]], W=["qT"])
            if i == NFULL:
                b.op("dve", lambda e: e.tensor_copy(out=qsT[0:96, :, :], in_=qT[0:96, :, 16:32]), R=["qT"], W=["qsT"])
            if KA1 < 2:
                return
            b.op("dve", lambda e: e.tensor_copy(out=kvb[0:w, 0:KVW], in_=kvrow[0:w, :]), R=["kvrow"], W=["kvb"])
            p = nextps()
            pvb = PS[p][:].bitcast(BF16)
            b.op("pe", lambda e, pvb=pvb: e.transpose(out=pvb[:, 0:w], in_=kvb[0:w, 0:128], identity=ident_b[0:w, 0:w]), R=["kvb", "ident_b"], W=[psk[p]])
            b.op("pe", lambda e, pvb=pvb: e.transpose(out=pvb[:, 128:128 + w], in_=kvb[0:w, 128:256], identity=ident_b[0:w, 0:w]), R=["kvb", "ident_b"], W=[psk[p]])
            b.op("pe", lambda e, pvb=pvb: e.transpose(out=pvb[:, 256:256 + w], in_=kvb[0:w, 192:320], identity=ident_b[0:w, 0:w]), R=["kvb", "ident_b"], W=[psk[p]])
            b.op("act", lambda e, pvb=pvb: e.copy(out=ckvT[:, :, 0:w], in_=pvb[:, 0:256].rearrange("p (k t) -> p k t", k=2)[:, :, 0:w]), R=[psk[p]], W=["ckvT"])
            t0 = 128 * i
            ropesrc = apx(pvb[64:96, 256:257], [pvb.ap[0][0], 32], [[0, H], [1, wk]])
            b.op("act", lambda e, ropesrc=ropesrc: e.copy(out=KT[64:96, :, t0:t0 + wk], in_=ropesrc), R=[psk[p]], W=["KT"])
            for g in range(2):
                pk = nextps()
                for hh in range(4):
                    h = 4 * g + hh
                    for kc in range(2):
                        b.op("pe", lambda e, pk=pk, hh=hh, h=h, kc=kc: e.matmul(PS[pk][0:64, hh * 128:hh * 128 + w], lhsT=w_uk_b[:, kc, h * 64:(h + 1) * 64],
                                                                                rhs=ckvT[:, kc, 0:w], start=(kc == 0), stop=(kc == 1)),
                             R=["w_uk_b", "ckvT"], W=[psk[pk]])
                b.op("act", lambda e, pk=pk, g=g: e.copy(out=KT[0:64, 4 * g:4 * g + 4, t0:t0 + wk],
                                                         in_=PS[pk][0:64, 0:512].rearrange("p (h t) -> p h t", h=4)[:, :, 0:wk]), R=[psk[pk]], W=["KT"])
            pvv = nextps()
            for kc in range(2):
                b.op("pe", lambda e, kc=kc: e.matmul(PS[pvv][0:w, 0:512], lhsT=ckvT[:, kc, 0:w], rhs=w_uv_b[:, kc, :], start=(kc == 0), stop=(kc == 1)),
                     R=["ckvT", "w_uv_b"], W=[psk[pvv]])
            b.op("dve", lambda e: e.tensor_copy(out=Vc[0:w, i, :, 0:64], in_=PS[pvv][0:w, 0:512].rearrange("p (h v) -> p h v", h=H)), R=[psk[pvv]], W=["Vc"])
            if KA1 < 3:
                return
            wq = wk
            po = [reserve(), reserve()]
            ptn = [0]
            def stage_a(h, j0):
                pog = po[h // 4]
                ocol = (h % 4) * 65
                js = list(range(j0, min(j0 + 4, i + 1)))
                psc = nextps()
                for jj, j in enumerate(js):
                    wkj = 128 if j < NFULL else 16
                    b.op("pe", lambda e, psc=psc, jj=jj, j=j, h=h, wkj=wkj: e.matmul(PS[psc][0:wkj, jj * 128:jj * 128 + wq], lhsT=KT[0:96, h, j * 128:j * 128 + wkj],
                                                                                      rhs=qT[0:96, h, 0:wq], start=True, stop=True),
                         R=["KT", "qT"], W=[psk[psc]])
                pt = PT[ptn[0] % 4]
                ptk = f"PT{ptn[0] % 4}"
                ptn[0] += 1
                n = len(js)
                wkmin = 128 if js[-1] < NFULL else 16
                if wkmin == 128 or n == 1:
                    wka = 128 if wkmin == 128 else 16
                    b.op("act", lambda e, psc=psc, pt=pt, n=n, wka=wka: e.activation(out=pt[0:wka, 0:n, 0:wq], in_=PS[psc][0:wka, 0:n * 128].rearrange("p (j t) -> p j t", j=n)[:, :, 0:wq],
                                                                                     func=AF.Exp, scale=SM_SCALE), R=[psk[psc]], W=[ptk])
                else:
                    b.op("act", lambda e, psc=psc, pt=pt, n=n: e.activation(out=pt[0:128, 0:n - 1, 0:wq], in_=PS[psc][0:128, 0:(n - 1) * 128].rearrange("p (j t) -> p j t", j=n - 1)[:, :, 0:wq],
                                                                            func=AF.Exp, scale=SM_SCALE), R=[psk[psc]], W=[ptk])
                    b.op("act", lambda e, psc=psc, pt=pt, n=n: e.activation(out=pt[0:16, n - 1, 0:wq], in_=PS[psc][0:16, (n - 1) * 128:(n - 1) * 128 + wq],
                                                                            func=AF.Exp, scale=SM_SCALE), R=[psk[psc]], W=[ptk])
                if js[-1] == i:
                    jj = len(js) - 1
                    b.op("dve", lambda e, pt=pt, jj=jj: e.tensor_tensor(out=pt[0:wq, jj, 0:wq], in0=pt[0:wq, jj, 0:wq], in1=maskc_b[0:wq, 0:wq], op=ALU.mult),
                         R=[ptk, "maskc_b"], W=[ptk])

                def stage_b():
                    for jj, j in enumerate(js):
                        wkj = 128 if j < NFULL else 16
                        b.op("pe", lambda e, jj=jj, j=j, wkj=wkj: e.matmul(PS[pog][0:wq, ocol:ocol + 65], lhsT=pt[0:wkj, jj, 0:wq],
                                                                           rhs=Vc[0:wkj, j, h, :], start=(j == 0), stop=(j == i)),
                             R=[ptk, "Vc"], W=[psk[pog]])
                return stage_b

            pend = []
            for h in range(H):
                for j0 in range(0, i + 1, 4):
                    pend.append(stage_a(h, j0))
                    if len(pend) > 2:
                        pend.pop(0)()
            for fn in pend:
                fn()
            for g in range(2):
                b.op("act", lambda e, g=g: e.copy(out=osb[0:wq, :], in_=PS[po[g]][0:wq, 0:260]), R=[psk[po[g]]], W=["osb"])
                ov = osb[0:wq, :].rearrange("p (h x) -> p h x", h=4)
                b.op("dve", lambda e, ov=ov, g=g: e.reciprocal(out=rec[0:wq, 4 * g:4 * g + 4], in_=ov[:, :, 64]), R=["osb"], W=["rec"])
                rb = apx(rec[0:wq, 4 * g:4 * g + 1], [rec[:].ap[0][0], wq], [[1, 4], [0, 64]])
                b.op("dve", lambda e, ov=ov, g=g, rb=rb: e.tensor_tensor(out=attn_tm[0:wq, 256 * g:256 * g + 256].rearrange("p (h v) -> p h v", h=4),
                                                                         in0=ov[:, :, 0:64], in1=rb, op=ALU.mult), R=["osb", "rec"], W=["attn_tm"])
            release(po[0])
            release(po[1])
            p = nextps()
            pvb = PS[p][:].bitcast(BF16)
            for c4 in range(4):
                b.op("pe", lambda e, c4=c4, pvb=pvb: e.transpose(out=pvb[:, c4 * 128:c4 * 128 + wq], in_=attn_tm[0:wq, c4 * 128:(c4 + 1) * 128], identity=ident_b[0:wq, 0:wq]),
                     R=["attn_tm", "ident_b"], W=[psk[p]])
            b.op("act", lambda e, pvb=pvb: e.copy(out=attnT[:, :, 0:wq], in_=pvb[:, 0:512].rearrange("p (c t) -> p c t", c=4)[:, :, 0:wq]), R=[psk[p]], W=["attnT"])
            for c4 in range(4):
                b.dma("sp", "oat", ats[c4 * 128:(c4 + 1) * 128, t0:t0 + wq], attnT[:, c4, 0:wq], R=["attnT"], W=["ats"])

        if KA1 >= 1:
            load_x(0, 0)
            for i in range(NTL):
                tile(i)

    with ExitStack() as pst:
        b.st = pst
        phase_A1()
        b.barrier()
        b.emit()
    b.st = b.st0


    def phase_S():
        w_uk_b = load_w_bf("w_uk_s", [128, 2, 512], I["w_uk"].rearrange("(k p) n -> p k n", p=128))
        w_uv_b = load_w_bf("w_uv_s", [128, 2, 512], I["w_uv"].rearrange("(k p) n -> p k n", p=128))
        w_ukT = b.sb("w_ukT", [64, H, 256], BF16)
        qfT = b.sb("qfT", [128, 3, SPC, H], BF16)
        b.op("pool", lambda e: e.memset(qfT[:], 0.0), W=["qfT"])
        ones_f = b.sb("ones_f", [1, 128], F32)
        b.op("pool", lambda e: e.memset(ones_f[:], 1.0), W=["ones_f"])
        idx = b.sb("idx", [128, 8], I32)
        idx8 = b.sb("idx8", [128, 8], I32)
        b.op("pool", lambda e: e.memset(idx[:], 0), W=["idx"])
        b.dma("sp", "c0", idx[0:PP2, :], I["pt"].rearrange("(pi s2) g -> (s2 g) pi", s2=2), W=["idx"], allow_slow_non_contiguous=True)
        b.op("dve", lambda e: e.tensor_scalar(out=idx8[:], in0=idx[:], scalar1=8, scalar2=None, op0=ALU.mult), R=["idx"], W=["idx8"])
        kvg = [b.sb(f"kvg{j}", [128, 16, KVW], F32) for j in range(2)]
        kvh = [b.sb(f"kvh{j}", [128, 16, 320], BF16) for j in range(2)]
        for j in range(2):
            b.op("pool", lambda e, j=j: e.memset(kvh[j][:], 0.0), W=[f"kvh{j}"])
            b.op("pool", lambda e, j=j: e.memset(kvh[j][:, :, 288:290], 1.0), W=[f"kvh{j}"])
        kvn = b.sb("kvn", [128, 8, KVW], F32)
        kvnb = b.sb("kvnb", [128, 8, 320], BF16)
        b.op("pool", lambda e: e.memset(kvn[:], 0.0), W=["kvn"])
        b.op("pool", lambda e: e.memset(kvnb[:], 0.0), W=["kvnb"])
        b.op("pool", lambda e: e.memset(kvnb[:, :, 288:290], 1.0), W=["kvnb"])
        kv2 = kvs.rearrange("(pi s2) c -> s2 pi c", s2=2)
        b.dma("sp", "c0", kvn[0:1, :, :], kv2[0:1], R=["kvs"], W=["kvn"])
        b.dma("sp", "c0", kvn[NPG:NPG + 1, :, :], kv2[1:2], R=["kvs"], W=["kvn"])
        b.op("dve", lambda e: e.tensor_copy(out=kvnb[:, :, 0:KVW], in_=kvn[:, :, :]), R=["kvn"], W=["kvnb"])
        KT3 = [b.sb(f"KT3_{j}", [128, 6, 128], BF16) for j in range(4)]
        PTs = [b.sb(f"PTs{j}", [128, 16, 16], BF16) for j in range(2)]
        PTn = b.sb("PTn", [128, 16], BF16)
        for j in range(2):
            b.op("pool", lambda e, j=j: e.memset(PTs[j][:], 0.0), W=[f"PTs{j}"])
        b.op("pool", lambda e: e.memset(PTn[:], 0.0), W=["PTn"])
        mloc = b.sb("mloc", [128, 1], F32)
        scs = b.sb("scs", [128, 256], F32)
        b.op("pool", lambda e: e.memset(mloc[:], 0.0), W=["mloc"])
        m2 = b.sb("m2", [1, 2], F32)
        negm = b.sb("negm", [128, 2], F32)
        olat = b.sb("olat", [16, 256], BF16)
        rl = b.sb("rl", [16, 1], F32)
        olatT = b.sb("olatT", [128, 2, 8, 16], BF16)
        attn_sT = b.sb("attn_sT", [128, 4, 16], BF16)

        for h in range(H):
            p = nextps()
            pvb = PS[p][:].bitcast(BF16)
            for kc in range(2):
                b.op("pe", lambda e, h=h, kc=kc, pvb=pvb: e.transpose(out=pvb[0:64, kc * 128:(kc + 1) * 128], in_=w_uk_b[:, kc, h * 64:(h + 1) * 64], identity=ident_b[:, :]),
                     R=["w_uk_s", "ident_b"], W=[psk[p]])
            b.op("act", lambda e, h=h, pvb=pvb: e.copy(out=w_ukT[0:64, h, :], in_=pvb[0:64, 0:256]), R=[psk[p]], W=["w_ukT"])
        for h in range(H):
            p = nextps()
            for kc in range(2):
                b.op("pe", lambda e, h=h, kc=kc, p=p: e.matmul(PS[p][:, kc * 16:(kc + 1) * 16], lhsT=w_ukT[0:64, h, kc * 128:(kc + 1) * 128], rhs=qsT[0:64, h, :],
                                                              start=True, stop=True), R=["w_ukT", "qsT"], W=[psk[p]])
            b.op("act", lambda e, h=h, p=p: e.copy(out=qfT[:, 0:2, :, h], in_=PS[p][:, 0:32].rearrange("p (k s) -> p k s", k=2)), R=[psk[p]], W=["qfT"])
        b.op("dve", lambda e: e.tensor_copy(out=qfT[64:96, 2, :, :], in_=qsT[64:96, :, :].rearrange("p h s -> p s h")), R=["qsT"], W=["qfT"])

        cache8 = I["cache"].rearrange("n (g r) c -> (n g) (r c)", r=16)

        def kt_scores(src, r0, nr, bufk, psc, col0, pi):
            pend = []
            for g0 in range(0, nr, 2):
                rs = list(range(g0, min(g0 + 2, nr)))
                kb = bufk[0] % 4
                bufk[0] += 1
                kt, ktk = KT3[kb], f"KT3_{kb}"
                p = nextps()
                pvb = PS[p][:].bitcast(BF16)
                for rr, r in enumerate(rs):
                    for kc, (c0, c1, m) in enumerate(((0, 128, 128), (128, 256, 128), (192, 320, 128))):
                        b.op("pe", lambda e, pvb=pvb, rr=rr, r=r, kc=kc, c0=c0, c1=c1, m=m: e.transpose(out=pvb[0:m, (rr * 3 + kc) * 128:(rr * 3 + kc) * 128 + PP2],
                                                                                                   in_=src[0:PP2, r0 + r, c0:c1], identity=ident_b[0:PP2, 0:PP2]),
                             R=[src_key[0], "ident_b"], W=[psk[p]])
                n3 = 3 * len(rs)
                eng = "act" if (bufk[0] % 2) else "dve"
                if eng == "act":
                    b.op("act", lambda e, kt=kt, pvb=pvb, n3=n3: e.copy(out=kt[:, 0:n3, 0:PP2], in_=pvb[:, 0:n3 * 128].rearrange("p (a t) -> p a t", a=n3)[:, :, 0:PP2]),
                         R=[psk[p]], W=[ktk])
                else:
                    b.op("dve", lambda e, kt=kt, pvb=pvb, n3=n3: e.tensor_copy(out=kt[:, 0:n3, 0:PP2], in_=pvb[:, 0:n3 * 128].rearrange("p (a t) -> p a t", a=n3)[:, :, 0:PP2]),
                         R=[psk[p]], W=[ktk])
                def stage_b(rs=rs, kt=kt, ktk=ktk):
                    for rr, r in enumerate(rs):
                        for kc, m in enumerate((128, 128, 96)):
                            b.op("pe", lambda e, rr=rr, r=r, kc=kc, m=m: e.matmul(PS[psc][0:PP2, col0 + 16 * r:col0 + 16 * r + 16], lhsT=kt[0:m, rr * 3 + kc, 0:PP2],
                                                                                rhs=qfT[0:m, kc, 2 * pi:2 * pi + 2, :].rearrange("p s h -> p (s h)"),
                                                                                start=(kc == 0), stop=(kc == 2)), R=[ktk, "qfT"], W=[psk[psc]])
                pend.append(stage_b)
                if len(pend) > 2:
                    pend.pop(0)()
            for fn in pend:
                fn()

        src_key = ["kvh0"]
        bufk = [0]
        cn = [0]
        for pi in range(SPC // 2):
            pacc = reserve()
            for ck in range(8):
                cb = cn[0] % 2
                cn[0] += 1
                gk, hk = f"kvg{cb}", f"kvh{cb}"
                b.raw_dma("pool", gk, lambda e, cb=cb, ck=ck, pi=pi: e.indirect_dma_start(
                    out=kvg[cb][0:PP2, :, :].rearrange("p a b -> p (a b)"), out_offset=None, in_=cache8[:, :], element_offset=ck * 16 * KVW,
                    in_offset=bass.IndirectOffsetOnAxis(ap=idx8[0:PP2, pi:pi + 1], axis=0)), R=["idx8"], W=[gk])
                ceng = "dve" if ck % 2 == 0 else "pool"
                b.op(ceng, lambda e, cb=cb: e.tensor_copy(out=kvh[cb][0:PP2, :, 0:KVW], in_=kvg[cb][0:PP2, :, :]), R=[gk], W=[hk])
                src_key[0] = hk
                psc = reserve()
                kt_scores(kvh[cb], 0, 16, bufk, psc, 0, pi)
                release(psc)
                sc3 = PS[psc][0:PP2, 0:256].rearrange("p (r x) -> p r x", r=16)
                if ck == 0:
                    b.op("act", lambda e, psc=psc: e.copy(out=scs[0:PP2, :], in_=PS[psc][0:PP2, 0:256]), R=[psk[psc]], W=["scs"])
                    ss3 = scs[0:PP2, :].rearrange("p (r x) -> p r x", r=16)
                    b.op("dve", lambda e, ss3=ss3: e.tensor_reduce(out=mloc[0:NPG, :], in_=ss3[0:NPG, :, 0:8], axis=AX.XY, op=ALU.max), R=["scs"], W=["mloc"])
                    b.op("dve", lambda e, ss3=ss3: e.tensor_reduce(out=mloc[NPG:PP2, :], in_=ss3[NPG:PP2, :, 8:16], axis=AX.XY, op=ALU.max), R=["scs"], W=["mloc"])
                    pm_ = nextps()
                    b.op("pe", lambda e, pm_=pm_: e.transpose(out=PS[pm_][0:1, 0:PP2], in_=mloc[0:PP2, 0:1], identity=ident_f[0:PP2, 0:PP2]), R=["mloc", "ident_f"], W=[psk[pm_]])
                    b.op("dve", lambda e, pm_=pm_: e.tensor_reduce(out=m2[0:1, 0:2], in_=PS[pm_][0:1, 0:PP2].rearrange("p (a g) -> p a g", a=2), axis=AX.X, op=ALU.max),
                         R=[psk[pm_]], W=["m2"])
                    b.op("dve", lambda e: e.tensor_scalar(out=m2[0:1, 0:2], in0=m2[0:1, 0:2], scalar1=-SM_SCALE, scalar2=None, op0=ALU.mult), R=["m2"], W=["m2"])
                    pb_ = nextps()
                    b.op("pe", lambda e, pb_=pb_: e.matmul(PS[pb_][0:PP2, 0:2], lhsT=ones_f[0:1, 0:PP2], rhs=m2[0:1, 0:2], start=True, stop=True), R=["ones_f", "m2"], W=[psk[pb_]])
                    b.op("act", lambda e, pb_=pb_: e.copy(out=negm[0:PP2, :], in_=PS[pb_][0:PP2, 0:2]), R=[psk[pb_]], W=["negm"])
                pt, ptk = PTs[cb], f"PTs{cb}"
                b.op("act", lambda e, pt=pt, sc3=sc3: e.activation(out=pt[0:NPG, :, 0:8], in_=sc3[0:NPG, :, 0:8], func=AF.Exp, scale=SM_SCALE, bias=negm[0:NPG, 0:1]),
                     R=[psk[psc], "negm"], W=[ptk])
                b.op("act", lambda e, pt=pt, sc3=sc3: e.activation(out=pt[NPG:PP2, :, 8:16], in_=sc3[NPG:PP2, :, 8:16], func=AF.Exp, scale=SM_SCALE, bias=negm[NPG:PP2, 1:2]),
                     R=[psk[psc], "negm"], W=[ptk])
                for r in range(16):
                    b.op("pe", lambda e, pt=pt, cb=cb, r=r, ck=ck, pacc=pacc: e.matmul(PS[pacc][0:16, 0:290], lhsT=pt[0:PP2, r, :], rhs=kvh[cb][0:PP2, r, 0:290],
                                                                                         start=(ck == 0 and r == 0), stop=False), R=[ptk, hk], W=[psk[pacc]])
            src_key[0] = "kvnb"
            psc = reserve()
            kt_scores(kvnb, pi, 1, bufk, psc, 0, pi)
            release(psc)
            b.op("act", lambda e, psc=psc: e.activation(out=PTn[0:1, 0:8], in_=PS[psc][0:1, 0:8], func=AF.Exp, scale=SM_SCALE, bias=negm[0:1, 0:1]),
                 R=[psk[psc], "negm"], W=["PTn"])
            b.op("act", lambda e, psc=psc: e.activation(out=PTn[NPG:NPG + 1, 8:16], in_=PS[psc][NPG:NPG + 1, 8:16], func=AF.Exp, scale=SM_SCALE, bias=negm[NPG:NPG + 1, 1:2]),
                 R=[psk[psc], "negm"], W=["PTn"])
            b.op("pe", lambda e, pi=pi, pacc=pacc: e.matmul(PS[pacc][0:16, 0:290], lhsT=PTn[0:PP2, :], rhs=kvnb[0:PP2, pi, 0:290], start=False, stop=True),
                 R=["PTn", "kvnb"], W=[psk[pacc]])
            b.op("dve", lambda e, pacc=pacc: e.reciprocal(out=rl[:, :], in_=PS[pacc][0:16, 288:289]), R=[psk[pacc]], W=["rl"])
            b.op("dve", lambda e, pacc=pacc: e.tensor_scalar(out=olat[:, :], in0=PS[pacc][0:16, 0:256], scalar1=rl[:, 0:1], scalar2=None, op0=ALU.mult),
                 R=[psk[pacc], "rl"], W=["olat"])
            p = nextps()
            pvb = PS[p][:].bitcast(BF16)
            for kc in range(2):
                b.op("pe", lambda e, kc=kc, pvb=pvb: e.transpose(out=pvb[:, kc * 16:(kc + 1) * 16], in_=olat[0:16, kc * 128:(kc + 1) * 128], identity=ident_b[0:16, 0:16]),
                     R=["olat", "ident_b"], W=[psk[p]])
            release(pacc)
            b.op("act", lambda e, pi=pi, pvb=pvb: e.copy(out=olatT[:, :, pi, :], in_=pvb[:, 0:32].rearrange("p (k x) -> p k x", k=2)), R=[psk[p]], W=["olatT"])
        p = nextps()
        ol5 = olatT[:].rearrange("p k a (s h) -> p k a s h", s=2)
        for h in range(H):
            pr = slice((h % 2) * 64, (h % 2) * 64 + 64)
            for kc in range(2):
                b.op("pe", lambda e, h=h, kc=kc, pr=pr, p=p: e.matmul(PS[p][pr, h * 16:(h + 1) * 16], lhsT=w_uv_b[:, kc, h * 64:(h + 1) * 64], rhs=ol5[:, kc, :, :, h],
                                                                     start=(kc == 0), stop=(kc == 1)), R=["w_uv_s", "olatT"], W=[psk[p]])
        for h in range(H):
            pr = slice((h % 2) * 64, (h % 2) * 64 + 64)
            b.op("act", lambda e, h=h, pr=pr, p=p: e.copy(out=attn_sT[pr, h // 2, :], in_=PS[p][pr, h * 16:(h + 1) * 16]), R=[psk[p]], W=["attn_sT"])
        for c4 in range(4):
            b.dma("sp", "oat", ats[c4 * 128:(c4 + 1) * 128, NFULL * 128 + 16:NFULL * 128 + 32], attn_sT[:, c4, :], R=["attn_sT"], W=["ats"])

    if stage >= 3:
        with ExitStack() as pst:
            b.st = pst
            phase_S()
            b.barrier()
            b.emit()
        b.st = b.st0


    pst = ExitStack()
    b.st = pst
    w_in_r = b.sb("w_in_r", [128, 8, RC], BF16)
    for kc in range(8):
        b.dma("pool", "wA", w_in_r[:, kc, :], I["w_in"][kc * 128:(kc + 1) * 128, C_RW:C_G], W=["w_in_r"])
    CH = BF16

    def flat(name, dt=F32, n=512):
        return b.sb(name, [128, n], dt)

    def v3(t, w, g=4):
        return t[:, 0:g * w].rearrange("p (g t) -> p g t", g=g)

    ppar = b.sb("ppar", [128, 64], F32)

    def ld_pp(src, c0, n):
        b.dma("sp", "c0", ppar[:, c0:c0 + n], src.rearrange("o (c p) -> p (o c)", p=128), W=["ppar"],
              allow_slow_non_contiguous=True)
    PP_MU, PP_W0, PP_A0, PP_KK, PP_KA, PP_RK, PP_LNW, PP_LNB, PP_OMKA = 0, 14, 18, 22, 26, 30, 34, 38, 42
    ld_pp(I["mu_shift"], PP_MU, 14)
    ld_pp(I["w0"], PP_W0, 4)
    ld_pp(I["a0"], PP_A0, 4)
    ld_pp(I["k_k"], PP_KK, 4)
    ld_pp(I["k_a"], PP_KA, 4)
    ld_pp(I["r_k"], PP_RK, 4)
    ld_pp(I["ln_w"], PP_LNW, 4)
    ld_pp(I["ln_b"], PP_LNB, 4)
    b.op("dve", lambda e: e.tensor_scalar(out=ppar[:, PP_OMKA:PP_OMKA + 4], in0=ppar[:, PP_KA:PP_KA + 4], scalar1=-1.0,
                                          scalar2=1.0, op0=ALU.mult, op1=ALU.add), R=["ppar"], W=["ppar"])
    w2_b = b.sb("w2_b", [128, RW], BF16)
    a2_b = b.sb("a2_b", [128, RW], BF16)
    g2_b = b.sb("g2_b", [128, RW], BF16)
    b.dma("pool", "wA", w2_b[0:64, :], I["w2"], W=["w2_b"])
    b.dma("pool", "wA", a2_b[64:128, :], I["a2"], W=["a2_b"])
    b.dma("pool", "wA", g2_b[:, :], I["g2"], W=["g2_b"])
    maskx_b = b.sb("maskx_b", [128, 128], BF16)
    maskjt_b = b.sb("maskjt_b", [128, 2, 128], BF16)
    cmask_b = b.sb("cmask_b", [128, 8], BF16)
    identb_b = b.sb("identb_b", [128, 64], BF16)
    onehot_b = b.sb("onehot_b", [128, 16, 128], BF16)
    b.dma("pool", "wA", maskx_b[:], I["c_maskx"], W=["maskx_b"])
    b.dma("pool", "wA", maskjt_b[:].rearrange("p a b -> p (a b)"), I["c_maskjt"], W=["maskjt_b"])
    b.dma("pool", "wA", cmask_b[:], I["c_cmask"], W=["cmask_b"])
    b.dma("pool", "wA", identb_b[:], I["c_identb"], W=["identb_b"])
    b.dma("pool", "wA", onehot_b[:].rearrange("p a b -> p (a b)"), I["c_onehot"], W=["onehot_b"])
    blk1_f = b.sb("blk1_f", [128, 128], F32)
    blk64_f = b.sb("blk64_f", [128, 128], F32)
    scanm = b.sb("scanm", [128, 512], F32)
    b.dma("sp", "c0", blk1_f[:], I["c_blk1"], W=["blk1_f"])
    b.dma("sp", "c0", scanm[:], I["c_scanm"], W=["scanm"])
    b.op("dve", lambda e: e.tensor_scalar(out=blk64_f[:], in0=blk1_f[:], scalar1=1.0 / 64, scalar2=None, op0=ALU.mult),
         R=["blk1_f"], W=["blk64_f"])

    cT = b.sb("cT", [128, 14, 129], F32)
    zt = b.sb("zt", [128, 14, 128], F32)
    b.op("pool", lambda e: e.memset(cT[:], 0.0), W=["cT"])
    sshT = b.sb("sshT", [128, 14, 16], F32)
    sshtm = b.sb("sshtm", [16, RC], F32)
    f = {n: flat("f_" + n) for n in ["ld", "asig", "kk", "t", "bb", "kp", "L", "Lx", "eL", "enL", "eLC"]}
    gsb = flat("gsb", BF16)
    tanhw = flat("tanhw", BF16, 128)
    alb = flat("alb", BF16, 128)
    sgb = flat("sgb", BF16, 128)
    vb = flat("vb", BF16)
    AR = b.sb("AR", [128, 1024], BF16)
    Kt, Bt, Kb, Bb = (flat(n, BF16) for n in ["Kt", "Bt", "Kb", "Bb"])
    Vtm = b.sb("Vtm", [128, 4, 128], BF16)
    NU = 4
    UB = []
    for uu in range(NU):
        UB.append((b.sb(f"YA{uu}", [128, 2, 128], BF16), b.sb(f"NK{uu}", [128, 2, 128], BF16), b.sb(f"Xm{uu}", [128, 128], BF16),
                   [b.sb(f"XY{j}_{uu}", [128, 2, 128], BF16) for j in range(2)], b.sb(f"Y8_{uu}", [128, 128], BF16),
                   [b.sb(f"Z{j}_{uu}", [128, 128], BF16) for j in range(2)], b.sb(f"KBtm{uu}", [128, 2, 64], BF16),
                   b.sb(f"Bexp{uu}", [128, 8, 64], BF16), b.sb(f"Vexp{uu}", [128, 8, 64], BF16), b.sb(f"Vhexp{uu}", [128, 8, 64], BF16)))
    Wd = b.sb("Wd", [128, 4, 8, 64], BF16)
    RhT = flat("RhT", CH)
    MT = [b.sb(f"MT{hp}", [128, 8, 128], CH) for hp in range(4)]
    GB = [b.sb(f"GB{hp}", [128, 8, 128], CH) for hp in range(4)]
    Sbd = [[b.sb(f"S{hp}_{j}", [128, 128], CH) for j in range(2)] for hp in range(4)]
    scur = [0, 0, 0, 0]
    for hp in range(4):
        b.op("pool", lambda e, hp=hp: e.memset(MT[hp][:], 0.0), W=[f"MT{hp}"])
        b.op("pool", lambda e, hp=hp: e.memset(GB[hp][:], 0.0), W=[f"GB{hp}"])
        b.op("pool", lambda e, hp=hp: e.memset(Sbd[hp][0][:], 0.0), W=[f"S{hp}_0"])
        b.op("pool", lambda e, hp=hp: e.memset(Sbd[hp][1][:], 0.0), W=[f"S{hp}_1"])
    ygT = flat("ygT", BF16)
    tm5 = b.sb("tm5", [48, 5, 256], BF16)
    xb = b.sb("xb", [128, 5, 4, 16], BF16)
    b.op("pool", lambda e: e.memset(tm5[:], 0.0), W=["tm5"])
    Sx = [b.sb(f"Sx{j}", [128, 4, 64], F32) for j in range(2)]
    stmp = b.sb("stmp", [128, 4, 64], F32)
    ssa = b.sb("ssa", [128, 4], F32)
    wkvo = b.sb("wkvo", [128, 4, 128], F32)

    b.dma("sp", "c0", sshtm[:], I["sshift"], W=["sshtm"])

    def prep_ssh():
        for g in range(4):
            js = list(range(4 * g, min(4 * g + 4, 14)))
            p = nextps()
            for jj, j in enumerate(js):
                b.op("pe", lambda e, jj=jj, j=j, p=p: e.transpose(out=PS[p][:, jj * 16:(jj + 1) * 16], in_=sshtm[0:16, j * 128:(j + 1) * 128],
                                                              identity=ident_f[0:16, 0:16]), R=["sshtm", "ident_f"], W=[psk[p]])
            n = len(js)
            b.op("act", lambda e, p=p, n=n, j0=js[0]: e.copy(out=sshT[:, j0:j0 + n, :], in_=PS[p][:, 0:n * 16].rearrange("p (j t) -> p j t", j=n)),
                 R=[psk[p]], W=["sshT"])
    prep_ssh()

    def pp(c0, hp):
        return ppar[:, c0 + hp:c0 + hp + 1]

    def rwkv_tile(i):
        w = tw(i)
        wu = 128 if i < NFULL else 16
        nch = wu // 16
        last = (i == NFULL)
        if i > 0:
            b.op("dve", lambda e: e.tensor_copy(out=cT[:, :, 0:1], in_=cT[:, :, 128:129]), R=["cT"], W=["cT"])
        for g in range(4):
            js = list(range(4 * g, min(4 * g + 4, 14)))
            p = nextps()
            for jj, j in enumerate(js):
                for kc in range(8):
                    b.op("pe", lambda e, jj=jj, j=j, kc=kc, p=p: e.matmul(PS[p][:, jj * 128:jj * 128 + w], lhsT=w_in_r[:, kc, 128 * j:128 * (j + 1)],
                                                                           rhs=hT[:, kc, 0:w], start=(kc == 0), stop=(kc == 7)),
                         R=["hT", "w_in_r"], W=[psk[p]])
            n = len(js)
            b.op("act", lambda e, p=p, n=n, j0=js[0]: e.copy(out=cT[:, j0:j0 + n, 1:1 + w],
                                                            in_=PS[p][:, 0:n * 128].rearrange("p (j t) -> p j t", j=n)[:, :, 0:w]),
                 R=[psk[p]], W=["cT"])
        wp = wu if last else w
        b.op("dve", lambda e: e.tensor_tensor(out=zt[:, :, 0:wp], in0=cT[:, :, 0:wp], in1=cT[:, :, 1:1 + wp], op=ALU.subtract),
             R=["cT"], W=["zt"])
        if last:
            b.op("dve", lambda e: e.tensor_tensor(out=zt[:, :, 16:32], in0=sshT[:, :, :], in1=cT[:, :, 17:33], op=ALU.subtract),
                 R=["cT", "sshT"], W=["zt"])
        for j in range(14):
            b.op("dve", lambda e, j=j: e.scalar_tensor_tensor(out=zt[:, j, 0:w], in0=zt[:, j, 0:w], scalar=ppar[:, PP_MU + j:PP_MU + j + 1],
                                                              in1=cT[:, j, 1:1 + w], op0=ALU.mult, op1=ALU.add),
                 R=["zt", "cT", "ppar"], W=["zt"])
        r3, k3, v3_ = zt[:, 0:4, 0:w], zt[:, 4:8, 0:w], zt[:, 8:12, 0:w]
        W4 = 4 * w
        ld3, as3, kk3, t3, b3, kp3 = (v3(f[n], w) for n in ["ld", "asig", "kk", "t", "bb", "kp"])
        b.op("act", lambda e: e.activation(out=tanhw[0:64, 0:w], in_=zt[0:64, 12, 0:w], func=AF.Tanh), R=["zt"], W=["tanhw"])
        b.op("pool", lambda e: e.tensor_copy(out=alb[64:128, 0:w], in_=zt[64:128, 12, 0:w]), R=["zt"], W=["alb"])
        b.op("act", lambda e: e.activation(out=sgb[:, 0:w], in_=zt[:, 13, 0:w], func=AF.Sigmoid), R=["zt"], W=["sgb"])
        b.op("pool", lambda e: e.tensor_copy(out=v3(vb, w), in_=v3_), R=["zt"], W=["vb"])
        pW, pA, pG = nextps(), nextps(), nextps()
        for hp in range(4):
            b.op("pe", lambda e, hp=hp: e.matmul(PS[pW][:, hp * w:(hp + 1) * w], lhsT=w2_b[0:64, hp * 128:(hp + 1) * 128], rhs=tanhw[0:64, 0:w],
                                                 start=True, stop=True), R=["w2_b", "tanhw"], W=[psk[pW]])
        for hp in range(4):
            b.op("pe", lambda e, hp=hp: e.matmul(PS[pA][:, hp * w:(hp + 1) * w], lhsT=a2_b[64:128, hp * 128:(hp + 1) * 128], rhs=alb[64:128, 0:w],
                                                 start=True, stop=True), R=["a2_b", "alb"], W=[psk[pA]])
        for hp in range(4):
            b.op("pe", lambda e, hp=hp: e.matmul(PS[pG][:, hp * w:(hp + 1) * w], lhsT=g2_b[:, hp * 128:(hp + 1) * 128], rhs=sgb[:, 0:w],
                                                 start=True, stop=True), R=["g2_b", "sgb"], W=[psk[pG]])
        for hp in range(4):
            b.op("act", lambda e, hp=hp: e.activation(out=ld3[:, hp, :], in_=PS[pW][:, hp * w:(hp + 1) * w], func=AF.Sigmoid, bias=pp(PP_W0, hp)),
                 R=[psk[pW], "ppar"], W=["f_ld"])
            b.op("act", lambda e, hp=hp: e.activation(out=as3[:, hp, :], in_=PS[pA][:, hp * w:(hp + 1) * w], func=AF.Sigmoid, bias=pp(PP_A0, hp)),
                 R=[psk[pA], "ppar"], W=["f_asig"])
        b.op("act", lambda e: e.copy(out=gsb[:, 0:W4], in_=PS[pG][:, 0:W4]), R=[psk[pG]], W=["gsb"])
        b.op("dve", lambda e: e.tensor_scalar(out=f["ld"][:, 0:W4], in0=f["ld"][:, 0:W4], scalar1=-0.6065306597126334, scalar2=None, op0=ALU.mult),
             R=["f_ld"], W=["f_ld"])
        for hp in range(4):
            b.op("act", lambda e, hp=hp: e.activation(out=kk3[:, hp, :], in_=zt[:, 4 + hp, 0:w], func=AF.Copy, scale=pp(PP_KK, hp)),
                 R=["zt", "ppar"], W=["f_kk"])
        b.op("act", lambda e: e.activation(out=f["t"][:, 0:W4], in_=f["kk"][:, 0:W4], func=AF.Square), R=["f_kk"], W=["f_t"])
        pN = nextps()
        b.op("pe", lambda e: e.matmul(PS[pN][:, 0:W4], lhsT=blk1_f[:], rhs=f["t"][:, 0:W4], start=True, stop=True),
             R=["blk1_f", "f_t"], W=[psk[pN]])
        b.op("act", lambda e: e.activation(out=f["t"][:, 0:W4], in_=PS[pN][:, 0:W4], func=AF.Sqrt), R=[psk[pN]], W=["f_t"])
        b.op("dve", lambda e: e.tensor_scalar(out=f["t"][:, 0:W4], in0=f["t"][:, 0:W4], scalar1=1e-12, scalar2=None, op0=ALU.max),
             R=["f_t"], W=["f_t"])
        b.op("dve", lambda e: e.reciprocal(out=f["t"][:, 0:W4], in_=f["t"][:, 0:W4]), R=["f_t"], W=["f_t"])
        b.op("dve", lambda e: e.tensor_tensor(out=f["kk"][:, 0:W4], in0=f["kk"][:, 0:W4], in1=f["t"][:, 0:W4], op=ALU.mult),
             R=["f_kk", "f_t"], W=["f_kk"])
        b.op("dve", lambda e: e.tensor_tensor(out=f["bb"][:, 0:W4], in0=f["kk"][:, 0:W4], in1=f["asig"][:, 0:W4], op=ALU.mult),
             R=["f_kk", "f_asig"], W=["f_bb"])
        for hp in range(4):
            b.op("dve", lambda e, hp=hp: e.tensor_scalar(out=t3[:, hp, :], in0=as3[:, hp, :], scalar1=pp(PP_KA, hp), scalar2=pp(PP_OMKA, hp),
                                                         op0=ALU.mult, op1=ALU.add), R=["f_asig", "ppar"], W=["f_t"])
        b.op("dve", lambda e: e.tensor_tensor(out=kp3, in0=k3, in1=t3, op=ALU.mult), R=["zt", "f_t"], W=["f_kp"])
        b.op("dve", lambda e: e.tensor_tensor_scan(out=f["L"][:, 0:W4], data0=scanm[:, 0:W4], data1=f["ld"][:, 0:W4], initial=0.0,
                                                   op0=ALU.mult, op1=ALU.add), R=["scanm", "f_ld"], W=["f_L"])
        b.op("dve", lambda e: e.tensor_tensor(out=f["Lx"][:, 0:W4], in0=f["L"][:, 0:W4], in1=f["ld"][:, 0:W4], op=ALU.subtract),
             R=["f_L", "f_ld"], W=["f_Lx"])
        ng = W4 // 16
        Lg = f["L"][:, 0:W4].rearrange("p (g t) -> p g t", t=16)
        Lend = apx(f["L"][:, 15:16], [f["L"][:].ap[0][0], 128], [[16, ng], [0, 16]])
        b.op("dve", lambda e: e.tensor_tensor(out=f["eLC"][:, 0:W4].rearrange("p (g t) -> p g t", t=16), in0=Lend, in1=Lg, op=ALU.subtract),
             R=["f_L"], W=["f_eLC"])
        b.op("act", lambda e: e.activation(out=f["eL"][:, 0:W4], in_=f["L"][:, 0:W4], func=AF.Exp), R=["f_L"], W=["f_eL"])
        b.op("act", lambda e: e.activation(out=f["enL"][:, 0:W4], in_=f["L"][:, 0:W4], func=AF.Exp, scale=-1.0), R=["f_L"], W=["f_enL"])
        b.op("act", lambda e: e.activation(out=f["Lx"][:, 0:W4], in_=f["Lx"][:, 0:W4], func=AF.Exp), R=["f_Lx"], W=["f_Lx"])
        b.op("act", lambda e: e.activation(out=f["eLC"][:, 0:W4], in_=f["eLC"][:, 0:W4], func=AF.Exp), R=["f_eLC"], W=["f_eLC"])
        AR4 = AR[:, 0:2 * W4].rearrange("p (h two t) -> p h two t", h=4, two=2)
        b.op("dve", lambda e: e.scalar_tensor_tensor(out=AR4[:, :, 0, :], in0=kk3, scalar=-1.0, in1=v3(f["Lx"], w), op0=ALU.mult, op1=ALU.mult),
             R=["f_kk", "f_Lx"], W=["AR"])
        b.op("dve", lambda e: e.tensor_tensor(out=AR4[:, :, 1, :], in0=r3, in1=v3(f["eL"], w), op=ALU.mult), R=["zt", "f_eL"], W=["AR"])
        b.op("dve", lambda e: e.tensor_tensor(out=Kt[:, 0:W4], in0=f["kp"][:, 0:W4], in1=f["enL"][:, 0:W4], op=ALU.mult), R=["f_kp", "f_enL"], W=["Kt"])
        b.op("dve", lambda e: e.tensor_tensor(out=Bt[:, 0:W4], in0=f["bb"][:, 0:W4], in1=f["enL"][:, 0:W4], op=ALU.mult), R=["f_bb", "f_enL"], W=["Bt"])
        b.op("pool", lambda e: e.tensor_tensor(out=Kb[:, 0:W4], in0=f["kp"][:, 0:W4], in1=f["eLC"][:, 0:W4], op=ALU.mult), R=["f_kp", "f_eLC"], W=["Kb"])
        b.op("pool", lambda e: e.tensor_tensor(out=Bb[:, 0:W4], in0=f["bb"][:, 0:W4], in1=f["eLC"][:, 0:W4], op=ALU.mult), R=["f_bb", "f_eLC"], W=["Bb"])
        for hp in range(4):
            wc = apx(f["eL"][:, hp * w + 15:hp * w + 16], [f["eL"][:].ap[0][0], 128], [[16, nch], [0, 64]])
            idb = apx(identb_b[:, 0:1], [identb_b[:].ap[0][0], 128], [[0, nch], [1, 64]])
            b.op("dve", lambda e, hp=hp, wc=wc, idb=idb: e.tensor_tensor(out=Wd[:, hp, 0:nch, :], in0=idb, in1=wc, op=ALU.mult),
                 R=["f_eL", "identb_b"], W=["Wd"])
        p = nextps()
        pvb = PS[p][:].bitcast(BF16)
        for hp in range(4):
            b.op("pe", lambda e, hp=hp, p=p: e.transpose(out=pvb[0:w, hp * 128:(hp + 1) * 128], in_=vb[:, hp * w:(hp + 1) * w], identity=ident_b[:, :]),
                 R=["vb", "ident_b"], W=[psk[p]])
        b.op("act", lambda e, p=p: e.copy(out=Vtm[0:w, :, :], in_=pvb[0:w, 0:512].rearrange("p (h x) -> p h x", h=4)), R=[psk[p]], W=["Vtm"])

        ARf = AR4
        units = [(hp, h2) for hp in range(4) for h2 in range(2)]
        for w0 in range(0, 8, NU):
            gens = [unit(i, hp, h2, wu, nch, w, uu) for uu, (hp, h2) in enumerate(units[w0:w0 + NU])]
            while gens:
                for g in list(gens):
                    try:
                        next(g)
                    except StopIteration:
                        gens.remove(g)
        pY = reserve()
        for c in range(nch):
            for hp in range(4):
                cur = scur[hp]
                S0, S1 = Sbd[hp][cur], Sbd[hp][1 - cur]
                k0, k1 = f"S{hp}_{cur}", f"S{hp}_{1 - cur}"
                b.op("pe", lambda e, hp=hp, c=c, S0=S0: e.matmul(PS[pY][:, hp * 128 + c * 16:hp * 128 + c * 16 + 16], lhsT=S0[:, :],
                                                                rhs=RhT[:, hp * 128 + c * 16:hp * 128 + c * 16 + 16], start=True, stop=True),
                     R=[k0, "RhT"], W=[psk[pY]])
                ps_ = nextps()
                b.op("pe", lambda e, hp=hp, c=c, S0=S0, ps_=ps_: e.matmul(PS[ps_][:, 0:128], lhsT=MT[hp][:, c, :], rhs=S0[:, :], start=True, stop=True),
                     R=[k0, f"MT{hp}"], W=[psk[ps_]])
                b.op("dve", lambda e, hp=hp, c=c, S1=S1, ps_=ps_: e.tensor_tensor(out=S1[:, :], in0=PS[ps_][:, 0:128], in1=GB[hp][:, c, :], op=ALU.add),
                     R=[psk[ps_], f"GB{hp}"], W=[k1])
                scur[hp] = 1 - cur
        ysb = f["L"]
        y3 = v3(ysb, w)
        for hp in range(4):
            b.op("dve", lambda e, hp=hp: e.tensor_tensor(out=y3[:, hp, 0:wu], in0=PS[pY][:, hp * 128:hp * 128 + wu], in1=f["enL"][:, hp * 128:hp * 128 + wu], op=ALU.add),
                 R=[psk[pY], "f_enL"], W=["f_L"])
        release(pY)
        if last:
            sample_rwkv(w, y3)
            wkv_prompt_out()
        finalize(i, w, y3)

    def unit(i, hp, h2, wu, nch, w, u):
        pr = slice(h2 * 64, h2 * 64 + 64)
        YA, NK, Xm, XY, Y8, Zs, KBtm, Bexp, Vexp, Vhexp = UB[u]
        sf = f"_{u}"
        AR4 = AR[:, 0:8 * w].rearrange("p (h two t) -> p h two t", h=4, two=2)
        At = AR4[pr, hp, 0, 0:wu]
        ARr = AR[pr, hp * 2 * w:(hp + 1) * 2 * w].rearrange("p (two t) -> p two t", two=2)[:, :, 0:wu]
        Kt_ = Kt[pr, hp * w:hp * w + wu]
        Bt_ = Bt[pr, hp * w:hp * w + wu]
        Kb_ = Kb[pr, hp * w:hp * w + wu]
        Bb_ = Bb[pr, hp * w:hp * w + wu]
        Vh = Vtm[0:wu, hp, h2 * 64:(h2 + 1) * 64]
        pa, pb = nextps(), nextps()
        o1 = PS[pa][0:wu, 0:2 * wu].rearrange("p (two t) -> p two t", two=2)
        o2 = PS[pa][0:wu, 256:256 + 2 * wu].rearrange("p (two t) -> p two t", two=2)
        b.op("pe", lambda e: e.matmul(o1, lhsT=Bt_, rhs=ARr, start=True, stop=True), R=["Bt", "AR"], W=[psk[pa]])
        b.op("pe", lambda e: e.matmul(o2, lhsT=Kt_, rhs=ARr, start=True, stop=True), R=["Kt", "AR"], W=[psk[pa]])
        b.op("pe", lambda e: e.matmul(PS[pb][0:wu, 0:wu], lhsT=At, rhs=Bt_, start=True, stop=True), R=["Bt", "AR"], W=[psk[pb]])
        mj = maskjt_b[0:wu, :, 0:wu]
        b.op("dve", lambda e: e.tensor_tensor(out=YA[0:wu, :, 0:wu], in0=o1, in1=mj, op=ALU.mult), R=[psk[pa], "maskjt_b"], W=["YA" + sf])
        b.op("dve", lambda e: e.tensor_tensor(out=NK[0:wu, :, 0:wu], in0=o2, in1=mj, op=ALU.mult), R=[psk[pa], "maskjt_b"], W=["NK" + sf])
        b.op("dve", lambda e: e.tensor_tensor(out=Xm[0:wu, 0:wu], in0=PS[pb][0:wu, 0:wu], in1=maskx_b[0:wu, 0:wu], op=ALU.mult),
             R=[psk[pb], "maskx_b"], W=["Xm" + sf])
        yield
        Y1, Arb, Nak, Ark = YA[0:wu, 0, 0:wu], YA[0:wu, 1, 0:wu], NK[0:wu, 0, 0:wu], NK[0:wu, 1, 0:wu]
        X1 = Xm[0:wu, 0:wu]
        Xp, Yp, Ypow = X1, Y1, [Y1]
        for lvl in range(2):
            pc = nextps()
            oc = PS[pc][0:wu, 0:2 * wu].rearrange("p (two t) -> p two t", two=2)
            rk = ["YA" + sf, "Xm" + sf] if lvl == 0 else [f"XY{lvl - 1}" + sf]
            b.op("pe", lambda e, oc=oc, Xp=Xp, Yp=Yp: e.matmul(oc[:, 0, :], lhsT=Yp, rhs=Xp, start=True, stop=True), R=rk, W=[psk[pc]])
            b.op("pe", lambda e, oc=oc, Xp=Xp, Yp=Yp: e.matmul(oc[:, 1, :], lhsT=Xp, rhs=Yp, start=True, stop=True), R=rk, W=[psk[pc]])
            b.op("act", lambda e, oc=oc, lvl=lvl: e.copy(out=XY[lvl][0:wu, :, 0:wu], in_=oc), R=[psk[pc]], W=[f"XY{lvl}" + sf])
            yield
            Xp, Yp = XY[lvl][0:wu, 0, 0:wu], XY[lvl][0:wu, 1, 0:wu]
            Ypow.append(Yp)
        pc = nextps()
        b.op("pe", lambda e, Xp=Xp, Yp=Yp, pc=pc: e.matmul(PS[pc][0:wu, 0:wu], lhsT=Xp, rhs=Yp, start=True, stop=True), R=["XY1" + sf], W=[psk[pc]])
        b.op("act", lambda e, pc=pc: e.copy(out=Y8[0:wu, 0:wu], in_=PS[pc][0:wu, 0:wu]), R=[psk[pc]], W=["Y8" + sf])
        Ypow.append(Y8[0:wu, 0:wu])
        yield
        ykeys = [["YA" + sf], ["XY0" + sf], ["XY1" + sf], ["Y8" + sf]]
        pz = nextps()
        pzb = PS[pz][:].bitcast(BF16)
        b.op("pe", lambda e: e.transpose(out=pzb[0:wu, 0:64], in_=At, identity=ident_b[pr, pr]), R=["AR", "ident_b"], W=[psk[pz]])
        b.op("pe", lambda e: e.matmul(PS[pz][0:wu, 64:128], lhsT=Nak, rhs=Vh, start=True, stop=True), R=["NK" + sf, "Vtm"], W=[psk[pz]])
        b.op("act", lambda e: e.copy(out=Zs[0][0:wu, 0:64], in_=pzb[0:wu, 0:64]), R=[psk[pz]], W=["Z0" + sf])
        b.op("act", lambda e: e.copy(out=Zs[0][0:wu, 64:128], in_=PS[pz][0:wu, 64:128]), R=[psk[pz]], W=["Z0" + sf])
        yield
        zc = 0
        for lvl in range(4):
            pq_ = nextps()
            b.op("pe", lambda e, lvl=lvl, zc=zc, pq_=pq_: e.matmul(PS[pq_][0:wu, 0:128], lhsT=Ypow[lvl], rhs=Zs[zc][0:wu, :], start=True, stop=True),
                 R=ykeys[lvl] + [f"Z{zc}" + sf], W=[psk[pq_]])
            b.op("dve", lambda e, zc=zc, pq_=pq_: e.tensor_tensor(out=Zs[1 - zc][0:wu, :], in0=PS[pq_][0:wu, 0:128], in1=Zs[zc][0:wu, :], op=ALU.add),
                 R=[psk[pq_], f"Z{zc}" + sf], W=[f"Z{1 - zc}" + sf])
            zc = 1 - zc
            yield
        Z4 = Zs[zc]
        zk = f"Z{zc}" + sf
        Ah, Vhh = Z4[0:wu, 0:64], Z4[0:wu, 64:128]
        pRY = reserve()
        b.op("pe", lambda e: e.matmul(PS[pRY][pr, 0:wu], lhsT=Ah, rhs=Arb, start=True, stop=True), R=[zk, "YA" + sf], W=[psk[pRY]])
        b.op("pe", lambda e: e.matmul(PS[pRY][pr, 128:128 + wu], lhsT=Vh, rhs=Ark, start=True, stop=False), R=["Vtm", "NK" + sf], W=[psk[pRY]])
        b.op("pe", lambda e: e.matmul(PS[pRY][pr, 128:128 + wu], lhsT=Vhh, rhs=Arb, start=False, stop=True), R=[zk, "YA" + sf], W=[psk[pRY]])
        yield
        pt_ = nextps()
        ptb = PS[pt_][:].bitcast(BF16)
        b.op("pe", lambda e: e.transpose(out=ptb[0:wu, 0:64], in_=Bb_, identity=ident_b[pr, pr]), R=["Bb", "ident_b"], W=[psk[pt_]])
        b.op("pe", lambda e: e.transpose(out=ptb[0:wu, 64:128], in_=Kb_, identity=ident_b[pr, pr]), R=["Kb", "ident_b"], W=[psk[pt_]])
        b.op("act", lambda e: e.copy(out=KBtm[0:wu, :, :], in_=ptb[0:wu, 0:128].rearrange("p (a k) -> p a k", a=2)), R=[psk[pt_]], W=["KBtm" + sf])
        yield

        def bc_c(ap2):
            return apx(ap2, [ap2.ap[0][0], wu], [[0, nch], [1, 64]])
        cmb = apx(cmask_b[0:wu, 0:1], [cmask_b[:].ap[0][0], wu], [[1, nch], [0, 64]])
        b.op("dve", lambda e: e.tensor_tensor(out=Bexp[0:wu, 0:nch, :], in0=bc_c(KBtm[0:wu, 0, :]), in1=cmb, op=ALU.mult), R=["KBtm" + sf, "cmask_b"], W=["Bexp" + sf])
        b.op("pool", lambda e: e.tensor_tensor(out=Vexp[0:wu, 0:nch, :], in0=bc_c(Vh), in1=cmb, op=ALU.mult), R=["Vtm", "cmask_b"], W=["Vexp" + sf])
        b.op("dve", lambda e: e.tensor_tensor(out=Vhexp[0:wu, 0:nch, :], in0=bc_c(Vhh), in1=cmb, op=ALU.mult), R=[zk, "cmask_b"], W=["Vhexp" + sf])
        yield
        N = nch * 64
        pM, pGm = nextps(), nextps()
        b.op("pe", lambda e: e.matmul(PS[pM][pr, 0:N], lhsT=Ah, rhs=Bexp[0:wu, 0:nch, :].rearrange("p c k -> p (c k)"), start=True, stop=False),
             R=[zk, "Bexp" + sf], W=[psk[pM]])
        b.op("pe", lambda e: e.matmul(PS[pM][pr, 0:N], lhsT=ident_b[pr, pr], rhs=Wd[pr, hp, 0:nch, :].rearrange("p c k -> p (c k)"), start=False, stop=True),
             R=["ident_b", "Wd"], W=[psk[pM]])
        b.op("pe", lambda e: e.matmul(PS[pGm][pr, 0:N], lhsT=KBtm[0:wu, 1, :], rhs=Vexp[0:wu, 0:nch, :].rearrange("p c k -> p (c k)"), start=True, stop=False),
             R=["KBtm" + sf, "Vexp" + sf], W=[psk[pGm]])
        b.op("pe", lambda e: e.matmul(PS[pGm][pr, 0:N], lhsT=KBtm[0:wu, 0, :], rhs=Vhexp[0:wu, 0:nch, :].rearrange("p c k -> p (c k)"), start=False, stop=True),
             R=["KBtm" + sf, "Vhexp" + sf], W=[psk[pGm]])
        b.op("act", lambda e: e.copy(out=MT[hp][pr, 0:nch, pr], in_=PS[pM][pr, 0:nch * 64].rearrange("p (c k) -> p c k", c=nch)),
             R=[psk[pM]], W=[f"MT{hp}"])
        b.op("dve", lambda e: e.tensor_copy(out=GB[hp][pr, 0:nch, pr], in_=PS[pGm][pr, 0:nch * 64].rearrange("p (c k) -> p c k", c=nch)),
             R=[psk[pGm]], W=[f"GB{hp}"])
        ARf = AR[:, 0:8 * w].rearrange("p (h two t) -> p h two t", h=4, two=2)
        b.op("dve", lambda e: e.tensor_tensor(out=RhT[pr, hp * 128:hp * 128 + wu], in0=PS[pRY][pr, 0:wu], in1=ARf[pr, hp, 1, 0:wu], op=ALU.add),
             R=[psk[pRY], "AR"], W=["RhT"])
        b.op("act", lambda e: e.copy(out=f["enL"][pr, hp * 128:hp * 128 + wu], in_=PS[pRY][pr, 128:128 + wu]),
             R=[psk[pRY]], W=["f_enL"])
        release(pRY)

    def finalize(i, w, y3):
        W4 = 4 * w
        ysb = f["L"]
        dd, sq, tt = f["Lx"], f["eL"], f["eLC"]
        pm = nextps()
        b.op("pe", lambda e: e.matmul(PS[pm][:, 0:W4], lhsT=blk64_f[:], rhs=ysb[:, 0:W4], start=True, stop=True), R=["blk64_f", "f_L"], W=[psk[pm]])
        b.op("dve", lambda e: e.tensor_tensor(out=dd[:, 0:W4], in0=ysb[:, 0:W4], in1=PS[pm][:, 0:W4], op=ALU.subtract), R=["f_L", psk[pm]], W=["f_Lx"])
        b.op("act", lambda e: e.activation(out=sq[:, 0:W4], in_=dd[:, 0:W4], func=AF.Square), R=["f_Lx"], W=["f_eL"])
        pv_ = nextps()
        b.op("pe", lambda e: e.matmul(PS[pv_][:, 0:W4], lhsT=blk64_f[:], rhs=sq[:, 0:W4], start=True, stop=True), R=["blk64_f", "f_eL"], W=[psk[pv_]])
        b.op("act", lambda e: e.activation(out=sq[:, 0:W4], in_=PS[pv_][:, 0:W4], func=AF.Sqrt, bias=GN_EPS), R=[psk[pv_]], W=["f_eL"])
        b.op("dve", lambda e: e.reciprocal(out=sq[:, 0:W4], in_=sq[:, 0:W4]), R=["f_eL"], W=["f_eL"])
        b.op("dve", lambda e: e.tensor_tensor(out=dd[:, 0:W4], in0=dd[:, 0:W4], in1=sq[:, 0:W4], op=ALU.mult), R=["f_Lx", "f_eL"], W=["f_Lx"])
        d3, t3, kp3 = v3(dd, w), v3(tt, w), v3(f["kp"], w)
        for hp in range(4):
            b.op("dve", lambda e, hp=hp: e.tensor_scalar(out=d3[:, hp, :], in0=d3[:, hp, :], scalar1=pp(PP_LNW, hp), scalar2=pp(PP_LNB, hp),
                                                         op0=ALU.mult, op1=ALU.add), R=["f_Lx", "ppar"], W=["f_Lx"])
            b.op("dve", lambda e, hp=hp: e.scalar_tensor_tensor(out=t3[:, hp, :], in0=zt[:, hp, 0:w], scalar=pp(PP_RK, hp), in1=kp3[:, hp, :],
                                                                op0=ALU.mult, op1=ALU.mult), R=["zt", "f_kp", "ppar"], W=["f_eLC"])
        pb_ = nextps()
        b.op("pe", lambda e: e.matmul(PS[pb_][:, 0:W4], lhsT=blk1_f[:], rhs=tt[:, 0:W4], start=True, stop=True), R=["blk1_f", "f_eLC"], W=[psk[pb_]])
        b.op("dve", lambda e: e.tensor_tensor(out=t3, in0=PS[pb_][:, 0:W4].rearrange("p (h t) -> p h t", h=4), in1=zt[:, 8:12, 0:w], op=ALU.mult),
             R=[psk[pb_], "zt"], W=["f_eLC"])
        b.op("dve", lambda e: e.tensor_tensor(out=dd[:, 0:W4], in0=dd[:, 0:W4], in1=tt[:, 0:W4], op=ALU.add), R=["f_Lx", "f_eLC"], W=["f_Lx"])
        b.op("dve", lambda e: e.tensor_tensor(out=ygT[:, 0:W4], in0=dd[:, 0:W4], in1=gsb[:, 0:W4], op=ALU.mult), R=["f_Lx", "gsb"], W=["ygT"])
        for hp in range(4):
            b.dma("sp", "oyg", ygs[hp * 128:(hp + 1) * 128, i * 128:i * 128 + w], ygT[:, hp * w:(hp + 1) * w], R=["ygT"], W=["ygs"])

    def sample_rwkv(w, y3):
        W4 = 4 * w
        b.op("act", lambda e: e.activation(out=f["t"][:, 0:W4], in_=f["ld"][:, 0:W4], func=AF.Exp), R=["f_ld"], W=["f_t"])
        b.op("dve", lambda e: e.tensor_scalar(out=f["asig"][:, 0:W4], in0=f["kk"][:, 0:W4], scalar1=-1.0, scalar2=None, op0=ALU.mult),
             R=["f_kk"], W=["f_asig"])
        srcs = [(v3(f["t"], w), "f_t"), (v3(f["asig"], w), "f_asig"), (v3(f["bb"], w), "f_bb"), (v3(f["kp"], w), "f_kp"), (zt[:, 0:4, 0:w], "zt")]
        for qi, (src, key) in enumerate(srcs):
            b.op("dve", lambda e, qi=qi, src=src: e.tensor_copy(out=xb[:, qi, :, :], in_=src[:, :, 16:32]), R=[key], W=["xb"])
            p = nextps()
            pvb_ = PS[p][:].bitcast(BF16)
            for hp in range(4):
                for h2 in range(2):
                    pr = slice(h2 * 64, h2 * 64 + 64)
                    b.op("pe", lambda e, pvb_=pvb_, qi=qi, hp=hp, h2=h2, pr=pr: e.transpose(out=pvb_[h2 * 32:h2 * 32 + 16, hp * 64:(hp + 1) * 64], in_=xb[pr, qi, hp, :],
                                                                                          identity=ident_b[pr, pr]),
                         R=["xb", "ident_b"], W=[psk[p]])
            for h2 in range(2):
                b.op("act", lambda e, pvb_=pvb_, h2=h2, qi=qi: e.copy(out=tm5[h2 * 32:h2 * 32 + 16, qi, :], in_=pvb_[h2 * 32:h2 * 32 + 16, 0:256]),
                     R=[psk[p]], W=["tm5"])
        for s in range(SPC):
            sx = Sx[s % 2]
            sk = f"Sx{s % 2}"
            for h2 in range(2):
                src = I["swkv"][s].rearrange("(hp two) v k -> two v hp k", two=2)[h2]
                b.dma("sp", sk, sx[h2 * 64:(h2 + 1) * 64, :, :], src, W=[sk])
            pbs = [nextps() for _ in range(5)]
            for qi in range(5):
                b.op("pe", lambda e, qi=qi, s=s, pbs=pbs: e.matmul(PS[pbs[qi]][:, 0:256], lhsT=onehot_b[0:48, s, :], rhs=tm5[0:48, qi, :], start=True, stop=True),
                     R=["onehot_b", "tm5"], W=[psk[pbs[qi]]])

            def bq(qi, pbs=pbs):
                return PS[pbs[qi]][:, 0:256].rearrange("p (h k) -> p h k", h=4)
            col = 16 + s
            vcol = apx(zt[:, 8, col:col + 1], [zt[:].ap[0][0], 128], [[128, 4], [0, 64]])
            sab = apx(ssa[:, 0:1], [ssa[:].ap[0][0], 128], [[1, 4], [0, 64]])
            b.op("dve", lambda e, sx=sx, bq=bq: e.tensor_tensor(out=stmp[:], in0=sx[:], in1=bq(1), op=ALU.mult), R=[sk, psk[pbs[1]]], W=["stmp"])
            b.op("dve", lambda e: e.tensor_reduce(out=ssa[:, :], in_=stmp[:], axis=AX.X, op=ALU.add), R=["stmp"], W=["ssa"])
            b.op("dve", lambda e, sx=sx, bq=bq: e.tensor_tensor(out=sx[:], in0=sx[:], in1=bq(0), op=ALU.mult), R=[sk, psk[pbs[0]]], W=[sk])
            b.op("dve", lambda e, bq=bq, sab=sab: e.tensor_tensor(out=stmp[:], in0=sab, in1=bq(2), op=ALU.mult), R=["ssa", psk[pbs[2]]], W=["stmp"])
            b.op("dve", lambda e, sx=sx: e.tensor_tensor(out=sx[:], in0=sx[:], in1=stmp[:], op=ALU.add), R=[sk, "stmp"], W=[sk])
            b.op("dve", lambda e, bq=bq, vcol=vcol: e.tensor_tensor(out=stmp[:], in0=vcol, in1=bq(3), op=ALU.mult), R=["zt", psk[pbs[3]]], W=["stmp"])
            b.op("dve", lambda e, sx=sx: e.tensor_tensor(out=sx[:], in0=sx[:], in1=stmp[:], op=ALU.add), R=[sk, "stmp"], W=[sk])
            b.op("dve", lambda e, sx=sx, bq=bq: e.tensor_tensor(out=stmp[:], in0=sx[:], in1=bq(4), op=ALU.mult), R=[sk, psk[pbs[4]]], W=["stmp"])
            b.op("dve", lambda e, col=col: e.tensor_reduce(out=y3[:, :, col], in_=stmp[:], axis=AX.X, op=ALU.add), R=["stmp"], W=["f_L"])
            for h2 in range(2):
                dst = O["wkv_s"][s].rearrange("(hp two) v k -> two v hp k", two=2)[h2]
                b.dma("sp", "owkvs", dst, sx[h2 * 64:(h2 + 1) * 64, :, :], R=[sk])

    def wkv_prompt_out():
        for hp in range(4):
            S0 = Sbd[hp][scur[hp]]
            k0 = f"S{hp}_{scur[hp]}"
            p = nextps()
            pb_ = PS[p][:].bitcast(BF16) if CH == BF16 else PS[p][:]
            idn = ident_b if CH == BF16 else ident_f
            b.op("pe", lambda e, S0=S0, pb_=pb_, idn=idn: e.transpose(out=pb_[:, 0:128], in_=S0[:, :], identity=idn[:, :]), R=[k0, "ident_b", "ident_f"], W=[psk[p]])
            b.op("act", lambda e, hp=hp, pb_=pb_: e.copy(out=wkvo[:, hp, :], in_=pb_[:, 0:128]), R=[psk[p]], W=["wkvo"])
            for h2 in range(2):
                b.dma("sp", "owkvp", O["wkv_p"][2 * hp + h2], wkvo[h2 * 64:(h2 + 1) * 64, hp, h2 * 64:(h2 + 1) * 64], R=["wkvo"])


    def tile_A2(i):
        w = tw(i)
        buf = i % 2
        if i + 1 < NTL:
            load_x(i + 1, (i + 1) % 2)
        norm_and_transpose(i, buf, gmixB, "gmixB")
        if i == NFULL:
            shrow = zt[:].rearrange("p a b -> p (a b)")
            for piece in range(4):
                p = nextps()
                c0 = piece * 448
                for kc in range(8):
                    b.op("pe", lambda e, kc=kc, p=p, c0=c0: e.matmul(PS[p][0:w, 0:448], lhsT=hT[:, kc, 0:w], rhs=w_in_r[:, kc, c0:c0 + 448],
                                                                     start=(kc == 0), stop=(kc == 7)), R=["hT", "w_in_r"], W=[psk[p]])
                b.op("act", lambda e, p=p, piece=piece: e.copy(out=shrow[0:w, piece * 448:(piece + 1) * 448], in_=PS[p][0:w, 0:448]),
                     R=[psk[p]], W=["zt"])
            b.dma("sp", "osh", O["sh_p"][:, :], shrow[15:16, :], R=["zt"])
            b.dma("sp", "osh", O["sh_s"][:, :], shrow[16:32, :], R=["zt"])
        rwkv_tile(i)

    if stage >= 2:
        load_x(0, 0)
        for i in range(NTL):
            tile_A2(i)
    b.barrier()
    b.emit()
    pst.close()
    b.st = b.st0

    def phase_B():
        wg_b = b.sb("wg_b", [128, 8, 2048], BF16)
        for kc in range(8):
            b.dma("pool", "wA", wg_b[:, kc, :], I["w_in"][kc * 128:(kc + 1) * 128, C_G:INC], W=["wg_b"])
        w_om = load_w_bf("w_om", [128, 4, D], I["w_o_mla"].rearrange("(k p) n -> p k n", p=128))
        w_or = load_w_bf("w_or", [128, 4, D], I["w_o_rwkv"].rearrange("(k p) n -> p k n", p=128))
        w_ob = load_w_bf("w_ob", [128, 8, D], I["w_out"].rearrange("(k p) n -> p k n", p=128))
        gsg = b.sb("gsg", [128, 16, 128], BF16)
        b.op("pool", lambda e: e.memset(gsg[:], 0.0), W=["gsg"])
        atT = b.sb("atT", [128, 4, 128], BF16)
        ygTt = b.sb("ygTt", [128, 4, 128], BF16)
        mT = b.sb("mT", [128, 8, 128], BF16)
        tmpa = b.sb("tmpa", [128, 512], F32)
        tmpb = b.sb("tmpb", [128, 512], F32)
        x1 = b.sb("x1", [128, D], F32)

        def tile(i):
            w = tw(i)
            buf = i % 2
            t0 = 128 * i
            if i + 1 < NTL:
                load_x(i + 1, (i + 1) % 2)
            b.dma("sp", "ldat", atT[:, :, 0:w], ats[:, t0:t0 + w].rearrange("(c p) t -> p c t", p=128), R=["ats"], W=["atT"])
            b.dma("sp", "ldyg", ygTt[:, :, 0:w], ygs[:, t0:t0 + w].rearrange("(c p) t -> p c t", p=128), R=["ygs"], W=["ygTt"])
            norm_and_transpose(i, buf, gmixB, "gmixB")
            for g in range(4):
                p = nextps()
                for jj in range(4):
                    j = 4 * g + jj
                    for kc in range(8):
                        b.op("pe", lambda e, p=p, jj=jj, j=j, kc=kc: e.matmul(PS[p][:, jj * 128:jj * 128 + w], lhsT=wg_b[:, kc, j * 128:(j + 1) * 128], rhs=hT[:, kc, 0:w],
                                                                               start=(kc == 0), stop=(kc == 7)), R=["wg_b", "hT"], W=[psk[p]])
                b.op("act", lambda e, p=p, g=g: e.activation(out=gsg[:, 4 * g:4 * g + 4, 0:w], in_=PS[p][:, 0:512].rearrange("p (j t) -> p j t", j=4)[:, :, 0:w], func=AF.Sigmoid),
                     R=[psk[p]], W=["gsg"])
            for g in range(2):
                pm_, pr_ = nextps(), nextps()
                for jj in range(4):
                    j = 4 * g + jj
                    for kc in range(4):
                        b.op("pe", lambda e, pm_=pm_, jj=jj, j=j, kc=kc: e.matmul(PS[pm_][:, jj * 128:jj * 128 + w], lhsT=w_om[:, kc, j * 128:(j + 1) * 128], rhs=atT[:, kc, 0:w],
                                                                                   start=(kc == 0), stop=(kc == 3)), R=["w_om", "atT"], W=[psk[pm_]])
                    for kc in range(4):
                        b.op("pe", lambda e, pr_=pr_, jj=jj, j=j, kc=kc: e.matmul(PS[pr_][:, jj * 128:jj * 128 + w], lhsT=w_or[:, kc, j * 128:(j + 1) * 128], rhs=ygTt[:, kc, 0:w],
                                                                                   start=(kc == 0), stop=(kc == 3)), R=["w_or", "ygTt"], W=[psk[pr_]])
                v1 = PS[pm_][:, 0:512].rearrange("p (j t) -> p j t", j=4)
                v2 = PS[pr_][:, 0:512].rearrange("p (j t) -> p j t", j=4)
                ta = tmpa[:, 0:512].rearrange("p (j t) -> p j t", j=4)
                tb = tmpb[:, 0:512].rearrange("p (j t) -> p j t", j=4)
                b.op("dve", lambda e, v1=v1, ta=ta, g=g: e.tensor_tensor(out=ta, in0=v1, in1=gsg[:, 4 * g:4 * g + 4, :], op=ALU.mult), R=[psk[pm_], "gsg"], W=["tmpa"])
                b.op("dve", lambda e, v2=v2, tb=tb, g=g: e.tensor_tensor(out=tb, in0=v2, in1=gsg[:, 8 + 4 * g:8 + 4 * g + 4, :], op=ALU.mult), R=[psk[pr_], "gsg"], W=["tmpb"])
                b.op("pool", lambda e, ta=ta, tb=tb, g=g: e.tensor_tensor(out=mT[:, 4 * g:4 * g + 4, :], in0=ta, in1=tb, op=ALU.add), R=["tmpa", "tmpb"], W=["mT"])
            for half in range(2):
                po_ = nextps()
                for kc in range(8):
                    b.op("pe", lambda e, po_=po_, kc=kc, half=half: e.matmul(PS[po_][0:w, 0:512], lhsT=mT[:, kc, 0:w], rhs=w_ob[:, kc, half * 512:(half + 1) * 512],
                                                                             start=(kc == 0), stop=(kc == 7)), R=["mT", "w_ob"], W=[psk[po_]])
                b.op("dve", lambda e, po_=po_, half=half: e.tensor_tensor(out=x1[0:w, half * 512:(half + 1) * 512], in0=PS[po_][0:w, 0:512], in1=xt[buf][0:w, half * 512:(half + 1) * 512], op=ALU.add),
                     R=[psk[po_], f"xt{buf}"], W=["x1"])
            b.dma("sp", "ox1", x1s[t0:t0 + w, :], x1[0:w, :], R=["x1"], W=["x1s"])

        load_x(0, 0)
        for i in range(NTL):
            tile(i)

    if stage >= 4:
        with ExitStack() as pst:
            b.st = pst
            phase_B()
            b.barrier()
            b.emit()
        b.st = b.st0

    def phase_C():
        w_up_b = b.sb("w_up_b", [128, 8, DFF], BF16)
        for kc in range(8):
            b.dma("pool", "wA", w_up_b[:, kc, :], I["w_up"][kc * 128:(kc + 1) * 128, :], W=["w_up_b"])
        w_dn_b = b.sb("w_dn_b", [128, 32, D], BF16)
        for k4 in range(8):
            b.dma("pool", "wA", w_dn_b[:, 4 * k4:4 * k4 + 4, :], I["w_down"][512 * k4:512 * (k4 + 1), :].rearrange("(k p) n -> p k n", p=128), W=["w_dn_b"])
        gffnB = bcast_load("gffnB", I["g_ffn"], D)
        gfinB = bcast_load("gfinB", I["g_final"], D)
        uT = b.sb("uT", [128, 32, 128], BF16)
        rl_ = b.sb("relu_t", [128, 512], BF16)
        x2 = b.sb("x2", [128, D], F32)
        yo = b.sb("yo", [128, D], F32)

        def tile(i):
            w = tw(i)
            buf = i % 2
            t0 = 128 * i
            if i + 1 < NTL:
                load_x(i + 1, (i + 1) % 2, src_scratch=True)
            norm_and_transpose(i, buf, gffnB, "gffnB")
            for g in range(8):
                p = nextps()
                for jj in range(4):
                    j = 4 * g + jj
                    for kc in range(8):
                        b.op("pe", lambda e, p=p, jj=jj, j=j, kc=kc: e.matmul(PS[p][:, jj * 128:jj * 128 + w], lhsT=w_up_b[:, kc, j * 128:(j + 1) * 128], rhs=hT[:, kc, 0:w],
                                                                               start=(kc == 0), stop=(kc == 7)), R=["w_up_b", "hT"], W=[psk[p]])
                rv = rl_[:, 0:4 * w].rearrange("p (j t) -> p j t", j=4)
                b.op("act", lambda e, p=p, rv=rv: e.activation(out=rv, in_=PS[p][:, 0:512].rearrange("p (j t) -> p j t", j=4)[:, :, 0:w], func=AF.Relu), R=[psk[p]], W=["relu_t"])
                b.op("pool", lambda e, g=g, rv=rv: e.tensor_tensor(out=uT[:, 4 * g:4 * g + 4, 0:w], in0=rv, in1=rv, op=ALU.mult), R=["relu_t"], W=["uT"])
            for half in range(2):
                po_ = nextps()
                for kc in range(32):
                    b.op("pe", lambda e, po_=po_, kc=kc, half=half: e.matmul(PS[po_][0:w, 0:512], lhsT=uT[:, kc, 0:w], rhs=w_dn_b[:, kc, half * 512:(half + 1) * 512],
                                                                             start=(kc == 0), stop=(kc == 31)), R=["uT", "w_dn_b"], W=[psk[po_]])
                b.op("dve", lambda e, po_=po_, half=half: e.tensor_tensor(out=x2[0:w, half * 512:(half + 1) * 512], in0=PS[po_][0:w, 0:512], in1=xt[buf][0:w, half * 512:(half + 1) * 512], op=ALU.add),
                     R=[psk[po_], f"xt{buf}"], W=["x2"])
            rms_rstd(x2[0:w, :], ["x2"], w, D, 3, NORM_EPS)
            b.op("dve", lambda e: e.scalar_tensor_tensor(out=yo[0:w, :], in0=x2[0:w, :], scalar=st1[0:w, 3:4], in1=gfinB[0:w, :], op0=ALU.mult, op1=ALU.mult),
                 R=["x2", "st1_3", "gfinB"], W=["yo"])
            if i == 0:
                b.dma("sp", "oy", O["y_p"][0:112, :], yo[16:128, :], R=["yo"])
            elif i < NFULL:
                b.dma("sp", "oy", O["y_p"][128 * i - 16:128 * i + 112, :], yo[:, :], R=["yo"])
            else:
                b.dma("sp", "oy", O["y_p"][SEQ - 16:SEQ, :], yo[0:16, :], R=["yo"])
                b.dma("sp", "oy", O["y_s"][:, :], yo[16:32, :], R=["yo"])

        load_x(0, 0, src_scratch=True)
        for i in range(NTL):
            tile(i)

    if stage >= 4:
        with ExitStack() as pst:
            b.st = pst
            phase_C()
            b.barrier()
            b.emit()
        b.st = b.st0


def kernel(**inputs):
    x_prompt = np.asarray(inputs["x_prompt"])
    x_sample = np.asarray(inputs["x_sample"])
    ncores, seq = x_prompt.shape[0], x_prompt.shape[1]
    cache = np.asarray(inputs["cache_kv"])[0]
    pt = np.asarray(inputs["page_table"]).astype(np.int32)
    npool, npg = cache.shape[0], pt.shape[1]
    cfg = make_cfg(seq, npg, npool, npg * 128)
    import os
    nc, consts = build(cfg, stage=int(os.environ.get('KSTAGE', '99')))
    wsh = dict(g_final=[1, D], g_mix=[1, D], w_in=[D, INC], g_q=[1, NQ], w_uq=[NQ, 768], g_kv=[1, NKV],
               w_uk=[NKV, 512], w_uv=[NKV, 512], w_o_mla=[512, D], mu_shift=[1, RC], w0=[1, RW], w2=[64, RW],
               a0=[1, RW], a2=[64, RW], g2=[128, RW], k_k=[1, RW], k_a=[1, RW], r_k=[1, RW], ln_w=[1, RW],
               ln_b=[1, RW], w_o_rwkv=[RW, D], w_out=[D, D], g_ffn=[1, D], w_up=[D, DFF], w_down=[DFF, D])
    shared = {}
    for n in WNAMES:
        shared[n] = np.ascontiguousarray(np.asarray(inputs[n], dtype=np.float32).reshape(wsh[n]))
    for n, a in consts.items():
        shared["c_" + n] = a
    shared["meta"] = np.ascontiguousarray(np.asarray(inputs["meta_tokens"], dtype=np.float32))
    shared["cache"] = np.ascontiguousarray(cache)
    swkv = np.asarray(inputs["state_wkv"])[0]
    sshift = np.asarray(inputs["state_shift"])[0]
    in_maps = []
    for c in range(ncores):
        m = dict(shared)
        m["xp"] = np.ascontiguousarray(x_prompt[c])
        m["xs"] = np.ascontiguousarray(x_sample[SPC * c:SPC * (c + 1), 0])
        m["pt"] = np.ascontiguousarray(pt[SPC * c:SPC * (c + 1)])
        m["swkv"] = np.ascontiguousarray(swkv[SPC * c:SPC * (c + 1)])
        m["sshift"] = np.ascontiguousarray(sshift[SPC * c:SPC * (c + 1)])
        in_maps.append(m)
    res = run_bass_kernel_spmd(nc, in_maps, core_ids=list(range(ncores)))
    r = res.results
    y_p = np.stack([r[c]["y_p"] for c in range(ncores)])
    y_s = np.concatenate([r[c]["y_s"] for c in range(ncores)])[:, None, :]
    kv_p = np.stack([r[c]["kv_p"] for c in range(ncores)])[None]
    wkv_p = np.stack([r[c]["wkv_p"] for c in range(ncores)])[None]
    sh_p = np.concatenate([r[c]["sh_p"] for c in range(ncores)])[None]
    kv_s = np.concatenate([r[c]["kv_s"] for c in range(ncores)])[None, :, None, :]
    wkv_s = np.concatenate([r[c]["wkv_s"] for c in range(ncores)])[None]
    sh_s = np.concatenate([r[c]["sh_s"] for c in range(ncores)])[None]
    return tuple(np.ascontiguousarray(a, dtype=np.float32) for a in (y_p, y_s, kv_p, wkv_p, sh_p, kv_s, wkv_s, sh_s))
```
